# Optimizing a Trainium2 kernel written in Bass

```python
import jax, jax.numpy as jnp
from jax import lax
import numpy as np

D_MODEL = 2048
BATCH = 4
SEQ = 8192
DEPTH = 4

GRID_W = 64
CTX_LEN = 256
N_MIXERS = 4
HEAD_DIM = 128
N_HEADS = D_MODEL // HEAD_DIM
NA_ROWS = 8
NA_COLS = 16
SWA_KV_HEADS = 2
SWA_WINDOW = 128
BLOCK = 128
GQA_KV_HEADS = 4
ML_HEADS = 4
ML_V_DIM = D_MODEL // ML_HEADS
ML_QK_DIM = ML_V_DIM // 2
ML_CHUNK = 64
ML_FGATE_BIAS = 3.0
D_FF = 5632
CONV_W = 3
ROPE_BASE = 10000.0
NORM_EPS = 1e-6
NEG_INF = -1e30

kernel_name = 'hybrid_diffusion_backbone'


def rms_norm(x, g):
    xf = x.astype(jnp.float32)
    y = xf * lax.rsqrt(jnp.mean(xf * xf, axis=-1, keepdims=True) + NORM_EPS)
    return (y * g.astype(jnp.float32)).astype(x.dtype)


def adaln(cond, w, b):
    return jnp.split(jax.nn.silu(cond) @ w + b, 6, axis=-1)


def modulate(h, shift, scale):
    return h * (1 + scale) + shift


def axial_rope_tables(n_tokens):
    t = jnp.arange(n_tokens)
    row = (t // GRID_W).astype(jnp.float32)
    col = (t % GRID_W).astype(jnp.float32)
    n_freq = HEAD_DIM // 4
    inv = ROPE_BASE ** (-jnp.arange(n_freq, dtype=jnp.float32) / n_freq)
    ang = jnp.stack([row[:, None] * inv, col[:, None] * inv], axis=1)
    return jnp.cos(ang), jnp.sin(ang)


def apply_rope(x, cos, sin):
    B, T, H, Dh = x.shape
    xr = x.astype(jnp.float32).reshape(B, T, H, 2, 2, Dh // 4)
    x1, x2 = xr[..., 0, :], xr[..., 1, :]
    c, s = cos[None, :, None], sin[None, :, None]
    out = jnp.stack([x1 * c - x2 * s, x2 * c + x1 * s], axis=-2)
    return out.reshape(B, T, H, Dh).astype(x.dtype)


def softmax_with_sink(s, sink):
    if sink is None:
        return jax.nn.softmax(s, axis=-1)
    m = jnp.maximum(jnp.max(s, axis=-1, keepdims=True), sink)
    e = jnp.exp(s - m)
    return e / (jnp.sum(e, axis=-1, keepdims=True) + jnp.exp(sink - m))


def multi_source_attention(q, sources, sink=None):
    scale = q.shape[-1] ** -0.5
    scores = []
    for k, _, mask in sources:
        s = jnp.einsum('bqngd,bknd->bngqk', q, k, preferred_element_type=jnp.float32) * scale
        scores.append(s if mask is None else jnp.where(mask, s, NEG_INF))
    p = softmax_with_sink(jnp.concatenate(scores, axis=-1), sink)
    out, start = None, 0
    for (_, v, _), s in zip(sources, scores):
        n = s.shape[-1]
        o = jnp.einsum('bngqk,bknd->bqngd', p[..., start:start + n].astype(v.dtype), v)
        out = o if out is None else out + o
        start += n
    return out


def project_gqa(h, w_qkv, q_g, k_g, n_kv):
    B, T, _ = h.shape
    q, k, v = jnp.split(h @ w_qkv, [N_HEADS * HEAD_DIM, (N_HEADS + n_kv) * HEAD_DIM], axis=-1)
    q = rms_norm(q.reshape(B, T, N_HEADS, HEAD_DIM), q_g)
    k = rms_norm(k.reshape(B, T, n_kv, HEAD_DIM), k_g)
    return q, k, v.reshape(B, T, n_kv, HEAD_DIM)


def group_heads(q, n_kv):
    B, T = q.shape[:2]
    return q.reshape(B, T, n_kv, N_HEADS // n_kv, HEAD_DIM)


def context_self_attention(qc, kc, vc, n_kv, w_o, sink=None):
    B, L = qc.shape[:2]
    o = multi_source_attention(group_heads(qc, n_kv), [(kc, vc, None)], sink)
    return o.reshape(B, L, D_MODEL) @ w_o


def neighbourhood_attention(hx, hc, w_qkv, q_g, k_g, rel_bias, w_o, need_ctx):
    B, S, _ = hx.shape
    rows = S // GRID_W
    kr = min(NA_ROWS, rows)
    q, k, v = project_gqa(hx, w_qkv, q_g, k_g, N_HEADS)
    qc, kc, vc = project_gqa(hc, w_qkv, q_g, k_g, N_HEADS)
    scale = HEAD_DIM ** -0.5
    qg = q.reshape(B, rows, GRID_W, N_HEADS, HEAD_DIM)
    kg = k.reshape(B, rows, GRID_W, N_HEADS, HEAD_DIM)
    vg = v.reshape(B, rows, GRID_W, N_HEADS, HEAD_DIM)
    row_start = jnp.clip(jnp.arange(rows) - kr // 2, 0, rows - kr)
    qcol = jnp.arange(GRID_W)
    col_idx = jnp.clip(qcol - NA_COLS // 2, 0, GRID_W - NA_COLS)[:, None] + jnp.arange(NA_COLS)
    dcol = col_idx - qcol[:, None] + (NA_COLS - 1)
    rb = rel_bias.astype(jnp.float32)
    n_loc = kr * NA_COLS

    def one_row(r):
        r0 = row_start[r]
        k_win = lax.dynamic_slice_in_dim(kg, r0, kr, axis=1)[:, :, col_idx]
        v_win = lax.dynamic_slice_in_dim(vg, r0, kr, axis=1)[:, :, col_idx]
        q_r = lax.dynamic_index_in_dim(qg, r, axis=1, keepdims=False)
        drow = r0 + jnp.arange(kr) - r + (NA_ROWS - 1)
        bias = rb[:, drow[None, :, None], dcol[:, None, :]]
        s_loc = jnp.einsum('bqhd,brqkhd->bhqrk', q_r, k_win, preferred_element_type=jnp.float32) * scale + bias
        s_ctx = jnp.einsum('bqhd,blhd->bhql', q_r, kc, preferred_element_type=jnp.float32) * scale
        p = jax.nn.softmax(jnp.concatenate([s_loc.reshape(B, N_HEADS, GRID_W, n_loc), s_ctx], axis=-1), axis=-1)
        p_loc = p[..., :n_loc].reshape(B, N_HEADS, GRID_W, kr, NA_COLS).astype(v.dtype)
        return (jnp.einsum('bhqrk,brqkhd->bqhd', p_loc, v_win)
                + jnp.einsum('bhql,blhd->bqhd', p[..., n_loc:].astype(v.dtype), vc))

    o = lax.map(one_row, jnp.arange(rows))
    out_x = jnp.moveaxis(o, 0, 1).reshape(B, S, D_MODEL) @ w_o
    out_c = context_self_attention(qc, kc, vc, N_HEADS, w_o) if need_ctx else None
    return out_x, out_c


def sliding_window_attention(hx, hc, w_qkv, q_g, k_g, sinks, w_o, cos, sin, need_ctx):
    B, S, _ = hx.shape
    q, k, v = project_gqa(hx, w_qkv, q_g, k_g, SWA_KV_HEADS)
    qc, kc, vc = project_gqa(hc, w_qkv, q_g, k_g, SWA_KV_HEADS)
    qb = group_heads(apply_rope(q, cos, sin), SWA_KV_HEADS)
    pad = ((0, 0), (BLOCK, BLOCK), (0, 0), (0, 0))
    k_pad = jnp.pad(apply_rope(k, cos, sin), pad)
    v_pad = jnp.pad(v, pad)
    rel = jnp.arange(3 * BLOCK)[None, :] - BLOCK - jnp.arange(BLOCK)[:, None]
    band = jnp.abs(rel) <= SWA_WINDOW
    sink = sinks.astype(jnp.float32).reshape(SWA_KV_HEADS, N_HEADS // SWA_KV_HEADS)[None, :, :, None, None]

    def one_block(i):
        start = i * BLOCK
        q_blk = lax.dynamic_slice_in_dim(qb, start, BLOCK, axis=1)
        k_blk = lax.dynamic_slice_in_dim(k_pad, start, 3 * BLOCK, axis=1)
        v_blk = lax.dynamic_slice_in_dim(v_pad, start, 3 * BLOCK, axis=1)
        kpos = start - BLOCK + jnp.arange(3 * BLOCK)
        mask = band & ((kpos >= 0) & (kpos < S))[None, :]
        return multi_source_attention(q_blk, [(k_blk, v_blk, mask), (kc, vc, None)], sink)

    o = lax.map(one_block, jnp.arange(S // BLOCK))
    out_x = jnp.moveaxis(o, 0, 1).reshape(B, S, D_MODEL) @ w_o
    out_c = context_self_attention(qc, kc, vc, SWA_KV_HEADS, w_o, sink) if need_ctx else None
    return out_x, out_c


def dense_gqa_attention(hx, hc, w_qkv, q_g, k_g, w_o, cos, sin, need_ctx):
    B, S, _ = hx.shape
    q, k, v = project_gqa(hx, w_qkv, q_g, k_g, GQA_KV_HEADS)
    qc, kc, vc = project_gqa(hc, w_qkv, q_g, k_g, GQA_KV_HEADS)
    qb = group_heads(apply_rope(q, cos, sin), GQA_KV_HEADS)
    k = apply_rope(k, cos, sin)

    def one_block(i):
        q_blk = lax.dynamic_slice_in_dim(qb, i * BLOCK, BLOCK, axis=1)
        return multi_source_attention(q_blk, [(k, v, None), (kc, vc, None)])

    o = lax.map(one_block, jnp.arange(S // BLOCK))
    out_x = jnp.moveaxis(o, 0, 1).reshape(B, S, D_MODEL) @ w_o
    out_c = context_self_attention(qc, kc, vc, GQA_KV_HEADS, w_o) if need_ctx else None
    return out_x, out_c


def mlstm_scan(q, k, v, log_i, log_f, state, emit):
    B, T = q.shape[:2]
    nc = T // ML_CHUNK

    def chunks(a):
        return jnp.moveaxis(a.reshape((B, nc, ML_CHUNK) + a.shape[2:]), 1, 0)

    lower = jnp.tril(jnp.ones((ML_CHUNK, ML_CHUNK), dtype=bool))

    def step(carry, xs):
        C, n, m = carry
        qc, kc, vc, lic, lfc = xs
        b = jnp.moveaxis(jnp.cumsum(lfc, axis=1), 1, 2)
        li = jnp.moveaxis(lic, 1, 2)
        out = None
        if emit:
            dmat = jnp.where(lower, b[..., :, None] - b[..., None, :] + li[..., None, :], NEG_INF)
            g = b + m[..., None]
            m_t = jnp.maximum(g, jnp.max(dmat, axis=-1))
            w = jnp.exp(dmat - m_t[..., None]) * jnp.einsum('bthd,bshd->bhts', qc, kc)
            w_prev = jnp.exp(g - m_t)
            num = jnp.einsum('bhts,bshv->bhtv', w, vc) + w_prev[..., None] * jnp.einsum('bthd,bhdv->bhtv', qc, C)
            den = jnp.sum(w, axis=-1) + w_prev * jnp.einsum('bthd,bhd->bht', qc, n)
            h = num / jnp.maximum(jnp.abs(den), jnp.exp(-m_t))[..., None]
            out = jnp.moveaxis(h, 1, 2)
        b_end = b[..., -1]
        lw = b_end[..., None] - b + li
        m_new = jnp.maximum(b_end + m, jnp.max(lw, axis=-1))
        decay = jnp.exp(b_end + m - m_new)
        ws = jnp.exp(lw - m_new[..., None])
        C_new = decay[..., None, None] * C + jnp.einsum('bhs,bshd,bshv->bhdv', ws, kc, vc)
        n_new = decay[..., None] * n + jnp.einsum('bhs,bshd->bhd', ws, kc)
        return (C_new, n_new, m_new), out

    state, hs = lax.scan(step, state, tuple(chunks(a) for a in (q, k, v, log_i, log_f)))
    if emit:
        hs = jnp.moveaxis(hs, 0, 1).reshape(B, T, ML_HEADS, ML_V_DIM)
    return hs, state


def mlstm_mixer(hx, hc, w_in, gate_b, head_g, w_o, need_ctx):
    f32 = jnp.float32
    qk, vd = ML_HEADS * ML_QK_DIM, ML_HEADS * ML_V_DIM

    def project(h):
        B, T, _ = h.shape
        q, k, v, o, gates = jnp.split(h @ w_in, [qk, 2 * qk, 2 * qk + vd, 2 * qk + 2 * vd], axis=-1)
        q = q.reshape(B, T, ML_HEADS, ML_QK_DIM).astype(f32)
        k = k.reshape(B, T, ML_HEADS, ML_QK_DIM).astype(f32) * ML_QK_DIM ** -0.5
        v = v.reshape(B, T, ML_HEADS, ML_V_DIM).astype(f32)
        gates = (gates.astype(f32) + gate_b.astype(f32)).reshape(B, T, 4, ML_HEADS)
        fwd = (gates[:, :, 0], jax.nn.log_sigmoid(gates[:, :, 1]))
        bwd = (gates[:, :, 2], jax.nn.log_sigmoid(gates[:, :, 3]))
        return (q, k, v), o, fwd, bwd

    def flip(a):
        return jnp.flip(a, axis=1)

    def run_both(qkv, fwd, bwd, st_f, st_b, emit):
        h_f, st_f = mlstm_scan(*qkv, *fwd, st_f, emit)
        h_b, st_b = mlstm_scan(*(flip(a) for a in qkv), *(flip(a) for a in bwd), st_b, emit)
        h = (h_f + flip(h_b)) if emit else None
        return h, st_f, st_b

    def read_out(h, o):
        B, T = h.shape[:2]
        hn = rms_norm(h, head_g.reshape(ML_HEADS, ML_V_DIM)).reshape(B, T, vd).astype(o.dtype)
        return (jax.nn.sigmoid(o) * hn) @ w_o

    qkv_x, o_x, fwd_x, bwd_x = project(hx)
    qkv_c, o_c, fwd_c, bwd_c = project(hc)
    B = hx.shape[0]
    init = (jnp.zeros((B, ML_HEADS, ML_QK_DIM, ML_V_DIM), f32),
            jnp.zeros((B, ML_HEADS, ML_QK_DIM), f32),
            jnp.zeros((B, ML_HEADS), f32))
    h_c, st_f, st_b = run_both(qkv_c, fwd_c, bwd_c, init, init, need_ctx)
    h_x, _, _ = run_both(qkv_x, fwd_x, bwd_x, st_f, st_b, True)
    out_x = read_out(h_x, o_x)
    out_c = read_out(h_c, o_c) if need_ctx else None
    return out_x, out_c


def conv_glu(h, w_in, conv_w, conv_b, w_out):
    T = h.shape[1]
    g, u = jnp.split(h @ w_in, 2, axis=-1)
    pad = CONV_W // 2
    gp = jnp.pad(g, ((0, 0), (pad, pad), (0, 0)))
    gc = conv_b
    for j in range(CONV_W):
        gc = gc + gp[:, j:j + T] * conv_w[j]
    return (jax.nn.gelu(gc) * u) @ w_out


def setup_inputs(seed: int = 0) -> dict:
    key = jax.random.key(seed)
    ks = iter(jax.random.split(key, 48))
    f32 = jnp.float32
    D = D_MODEL

    def nrm(shape, scale):
        return scale * jax.random.normal(next(ks), shape, f32)

    def gain(shape):
        return 1.0 + nrm(shape, 0.02)

    nA, nB, nC, nD = (len(range(m, DEPTH, N_MIXERS)) for m in range(N_MIXERS))
    hd = N_HEADS * HEAD_DIM
    w_swa = (N_HEADS + 2 * SWA_KV_HEADS) * HEAD_DIM
    w_gqa = (N_HEADS + 2 * GQA_KV_HEADS) * HEAD_DIM
    w_ml = 2 * ML_HEADS * ML_QK_DIM + 2 * ML_HEADS * ML_V_DIM + 4 * ML_HEADS
    gate_base = jnp.array([0.0, ML_FGATE_BIAS, 0.0, ML_FGATE_BIAS], f32)[None, :, None]
    return {
        'x': nrm((BATCH, SEQ, D), 1.0),
        'c': nrm((BATCH, D), 1.0),
        'ctx': nrm((BATCH, CTX_LEN, D), 1.0),
        'c_ctx': nrm((D,), 1.0),
        'ada_w': nrm((DEPTH, D, 6 * D), 0.5 * D ** -0.5),
        'ada_b': nrm((DEPTH, 6 * D), 0.02),
        'norm1_g': gain((DEPTH, D)),
        'norm2_g': gain((DEPTH, D)),
        'ffn_w_in': nrm((DEPTH, D, 2 * D_FF), D ** -0.5),
        'ffn_conv_w': nrm((DEPTH, CONV_W, D_FF), CONV_W ** -0.5),
        'ffn_conv_b': nrm((DEPTH, D_FF), 0.02),
        'ffn_w_out': nrm((DEPTH, D_FF, D), D_FF ** -0.5),
        'na_w_qkv': nrm((nA, D, 3 * hd), D ** -0.5),
        'na_q_g': gain((nA, HEAD_DIM)),
        'na_k_g': gain((nA, HEAD_DIM)),
        'na_rel_bias': nrm((nA, N_HEADS, 2 * NA_ROWS - 1, 2 * NA_COLS - 1), 0.1),
        'na_w_o': nrm((nA, hd, D), hd ** -0.5),
        'swa_w_qkv': nrm((nB, D, w_swa), D ** -0.5),
        'swa_q_g': gain((nB, HEAD_DIM)),
        'swa_k_g': gain((nB, HEAD_DIM)),
        'swa_sinks': nrm((nB, N_HEADS), 1.0),
        'swa_w_o': nrm((nB, hd, D), hd ** -0.5),
        'ml_w_in': nrm((nC, D, w_ml), D ** -0.5),
        'ml_gate_b': (gate_base + nrm((nC, 4, ML_HEADS), 0.1)).reshape(nC, 4 * ML_HEADS),
        'ml_head_g': gain((nC, ML_HEADS * ML_V_DIM)),
        'ml_w_o': nrm((nC, ML_HEADS * ML_V_DIM, D), (ML_HEADS * ML_V_DIM) ** -0.5),
        'gqa_w_qkv': nrm((nD, D, w_gqa), D ** -0.5),
        'gqa_q_g': gain((nD, HEAD_DIM)),
        'gqa_k_g': gain((nD, HEAD_DIM)),
        'gqa_w_o': nrm((nD, hd, D), hd ** -0.5),
    }


def reference(x, c, ctx, c_ctx, ada_w, ada_b, norm1_g, norm2_g, ffn_w_in, ffn_conv_w, ffn_conv_b, ffn_w_out,
              na_w_qkv, na_q_g, na_k_g, na_rel_bias, na_w_o,
              swa_w_qkv, swa_q_g, swa_k_g, swa_sinks, swa_w_o,
              ml_w_in, ml_gate_b, ml_head_g, ml_w_o,
              gqa_w_qkv, gqa_q_g, gqa_k_g, gqa_w_o):
    S = x.shape[1]
    cos, sin = axial_rope_tables(S)
    xc = ctx
    for i in range(DEPTH):
        kind, j = i % N_MIXERS, i // N_MIXERS
        need_ctx = i < DEPTH - 1
        sh1, sc1, g1, sh2, sc2, g2 = adaln(c[:, None, :], ada_w[i], ada_b[i])
        csh1, csc1, cg1, csh2, csc2, cg2 = adaln(c_ctx[None, :], ada_w[i], ada_b[i])
        hx = modulate(rms_norm(x, norm1_g[i]), sh1, sc1)
        hc = modulate(rms_norm(xc, norm1_g[i]), csh1, csc1)
        if kind == 0:
            ox, oc = neighbourhood_attention(hx, hc, na_w_qkv[j], na_q_g[j], na_k_g[j], na_rel_bias[j], na_w_o[j], need_ctx)
        elif kind == 1:
            ox, oc = sliding_window_attention(hx, hc, swa_w_qkv[j], swa_q_g[j], swa_k_g[j], swa_sinks[j], swa_w_o[j],
                                              cos, sin, need_ctx)
        elif kind == 2:
            ox, oc = mlstm_mixer(hx, hc, ml_w_in[j], ml_gate_b[j], ml_head_g[j], ml_w_o[j], need_ctx)
        else:
            ox, oc = dense_gqa_attention(hx, hc, gqa_w_qkv[j], gqa_q_g[j], gqa_k_g[j], gqa_w_o[j], cos, sin, need_ctx)
        x = x + g1 * ox
        hx = modulate(rms_norm(x, norm2_g[i]), sh2, sc2)
        x = x + g2 * conv_glu(hx, ffn_w_in[i], ffn_conv_w[i], ffn_conv_b[i], ffn_w_out[i])
        if need_ctx:
            xc = xc + cg1 * oc
            hc = modulate(rms_norm(xc, norm2_g[i]), csh2, csc2)
            xc = xc + cg2 * conv_glu(hc, ffn_w_in[i], ffn_conv_w[i], ffn_conv_b[i], ffn_w_out[i])
    return x
```

```python
import math
from contextlib import ExitStack

import ml_dtypes
import numpy as np

import concourse.bass as bass
import concourse.mybir as mybir
from concourse.bass_utils import run_bass_kernel_spmd

F32, BF16 = mybir.dt.float32, mybir.dt.bfloat16
AF = mybir.ActivationFunctionType
ALU = mybir.AluOpType

D = 2048
KC = 16
DFF = 5632
FC = 44
CTX = 256
GW = 64
EPS = 1e-6
NSV = 96 + 16 + 16 + 132 + 44 + 1 + 1 + 1 + 16
SV_ADAB, SV_N1, SV_N2, SV_CW, SV_CB, SV_QG, SV_KG, SV_GB, SV_SINK = 0, 96, 112, 128, 260, 304, 305, 306, 307
GK = 1.5957691216057308
MIX_COLS = {0: 6144, 1: 2560, 2: 6160, 3: 3072}
NKV = {0: 16, 1: 2, 3: 4}


class Tok:
    __slots__ = ("w", "r")

    def __init__(self):
        self.w = {}
        self.r = {}


class KB:
    def __init__(self, nc, es, nring=6):
        self.nc = nc
        self.E = {"pe": nc.tensor, "act": nc.scalar, "dve": nc.vector, "pool": nc.gpsimd, "sp": nc.sync}
        self.csem = {e: es.enter_context(nc.semaphore("c_" + e)) for e in ("pe", "act", "dve", "pool")}
        self.ccnt = {e: 0 for e in self.csem}
        self.NR = nring
        self.dsem = {e: [es.enter_context(nc.semaphore(f"d_{e}{i}")) for i in range(nring)] for e in ("sp", "pool", "act")}
        self.dcnt = {e: [0] * nring for e in self.dsem}
        self.dnext = {e: 0 for e in self.dsem}
        self.waited = {e: {} for e in self.E}
        self.pending = {e: [] for e in self.csem}
        self.ninst = 0

    def _wait(self, e, ev):
        if ev is None:
            return
        sem, val, key = ev
        if key == "cpe" and e == "pe":
            return
        w = self.waited[e]
        if w.get(key, 0) >= val:
            return
        self.E[e].wait_ge(sem, val)
        self.ninst += 1
        w[key] = val

    def _deps(self, e, reads, writes):
        for t in reads:
            for ev in list(t.w.values()):
                self._wait(e, ev)
        for t in writes:
            for ev in list(t.w.values()):
                self._wait(e, ev)
            for ev in list(t.r.values()):
                self._wait(e, ev)

    @staticmethod
    def _commit(ev, reads, writes):
        for t in reads:
            t.r[ev[2]] = ev
        for t in writes:
            t.w[ev[2]] = ev
            t.r = {}

    def op(self, e, fn, reads=(), writes=(), signal=True):
        self._deps(e, reads, writes)
        ins = fn(self.E[e])
        self.ninst += 1
        if signal:
            self.ccnt[e] += 1
            ins.then_inc(self.csem[e], 1)
            ev = (self.csem[e], self.ccnt[e], "c" + e)
            self._commit(ev, reads, writes)
            for (r, w) in self.pending[e]:
                self._commit(ev, r, w)
            self.pending[e] = []
        else:
            self.pending[e].append((tuple(reads), tuple(writes)))
        return ins

    def dma(self, q, out, in_, reads=(), writes=()):
        i = self.dnext[q]
        self.dnext[q] = (i + 1) % self.NR
        sem = self.dsem[q][i]
        key = f"d{q}{i}"
        if self.dcnt[q][i]:
            self._wait(q, (sem, self.dcnt[q][i], key))
        self._deps(q, reads, writes)
        self.E[q].dma_start(out=out, in_=in_).then_inc(sem, 16)
        self.ninst += 1
        self.dcnt[q][i] += 16
        ev = (sem, self.dcnt[q][i], key)
        self._commit(ev, reads, writes)

    def barrier(self, engines=("pe", "act", "dve", "pool", "sp")):
        for e in engines:
            for q in self.dsem:
                for i in range(self.NR):
                    if self.dcnt[q][i]:
                        self._wait(e, (self.dsem[q][i], self.dcnt[q][i], f"d{q}{i}"))
            for c in self.csem:
                if self.ccnt[c] and c != e:
                    self._wait(e, (self.csem[c], self.ccnt[c], "c" + c))


class Prog:
    def __init__(self, S, layers, final_out=True):
        self.S, self.T = S, S + CTX
        self.layers = layers
        self.nc = nc = bass.Bass("TRN2", target_bir_lowering=False)
        self.es = ExitStack()
        self.kb = KB(nc, self.es)
        T = self.T
        di = lambda name, shape, dt=F32: nc.dram_tensor(name, list(shape), dt, kind="ExternalInput").ap()
        ds = lambda name, shape, dt: nc.dram_tensor(name, list(shape), dt).ap()
        self.in_xT = di("xT", [D, S])
        self.in_cxT = di("cxT", [D, CTX])
        self.in_cc = di("cc", [128, KC, 2])
        self.in_ones = di("ones", [128, 128], BF16)
        self.in_ident = di("ident", [128, 128], BF16)
        self.in_rot = di("rot", [128, 128], BF16)
        self.in_cos = di("cosT", [128, S])
        self.in_sin = di("sinT", [128, S])
        self.in_mprev = di("mprev", [128, 128], BF16)
        self.in_mnext = di("mnext", [128, 128], BF16)
        self.W = {}
        self.natab = None
        for (li, kind, _) in layers:
            w = {}
            w["ada"] = di(f"ada_w{li}", [D, 6 * D])
            w["win"] = di(f"ffn_w_in{li}", [D, 2 * DFF])
            w["wout"] = di(f"ffn_w_out{li}", [DFF, D])
            w["mix"] = di(f"mix_w_in{li}", [D, MIX_COLS[kind]])
            w["wo"] = di(f"mix_w_o{li}", [D, D])
            w["sv"] = di(f"sv{li}", [128, NSV])
            for k in ("ada", "win", "wout", "mix", "wo"):
                w[k + "_b"] = ds(f"{k}_b{li}", w[k].shape, BF16)
            if kind == 0:
                self.npat = len(na_patterns(S)[1])
                w["natab"] = di(f"natab{li}", [self.npat * 16 * 128, 5 * 128])
            if kind == 2:
                w["headg"] = di(f"headg{li}", [128, D])
            self.W[li] = w
        self.out = nc.dram_tensor("outT", [D, S], F32, kind="ExternalOutput").ap()
        self.XT = [ds("XT0", [D, T], F32), ds("XT1", [D, T], F32)]
        self.QT = ds("QT", [D, T], BF16)
        self.KT = ds("KT", [D, T], BF16)
        self.V = ds("V", [T, D], BF16)
        self.OT = ds("OT", [D, T], BF16)
        if any(k == 2 for (_, k, _) in layers):
            self.in_identf = di("identf", [128, 128])
            self.in_onesf = di("onesf", [128, 128])
            self.in_tri = di("tri", [64, 128])
            self.Ktok = ds("Ktok", [T, 1024], BF16)
            self.Og = ds("Og", [T, D], BF16)
            self.G = ds("G", [T, 16], F32)
            self.H = [ds("H0", [T, D], F32), ds("H1", [T, D], F32)]
            self.t_Ktok, self.t_Og, self.t_G, self.t_H = Tok(), Tok(), Tok(), [Tok(), Tok()]
        self.t_XT = [Tok(), Tok()]
        self.t_QT, self.t_KT, self.t_V, self.t_OT = Tok(), Tok(), Tok(), Tok()
        self.t_W = Tok()
        es = self.es
        sb = lambda name, shape, dt: es.enter_context(nc.sbuf_tensor(self.uname(name), list(shape), dt))
        self.ps = [es.enter_context(nc.psum_tensor(f"ps{i}", [128, 512], F32)) for i in range(8)]
        self.t_ps = [Tok() for _ in range(8)]
        self.ones = sb("ones_sb", [128, 128], BF16)
        self.ident = sb("ident_sb", [128, 128], BF16)
        self.rot = sb("rot_sb", [128, 128], BF16)
        self.mprev = sb("mprev_sb", [128, 128], BF16)
        self.mnext = sb("mnext_sb", [128, 128], BF16)
        self.t_const = Tok()
        self.MOD, self.SCA1, self.SCA2, self.SV, self.t_mod = {}, {}, {}, {}, {}
        for (li, _, _) in layers:
            self.MOD[li] = sb(f"mod{li}", [128, 96, 2], F32)
            self.SCA1[li] = sb(f"sca1_{li}", [128, KC, 2], F32)
            self.SCA2[li] = sb(f"sca2_{li}", [128, KC, 2], F32)
            self.SV[li] = sb(f"svs{li}", [128, NSV], F32)
            self.t_mod[li] = Tok()

    def uname(self, name):
        self._uid = getattr(self, "_uid", 0) + 1
        return f"s{self._uid}_{name}"

    def wview(self, buf, kc, n):
        return buf[:, 0:kc * n].rearrange("p (k n) -> p k n", n=n)

    def norm_mod(self, pes, xt, t_xt, N, col, SCA, shift_base, li, ht, t_ht, sq, t_sq, tmp, t_tmp, psn, rs, t_rs):
        kb = self.kb
        ps, tps = self.ps[psn], self.t_ps[psn]
        for kc in range(KC):
            b = kc % 2
            kb.op("act", lambda e: e.activation(out=sq[b][:, :N], in_=xt[:, kc, :N], func=AF.Square),
                  reads=[t_xt], writes=[t_sq[b]])
            kb.op("pe", lambda e: e.matmul(ps[:, :N], lhsT=self.ones[:], rhs=sq[b][:, :N], start=(kc == 0), stop=(kc == KC - 1)),
                  reads=[t_sq[b], self.t_const], writes=[tps])
        kb.op("dve", lambda e: e.tensor_scalar(rs[:, :N], ps[:, :N], 1.0 / D, EPS, ALU.mult, ALU.add), reads=[tps], writes=[t_rs])
        kb.op("act", lambda e: e.activation(out=rs[:, :N], in_=rs[:, :N], func=AF.Sqrt), reads=[t_rs], writes=[t_rs])
        kb.op("dve", lambda e: e.reciprocal(rs[:, :N], rs[:, :N]), reads=[t_rs], writes=[t_rs])
        for kc in range(KC):
            b = kc % 2
            kb.op("dve", lambda e: e.tensor_tensor(tmp[b][:, :N], xt[:, kc, :N], rs[:, :N], ALU.mult),
                  reads=[t_xt, t_rs], writes=[t_tmp[b]])
            kb.op("act", lambda e: e.activation(out=ht[:, kc, :N], in_=tmp[b][:, :N], func=AF.Identity,
                                                scale=SCA[:, kc, col:col + 1], bias=self.MOD[li][:, shift_base + kc, col:col + 1]),
                  reads=[t_tmp[b], self.t_mod[li]], writes=[t_ht])

    def phase0(self):
        kb, nc = self.kb, self.nc
        S, T = self.S, self.T
        for (dst, src) in ((self.ones, self.in_ones), (self.ident, self.in_ident), (self.rot, self.in_rot),
                           (self.mprev, self.in_mprev), (self.mnext, self.in_mnext)):
            kb.dma("sp", dst[:], src[:, :], writes=[self.t_const])
        for r in range(0, D, 256):
            kb.dma("sp", self.XT[0][r:r + 256, 0:S], self.in_xT[r:r + 256, :], writes=[self.t_XT[0]])
        kb.dma("sp", self.XT[0][:, S:T], self.in_cxT[:, :], writes=[self.t_XT[0]])
        for (li, kind, _) in self.layers:
            w = self.W[li]
            for k in ("ada", "mix", "wo", "win", "wout"):
                rows = w[k].shape[0]
                for r in range(0, rows, 256):
                    r1 = min(rows, r + 256)
                    kb.dma("pool", w[k + "_b"][r:r1, :], w[k][r:r1, :], writes=[self.t_W])
        with ExitStack() as pes:
            sb = lambda name, shape, dt: pes.enter_context(nc.sbuf_tensor(self.uname(name), list(shape), dt))
            cc = sb("cc", [128, KC, 2], F32)
            scb = sb("scb", [128, KC, 2], BF16)
            t_cc, t_scb = Tok(), Tok()
            wt = [sb(f"p0w{i}", [128, KC * 512], BF16) for i in range(2)]
            t_wt = [Tok(), Tok()]
            tmp = sb("p0tmp", [128, KC, 2], F32)
            t_tmp = Tok()
            kb.dma("sp", cc[:], self.in_cc[:, :, :], writes=[t_cc])
            kb.op("act", lambda e: e.activation(out=scb[:], in_=cc[:], func=AF.Silu), reads=[t_cc], writes=[t_scb])
            g = 0
            for (li, kind, _) in self.layers:
                w = self.W[li]
                kb.dma("sp", self.SV[li][:], w["sv"][:, :], writes=[self.t_mod[li]])
                for cg in range(24):
                    b = g % 2
                    g += 1
                    wv = self.wview(wt[b], KC, 512)
                    kb.dma("sp", wv, w["ada_b"][:, cg * 512:(cg + 1) * 512].rearrange("(kc p) n -> p kc n", p=128),
                           reads=[self.t_W], writes=[t_wt[b]])
                    for j in range(4):
                        c = cg * 4 + j
                        pi = c % 2
                        for kc in range(KC):
                            kb.op("pe", lambda e: e.matmul(self.ps[pi][:, 0:2], lhsT=wv[:, kc, j * 128:(j + 1) * 128], rhs=scb[:, kc, :],
                                                           start=(kc == 0), stop=(kc == KC - 1)),
                                  reads=[t_wt[b], t_scb], writes=[self.t_ps[pi]], signal=(kc == KC - 1))
                        kb.op("act", lambda e: e.activation(out=self.MOD[li][:, c, :], in_=self.ps[pi][:, 0:2], func=AF.Identity,
                                                            bias=self.SV[li][:, SV_ADAB + c:SV_ADAB + c + 1], scale=1.0),
                              reads=[self.t_ps[pi], self.t_mod[li]], writes=[self.t_mod[li]])
                for (SCA, sc0, g0) in ((self.SCA1[li], 16, SV_N1), (self.SCA2[li], 64, SV_N2)):
                    kb.op("dve", lambda e: e.tensor_scalar_add(tmp[:], self.MOD[li][:, sc0:sc0 + 16, :], 1.0),
                          reads=[self.t_mod[li]], writes=[t_tmp])
                    for col in range(2):
                        kb.op("dve", lambda e: e.tensor_tensor(SCA[:, :, col], tmp[:, :, col], self.SV[li][:, g0:g0 + 16], ALU.mult),
                              reads=[t_tmp, self.t_mod[li]], writes=[self.t_mod[li]])
            kb.barrier()

    def phaseA(self, li, kind, xi):
        kb, nc = self.kb, self.nc
        S, T = self.S, self.T
        w = self.W[li]
        nq, nkv = 16, NKV[kind]
        rope = kind in (1, 3)
        XT, t_XT = self.XT[xi], self.t_XT[xi]
        tiles = [(t0, min(512, S - t0), 0) for t0 in range(0, S, 512)] + [(S, CTX, 1)]
        with ExitStack() as pes:
            sb = lambda name, shape, dt: pes.enter_context(nc.sbuf_tensor(self.uname(name), list(shape), dt))
            xt = sb("a_xt", [128, KC, 512], F32); t_xt = Tok()
            ht = sb("a_ht", [128, KC, 512], BF16); t_ht = Tok()
            sq = [sb(f"a_sq{i}", [128, 512], BF16) for i in range(2)]; t_sq = [Tok(), Tok()]
            tmp = [sb(f"a_tmp{i}", [128, 512], F32) for i in range(2)]; t_tmp = [Tok(), Tok()]
            rs = sb("a_rs", [128, 512], F32); t_rs = Tok()
            wt = [sb(f"a_w{i}", [128, KC * 512], BF16) for i in range(2)]; t_wt = [Tok(), Tok()]
            cs = [sb("a_cos", [128, 512], F32), sb("a_sin", [128, 512], F32)]; t_cs = Tok()
            sq2 = [sb(f"a_sq2{i}", [128, 512], BF16) for i in range(2)]; t_sq2 = [Tok(), Tok()]
            r2 = [sb(f"a_r2{i}", [128, 512], F32) for i in range(2)]; t_r2 = [Tok(), Tok()]
            qn = [sb(f"a_qn{i}", [128, 512], BF16) for i in range(2)]; t_qn = [Tok(), Tok()]
            t1 = [sb(f"a_t1{i}", [128, 512], F32) for i in range(2)]; t_t1 = [Tok(), Tok()]
            t2 = [sb(f"a_t2{i}", [128, 512], F32) for i in range(2)]; t_t2 = [Tok(), Tok()]
            qo = [sb(f"a_qo{i}", [128, 512], BF16) for i in range(2)]; t_qo = [Tok(), Tok()]
            vt = [sb(f"a_vt{i}", [128, 512], BF16) for i in range(2)]; t_vt = [Tok(), Tok()]
            gs = sb("a_gs", [128, 2], F32); t_gs = Tok()
            kb.op("dve", lambda e: e.tensor_scalar_mul(gs[:, 0:1], self.SV[li][:, SV_QG:SV_QG + 1], 128.0 ** -0.5),
                  reads=[self.t_mod[li]], writes=[t_gs])
            kb.op("dve", lambda e: e.tensor_copy(gs[:, 1:2], self.SV[li][:, SV_KG:SV_KG + 1]), reads=[self.t_mod[li]], writes=[t_gs])
            wg = 0
            cnt = 0
            for (t0, N, col) in tiles:
                kb.dma("sp", xt[:, :, :N], XT[:, t0:t0 + N].rearrange("(kc p) n -> p kc n", p=128), reads=[t_XT], writes=[t_xt])
                if rope and col == 0:
                    kb.dma("sp", cs[0][:, :N], self.in_cos[:, t0:t0 + N], writes=[t_cs])
                    kb.dma("sp", cs[1][:, :N], self.in_sin[:, t0:t0 + N], writes=[t_cs])
                self.norm_mod(pes, xt, t_xt, N, col, self.SCA1[li], 0, li, ht, t_ht, sq, t_sq, tmp, t_tmp, 7, rs, t_rs)
                nqk = nq + nkv
                for cg in range((nqk + 3) // 4):
                    b = wg % 2
                    wg += 1
                    nch = min(4, nqk - cg * 4)
                    wv = self.wview(wt[b], KC, 512)
                    kb.dma("sp", wv[:, :, :nch * 128], w["mix_b"][:, cg * 512:cg * 512 + nch * 128].rearrange("(kc p) n -> p kc n", p=128),
                           reads=[self.t_W], writes=[t_wt[b]])
                    for j in range(nch):
                        c = cg * 4 + j
                        isq = c < nq
                        pi = cnt % 2
                        sb_ = cnt % 2
                        cnt += 1
                        psq, tpsq = self.ps[pi], self.t_ps[pi]
                        for kc in range(KC):
                            kb.op("pe", lambda e: e.matmul(psq[:, :N], lhsT=wv[:, kc, j * 128:(j + 1) * 128], rhs=ht[:, kc, :N],
                                                           start=(kc == 0), stop=(kc == KC - 1)),
                                  reads=[t_wt[b], t_ht], writes=[tpsq], signal=(kc == KC - 1))
                        kb.op("act", lambda e: e.activation(out=sq2[sb_][:, :N], in_=psq[:, :N], func=AF.Square), reads=[tpsq], writes=[t_sq2[sb_]])
                        pss, tpss = self.ps[2 + pi], self.t_ps[2 + pi]
                        kb.op("pe", lambda e: e.matmul(pss[:, :N], lhsT=self.ones[:], rhs=sq2[sb_][:, :N], start=True, stop=True),
                              reads=[t_sq2[sb_], self.t_const], writes=[tpss])
                        kb.op("dve", lambda e: e.tensor_scalar(r2[sb_][:, :N], pss[:, :N], 1.0 / 128, EPS, ALU.mult, ALU.add),
                              reads=[tpss], writes=[t_r2[sb_]])
                        kb.op("act", lambda e: e.activation(out=r2[sb_][:, :N], in_=r2[sb_][:, :N], func=AF.Sqrt),
                              reads=[t_r2[sb_]], writes=[t_r2[sb_]])
                        kb.op("dve", lambda e: e.reciprocal(r2[sb_][:, :N], r2[sb_][:, :N]), reads=[t_r2[sb_]], writes=[t_r2[sb_]])
                        gcol = gs[:, 0:1] if isq else gs[:, 1:2]
                        dstT = self.QT if isq else self.KT
                        t_dst = self.t_QT if isq else self.t_KT
                        row0 = (c if isq else c - nq) * 128
                        if rope and col == 0:
                            kb.op("dve", lambda e: e.scalar_tensor_tensor(qn[sb_][:, :N], psq[:, :N], gcol, r2[sb_][:, :N], ALU.mult, ALU.mult),
                                  reads=[tpsq, t_r2[sb_], t_gs], writes=[t_qn[sb_]])
                            psr, tpsr = self.ps[4 + pi], self.t_ps[4 + pi]
                            kb.op("pe", lambda e: e.matmul(psr[:, :N], lhsT=self.rot[:], rhs=qn[sb_][:, :N], start=True, stop=True),
                                  reads=[t_qn[sb_], self.t_const], writes=[tpsr])
                            kb.op("pool", lambda e: e.tensor_tensor(t1[sb_][:, :N], qn[sb_][:, :N], cs[0][:, :N], ALU.mult),
                                  reads=[t_qn[sb_], t_cs], writes=[t_t1[sb_]])
                            kb.op("dve", lambda e: e.tensor_tensor(t2[sb_][:, :N], psr[:, :N], cs[1][:, :N], ALU.mult),
                                  reads=[tpsr, t_cs], writes=[t_t2[sb_]])
                            kb.op("dve", lambda e: e.tensor_tensor(qo[sb_][:, :N], t1[sb_][:, :N], t2[sb_][:, :N], ALU.add),
                                  reads=[t_t1[sb_], t_t2[sb_]], writes=[t_qo[sb_]])
                        else:
                            kb.op("dve", lambda e: e.scalar_tensor_tensor(qo[sb_][:, :N], psq[:, :N], gcol, r2[sb_][:, :N], ALU.mult, ALU.mult),
                                  reads=[tpsq, t_r2[sb_], t_gs], writes=[t_qo[sb_]])
                        kb.dma("pool", dstT[row0:row0 + 128, t0:t0 + N], qo[sb_][:, :N], reads=[t_qo[sb_]], writes=[t_dst])
                vbase = nqk * 128
                vw = nkv * 128
                for vg in range((vw + 511) // 512):
                    b = wg % 2
                    wg += 1
                    wd = min(512, vw - vg * 512)
                    wv = self.wview(wt[b], KC, 512)
                    kb.dma("sp", wv[:, :, :wd], w["mix_b"][:, vbase + vg * 512:vbase + vg * 512 + wd].rearrange("(kc p) n -> p kc n", p=128),
                           reads=[self.t_W], writes=[t_wt[b]])
                    for tb in range(N // 128):
                        pi = 6
                        vb = cnt % 2
                        cnt += 1
                        for kc in range(KC):
                            kb.op("pe", lambda e: e.matmul(self.ps[pi][:, :wd], lhsT=ht[:, kc, tb * 128:(tb + 1) * 128], rhs=wv[:, kc, :wd],
                                                           start=(kc == 0), stop=(kc == KC - 1)),
                                  reads=[t_wt[b], t_ht], writes=[self.t_ps[pi]], signal=(kc == KC - 1))
                        kb.op("act", lambda e: e.activation(out=vt[vb][:, :wd], in_=self.ps[pi][:, :wd], func=AF.Copy),
                              reads=[self.t_ps[pi]], writes=[t_vt[vb]])
                        kb.dma("pool", self.V[t0 + tb * 128:t0 + (tb + 1) * 128, vg * 512:vg * 512 + wd], vt[vb][:, :wd],
                               reads=[t_vt[vb]], writes=[self.t_V])
            kb.barrier()

    def phaseB(self, li, kind, need_ctx):
        kb, nc = self.kb, self.nc
        S, T = self.S, self.T
        w = self.W[li]
        nkv = NKV[kind]
        grp = 16 // nkv
        TB = T // 128
        SB = S // 128
        ctxb = [SB, SB + 1]
        sink = kind == 1
        with ExitStack() as pes:
            sb = lambda name, shape, dt: pes.enter_context(nc.sbuf_tensor(self.uname(name), list(shape), dt))
            KTh = sb("b_kt", [128, T], BF16); t_k = Tok()
            Vh = sb("b_v", [128, TB, 128], BF16); t_v = Tok()
            QTh = [sb(f"b_qt{i}", [128, T], BF16) for i in range(2)]; t_q = [Tok(), Tok()]
            pt = [sb(f"b_pt{i}", [128, 512], BF16) for i in range(3)]; t_pt = [Tok() for _ in range(3)]
            rec = sb("b_rec", [128, 512], F32); t_rec = Tok()
            ot = [sb(f"b_ot{i}", [128, 512], BF16) for i in range(2)]; t_ot = [Tok(), Tok()]
            esink = sb("b_esink", [128, 16], F32); t_es = Tok()
            if kind == 0:
                npat = self.npat
                tab = sb("b_tab", [128, npat, 5 * 128], F32); t_tab = Tok()
                etab = sb("b_etab", [128, npat, 5 * 128], BF16); t_etab = Tok()
                pat_of_R, pats = na_patterns(S)
            if sink:
                kb.op("act", lambda e: e.activation(out=esink[:], in_=self.SV[li][:, SV_SINK:SV_SINK + 16], func=AF.Exp),
                      reads=[self.t_mod[li]], writes=[t_es])
            chunks = []
            if kind == 3:
                for q0 in range(0, S, 512):
                    chunks.append((q0, min(512, S - q0), [(b, None) for b in range(TB)]))
            elif kind == 1:
                for i in range(SB):
                    bl = []
                    if i > 0:
                        bl.append((i - 1, "prev"))
                    bl.append((i, None))
                    if i + 1 < SB:
                        bl.append((i + 1, "next"))
                    bl += [(b, None) for b in ctxb]
                    chunks.append((i * 128, 128, bl))
            else:
                for R in range(SB):
                    p = pat_of_R[R]
                    bb = na_base_block(R, SB)
                    bl = [(bb + j, ("na", p, j)) for j in range(5)] + [(b, None) for b in ctxb]
                    chunks.append((R * 128, 128, bl))
            if need_ctx:
                chunks.append((S, CTX, [(b, None) for b in ctxb]))
            cnt = 0
            qi = 0
            for kvh in range(nkv):
                kb.dma("sp", KTh[:], self.KT[kvh * 128:(kvh + 1) * 128, :], reads=[self.t_KT], writes=[t_k])
                kb.dma("sp", Vh[:], self.V[:, kvh * 128:(kvh + 1) * 128].rearrange("(tb p) d -> p tb d", p=128), reads=[self.t_V], writes=[t_v])
                for qh in range(kvh * grp, (kvh + 1) * grp):
                    qb = qi % 2
                    qi += 1
                    kb.dma("sp", QTh[qb][:], self.QT[qh * 128:(qh + 1) * 128, :], reads=[self.t_QT], writes=[t_q[qb]])
                    if kind == 0:
                        for p in range(npat):
                            r0 = (p * 16 + qh) * 128
                            kb.dma("sp", tab[:, p, :], w["natab"][r0:r0 + 128, :], writes=[t_tab])
                        kb.op("act", lambda e: e.activation(out=etab[:], in_=tab[:], func=AF.Exp), reads=[t_tab], writes=[t_etab])
                    for (q0, N, blocks) in chunks:
                        po, tpo = self.ps[0 + (cnt % 2)], self.t_ps[0 + (cnt % 2)]
                        pd, tpd = self.ps[2 + (cnt % 2)], self.t_ps[2 + (cnt % 2)]
                        ob = cnt % 2
                        cnt += 1
                        nb = len(blocks)

                        def emit_s(i):
                            kbk, mask = blocks[i]
                            psn = 4 + (i % 3)
                            pb = i % 3
                            kb.op("pe", lambda e: e.matmul(self.ps[psn][:, :N], lhsT=KTh[:, kbk * 128:(kbk + 1) * 128], rhs=QTh[qb][:, q0:q0 + N],
                                                           start=True, stop=True),
                                  reads=[t_k, t_q[qb]], writes=[self.t_ps[psn]])
                            kb.op("act", lambda e: e.activation(out=pt[pb][:, :N], in_=self.ps[psn][:, :N], func=AF.Exp),
                                  reads=[self.t_ps[psn]], writes=[t_pt[pb]])
                            if mask is not None:
                                if mask == "prev":
                                    m, tm = self.mprev[:, :], self.t_const
                                elif mask == "next":
                                    m, tm = self.mnext[:, :], self.t_const
                                else:
                                    m, tm = etab[:, mask[1], mask[2] * 128:(mask[2] + 1) * 128], t_etab
                                kb.op("dve", lambda e: e.tensor_tensor(pt[pb][:, :N], pt[pb][:, :N], m, ALU.mult),
                                      reads=[t_pt[pb], tm], writes=[t_pt[pb]])

                        def emit_pv(i):
                            kbk, _ = blocks[i]
                            pb = i % 3
                            kb.op("pe", lambda e: e.matmul(po[:, :N], lhsT=Vh[:, kbk, :], rhs=pt[pb][:, :N], start=(i == 0), stop=(i == nb - 1)),
                                  reads=[t_v, t_pt[pb]], writes=[tpo], signal=(i == nb - 1))
                            kb.op("pe", lambda e: e.matmul(pd[:, :N], lhsT=self.ones[:], rhs=pt[pb][:, :N], start=(i == 0), stop=(i == nb - 1)),
                                  reads=[self.t_const, t_pt[pb]], writes=[tpd], signal=True)

                        for i in range(nb):
                            emit_s(i)
                            if i > 0:
                                emit_pv(i - 1)
                        emit_pv(nb - 1)
                        if sink:
                            kb.op("dve", lambda e: e.tensor_scalar_add(rec[:, :N], pd[:, :N], esink[:, qh:qh + 1]), reads=[tpd, t_es], writes=[t_rec])
                            kb.op("dve", lambda e: e.reciprocal(rec[:, :N], rec[:, :N]), reads=[t_rec], writes=[t_rec])
                        else:
                            kb.op("dve", lambda e: e.reciprocal(rec[:, :N], pd[:, :N]), reads=[tpd], writes=[t_rec])
                        kb.op("dve", lambda e: e.tensor_tensor(ot[ob][:, :N], po[:, :N], rec[:, :N], ALU.mult), reads=[tpo, t_rec], writes=[t_ot[ob]])
                        kb.dma("pool", self.OT[qh * 128:(qh + 1) * 128, q0:q0 + N], ot[ob][:, :N], reads=[t_ot[ob]], writes=[self.t_OT])
            kb.barrier()

    def phaseC(self, li, need_ctx, xi, final):
        kb, nc = self.kb, self.nc
        S, T = self.S, self.T
        w = self.W[li]
        XT, t_XT = self.XT[xi], self.t_XT[xi]
        XO, t_XO = self.XT[1 - xi], self.t_XT[1 - xi]
        wins = []
        for w0 in range(0, S, 510):
            n = min(510, S - w0)
            wins.append((w0, n, w0 > 0, w0 + n < S, 0))
        if need_ctx:
            wins.append((S, CTX, False, False, 1))
        with ExitStack() as pes:
            sb = lambda name, shape, dt: pes.enter_context(nc.sbuf_tensor(self.uname(name), list(shape), dt))
            xt = sb("c_xt", [128, KC, 512], F32); t_xt = Tok()
            ab = sb("c_ab", [128, KC, 512], BF16); t_ab = Tok()
            act = sb("c_act", [128, FC, 512], BF16); t_act = Tok()
            wt = [sb(f"c_w{i}", [128, FC * 256], BF16) for i in range(3)]; t_wt = [Tok() for _ in range(3)]
            sq = [sb(f"c_sq{i}", [128, 512], BF16) for i in range(2)]; t_sq = [Tok(), Tok()]
            tmp = [sb(f"c_tmp{i}", [128, 512], F32) for i in range(2)]; t_tmp = [Tok(), Tok()]
            rs = sb("c_rs", [128, 512], F32); t_rs = Tok()
            ga = [sb(f"c_ga{i}", [128, 512], F32) for i in range(2)]; t_ga = [Tok(), Tok()]
            gq = [sb(f"c_gq{i}", [128, 512], F32) for i in range(2)]; t_gq = [Tok(), Tok()]
            gz = [sb(f"c_gz{i}", [128, 512], F32) for i in range(2)]; t_gz = [Tok(), Tok()]
            gy = [sb(f"c_gy{i}", [128, 512], F32) for i in range(2)]; t_gy = [Tok(), Tok()]
            xo = [sb(f"c_xo{i}", [128, 512], F32) for i in range(2)]; t_xo = [Tok(), Tok()]
            SVl = self.SV[li]
            MOD = self.MOD[li]
            wg = 0
            cnt = 0
            for (w0, n, left, right, col) in wins:
                a0 = w0 - (1 if left else 0)
                N = n + (1 if left else 0) + (1 if right else 0)
                lo = 1 if left else 0
                kb.dma("sp", ab[:, :, :N], self.OT[:, a0:a0 + N].rearrange("(kc p) n -> p kc n", p=128), reads=[self.t_OT], writes=[t_ab])
                kb.dma("sp", xt[:, :, :N], XT[:, a0:a0 + N].rearrange("(kc p) n -> p kc n", p=128), reads=[t_XT], writes=[t_xt])
                for ng in range(4):
                    b = wg % 3
                    wg += 1
                    wv = self.wview(wt[b], KC, 512)
                    kb.dma("sp", wv, w["wo_b"][:, ng * 512:(ng + 1) * 512].rearrange("(kc p) n -> p kc n", p=128), reads=[self.t_W], writes=[t_wt[b]])
                    for j in range(4):
                        c = ng * 4 + j
                        pi = cnt % 2
                        cnt += 1
                        for kc in range(KC):
                            kb.op("pe", lambda e: e.matmul(self.ps[pi][:, :N], lhsT=wv[:, kc, j * 128:(j + 1) * 128], rhs=ab[:, kc, :N],
                                                           start=(kc == 0), stop=(kc == KC - 1)),
                                  reads=[t_wt[b], t_ab], writes=[self.t_ps[pi]], signal=(kc == KC - 1))
                        kb.op("dve", lambda e: e.scalar_tensor_tensor(xt[:, c, :N], self.ps[pi][:, :N], MOD[:, 32 + c, col:col + 1], xt[:, c, :N],
                                                                      ALU.mult, ALU.add),
                              reads=[self.t_ps[pi], t_xt, self.t_mod[li]], writes=[t_xt])
                self.norm_mod(pes, xt, t_xt, N, col, self.SCA2[li], 48, li, ab, t_ab, sq, t_sq, tmp, t_tmp, 7, rs, t_rs)
                for cg in range(FC // 4):
                    bg = wg % 3
                    wg += 1
                    bu = wg % 3
                    wg += 1
                    wvg = self.wview(wt[bg], KC, 512)
                    wvu = self.wview(wt[bu], KC, 512)
                    kb.dma("sp", wvg, w["win_b"][:, cg * 512:(cg + 1) * 512].rearrange("(kc p) n -> p kc n", p=128), reads=[self.t_W], writes=[t_wt[bg]])
                    kb.dma("sp", wvu, w["win_b"][:, DFF + cg * 512:DFF + (cg + 1) * 512].rearrange("(kc p) n -> p kc n", p=128),
                           reads=[self.t_W], writes=[t_wt[bu]])
                    for j in range(4):
                        c = cg * 4 + j
                        eb = cnt % 2
                        cnt += 1
                        pg, tpg = self.ps[2 + eb], self.t_ps[2 + eb]
                        pu, tpu = self.ps[4 + eb], self.t_ps[4 + eb]
                        for kc in range(KC):
                            kb.op("pe", lambda e: e.matmul(pg[:, :N], lhsT=wvg[:, kc, j * 128:(j + 1) * 128], rhs=ab[:, kc, :N],
                                                           start=(kc == 0), stop=(kc == KC - 1)),
                                  reads=[t_wt[bg], t_ab], writes=[tpg], signal=(kc == KC - 1))
                        for kc in range(KC):
                            kb.op("pe", lambda e: e.matmul(pu[:, :N], lhsT=wvu[:, kc, j * 128:(j + 1) * 128], rhs=ab[:, kc, :N],
                                                           start=(kc == 0), stop=(kc == KC - 1)),
                                  reads=[t_wt[bu], t_ab], writes=[tpu], signal=(kc == KC - 1))
                        cw = lambda jj: SVl[:, SV_CW + jj * FC + c:SV_CW + jj * FC + c + 1]
                        a_, ta_ = ga[eb], t_ga[eb]
                        kb.op("act", lambda e: e.activation(out=a_[:, :n], in_=pg[:, lo:lo + n], func=AF.Identity, scale=cw(1),
                                                            bias=SVl[:, SV_CB + c:SV_CB + c + 1]),
                              reads=[tpg, self.t_mod[li]], writes=[ta_])
                        if left:
                            kb.op("dve", lambda e: e.scalar_tensor_tensor(a_[:, :n], pg[:, lo - 1:lo - 1 + n], cw(0), a_[:, :n], ALU.mult, ALU.add),
                                  reads=[tpg, ta_, self.t_mod[li]], writes=[ta_])
                        else:
                            kb.op("dve", lambda e: e.scalar_tensor_tensor(a_[:, 1:n], pg[:, lo:lo + n - 1], cw(0), a_[:, 1:n], ALU.mult, ALU.add),
                                  reads=[tpg, ta_, self.t_mod[li]], writes=[ta_])
                        if right:
                            kb.op("dve", lambda e: e.scalar_tensor_tensor(a_[:, :n], pg[:, lo + 1:lo + 1 + n], cw(2), a_[:, :n], ALU.mult, ALU.add),
                                  reads=[tpg, ta_, self.t_mod[li]], writes=[ta_])
                        else:
                            kb.op("dve", lambda e: e.scalar_tensor_tensor(a_[:, :n - 1], pg[:, lo + 1:lo + n], cw(2), a_[:, :n - 1], ALU.mult, ALU.add),
                                  reads=[tpg, ta_, self.t_mod[li]], writes=[ta_])
                        kb.op("act", lambda e: e.activation(out=gq[eb][:, :n], in_=a_[:, :n], func=AF.Square), reads=[ta_], writes=[t_gq[eb]])
                        kb.op("dve", lambda e: e.tensor_scalar(gq[eb][:, :n], gq[eb][:, :n], 0.044715 * GK, GK, ALU.mult, ALU.add),
                              reads=[t_gq[eb]], writes=[t_gq[eb]])
                        kb.op("pool", lambda e: e.tensor_tensor(gz[eb][:, :n], gq[eb][:, :n], a_[:, :n], ALU.mult), reads=[t_gq[eb], ta_], writes=[t_gz[eb]])
                        kb.op("act", lambda e: e.activation(out=gz[eb][:, :n], in_=gz[eb][:, :n], func=AF.Sigmoid), reads=[t_gz[eb]], writes=[t_gz[eb]])
                        kb.op("pool", lambda e: e.tensor_tensor(gy[eb][:, :n], gz[eb][:, :n], a_[:, :n], ALU.mult), reads=[t_gz[eb], ta_], writes=[t_gy[eb]])
                        kb.op("dve", lambda e: e.tensor_tensor(act[:, c, :n], gy[eb][:, :n], pu[:, lo:lo + n], ALU.mult),
                              reads=[t_gy[eb], tpu], writes=[t_act])
                for ng in range(8):
                    b = wg % 3
                    wg += 1
                    wv = self.wview(wt[b], FC, 256)
                    kb.dma("sp", wv, w["wout_b"][:, ng * 256:(ng + 1) * 256].rearrange("(kc p) n -> p kc n", p=128), reads=[self.t_W], writes=[t_wt[b]])
                    for j in range(2):
                        c = ng * 2 + j
                        pi = cnt % 2
                        ob = cnt % 2
                        cnt += 1
                        for kc in range(FC):
                            kb.op("pe", lambda e: e.matmul(self.ps[pi][:, :n], lhsT=wv[:, kc, j * 128:(j + 1) * 128], rhs=act[:, kc, :n],
                                                           start=(kc == 0), stop=(kc == FC - 1)),
                                  reads=[t_wt[b], t_act], writes=[self.t_ps[pi]], signal=(kc == FC - 1))
                        kb.op("dve", lambda e: e.scalar_tensor_tensor(xo[ob][:, :n], self.ps[pi][:, :n], MOD[:, 80 + c, col:col + 1], xt[:, c, lo:lo + n],
                                                                      ALU.mult, ALU.add),
                              reads=[self.t_ps[pi], t_xt, self.t_mod[li]], writes=[t_xo[ob]])
                        if final:
                            if col == 0:
                                kb.dma("pool", self.out[c * 128:(c + 1) * 128, w0:w0 + n], xo[ob][:, :n], reads=[t_xo[ob]])
                        else:
                            kb.dma("pool", XO[c * 128:(c + 1) * 128, w0:w0 + n], xo[ob][:, :n], reads=[t_xo[ob]], writes=[t_XO])
            kb.barrier()

    def phaseA_ml(self, li, xi):
        kb, nc = self.kb, self.nc
        S, T = self.S, self.T
        w = self.W[li]
        XT, t_XT = self.XT[xi], self.t_XT[xi]
        tiles = [(t0, min(512, S - t0), 0) for t0 in range(0, S, 512)] + [(S, CTX, 1)]
        with ExitStack() as pes:
            sb = lambda name, shape, dt: pes.enter_context(nc.sbuf_tensor(self.uname(name), list(shape), dt))
            xt = sb("a_xt", [128, KC, 512], F32); t_xt = Tok()
            ht = sb("a_ht", [128, KC, 512], BF16); t_ht = Tok()
            sq = [sb(f"a_sq{i}", [128, 512], BF16) for i in range(2)]; t_sq = [Tok(), Tok()]
            tmp = [sb(f"a_tmp{i}", [128, 512], F32) for i in range(2)]; t_tmp = [Tok(), Tok()]
            rs = sb("a_rs", [128, 512], F32); t_rs = Tok()
            wt = [sb(f"a_w{i}", [128, KC * 512], BF16) for i in range(2)]; t_wt = [Tok(), Tok()]
            qo = [sb(f"a_qo{i}", [128, 512], BF16) for i in range(2)]; t_qo = [Tok(), Tok()]
            vt = [sb(f"a_vt{i}", [128, 512], BF16) for i in range(2)]; t_vt = [Tok(), Tok()]
            gt = [sb(f"a_gt{i}", [128, 16], F32) for i in range(2)]; t_gt = [Tok(), Tok()]
            ge = [sb(f"a_ge{i}", [128, 8], F32) for i in range(2)]; t_ge = [Tok(), Tok()]
            SVl = self.SV[li]
            wg = 0
            cnt = 0
            for (t0, N, col) in tiles:
                kb.dma("sp", xt[:, :, :N], XT[:, t0:t0 + N].rearrange("(kc p) n -> p kc n", p=128), reads=[t_XT], writes=[t_xt])
                self.norm_mod(pes, xt, t_xt, N, col, self.SCA1[li], 0, li, ht, t_ht, sq, t_sq, tmp, t_tmp, 7, rs, t_rs)
                for cg in range(4):
                    b = wg % 2
                    wg += 1
                    wv = self.wview(wt[b], KC, 512)
                    kb.dma("sp", wv, w["mix_b"][:, cg * 512:(cg + 1) * 512].rearrange("(kc p) n -> p kc n", p=128), reads=[self.t_W], writes=[t_wt[b]])
                    for j in range(4):
                        c = cg * 4 + j
                        pi = cnt % 2
                        ob = cnt % 2
                        cnt += 1
                        for kc in range(KC):
                            kb.op("pe", lambda e: e.matmul(self.ps[pi][:, :N], lhsT=wv[:, kc, j * 128:(j + 1) * 128], rhs=ht[:, kc, :N],
                                                           start=(kc == 0), stop=(kc == KC - 1)),
                                  reads=[t_wt[b], t_ht], writes=[self.t_ps[pi]], signal=(kc == KC - 1))
                        isq = c < 8
                        kb.op("act", lambda e: e.activation(out=qo[ob][:, :N], in_=self.ps[pi][:, :N], func=AF.Copy, scale=(1.0 if isq else 0.0625)),
                              reads=[self.t_ps[pi]], writes=[t_qo[ob]])
                        dstT, t_dst = (self.QT, self.t_QT) if isq else (self.KT, self.t_KT)
                        row0 = (c if isq else c - 8) * 128
                        kb.dma("pool", dstT[row0:row0 + 128, t0:t0 + N], qo[ob][:, :N], reads=[t_qo[ob]], writes=[t_dst])
                groups = [(1024 + g * 512, 512, "k", g * 512) for g in range(2)] + [(2048 + g * 512, 512, "v", g * 512) for g in range(4)] \
                    + [(4096 + g * 512, 512, "o", g * 512) for g in range(4)] + [(6144, 16, "g", 0)]
                for (c0, wd, what, d0) in groups:
                    b = wg % 2
                    wg += 1
                    wv = self.wview(wt[b], KC, 512)
                    kb.dma("sp", wv[:, :, :wd], w["mix_b"][:, c0:c0 + wd].rearrange("(kc p) n -> p kc n", p=128), reads=[self.t_W], writes=[t_wt[b]])
                    for tb in range(N // 128):
                        pi = 2 + cnt % 2
                        vb = cnt % 2
                        cnt += 1
                        r0 = t0 + tb * 128
                        for kc in range(KC):
                            kb.op("pe", lambda e: e.matmul(self.ps[pi][:, :wd], lhsT=ht[:, kc, tb * 128:(tb + 1) * 128], rhs=wv[:, kc, :wd],
                                                           start=(kc == 0), stop=(kc == KC - 1)),
                                  reads=[t_wt[b], t_ht], writes=[self.t_ps[pi]], signal=(kc == KC - 1))
                        if what == "g":
                            g_, tg_ = gt[vb], t_gt[vb]
                            kb.op("dve", lambda e: e.tensor_tensor(g_[:, :], self.ps[pi][:, :16], SVl[:, SV_SINK:SV_SINK + 16], ALU.add),
                                  reads=[self.t_ps[pi], self.t_mod[li]], writes=[tg_])
                            e_, te_ = ge[vb], t_ge[vb]
                            for (src, dst) in ((4, 0), (12, 4)):
                                kb.op("act", lambda e: e.activation(out=e_[:, dst:dst + 4], in_=g_[:, src:src + 4], func=AF.Exp, scale=-1.0),
                                      reads=[tg_], writes=[te_])
                            kb.op("dve", lambda e: e.tensor_scalar_add(e_[:, :], e_[:, :], 1.0), reads=[te_], writes=[te_])
                            kb.op("act", lambda e: e.activation(out=e_[:, :], in_=e_[:, :], func=AF.Ln), reads=[te_], writes=[te_])
                            for (src, dst) in ((0, 4), (4, 12)):
                                kb.op("dve", lambda e: e.tensor_scalar_mul(g_[:, dst:dst + 4], e_[:, src:src + 4], -1.0), reads=[te_, tg_], writes=[tg_])
                            kb.dma("pool", self.G[r0:r0 + 128, :], g_[:, :], reads=[tg_], writes=[self.t_G])
                        else:
                            kb.op("act", lambda e: e.activation(out=vt[vb][:, :wd], in_=self.ps[pi][:, :wd], func=AF.Copy,
                                                                scale=(0.0625 if what == "k" else 1.0)),
                                  reads=[self.t_ps[pi]], writes=[t_vt[vb]])
                            dst, t_dst = {"k": (self.Ktok, self.t_Ktok), "v": (self.V, self.t_V), "o": (self.Og, self.t_Og)}[what]
                            kb.dma("pool", dst[r0:r0 + 128, d0:d0 + wd], vt[vb][:, :wd], reads=[t_vt[vb]], writes=[t_dst])
            kb.barrier()

    def phaseB_ml(self, li):
        kb, nc = self.kb, self.nc
        S, T = self.S, self.T
        L = 64
        nxc, ncc = S // L, CTX // L
        with ExitStack() as pes:
            sb = lambda name, shape, dt: pes.enter_context(nc.sbuf_tensor(self.uname(name), list(shape), dt))
            identf = sb("m_identf", [128, 128], F32)
            onesf = sb("m_onesf", [128, 128], F32)
            tri = sb("m_tri", [64, 128], F32)
            t_c = Tok()
            kb.dma("sp", identf[:], self.in_identf[:, :], writes=[t_c])
            kb.dma("sp", onesf[:], self.in_onesf[:, :], writes=[t_c])
            kb.dma("sp", tri[:], self.in_tri[:, :], writes=[t_c])
            C = [[sb(f"m_C{d}{h}", [128, 2, 512], F32) for h in range(4)] for d in range(2)]
            Cb = [[sb(f"m_Cb{d}{h}", [128, 2, 512], BF16) for h in range(4)] for d in range(2)]
            nn = [[sb(f"m_n{d}{h}", [128, 2], F32) for h in range(4)] for d in range(2)]
            nb = [[sb(f"m_nb{d}{h}", [128, 2], BF16) for h in range(4)] for d in range(2)]
            t_C = [[Tok() for h in range(4)] for d in range(2)]
            t_Cb = [[Tok() for h in range(4)] for d in range(2)]
            t_n = [[Tok() for h in range(4)] for d in range(2)]
            t_nb = [[Tok() for h in range(4)] for d in range(2)]
            for d in range(2):
                for h in range(4):
                    kb.op("pool", lambda e: e.memset(C[d][h][:], 0.0), writes=[t_C[d][h]])
                    kb.op("pool", lambda e: e.memset(Cb[d][h][:], 0.0), writes=[t_Cb[d][h]])
                    kb.op("pool", lambda e: e.memset(nn[d][h][:], 0.0), writes=[t_n[d][h]])
                    kb.op("pool", lambda e: e.memset(nb[d][h][:], 0.0), writes=[t_nb[d][h]])
            NB = 2
            qT = [sb(f"m_qT{i}", [128, 8, L], BF16) for i in range(NB)]; t_qT = [Tok() for _ in range(NB)]
            kT = [sb(f"m_kT{i}", [128, 8, L], BF16) for i in range(NB)]; t_kT = [Tok() for _ in range(NB)]
            kk = [sb(f"m_kk{i}", [L, 1024], BF16) for i in range(NB)]; t_kk = [Tok() for _ in range(NB)]
            vv = [sb(f"m_vv{i}", [L, 2048], BF16) for i in range(NB)]; t_vv = [Tok() for _ in range(NB)]
            gg = [sb(f"m_gg{i}", [L, 16], F32) for i in range(NB)]; t_gg = [Tok() for _ in range(NB)]
            bb = sb("m_b", [L, 4], F32); t_bb = Tok()
            wprev = sb("m_wprev", [L, 4], F32); t_wprev = Tok()
            lmb = sb("m_lmb", [L, 4], F32); t_lmb = Tok()
            ws = sb("m_ws", [L, 4], F32); t_ws = Tok()
            decay = sb("m_decay", [128, 4], F32); t_decay = Tok()
            diagb = [sb(f"m_diagb{i}", [L, L], F32) for i in range(2)]; t_diagb = [Tok(), Tok()]
            Eh = [sb(f"m_E{i}", [L, L], F32) for i in range(2)]; t_Eh = [Tok(), Tok()]
            WT = [sb(f"m_WT{i}", [L, L], BF16) for i in range(2)]; t_WT = [Tok(), Tok()]
            n1 = [sb(f"m_n1{i}", [L, 512], F32) for i in range(2)]; t_n1 = [Tok(), Tok()]
            hh = [sb(f"m_hh{i}", [L, 512], F32) for i in range(2)]; t_hh = [Tok(), Tok()]
            den = [sb(f"m_den{i}", [L, 2], F32) for i in range(2)]; t_den = [Tok(), Tok()]
            kp = [sb(f"m_kp{i}", [L, 256], BF16) for i in range(2)]; t_kp = [Tok(), Tok()]
            ps, tps = self.ps, self.t_ps
            cidx = 0
            it = 0
            for d in range(2):
                if d == 0:
                    order = [S + c * L for c in range(ncc)] + [c * L for c in range(nxc)]
                else:
                    order = [S + c * L for c in reversed(range(ncc))] + [c * L for c in reversed(range(nxc))]
                lic, lfc = (0, 4) if d == 0 else (8, 12)
                trid = tri[:, 0:64] if d == 0 else tri[:, 64:128]
                for tk0 in order:
                    cb = cidx % NB
                    cidx += 1
                    kb.dma("sp", qT[cb][:], self.QT[0:1024, tk0:tk0 + L].rearrange("(c p) n -> p c n", p=128), reads=[self.t_QT], writes=[t_qT[cb]])
                    kb.dma("sp", kT[cb][:], self.KT[0:1024, tk0:tk0 + L].rearrange("(c p) n -> p c n", p=128), reads=[self.t_KT], writes=[t_kT[cb]])
                    kb.dma("sp", kk[cb][:], self.Ktok[tk0:tk0 + L, :], reads=[self.t_Ktok], writes=[t_kk[cb]])
                    kb.dma("sp", vv[cb][:], self.V[tk0:tk0 + L, :], reads=[self.t_V], writes=[t_vv[cb]])
                    kb.dma("sp", gg[cb][:], self.G[tk0:tk0 + L, :], reads=[self.t_G], writes=[t_gg[cb]])
                    G_ = gg[cb]
                    kb.op("pe", lambda e: e.matmul(ps[0][0:L, 0:4], lhsT=trid, rhs=G_[:, lfc:lfc + 4], start=True, stop=True),
                          reads=[t_c, t_gg[cb]], writes=[tps[0]])
                    kb.op("pe", lambda e: e.matmul(ps[0][:, 16:20], lhsT=onesf[0:L, :], rhs=G_[:, lfc:lfc + 4], start=True, stop=True),
                          reads=[t_c, t_gg[cb]], writes=[tps[0]])
                    kb.op("dve", lambda e: e.tensor_copy(bb[:, :], ps[0][0:L, 0:4]), reads=[tps[0]], writes=[t_bb])
                    kb.op("act", lambda e: e.activation(out=wprev[:, :], in_=ps[0][0:L, 0:4], func=AF.Exp), reads=[tps[0]], writes=[t_wprev])
                    kb.op("act", lambda e: e.activation(out=decay[:, :], in_=ps[0][:, 16:20], func=AF.Exp), reads=[tps[0]], writes=[t_decay])
                    kb.op("dve", lambda e: e.tensor_tensor(lmb[:, :], G_[:, lic:lic + 4], bb[:, :], ALU.subtract), reads=[t_gg[cb], t_bb], writes=[t_lmb])
                    kb.op("dve", lambda e: e.tensor_tensor(ws[:, :], lmb[:, :], ps[0][0:L, 16:20], ALU.add), reads=[t_lmb, tps[0]], writes=[t_ws])
                    kb.op("act", lambda e: e.activation(out=ws[:, :], in_=ws[:, :], func=AF.Exp), reads=[t_ws], writes=[t_ws])
                    for h in range(4):
                        hb = it % 2
                        it += 1
                        kb.op("dve", lambda e: e.tensor_scalar_mul(diagb[hb][:, :], identf[0:L, 0:L], bb[:, h:h + 1]), reads=[t_c, t_bb], writes=[t_diagb[hb]])
                        kb.op("pe", lambda e: e.matmul(ps[1][0:L, 0:L], lhsT=onesf[0:L, 0:L], rhs=diagb[hb][:, :], start=True, stop=True),
                              reads=[t_c, t_diagb[hb]], writes=[tps[1]])
                        kb.op("dve", lambda e: e.tensor_scalar(Eh[hb][:, :], ps[1][0:L, 0:L], lmb[:, h:h + 1], 60.0, ALU.add, ALU.min),
                              reads=[tps[1], t_lmb], writes=[t_Eh[hb]])
                        kb.op("act", lambda e: e.activation(out=Eh[hb][:, :], in_=Eh[hb][:, :], func=AF.Exp), reads=[t_Eh[hb]], writes=[t_Eh[hb]])
                        kb.op("pool", lambda e: e.tensor_tensor(Eh[hb][:, :], Eh[hb][:, :], trid, ALU.mult), reads=[t_Eh[hb], t_c], writes=[t_Eh[hb]])
                        for i in range(2):
                            kb.op("pe", lambda e: e.matmul(ps[2][0:L, 0:L], lhsT=kT[cb][:, 2 * h + i, :], rhs=qT[cb][:, 2 * h + i, :], start=(i == 0), stop=(i == 1)),
                                  reads=[t_kT[cb], t_qT[cb]], writes=[tps[2]], signal=(i == 1))
                        kb.op("dve", lambda e: e.tensor_tensor(WT[hb][:, :], Eh[hb][:, :], ps[2][0:L, 0:L], ALU.mult), reads=[t_Eh[hb], tps[2]], writes=[t_WT[hb]])
                        vh = vv[cb][:, h * 512:(h + 1) * 512]
                        kb.op("pe", lambda e: e.matmul(ps[3][0:L, :], lhsT=WT[hb][:, :], rhs=vh, start=True, stop=True),
                              reads=[t_WT[hb], t_vv[cb]], writes=[tps[3]])
                        kb.op("pe", lambda e: e.matmul(ps[7][0:L, 0:1], lhsT=WT[hb][:, :], rhs=self.ones[0:L, 0:1], start=True, stop=True),
                              reads=[t_WT[hb], self.t_const], writes=[tps[7]])
                        for i in range(2):
                            kb.op("pe", lambda e: e.matmul(ps[4][0:L, :], lhsT=qT[cb][:, 2 * h + i, :], rhs=Cb[d][h][:, i, :], start=(i == 0), stop=(i == 1)),
                                  reads=[t_qT[cb], t_Cb[d][h]], writes=[tps[4]], signal=(i == 1))
                        for i in range(2):
                            kb.op("pe", lambda e: e.matmul(ps[7][0:L, 8:9], lhsT=qT[cb][:, 2 * h + i, :], rhs=nb[d][h][:, i:i + 1], start=(i == 0), stop=(i == 1)),
                                  reads=[t_qT[cb], t_nb[d][h]], writes=[tps[7]], signal=(i == 1))
                        kb.op("act", lambda e: e.activation(out=n1[hb][:, :], in_=ps[3][0:L, :], func=AF.Copy), reads=[tps[3]], writes=[t_n1[hb]])
                        kb.op("dve", lambda e: e.scalar_tensor_tensor(hh[hb][:, :], ps[4][0:L, :], wprev[:, h:h + 1], n1[hb][:, :], ALU.mult, ALU.add),
                              reads=[tps[4], t_wprev, t_n1[hb]], writes=[t_hh[hb]])
                        kb.op("act", lambda e: e.activation(out=den[hb][:, 0:1], in_=ps[7][0:L, 0:1], func=AF.Copy), reads=[tps[7]], writes=[t_den[hb]])
                        kb.op("dve", lambda e: e.scalar_tensor_tensor(den[hb][:, 1:2], ps[7][0:L, 8:9], wprev[:, h:h + 1], den[hb][:, 0:1], ALU.mult, ALU.add),
                              reads=[tps[7], t_wprev, t_den[hb]], writes=[t_den[hb]])
                        kb.op("act", lambda e: e.activation(out=den[hb][:, 1:2], in_=den[hb][:, 1:2], func=AF.Abs), reads=[t_den[hb]], writes=[t_den[hb]])
                        kb.op("dve", lambda e: e.tensor_scalar_max(den[hb][:, 1:2], den[hb][:, 1:2], 1.0), reads=[t_den[hb]], writes=[t_den[hb]])
                        kb.op("dve", lambda e: e.reciprocal(den[hb][:, 1:2], den[hb][:, 1:2]), reads=[t_den[hb]], writes=[t_den[hb]])
                        kb.op("dve", lambda e: e.tensor_scalar_mul(hh[hb][:, :], hh[hb][:, :], den[hb][:, 1:2]), reads=[t_hh[hb], t_den[hb]], writes=[t_hh[hb]])
                        kb.dma("pool", self.H[d][tk0:tk0 + L, h * 512:(h + 1) * 512], hh[hb][:, :], reads=[t_hh[hb]], writes=[self.t_H[d]])
                        kb.op("pool", lambda e: e.tensor_scalar_mul(kp[hb][:, :], kk[cb][:, h * 256:(h + 1) * 256], ws[:, h:h + 1]),
                              reads=[t_kk[cb], t_ws], writes=[t_kp[hb]])
                        for i in range(2):
                            kb.op("pe", lambda e: e.matmul(ps[5 + i][:, :], lhsT=kp[hb][:, i * 128:(i + 1) * 128], rhs=vh, start=True, stop=True),
                                  reads=[t_kp[hb], t_vv[cb]], writes=[tps[5 + i]])
                            kb.op("pe", lambda e: e.matmul(ps[7][:, 16 + 8 * i:17 + 8 * i], lhsT=kp[hb][:, i * 128:(i + 1) * 128], rhs=self.ones[0:L, 0:1],
                                                           start=True, stop=True),
                                  reads=[t_kp[hb], self.t_const], writes=[tps[7]])
                        for i in range(2):
                            kb.op("dve", lambda e: e.scalar_tensor_tensor(C[d][h][:, i, :], C[d][h][:, i, :], decay[:, h:h + 1], ps[5 + i][:, :], ALU.mult, ALU.add),
                                  reads=[t_C[d][h], t_decay, tps[5 + i]], writes=[t_C[d][h]])
                            kb.op("act", lambda e: e.activation(out=Cb[d][h][:, i, :], in_=C[d][h][:, i, :], func=AF.Copy), reads=[t_C[d][h]], writes=[t_Cb[d][h]])
                            kb.op("dve", lambda e: e.scalar_tensor_tensor(nn[d][h][:, i:i + 1], nn[d][h][:, i:i + 1], decay[:, h:h + 1],
                                                                          ps[7][:, 16 + 8 * i:17 + 8 * i], ALU.mult, ALU.add),
                                  reads=[t_n[d][h], t_decay, tps[7]], writes=[t_n[d][h]])
                        kb.op("act", lambda e: e.activation(out=nb[d][h][:, :], in_=nn[d][h][:, :], func=AF.Copy), reads=[t_n[d][h]], writes=[t_nb[d][h]])
            kb.barrier()

    def phaseR_ml(self, li):
        kb, nc = self.kb, self.nc
        S, T = self.S, self.T
        w = self.W[li]
        with ExitStack() as pes:
            sb = lambda name, shape, dt: pes.enter_context(nc.sbuf_tensor(self.uname(name), list(shape), dt))
            hg = sb("r_hg", [128, D], F32); t_hg = Tok()
            kb.dma("sp", hg[:], w["headg"][:, :], writes=[t_hg])
            hf = [sb(f"r_hf{i}", [128, D], F32) for i in range(2)]; t_hf = [Tok(), Tok()]
            hb_ = [sb(f"r_hb{i}", [128, D], F32) for i in range(2)]; t_hb = [Tok(), Tok()]
            og = [sb(f"r_og{i}", [128, D], BF16) for i in range(2)]; t_og = [Tok(), Tok()]
            sgm = sb("r_sg", [128, D], F32); t_sg = Tok()
            sqb = sb("r_sq", [128, D], F32); t_sqb = Tok()
            ss = sb("r_ss", [128, 4], F32); t_ss = Tok()
            y = sb("r_y", [128, D], BF16); t_y = Tok()
            ot = [sb(f"r_ot{i}", [128, 512], BF16) for i in range(2)]; t_ot = [Tok(), Tok()]
            cnt = 0
            for bi in range(T // 128):
                r0 = bi * 128
                b = bi % 2
                kb.dma("sp", hf[b][:], self.H[0][r0:r0 + 128, :], reads=[self.t_H[0]], writes=[t_hf[b]])
                kb.dma("sp", hb_[b][:], self.H[1][r0:r0 + 128, :], reads=[self.t_H[1]], writes=[t_hb[b]])
                kb.dma("sp", og[b][:], self.Og[r0:r0 + 128, :], reads=[self.t_Og], writes=[t_og[b]])
                kb.op("dve", lambda e: e.tensor_tensor(hf[b][:], hf[b][:], hb_[b][:], ALU.add), reads=[t_hf[b], t_hb[b]], writes=[t_hf[b]])
                kb.op("act", lambda e: e.activation(out=sqb[:], in_=hf[b][:], func=AF.Square), reads=[t_hf[b]], writes=[t_sqb])
                kb.op("dve", lambda e: e.reduce_sum(ss[:, :], sqb[:].rearrange("p (h n) -> p h n", h=4), mybir.AxisListType.X), reads=[t_sqb], writes=[t_ss])
                kb.op("dve", lambda e: e.tensor_scalar(ss[:, :], ss[:, :], 1.0 / 512, EPS, ALU.mult, ALU.add), reads=[t_ss], writes=[t_ss])
                kb.op("act", lambda e: e.activation(out=ss[:, :], in_=ss[:, :], func=AF.Sqrt), reads=[t_ss], writes=[t_ss])
                kb.op("dve", lambda e: e.reciprocal(ss[:, :], ss[:, :]), reads=[t_ss], writes=[t_ss])
                kb.op("act", lambda e: e.activation(out=sgm[:], in_=og[b][:], func=AF.Sigmoid), reads=[t_og[b]], writes=[t_sg])
                kb.op("pool", lambda e: e.tensor_tensor(sgm[:], sgm[:], hg[:], ALU.mult), reads=[t_sg, t_hg], writes=[t_sg])
                for h in range(4):
                    kb.op("dve", lambda e: e.scalar_tensor_tensor(y[:, h * 512:(h + 1) * 512], hf[b][:, h * 512:(h + 1) * 512], ss[:, h:h + 1],
                                                                  sgm[:, h * 512:(h + 1) * 512], ALU.mult, ALU.mult),
                          reads=[t_hf[b], t_ss, t_sg], writes=[t_y])
                for g4 in range(4):
                    pi = cnt % 2
                    ob = cnt % 2
                    cnt += 1
                    for jj in range(4):
                        j = g4 * 4 + jj
                        kb.op("pe", lambda e: e.matmul(self.ps[pi][:, jj * 128:(jj + 1) * 128], lhsT=y[:, j * 128:(j + 1) * 128], rhs=self.ident[:, :],
                                                       start=True, stop=True),
                              reads=[t_y, self.t_const], writes=[self.t_ps[pi]], signal=(jj == 3))
                    kb.op("act", lambda e: e.activation(out=ot[ob][:, :], in_=self.ps[pi][:, :], func=AF.Copy), reads=[self.t_ps[pi]], writes=[t_ot[ob]])
                    kb.dma("pool", self.OT[g4 * 512:(g4 + 1) * 512, r0:r0 + 128].rearrange("(jj p) n -> p jj n", p=128),
                           ot[ob][:, :].rearrange("p (jj n) -> p jj n", n=128), reads=[t_ot[ob]], writes=[self.t_OT])
            kb.barrier()

    def build(self):
        self.phase0()
        xi = 0
        for idx, (li, kind, need_ctx) in enumerate(self.layers):
            final = idx == len(self.layers) - 1
            if kind == 2:
                self.phaseA_ml(li, xi)
                self.phaseB_ml(li)
                self.phaseR_ml(li)
            else:
                self.phaseA(li, kind, xi)
                self.phaseB(li, kind, need_ctx)
            self.phaseC(li, need_ctx, xi, final)
            xi = 1 - xi
        self.kb.barrier(engines=("sp",))
        self.es.close()
        return self.nc


def na_patterns(S):
    rows = S // GW
    kr = min(8, rows)
    SB = S // 128
    pats, pat_of_R, keymap = [], [], {}
    for R in range(SB):
        bb = na_base_block(R, SB)
        r0a = min(max(2 * R - kr // 2, 0), rows - kr)
        r0b = min(max(2 * R + 1 - kr // 2, 0), rows - kr)
        key = (bb - R, r0a - 2 * R, r0b - 2 * R)
        if key not in keymap:
            keymap[key] = len(pats)
            pats.append((bb, R))
        pat_of_R.append(keymap[key])
    return pat_of_R, pats


def na_base_block(R, SB):
    return min(max(R - 2, 0), SB - 5)


def build_na_table(rel_bias, S):
    rows = S // GW
    kr = min(8, rows)
    SB = S // 128
    pat_of_R, pats = na_patterns(S)
    npat = len(pats)
    rb = np.asarray(rel_bias, np.float32)
    tab = np.full((npat, 16, 128, 5, 128), -30000.0, np.float32)
    i = np.arange(128)
    qr_off, qc = i // GW, i % GW
    c0 = np.clip(qc - 8, 0, GW - 16)
    for p, (bb, R) in enumerate(pats):
        qr = 2 * R + qr_off
        r0 = np.clip(qr - kr // 2, 0, rows - kr)
        for jb in range(5):
            j = np.arange(128)
            kr_ = 2 * (bb + jb) + j // GW
            kc_ = j % GW
            ok = ((kr_[:, None] >= r0[None, :]) & (kr_[:, None] < r0[None, :] + kr)
                  & (kc_[:, None] >= c0[None, :]) & (kc_[:, None] < c0[None, :] + 16))
            drow = np.clip(kr_[:, None] - qr[None, :] + 7, 0, 14)
            dcol = np.clip(kc_[:, None] - qc[None, :] + 15, 0, 30)
            vals = rb[:, drow, dcol]
            tab[p, :, :, jb, :] = np.where(ok[None], vals, np.float32(-30000.0))
    return tab.reshape(npat * 16 * 128, 5 * 128)


def rope_tables(S):
    t = np.arange(S)
    row = (t // GW).astype(np.float32)
    colp = (t % GW).astype(np.float32)
    inv = (10000.0 ** (-np.arange(32, dtype=np.float32) / 32)).astype(np.float32)
    cosT = np.zeros((128, S), np.float32)
    sinT = np.zeros((128, S), np.float32)
    for a, pos in enumerate((row, colp)):
        ang = (pos[None, :] * inv[:, None]).astype(np.float32)
        for p in range(2):
            cosT[a * 64 + p * 32:a * 64 + (p + 1) * 32] = np.cos(ang)
            sinT[a * 64 + p * 32:a * 64 + (p + 1) * 32] = np.sin(ang)
    rot = np.zeros((128, 128), np.float32)
    for a in range(2):
        for f in range(32):
            d1, d2 = a * 64 + f, a * 64 + 32 + f
            rot[d2, d1] = -1.0
            rot[d1, d2] = 1.0
    return cosT, sinT, rot


def fm(v, ncol):
    return np.ascontiguousarray(np.asarray(v, np.float32).reshape(ncol, 128).T)


def pack_sv(li, kind, P):
    sv = np.zeros((128, NSV), np.float32)
    sv[:, SV_ADAB:SV_ADAB + 96] = fm(P["ada_b"][li], 96)
    sv[:, SV_N1:SV_N1 + 16] = fm(P["norm1_g"][li], 16)
    sv[:, SV_N2:SV_N2 + 16] = fm(P["norm2_g"][li], 16)
    for j in range(3):
        sv[:, SV_CW + j * FC:SV_CW + (j + 1) * FC] = fm(P["ffn_conv_w"][li][j], FC)
    sv[:, SV_CB:SV_CB + FC] = fm(P["ffn_conv_b"][li], FC)
    pre = {0: "na", 1: "swa", 3: "gqa"}.get(kind)
    if pre:
        sv[:, SV_QG] = np.asarray(P[pre + "_q_g"][0], np.float32)
        sv[:, SV_KG] = np.asarray(P[pre + "_k_g"][0], np.float32)
    if kind == 1:
        sv[:, SV_SINK:SV_SINK + 16] = np.asarray(P["swa_sinks"][0], np.float32)[None, :]
    if kind == 2:
        sv[:, SV_SINK:SV_SINK + 16] = np.asarray(P["ml_gate_b"][0], np.float32)[None, :]
    return sv


def core_inputs(b, S, layers, P, consts):
    m = dict(consts)
    m["xT"] = np.ascontiguousarray(np.asarray(P["x"][b, :S], np.float32).T)
    m["cxT"] = np.ascontiguousarray(np.asarray(P["ctx"][b], np.float32).T)
    cc = np.stack([fm(P["c"][b], KC), fm(P["c_ctx"], KC)], axis=-1)
    m["cc"] = np.ascontiguousarray(cc)
    return m


def shared_inputs(S, layers, P):
    cosT, sinT, rot = rope_tables(S)
    bf = ml_dtypes.bfloat16
    j = np.arange(128)
    m = {
        "ones": np.ones((128, 128), bf), "ident": np.eye(128, dtype=np.float32).astype(bf), "rot": rot.astype(bf),
        "cosT": cosT, "sinT": sinT,
        "mprev": (j[:, None] >= j[None, :]).astype(np.float32).astype(bf),
        "mnext": (j[:, None] <= j[None, :]).astype(np.float32).astype(bf),
    }
    if any(k == 2 for (_, k, _) in layers):
        jj = np.arange(64)
        m["identf"] = np.eye(128, dtype=np.float32)
        m["onesf"] = np.ones((128, 128), np.float32)
        m["tri"] = np.concatenate([(jj[:, None] <= jj[None, :]), (jj[:, None] >= jj[None, :])], axis=1).astype(np.float32)
    mixw = {0: ("na_w_qkv", "na_w_o"), 1: ("swa_w_qkv", "swa_w_o"), 2: ("ml_w_in", "ml_w_o"), 3: ("gqa_w_qkv", "gqa_w_o")}
    for (li, kind, _) in layers:
        m[f"ada_w{li}"] = np.asarray(P["ada_w"][li], np.float32)
        m[f"ffn_w_in{li}"] = np.asarray(P["ffn_w_in"][li], np.float32)
        m[f"ffn_w_out{li}"] = np.asarray(P["ffn_w_out"][li], np.float32)
        m[f"mix_w_in{li}"] = np.asarray(P[mixw[kind][0]][0], np.float32)
        m[f"mix_w_o{li}"] = np.asarray(P[mixw[kind][1]][0], np.float32)
        m[f"sv{li}"] = pack_sv(li, kind, P)
        if kind == 0:
            m[f"natab{li}"] = build_na_table(P["na_rel_bias"][0], S)
        if kind == 2:
            m[f"headg{li}"] = np.ascontiguousarray(np.broadcast_to(np.asarray(P["ml_head_g"][0], np.float32)[None, :], (128, D)))
    return m


def run_model(P, S, layers, batches, trace=False):
    prog = Prog(S, layers)
    nc = prog.build()
    shared = shared_inputs(S, layers, P)
    in_maps = [core_inputs(b, S, layers, P, shared) for b in batches]
    res = run_bass_kernel_spmd(nc, in_maps, core_ids=list(range(len(batches))), trace=trace)
    outs = [np.ascontiguousarray(r["outT"].T) for r in res.results]
    return np.stack(outs, 0), res


def kernel(**inputs):
    S = inputs["x"].shape[1]
    layers = [(i, i % 4, i < 3) for i in range(4)]
    out, _ = run_model(inputs, S, layers, list(range(inputs["x"].shape[0])))
    return out.astype(np.float32)
```

```python
import math
from contextlib import ExitStack

import ml_dtypes
import numpy as np

import concourse.bass as bass
import concourse.mybir as mybir
from concourse.bass_utils import run_bass_kernel_spmd

F32, BF16 = mybir.dt.float32, mybir.dt.bfloat16
AF = mybir.ActivationFunctionType
ALU = mybir.AluOpType

D = 2048
KC = 16
DFF = 5632
FC = 44
CTX = 256
GW = 64
EPS = 1e-6
NSV = 96 + 16 + 16 + 132 + 44 + 1 + 1 + 1 + 16
SV_ADAB, SV_N1, SV_N2, SV_CW, SV_CB, SV_QG, SV_KG, SV_GB, SV_SINK = 0, 96, 112, 128, 260, 304, 305, 306, 307
GK = 1.5957691216057308
MIX_COLS = {0: 6144, 1: 2560, 2: 6160, 3: 3072}
NKV = {0: 16, 1: 2, 3: 4}


class Tok:
    __slots__ = ("w", "r")

    def __init__(self):
        self.w = {}
        self.r = {}


class KB:
    def __init__(self, nc, es, nring=6):
        self.nc = nc
        self.E = {"pe": nc.tensor, "act": nc.scalar, "dve": nc.vector, "pool": nc.gpsimd, "sp": nc.sync}
        self.csem = {e: es.enter_context(nc.semaphore("c_" + e)) for e in ("pe", "act", "dve", "pool")}
        self.ccnt = {e: 0 for e in self.csem}
        self.NR = nring
        self.dsem = {e: [es.enter_context(nc.semaphore(f"d_{e}{i}")) for i in range(nring)] for e in ("sp", "pool", "act")}
        self.dcnt = {e: [0] * nring for e in self.dsem}
        self.dnext = {e: 0 for e in self.dsem}
        self.waited = {e: {} for e in self.E}
        self.pending = {e: [] for e in self.csem}
        self.ninst = 0
        self.bg = []
        self.bgcnt = 0

    def _wait(self, e, ev):
        if ev is None:
            return
        sem, val, key = ev
        if key == "cpe" and e == "pe":
            return
        w = self.waited[e]
        if w.get(key, 0) >= val:
            return
        self.E[e].wait_ge(sem, val)
        self.ninst += 1
        w[key] = val

    def _deps(self, e, reads, writes):
        for t in reads:
            for ev in list(t.w.values()):
                self._wait(e, ev)
        for t in writes:
            for ev in list(t.w.values()):
                self._wait(e, ev)
            for ev in list(t.r.values()):
                self._wait(e, ev)

    @staticmethod
    def _commit(ev, reads, writes):
        for t in reads:
            t.r[ev[2]] = ev
        for t in writes:
            t.w[ev[2]] = ev
            t.r = {}

    def op(self, e, fn, reads=(), writes=(), signal=True):
        self._deps(e, reads, writes)
        ins = fn(self.E[e])
        self.ninst += 1
        if signal:
            self.ccnt[e] += 1
            ins.then_inc(self.csem[e], 1)
            ev = (self.csem[e], self.ccnt[e], "c" + e)
            self._commit(ev, reads, writes)
            for (r, w) in self.pending[e]:
                self._commit(ev, r, w)
            self.pending[e] = []
        else:
            self.pending[e].append((tuple(reads), tuple(writes)))
        return ins

    def dma(self, q, out, in_, reads=(), writes=()):
        i = self.dnext[q]
        self.dnext[q] = (i + 1) % self.NR
        sem = self.dsem[q][i]
        key = f"d{q}{i}"
        if self.dcnt[q][i]:
            self._wait(q, (sem, self.dcnt[q][i], key))
        self._deps(q, reads, writes)
        self.E[q].dma_start(out=out, in_=in_).then_inc(sem, 16)
        self.ninst += 1
        self.dcnt[q][i] += 16
        ev = (sem, self.dcnt[q][i], key)
        self._commit(ev, reads, writes)
        if q == "pool" and self.bg and not getattr(self, "_in_bg", False):
            self.bgcnt += 1
            if self.bgcnt % 2 == 0:
                self._in_bg = True
                o, i_, wr = self.bg.pop(0)
                self.dma("pool", o, i_, writes=wr)
                self._in_bg = False

    def flush_bg(self, tok):
        keep = []
        self._in_bg = True
        for (o, i_, wr) in self.bg:
            if tok in wr:
                self.dma("pool", o, i_, writes=wr)
            else:
                keep.append((o, i_, wr))
        self._in_bg = False
        self.bg = keep

    def barrier(self, engines=("pe", "act", "dve", "pool", "sp")):
        for e in engines:
            for q in self.dsem:
                for i in range(self.NR):
                    if self.dcnt[q][i]:
                        self._wait(e, (self.dsem[q][i], self.dcnt[q][i], f"d{q}{i}"))
            for c in self.csem:
                if self.ccnt[c] and c != e:
                    self._wait(e, (self.csem[c], self.ccnt[c], "c" + c))


class Prog:
    def __init__(self, S, layers, final_out=True):
        self.S, self.T = S, S + CTX
        self.layers = layers
        self.nc = nc = bass.Bass("TRN2", target_bir_lowering=False)
        self.es = ExitStack()
        self.kb = KB(nc, self.es)
        T = self.T
        di = lambda name, shape, dt=F32: nc.dram_tensor(name, list(shape), dt, kind="ExternalInput").ap()
        ds = lambda name, shape, dt: nc.dram_tensor(name, list(shape), dt).ap()
        self.in_xT = di("xT", [D, S])
        self.in_cxT = di("cxT", [D, CTX])
        self.in_cc = di("cc", [128, KC, 2])
        self.in_ones = di("ones", [128, 128], BF16)
        self.in_ident = di("ident", [128, 128], BF16)
        self.in_rot = di("rot", [128, 128], BF16)
        self.in_cos = di("cosT", [128, S])
        self.in_sin = di("sinT", [128, S])
        self.in_mprev = di("mprev", [128, 128], BF16)
        self.in_mnext = di("mnext", [128, 128], BF16)
        self.in_swam = di("swam", [128, 6 * 512], BF16)
        self.W = {}
        self.natab = None
        for (li, kind, _) in layers:
            w = {}
            w["ada"] = di(f"ada_w{li}", [D, 6 * D])
            w["win"] = di(f"ffn_w_in{li}", [D, 2 * DFF])
            w["wout"] = di(f"ffn_w_out{li}", [DFF, D])
            w["mix"] = di(f"mix_w_in{li}", [D, MIX_COLS[kind]])
            w["wo"] = di(f"mix_w_o{li}", [D, D])
            w["sv"] = di(f"sv{li}", [128, NSV])
            for k in ("ada", "win", "wout", "mix", "wo"):
                w[k + "_b"] = ds(f"{k}_b{li}", w[k].shape, BF16)
            if kind == 0:
                self.npat = len(na_patterns(S)[1])
                w["natab"] = di(f"natab{li}", [self.npat * 16 * 128, 8 * 512])
            if kind == 2:
                w["headg"] = di(f"headg{li}", [128, D])
            self.W[li] = w
        self.out = nc.dram_tensor("outT", [D, S], F32, kind="ExternalOutput").ap()
        self.XT = [ds("XT0", [D, T], F32), ds("XT1", [D, T], F32)]
        self.QT = ds("QT", [D, T], BF16)
        self.KT = ds("KT", [D, T], BF16)
        self.V = ds("V", [T, D], BF16)
        self.OT = ds("OT", [D, T], BF16)
        if any(k == 2 for (_, k, _) in layers):
            self.in_identf = di("identf", [128, 128])
            self.in_onesf = di("onesf", [128, 128])
            self.in_tri = di("tri", [64, 128])
            self.Ktok = ds("Ktok", [T, 1024], BF16)
            self.Og = ds("Og", [T, D], BF16)
            self.G = ds("G", [T, 16], F32)
            self.H = [ds("H0", [T, D], F32), ds("H1", [T, D], F32)]
            self.t_Ktok, self.t_Og, self.t_G, self.t_H = Tok(), Tok(), Tok(), [Tok(), Tok()]
        self.t_XT = [Tok(), Tok()]
        self.t_QT, self.t_KT, self.t_V, self.t_OT = Tok(), Tok(), Tok(), Tok()
        self.t_W = Tok()
        self.t_Wl = {li: Tok() for (li, _, _) in layers}
        es = self.es
        sb = lambda name, shape, dt: es.enter_context(nc.sbuf_tensor(self.uname(name), list(shape), dt))
        self.ps = [es.enter_context(nc.psum_tensor(f"ps{i}", [128, 512], F32)) for i in range(8)]
        self.t_ps = [Tok() for _ in range(8)]
        self.ones = sb("ones_sb", [128, 128], BF16)
        self.ident = sb("ident_sb", [128, 128], BF16)
        self.rot = sb("rot_sb", [128, 128], BF16)
        self.mprev = sb("mprev_sb", [128, 128], BF16)
        self.mnext = sb("mnext_sb", [128, 128], BF16)
        self.t_const = Tok()
        self.MOD, self.SCA1, self.SCA2, self.SV, self.t_mod = {}, {}, {}, {}, {}
        for (li, _, _) in layers:
            self.MOD[li] = sb(f"mod{li}", [128, 96, 2], F32)
            self.SCA1[li] = sb(f"sca1_{li}", [128, KC, 2], F32)
            self.SCA2[li] = sb(f"sca2_{li}", [128, KC, 2], F32)
            self.SV[li] = sb(f"svs{li}", [128, NSV], F32)
            self.t_mod[li] = Tok()

    def uname(self, name):
        self._uid = getattr(self, "_uid", 0) + 1
        return f"s{self._uid}_{name}"

    def wview(self, buf, kc, n):
        return buf[:, 0:kc * n].rearrange("p (k n) -> p k n", n=n)

    def norm_mod(self, pes, xt, t_xt, N, col, SCA, shift_base, li, ht, t_ht, sq, t_sq, tmp, t_tmp, psn, rs, t_rs):
        kb = self.kb
        ps, tps = self.ps[psn], self.t_ps[psn]
        for kc in range(KC):
            b = kc % 2
            kb.op("act", lambda e: e.activation(out=sq[b][:, :N], in_=xt[:, kc, :N], func=AF.Square),
                  reads=[t_xt], writes=[t_sq[b]])
            kb.op("pe", lambda e: e.matmul(ps[:, :N], lhsT=self.ones[:], rhs=sq[b][:, :N], start=(kc == 0), stop=(kc == KC - 1)),
                  reads=[t_sq[b], self.t_const], writes=[tps])
        kb.op("dve", lambda e: e.tensor_scalar(rs[:, :N], ps[:, :N], 1.0 / D, EPS, ALU.mult, ALU.add), reads=[tps], writes=[t_rs])
        kb.op("act", lambda e: e.activation(out=rs[:, :N], in_=rs[:, :N], func=AF.Sqrt), reads=[t_rs], writes=[t_rs])
        kb.op("dve", lambda e: e.reciprocal(rs[:, :N], rs[:, :N]), reads=[t_rs], writes=[t_rs])
        for kc in range(KC):
            b = kc % 2
            kb.op("dve", lambda e: e.tensor_tensor(tmp[b][:, :N], xt[:, kc, :N], rs[:, :N], ALU.mult),
                  reads=[t_xt, t_rs], writes=[t_tmp[b]])
            kb.op("act", lambda e: e.activation(out=ht[:, kc, :N], in_=tmp[b][:, :N], func=AF.Identity,
                                                scale=SCA[:, kc, col:col + 1], bias=self.MOD[li][:, shift_base + kc, col:col + 1]),
                  reads=[t_tmp[b], self.t_mod[li]], writes=[t_ht])

    def phase0(self):
        kb, nc = self.kb, self.nc
        S, T = self.S, self.T
        for (dst, src) in ((self.ones, self.in_ones), (self.ident, self.in_ident), (self.rot, self.in_rot),
                           (self.mprev, self.in_mprev), (self.mnext, self.in_mnext)):
            kb.dma("sp", dst[:], src[:, :], writes=[self.t_const])
        for r in range(0, D, 256):
            kb.dma("sp", self.XT[0][r:r + 256, 0:S], self.in_xT[r:r + 256, :], writes=[self.t_XT[0]])
        kb.dma("sp", self.XT[0][:, S:T], self.in_cxT[:, :], writes=[self.t_XT[0]])
        for (li, kind, _) in self.layers:
            w = self.W[li]
            for r in range(0, D, 256):
                kb.dma("pool", w["ada_b"][r:r + 256, :], w["ada"][r:r + 256, :], writes=[self.t_W])
        for idx, (li, kind, _) in enumerate(self.layers):
            w = self.W[li]
            for k in ("mix", "wo", "win", "wout"):
                rows = w[k].shape[0]
                step = 256 if idx == 0 else 64
                for r in range(0, rows, step):
                    r1 = min(rows, r + step)
                    if idx == 0:
                        kb.dma("pool", w[k + "_b"][r:r1, :], w[k][r:r1, :], writes=[self.t_Wl[li]])
                    else:
                        kb.bg.append((w[k + "_b"][r:r1, :], w[k][r:r1, :], [self.t_Wl[li]]))
        with ExitStack() as pes:
            sb = lambda name, shape, dt: pes.enter_context(nc.sbuf_tensor(self.uname(name), list(shape), dt))
            cc = sb("cc", [128, KC, 2], F32)
            scb = sb("scb", [128, KC, 2], BF16)
            t_cc, t_scb = Tok(), Tok()
            wt = [sb(f"p0w{i}", [128, KC * 512], BF16) for i in range(2)]
            t_wt = [Tok(), Tok()]
            tmp = sb("p0tmp", [128, KC, 2], F32)
            t_tmp = Tok()
            kb.dma("sp", cc[:], self.in_cc[:, :, :], writes=[t_cc])
            kb.op("act", lambda e: e.activation(out=scb[:], in_=cc[:], func=AF.Silu), reads=[t_cc], writes=[t_scb])
            g = 0
            for (li, kind, _) in self.layers:
                w = self.W[li]
                kb.dma("sp", self.SV[li][:], w["sv"][:, :], writes=[self.t_mod[li]])
                for cg in range(24):
                    b = g % 2
                    g += 1
                    wv = self.wview(wt[b], KC, 512)
                    kb.dma("sp", wv, w["ada_b"][:, cg * 512:(cg + 1) * 512].rearrange("(kc p) n -> p kc n", p=128),
                           reads=[self.t_W], writes=[t_wt[b]])
                    for j in range(4):
                        c = cg * 4 + j
                        pi = c % 2
                        for kc in range(KC):
                            kb.op("pe", lambda e: e.matmul(self.ps[pi][:, 0:2], lhsT=wv[:, kc, j * 128:(j + 1) * 128], rhs=scb[:, kc, :],
                                                           start=(kc == 0), stop=(kc == KC - 1)),
                                  reads=[t_wt[b], t_scb], writes=[self.t_ps[pi]], signal=(kc == KC - 1))
                        kb.op("act", lambda e: e.activation(out=self.MOD[li][:, c, :], in_=self.ps[pi][:, 0:2], func=AF.Identity,
                                                            bias=self.SV[li][:, SV_ADAB + c:SV_ADAB + c + 1], scale=1.0),
                              reads=[self.t_ps[pi], self.t_mod[li]], writes=[self.t_mod[li]])
                for (SCA, sc0, g0) in ((self.SCA1[li], 16, SV_N1), (self.SCA2[li], 64, SV_N2)):
                    kb.op("dve", lambda e: e.tensor_scalar_add(tmp[:], self.MOD[li][:, sc0:sc0 + 16, :], 1.0),
                          reads=[self.t_mod[li]], writes=[t_tmp])
                    for col in range(2):
                        kb.op("dve", lambda e: e.tensor_tensor(SCA[:, :, col], tmp[:, :, col], self.SV[li][:, g0:g0 + 16], ALU.mult),
                              reads=[t_tmp, self.t_mod[li]], writes=[self.t_mod[li]])
            kb.barrier()

    def phaseA(self, li, kind, xi):
        kb, nc = self.kb, self.nc
        S, T = self.S, self.T
        w = self.W[li]
        nq, nkv = 16, NKV[kind]
        rope = kind in (1, 3)
        XT, t_XT = self.XT[xi], self.t_XT[xi]
        tiles = [(t0, min(512, S - t0), 0) for t0 in range(0, S, 512)] + [(S, CTX, 1)]
        with ExitStack() as pes:
            sb = lambda name, shape, dt: pes.enter_context(nc.sbuf_tensor(self.uname(name), list(shape), dt))
            xt = sb("a_xt", [128, KC, 512], F32); t_xt = Tok()
            ht = sb("a_ht", [128, KC, 512], BF16); t_ht = Tok()
            sq = [sb(f"a_sq{i}", [128, 512], BF16) for i in range(2)]; t_sq = [Tok(), Tok()]
            tmp = [sb(f"a_tmp{i}", [128, 512], F32) for i in range(2)]; t_tmp = [Tok(), Tok()]
            rs = sb("a_rs", [128, 512], F32); t_rs = Tok()
            wt = [sb(f"a_w{i}", [128, KC * 512], BF16) for i in range(2)]; t_wt = [Tok(), Tok()]
            cs = [sb("a_cos", [128, 512], F32), sb("a_sin", [128, 512], F32)]; t_cs = Tok()
            sq2 = [sb(f"a_sq2{i}", [128, 512], BF16) for i in range(2)]; t_sq2 = [Tok(), Tok()]
            r2 = [sb(f"a_r2{i}", [128, 512], F32) for i in range(2)]; t_r2 = [Tok(), Tok()]
            qn = [sb(f"a_qn{i}", [128, 512], BF16) for i in range(2)]; t_qn = [Tok(), Tok()]
            t1 = [sb(f"a_t1{i}", [128, 512], F32) for i in range(2)]; t_t1 = [Tok(), Tok()]
            t2 = [sb(f"a_t2{i}", [128, 512], F32) for i in range(2)]; t_t2 = [Tok(), Tok()]
            qo = [sb(f"a_qo{i}", [128, 512], BF16) for i in range(2)]; t_qo = [Tok(), Tok()]
            vt = [sb(f"a_vt{i}", [128, 512], BF16) for i in range(2)]; t_vt = [Tok(), Tok()]
            gs = sb("a_gs", [128, 2], F32); t_gs = Tok()
            kb.op("dve", lambda e: e.tensor_scalar_mul(gs[:, 0:1], self.SV[li][:, SV_QG:SV_QG + 1], 128.0 ** -0.5),
                  reads=[self.t_mod[li]], writes=[t_gs])
            kb.op("dve", lambda e: e.tensor_copy(gs[:, 1:2], self.SV[li][:, SV_KG:SV_KG + 1]), reads=[self.t_mod[li]], writes=[t_gs])
            wg = 0
            cnt = 0
            for (t0, N, col) in tiles:
                kb.dma("sp", xt[:, :, :N], XT[:, t0:t0 + N].rearrange("(kc p) n -> p kc n", p=128), reads=[t_XT], writes=[t_xt])
                if rope and col == 0:
                    kb.dma("sp", cs[0][:, :N], self.in_cos[:, t0:t0 + N], writes=[t_cs])
                    kb.dma("sp", cs[1][:, :N], self.in_sin[:, t0:t0 + N], writes=[t_cs])
                self.norm_mod(pes, xt, t_xt, N, col, self.SCA1[li], 0, li, ht, t_ht, sq, t_sq, tmp, t_tmp, 7, rs, t_rs)
                nqk = nq + nkv
                pend1, pend2 = None, None

                def make_stage1(c, pi, sb_, N=N, t0=t0, col=col):
                    def stage1():
                        isq = c < nq
                        psq, tpsq = self.ps[pi], self.t_ps[pi]
                        kb.op("act", lambda e: e.activation(out=sq2[sb_][:, :N], in_=psq[:, :N], func=AF.Square), reads=[tpsq], writes=[t_sq2[sb_]])
                        pss, tpss = self.ps[3], self.t_ps[3]
                        kb.op("pe", lambda e: e.matmul(pss[:, :N], lhsT=self.ones[:], rhs=sq2[sb_][:, :N], start=True, stop=True),
                              reads=[t_sq2[sb_], self.t_const], writes=[tpss])
                        kb.op("dve", lambda e: e.tensor_scalar(r2[sb_][:, :N], pss[:, :N], 1.0 / 128, EPS, ALU.mult, ALU.add),
                              reads=[tpss], writes=[t_r2[sb_]])
                        kb.op("act", lambda e: e.activation(out=r2[sb_][:, :N], in_=r2[sb_][:, :N], func=AF.Sqrt),
                              reads=[t_r2[sb_]], writes=[t_r2[sb_]])
                        kb.op("dve", lambda e: e.reciprocal(r2[sb_][:, :N], r2[sb_][:, :N]), reads=[t_r2[sb_]], writes=[t_r2[sb_]])
                        gcol = gs[:, 0:1] if isq else gs[:, 1:2]
                        dstT = self.QT if isq else self.KT
                        t_dst = self.t_QT if isq else self.t_KT
                        row0 = (c if isq else c - nq) * 128
                        if rope and col == 0:
                            kb.op("dve", lambda e: e.scalar_tensor_tensor(qn[sb_][:, :N], psq[:, :N], gcol, r2[sb_][:, :N], ALU.mult, ALU.mult),
                                  reads=[tpsq, t_r2[sb_], t_gs], writes=[t_qn[sb_]])

                            def stage2():
                                psr, tpsr = self.ps[4 + sb_], self.t_ps[4 + sb_]
                                kb.op("pe", lambda e: e.matmul(psr[:, :N], lhsT=self.rot[:], rhs=qn[sb_][:, :N], start=True, stop=True),
                                      reads=[t_qn[sb_], self.t_const], writes=[tpsr])
                                kb.op("pool", lambda e: e.tensor_tensor(t1[sb_][:, :N], qn[sb_][:, :N], cs[0][:, :N], ALU.mult),
                                      reads=[t_qn[sb_], t_cs], writes=[t_t1[sb_]])
                                kb.op("dve", lambda e: e.tensor_tensor(t2[sb_][:, :N], psr[:, :N], cs[1][:, :N], ALU.mult),
                                      reads=[tpsr, t_cs], writes=[t_t2[sb_]])
                                kb.op("dve", lambda e: e.tensor_tensor(qo[sb_][:, :N], t1[sb_][:, :N], t2[sb_][:, :N], ALU.add),
                                      reads=[t_t1[sb_], t_t2[sb_]], writes=[t_qo[sb_]])
                                kb.dma("pool", dstT[row0:row0 + 128, t0:t0 + N], qo[sb_][:, :N], reads=[t_qo[sb_]], writes=[t_dst])
                            return stage2
                        kb.op("dve", lambda e: e.scalar_tensor_tensor(qo[sb_][:, :N], psq[:, :N], gcol, r2[sb_][:, :N], ALU.mult, ALU.mult),
                              reads=[tpsq, t_r2[sb_], t_gs], writes=[t_qo[sb_]])
                        kb.dma("pool", dstT[row0:row0 + 128, t0:t0 + N], qo[sb_][:, :N], reads=[t_qo[sb_]], writes=[t_dst])
                        return None
                    return stage1

                for cg in range((nqk + 3) // 4):
                    b = wg % 2
                    wg += 1
                    nch = min(4, nqk - cg * 4)
                    wv = self.wview(wt[b], KC, 512)
                    kb.dma("sp", wv[:, :, :nch * 128], w["mix_b"][:, cg * 512:cg * 512 + nch * 128].rearrange("(kc p) n -> p kc n", p=128),
                           reads=[self.t_Wl[li]], writes=[t_wt[b]])
                    for j in range(nch):
                        c = cg * 4 + j
                        pi = cnt % 3
                        sb_ = cnt % 2
                        cnt += 1
                        psq, tpsq = self.ps[pi], self.t_ps[pi]
                        for kc in range(KC):
                            kb.op("pe", lambda e: e.matmul(psq[:, :N], lhsT=wv[:, kc, j * 128:(j + 1) * 128], rhs=ht[:, kc, :N],
                                                           start=(kc == 0), stop=(kc == KC - 1)),
                                  reads=[t_wt[b], t_ht], writes=[tpsq], signal=(kc == KC - 1))
                        s2 = pend1() if pend1 else None
                        if pend2:
                            pend2()
                        pend2 = s2
                        pend1 = make_stage1(c, pi, sb_)
                s2 = pend1() if pend1 else None
                if pend2:
                    pend2()
                if s2:
                    s2()
                vbase = nqk * 128
                vw = nkv * 128
                for vg in range((vw + 511) // 512):
                    b = wg % 2
                    wg += 1
                    wd = min(512, vw - vg * 512)
                    wv = self.wview(wt[b], KC, 512)
                    kb.dma("sp", wv[:, :, :wd], w["mix_b"][:, vbase + vg * 512:vbase + vg * 512 + wd].rearrange("(kc p) n -> p kc n", p=128),
                           reads=[self.t_Wl[li]], writes=[t_wt[b]])
                    for tb in range(N // 128):
                        pi = 6 + cnt % 2
                        vb = cnt % 2
                        cnt += 1
                        for kc in range(KC):
                            kb.op("pe", lambda e: e.matmul(self.ps[pi][:, :wd], lhsT=ht[:, kc, tb * 128:(tb + 1) * 128], rhs=wv[:, kc, :wd],
                                                           start=(kc == 0), stop=(kc == KC - 1)),
                                  reads=[t_wt[b], t_ht], writes=[self.t_ps[pi]], signal=(kc == KC - 1))
                        kb.op("act", lambda e: e.activation(out=vt[vb][:, :wd], in_=self.ps[pi][:, :wd], func=AF.Copy),
                              reads=[self.t_ps[pi]], writes=[t_vt[vb]])
                        kb.dma("pool", self.V[t0 + tb * 128:t0 + (tb + 1) * 128, vg * 512:vg * 512 + wd], vt[vb][:, :wd],
                               reads=[t_vt[vb]], writes=[self.t_V])
            kb.barrier()

    def phaseB(self, li, kind, need_ctx):
        kb, nc = self.kb, self.nc
        S, T = self.S, self.T
        w = self.W[li]
        nkv = NKV[kind]
        grp = 16 // nkv
        TB = T // 128
        SB = S // 128
        ctxb = [SB, SB + 1]
        sink = kind == 1
        with ExitStack() as pes:
            sb = lambda name, shape, dt: pes.enter_context(nc.sbuf_tensor(self.uname(name), list(shape), dt))
            KTh = sb("b_kt", [128, T], BF16); t_k = Tok()
            Vh = sb("b_v", [128, TB, 128], BF16); t_v = Tok()
            QTh = [sb(f"b_qt{i}", [128, T], BF16) for i in range(2)]; t_q = [Tok(), Tok()]
            pt = [sb(f"b_pt{i}", [128, 512], BF16) for i in range(3)]; t_pt = [Tok() for _ in range(3)]
            rec = sb("b_rec", [128, 512], F32); t_rec = Tok()
            ot = [sb(f"b_ot{i}", [128, 512], BF16) for i in range(2)]; t_ot = [Tok(), Tok()]
            esink = sb("b_esink", [128, 16], F32); t_es = Tok()
            if kind == 0:
                npat = self.npat
                tab = sb("b_tab", [128, 8 * 512], F32); t_tab = Tok()
                etab = sb("b_etab", [128, npat, 8 * 512], BF16); t_etab = Tok()
                pat_of_C, pats = na_patterns(S)
            if kind == 1:
                swam = sb("b_swam", [128, 6, 512], BF16); t_swam = Tok()
                kb.dma("sp", swam[:], self.in_swam[:, :].rearrange("p (r n) -> p r n", n=512), writes=[t_swam])
            if sink:
                kb.op("act", lambda e: e.activation(out=esink[:], in_=self.SV[li][:, SV_SINK:SV_SINK + 16], func=AF.Exp),
                      reads=[self.t_mod[li]], writes=[t_es])
            chunks = []
            if kind == 3:
                for q0 in range(0, S, 512):
                    chunks.append((q0, min(512, S - q0), [(b, None) for b in range(TB)]))
            elif kind == 1:
                for ci in range(S // 512):
                    bl = [(kbk, ("swaw", kbk - 4 * ci + 1)) for kbk in range(4 * ci - 1, 4 * ci + 5) if 0 <= kbk < SB]
                    bl += [(b, None) for b in ctxb]
                    chunks.append((ci * 512, 512, bl))
            else:
                for ci in range(S // 512):
                    p = pat_of_C[ci]
                    bbw = na_base_block(ci, SB)
                    bl = [(bbw + j, ("naw", p, j)) for j in range(8)] + [(b, None) for b in ctxb]
                    chunks.append((ci * 512, 512, bl))
            if need_ctx:
                chunks.append((S, CTX, [(b, None) for b in ctxb]))
            cnt = 0
            qi = 0
            for kvh in range(nkv):
                kb.dma("sp", KTh[:], self.KT[kvh * 128:(kvh + 1) * 128, :], reads=[self.t_KT], writes=[t_k])
                kb.dma("sp", Vh[:], self.V[:, kvh * 128:(kvh + 1) * 128].rearrange("(tb p) d -> p tb d", p=128), reads=[self.t_V], writes=[t_v])
                for qh in range(kvh * grp, (kvh + 1) * grp):
                    qb = qi % 2
                    qi += 1
                    kb.dma("sp", QTh[qb][:], self.QT[qh * 128:(qh + 1) * 128, :], reads=[self.t_QT], writes=[t_q[qb]])
                    if kind == 0:
                        for p in range(npat):
                            r0 = (p * 16 + qh) * 128
                            kb.dma("sp", tab[:, :], w["natab"][r0:r0 + 128, :], writes=[t_tab])
                            kb.op("act", lambda e: e.activation(out=etab[:, p, :], in_=tab[:, :], func=AF.Exp), reads=[t_tab], writes=[t_etab])
                    for (q0, N, blocks) in chunks:
                        po, tpo = self.ps[0 + (cnt % 2)], self.t_ps[0 + (cnt % 2)]
                        pd, tpd = self.ps[2 + (cnt % 2)], self.t_ps[2 + (cnt % 2)]
                        ob = cnt % 2
                        cnt += 1
                        nb = len(blocks)

                        def emit_s(i):
                            kbk, mask = blocks[i]
                            psn = 4 + (i % 3)
                            pb = i % 3
                            kb.op("pe", lambda e: e.matmul(self.ps[psn][:, :N], lhsT=KTh[:, kbk * 128:(kbk + 1) * 128], rhs=QTh[qb][:, q0:q0 + N],
                                                           start=True, stop=True),
                                  reads=[t_k, t_q[qb]], writes=[self.t_ps[psn]])
                            kb.op("act", lambda e: e.activation(out=pt[pb][:, :N], in_=self.ps[psn][:, :N], func=AF.Exp),
                                  reads=[self.t_ps[psn]], writes=[t_pt[pb]])
                            if mask is not None:
                                if mask[0] == "swaw":
                                    m, tm = swam[:, mask[1], :N], t_swam
                                else:
                                    m, tm = etab[:, mask[1], mask[2] * 512:mask[2] * 512 + N], t_etab
                                kb.op("dve", lambda e: e.tensor_tensor(pt[pb][:, :N], pt[pb][:, :N], m, ALU.mult),
                                      reads=[t_pt[pb], tm], writes=[t_pt[pb]])

                        def emit_pv(i):
                            kbk, _ = blocks[i]
                            pb = i % 3
                            kb.op("pe", lambda e: e.matmul(po[:, :N], lhsT=Vh[:, kbk, :], rhs=pt[pb][:, :N], start=(i == 0), stop=(i == nb - 1)),
                                  reads=[t_v, t_pt[pb]], writes=[tpo], signal=(i == nb - 1))
                            kb.op("pe", lambda e: e.matmul(pd[:, :N], lhsT=self.ones[:], rhs=pt[pb][:, :N], start=(i == 0), stop=(i == nb - 1)),
                                  reads=[self.t_const, t_pt[pb]], writes=[tpd], signal=True)

                        DEPTH = 2
                        for i in range(min(DEPTH, nb)):
                            emit_s(i)
                        for i in range(nb):
                            if i + DEPTH < nb:
                                emit_s(i + DEPTH)
                            emit_pv(i)
                        if sink:
                            kb.op("dve", lambda e: e.tensor_scalar_add(rec[:, :N], pd[:, :N], esink[:, qh:qh + 1]), reads=[tpd, t_es], writes=[t_rec])
                            kb.op("dve", lambda e: e.reciprocal(rec[:, :N], rec[:, :N]), reads=[t_rec], writes=[t_rec])
                        else:
                            kb.op("dve", lambda e: e.reciprocal(rec[:, :N], pd[:, :N]), reads=[tpd], writes=[t_rec])
                        kb.op("dve", lambda e: e.tensor_tensor(ot[ob][:, :N], po[:, :N], rec[:, :N], ALU.mult), reads=[tpo, t_rec], writes=[t_ot[ob]])
                        kb.dma("pool", self.OT[qh * 128:(qh + 1) * 128, q0:q0 + N], ot[ob][:, :N], reads=[t_ot[ob]], writes=[self.t_OT])
            kb.barrier()

    def phaseC(self, li, need_ctx, xi, final):
        kb, nc = self.kb, self.nc
        S, T = self.S, self.T
        w = self.W[li]
        XT, t_XT = self.XT[xi], self.t_XT[xi]
        XO, t_XO = self.XT[1 - xi], self.t_XT[1 - xi]
        wins = []
        for w0 in range(0, S, 510):
            n = min(510, S - w0)
            wins.append((w0, n, w0 > 0, w0 + n < S, 0))
        if need_ctx:
            wins.append((S, CTX, False, False, 1))
        with ExitStack() as pes:
            sb = lambda name, shape, dt: pes.enter_context(nc.sbuf_tensor(self.uname(name), list(shape), dt))
            xt = sb("c_xt", [128, KC, 512], F32); t_xt = Tok()
            ab = sb("c_ab", [128, KC, 512], BF16); t_ab = Tok()
            act = sb("c_act", [128, FC, 512], BF16); t_act = Tok()
            wt = [sb(f"c_w{i}", [128, FC * 256], BF16) for i in range(3)]; t_wt = [Tok() for _ in range(3)]
            sq = [sb(f"c_sq{i}", [128, 512], BF16) for i in range(2)]; t_sq = [Tok(), Tok()]
            tmp = [sb(f"c_tmp{i}", [128, 512], F32) for i in range(2)]; t_tmp = [Tok(), Tok()]
            rs = sb("c_rs", [128, 512], F32); t_rs = Tok()
            ga = [sb(f"c_ga{i}", [128, 512], F32) for i in range(2)]; t_ga = [Tok(), Tok()]
            gq = [sb(f"c_gq{i}", [128, 512], F32) for i in range(2)]; t_gq = [Tok(), Tok()]
            gz = [sb(f"c_gz{i}", [128, 512], F32) for i in range(2)]; t_gz = [Tok(), Tok()]
            gy = [sb(f"c_gy{i}", [128, 512], F32) for i in range(2)]; t_gy = [Tok(), Tok()]
            xo = [sb(f"c_xo{i}", [128, 512], F32) for i in range(2)]; t_xo = [Tok(), Tok()]
            SVl = self.SV[li]
            MOD = self.MOD[li]
            wg = 0
            cnt = 0
            for (w0, n, left, right, col) in wins:
                a0 = w0 - (1 if left else 0)
                N = n + (1 if left else 0) + (1 if right else 0)
                lo = 1 if left else 0
                kb.dma("sp", ab[:, :, :N], self.OT[:, a0:a0 + N].rearrange("(kc p) n -> p kc n", p=128), reads=[self.t_OT], writes=[t_ab])
                kb.dma("sp", xt[:, :, :N], XT[:, a0:a0 + N].rearrange("(kc p) n -> p kc n", p=128), reads=[t_XT], writes=[t_xt])
                for ng in range(4):
                    b = wg % 3
                    wg += 1
                    wv = self.wview(wt[b], KC, 512)
                    kb.dma("sp", wv, w["wo_b"][:, ng * 512:(ng + 1) * 512].rearrange("(kc p) n -> p kc n", p=128), reads=[self.t_Wl[li]], writes=[t_wt[b]])
                    for j in range(4):
                        c = ng * 4 + j
                        pi = cnt % 2
                        cnt += 1
                        for kc in range(KC):
                            kb.op("pe", lambda e: e.matmul(self.ps[pi][:, :N], lhsT=wv[:, kc, j * 128:(j + 1) * 128], rhs=ab[:, kc, :N],
                                                           start=(kc == 0), stop=(kc == KC - 1)),
                                  reads=[t_wt[b], t_ab], writes=[self.t_ps[pi]], signal=(kc == KC - 1))
                        kb.op("dve", lambda e: e.scalar_tensor_tensor(xt[:, c, :N], self.ps[pi][:, :N], MOD[:, 32 + c, col:col + 1], xt[:, c, :N],
                                                                      ALU.mult, ALU.add),
                              reads=[self.t_ps[pi], t_xt, self.t_mod[li]], writes=[t_xt])
                self.norm_mod(pes, xt, t_xt, N, col, self.SCA2[li], 48, li, ab, t_ab, sq, t_sq, tmp, t_tmp, 7, rs, t_rs)
                for cg in range(FC // 4):
                    bg = wg % 3
                    wg += 1
                    bu = wg % 3
                    wg += 1
                    wvg = self.wview(wt[bg], KC, 512)
                    wvu = self.wview(wt[bu], KC, 512)
                    kb.dma("sp", wvg, w["win_b"][:, cg * 512:(cg + 1) * 512].rearrange("(kc p) n -> p kc n", p=128), reads=[self.t_Wl[li]], writes=[t_wt[bg]])
                    kb.dma("sp", wvu, w["win_b"][:, DFF + cg * 512:DFF + (cg + 1) * 512].rearrange("(kc p) n -> p kc n", p=128),
                           reads=[self.t_Wl[li]], writes=[t_wt[bu]])
                    for j in range(4):
                        c = cg * 4 + j
                        eb = cnt % 2
                        cnt += 1
                        pg, tpg = self.ps[2 + eb], self.t_ps[2 + eb]
                        pu, tpu = self.ps[4 + eb], self.t_ps[4 + eb]
                        for kc in range(KC):
                            kb.op("pe", lambda e: e.matmul(pg[:, :N], lhsT=wvg[:, kc, j * 128:(j + 1) * 128], rhs=ab[:, kc, :N],
                                                           start=(kc == 0), stop=(kc == KC - 1)),
                                  reads=[t_wt[bg], t_ab], writes=[tpg], signal=(kc == KC - 1))
                        for kc in range(KC):
                            kb.op("pe", lambda e: e.matmul(pu[:, :N], lhsT=wvu[:, kc, j * 128:(j + 1) * 128], rhs=ab[:, kc, :N],
                                                           start=(kc == 0), stop=(kc == KC - 1)),
                                  reads=[t_wt[bu], t_ab], writes=[tpu], signal=(kc == KC - 1))
                        cw = lambda jj: SVl[:, SV_CW + jj * FC + c:SV_CW + jj * FC + c + 1]
                        a_, ta_ = ga[eb], t_ga[eb]
                        kb.op("act", lambda e: e.activation(out=a_[:, :n], in_=pg[:, lo:lo + n], func=AF.Identity, scale=cw(1),
                                                            bias=SVl[:, SV_CB + c:SV_CB + c + 1]),
                              reads=[tpg, self.t_mod[li]], writes=[ta_])
                        if left:
                            kb.op("dve", lambda e: e.scalar_tensor_tensor(a_[:, :n], pg[:, lo - 1:lo - 1 + n], cw(0), a_[:, :n], ALU.mult, ALU.add),
                                  reads=[tpg, ta_, self.t_mod[li]], writes=[ta_])
                        else:
                            kb.op("dve", lambda e: e.scalar_tensor_tensor(a_[:, 1:n], pg[:, lo:lo + n - 1], cw(0), a_[:, 1:n], ALU.mult, ALU.add),
                                  reads=[tpg, ta_, self.t_mod[li]], writes=[ta_])
                        if right:
                            kb.op("dve", lambda e: e.scalar_tensor_tensor(a_[:, :n], pg[:, lo + 1:lo + 1 + n], cw(2), a_[:, :n], ALU.mult, ALU.add),
                                  reads=[tpg, ta_, self.t_mod[li]], writes=[ta_])
                        else:
                            kb.op("dve", lambda e: e.scalar_tensor_tensor(a_[:, :n - 1], pg[:, lo + 1:lo + n], cw(2), a_[:, :n - 1], ALU.mult, ALU.add),
                                  reads=[tpg, ta_, self.t_mod[li]], writes=[ta_])
                        kb.op("act", lambda e: e.activation(out=gq[eb][:, :n], in_=a_[:, :n], func=AF.Square), reads=[ta_], writes=[t_gq[eb]])
                        kb.op("dve", lambda e: e.tensor_scalar(gq[eb][:, :n], gq[eb][:, :n], 0.044715 * GK, GK, ALU.mult, ALU.add),
                              reads=[t_gq[eb]], writes=[t_gq[eb]])
                        kb.op("pool", lambda e: e.tensor_tensor(gz[eb][:, :n], gq[eb][:, :n], a_[:, :n], ALU.mult), reads=[t_gq[eb], ta_], writes=[t_gz[eb]])
                        kb.op("act", lambda e: e.activation(out=gz[eb][:, :n], in_=gz[eb][:, :n], func=AF.Sigmoid), reads=[t_gz[eb]], writes=[t_gz[eb]])
                        kb.op("pool", lambda e: e.tensor_tensor(gy[eb][:, :n], gz[eb][:, :n], a_[:, :n], ALU.mult), reads=[t_gz[eb], ta_], writes=[t_gy[eb]])
                        kb.op("dve", lambda e: e.tensor_tensor(act[:, c, :n], gy[eb][:, :n], pu[:, lo:lo + n], ALU.mult),
                              reads=[t_gy[eb], tpu], writes=[t_act])
                for ng in range(8):
                    b = wg % 3
                    wg += 1
                    wv = self.wview(wt[b], FC, 256)
                    kb.dma("sp", wv, w["wout_b"][:, ng * 256:(ng + 1) * 256].rearrange("(kc p) n -> p kc n", p=128), reads=[self.t_Wl[li]], writes=[t_wt[b]])
                    for j in range(2):
                        c = ng * 2 + j
                        pi = cnt % 2
                        ob = cnt % 2
                        cnt += 1
                        for kc in range(FC):
                            kb.op("pe", lambda e: e.matmul(self.ps[pi][:, :n], lhsT=wv[:, kc, j * 128:(j + 1) * 128], rhs=act[:, kc, :n],
                                                           start=(kc == 0), stop=(kc == FC - 1)),
                                  reads=[t_wt[b], t_act], writes=[self.t_ps[pi]], signal=(kc == FC - 1))
                        kb.op("dve", lambda e: e.scalar_tensor_tensor(xo[ob][:, :n], self.ps[pi][:, :n], MOD[:, 80 + c, col:col + 1], xt[:, c, lo:lo + n],
                                                                      ALU.mult, ALU.add),
                              reads=[self.t_ps[pi], t_xt, self.t_mod[li]], writes=[t_xo[ob]])
                        if final:
                            if col == 0:
                                kb.dma("pool", self.out[c * 128:(c + 1) * 128, w0:w0 + n], xo[ob][:, :n], reads=[t_xo[ob]])
                        else:
                            kb.dma("pool", XO[c * 128:(c + 1) * 128, w0:w0 + n], xo[ob][:, :n], reads=[t_xo[ob]], writes=[t_XO])
            kb.barrier()

    def phaseA_ml(self, li, xi):
        kb, nc = self.kb, self.nc
        S, T = self.S, self.T
        w = self.W[li]
        XT, t_XT = self.XT[xi], self.t_XT[xi]
        tiles = [(t0, min(512, S - t0), 0) for t0 in range(0, S, 512)] + [(S, CTX, 1)]
        with ExitStack() as pes:
            sb = lambda name, shape, dt: pes.enter_context(nc.sbuf_tensor(self.uname(name), list(shape), dt))
            xt = sb("a_xt", [128, KC, 512], F32); t_xt = Tok()
            ht = sb("a_ht", [128, KC, 512], BF16); t_ht = Tok()
            sq = [sb(f"a_sq{i}", [128, 512], BF16) for i in range(2)]; t_sq = [Tok(), Tok()]
            tmp = [sb(f"a_tmp{i}", [128, 512], F32) for i in range(2)]; t_tmp = [Tok(), Tok()]
            rs = sb("a_rs", [128, 512], F32); t_rs = Tok()
            wt = [sb(f"a_w{i}", [128, KC * 512], BF16) for i in range(2)]; t_wt = [Tok(), Tok()]
            qo = [sb(f"a_qo{i}", [128, 512], BF16) for i in range(2)]; t_qo = [Tok(), Tok()]
            vt = [sb(f"a_vt{i}", [128, 512], BF16) for i in range(2)]; t_vt = [Tok(), Tok()]
            gt = [sb(f"a_gt{i}", [128, 16], F32) for i in range(2)]; t_gt = [Tok(), Tok()]
            ge = [sb(f"a_ge{i}", [128, 8], F32) for i in range(2)]; t_ge = [Tok(), Tok()]
            SVl = self.SV[li]
            wg = 0
            cnt = 0
            for (t0, N, col) in tiles:
                kb.dma("sp", xt[:, :, :N], XT[:, t0:t0 + N].rearrange("(kc p) n -> p kc n", p=128), reads=[t_XT], writes=[t_xt])
                self.norm_mod(pes, xt, t_xt, N, col, self.SCA1[li], 0, li, ht, t_ht, sq, t_sq, tmp, t_tmp, 7, rs, t_rs)
                for cg in range(4):
                    b = wg % 2
                    wg += 1
                    wv = self.wview(wt[b], KC, 512)
                    kb.dma("sp", wv, w["mix_b"][:, cg * 512:(cg + 1) * 512].rearrange("(kc p) n -> p kc n", p=128), reads=[self.t_Wl[li]], writes=[t_wt[b]])
                    for j in range(4):
                        c = cg * 4 + j
                        pi = cnt % 2
                        ob = cnt % 2
                        cnt += 1
                        for kc in range(KC):
                            kb.op("pe", lambda e: e.matmul(self.ps[pi][:, :N], lhsT=wv[:, kc, j * 128:(j + 1) * 128], rhs=ht[:, kc, :N],
                                                           start=(kc == 0), stop=(kc == KC - 1)),
                                  reads=[t_wt[b], t_ht], writes=[self.t_ps[pi]], signal=(kc == KC - 1))
                        isq = c < 8
                        kb.op("act", lambda e: e.activation(out=qo[ob][:, :N], in_=self.ps[pi][:, :N], func=AF.Copy, scale=(1.0 if isq else 0.0625)),
                              reads=[self.t_ps[pi]], writes=[t_qo[ob]])
                        dstT, t_dst = (self.QT, self.t_QT) if isq else (self.KT, self.t_KT)
                        row0 = (c if isq else c - 8) * 128
                        kb.dma("pool", dstT[row0:row0 + 128, t0:t0 + N], qo[ob][:, :N], reads=[t_qo[ob]], writes=[t_dst])
                groups = [(1024 + g * 512, 512, "k", g * 512) for g in range(2)] + [(2048 + g * 512, 512, "v", g * 512) for g in range(4)] \
                    + [(4096 + g * 512, 512, "o", g * 512) for g in range(4)] + [(6144, 16, "g", 0)]
                for (c0, wd, what, d0) in groups:
                    b = wg % 2
                    wg += 1
                    wv = self.wview(wt[b], KC, 512)
                    kb.dma("sp", wv[:, :, :wd], w["mix_b"][:, c0:c0 + wd].rearrange("(kc p) n -> p kc n", p=128), reads=[self.t_Wl[li]], writes=[t_wt[b]])
                    for tb in range(N // 128):
                        pi = 2 + cnt % 2
                        vb = cnt % 2
                        cnt += 1
                        r0 = t0 + tb * 128
                        for kc in range(KC):
                            kb.op("pe", lambda e: e.matmul(self.ps[pi][:, :wd], lhsT=ht[:, kc, tb * 128:(tb + 1) * 128], rhs=wv[:, kc, :wd],
                                                           start=(kc == 0), stop=(kc == KC - 1)),
                                  reads=[t_wt[b], t_ht], writes=[self.t_ps[pi]], signal=(kc == KC - 1))
                        if what == "g":
                            g_, tg_ = gt[vb], t_gt[vb]
                            kb.op("dve", lambda e: e.tensor_tensor(g_[:, :], self.ps[pi][:, :16], SVl[:, SV_SINK:SV_SINK + 16], ALU.add),
                                  reads=[self.t_ps[pi], self.t_mod[li]], writes=[tg_])
                            e_, te_ = ge[vb], t_ge[vb]
                            for (src, dst) in ((4, 0), (12, 4)):
                                kb.op("act", lambda e: e.activation(out=e_[:, dst:dst + 4], in_=g_[:, src:src + 4], func=AF.Exp, scale=-1.0),
                                      reads=[tg_], writes=[te_])
                            kb.op("dve", lambda e: e.tensor_scalar_add(e_[:, :], e_[:, :], 1.0), reads=[te_], writes=[te_])
                            kb.op("act", lambda e: e.activation(out=e_[:, :], in_=e_[:, :], func=AF.Ln), reads=[te_], writes=[te_])
                            for (src, dst) in ((0, 4), (4, 12)):
                                kb.op("dve", lambda e: e.tensor_scalar_mul(g_[:, dst:dst + 4], e_[:, src:src + 4], -1.0), reads=[te_, tg_], writes=[tg_])
                            kb.dma("pool", self.G[r0:r0 + 128, :], g_[:, :], reads=[tg_], writes=[self.t_G])
                        else:
                            kb.op("act", lambda e: e.activation(out=vt[vb][:, :wd], in_=self.ps[pi][:, :wd], func=AF.Copy,
                                                                scale=(0.0625 if what == "k" else 1.0)),
                                  reads=[self.t_ps[pi]], writes=[t_vt[vb]])
                            dst, t_dst = {"k": (self.Ktok, self.t_Ktok), "v": (self.V, self.t_V), "o": (self.Og, self.t_Og)}[what]
                            kb.dma("pool", dst[r0:r0 + 128, d0:d0 + wd], vt[vb][:, :wd], reads=[t_vt[vb]], writes=[t_dst])
            kb.barrier()

    def phaseB_ml(self, li):
        kb, nc = self.kb, self.nc
        S, T = self.S, self.T
        L = 64
        nxc, ncc = S // L, CTX // L
        with ExitStack() as pes:
            sb = lambda name, shape, dt: pes.enter_context(nc.sbuf_tensor(self.uname(name), list(shape), dt))
            identf = sb("m_identf", [128, 128], F32)
            onesf = sb("m_onesf", [128, 128], F32)
            tri = sb("m_tri", [64, 128], F32)
            t_c = Tok()
            kb.dma("sp", identf[:], self.in_identf[:, :], writes=[t_c])
            kb.dma("sp", onesf[:], self.in_onesf[:, :], writes=[t_c])
            kb.dma("sp", tri[:], self.in_tri[:, :], writes=[t_c])
            C = [[sb(f"m_C{d}{h}", [128, 2, 512], F32) for h in range(4)] for d in range(2)]
            Cb = [[sb(f"m_Cb{d}{h}", [128, 2, 512], BF16) for h in range(4)] for d in range(2)]
            nn = [[sb(f"m_n{d}{h}", [128, 2], F32) for h in range(4)] for d in range(2)]
            nb = [[sb(f"m_nb{d}{h}", [128, 2], BF16) for h in range(4)] for d in range(2)]
            t_C = [[Tok() for h in range(4)] for d in range(2)]
            t_Cb = [[Tok() for h in range(4)] for d in range(2)]
            t_n = [[Tok() for h in range(4)] for d in range(2)]
            t_nb = [[Tok() for h in range(4)] for d in range(2)]
            for d in range(2):
                for h in range(4):
                    kb.op("pool", lambda e: e.memset(C[d][h][:], 0.0), writes=[t_C[d][h]])
                    kb.op("pool", lambda e: e.memset(Cb[d][h][:], 0.0), writes=[t_Cb[d][h]])
                    kb.op("pool", lambda e: e.memset(nn[d][h][:], 0.0), writes=[t_n[d][h]])
                    kb.op("pool", lambda e: e.memset(nb[d][h][:], 0.0), writes=[t_nb[d][h]])
            NB = 2
            D2 = range(2)
            mk = lambda name, shape, dt, n: [sb(f"{name}_{i}", shape, dt) for i in range(n)]
            tk = lambda n: [Tok() for _ in range(n)]
            qT = mk("m_qT", [128, 8, L], BF16, NB); t_qT = tk(NB)
            kT = mk("m_kT", [128, 8, L], BF16, NB); t_kT = tk(NB)
            kk = mk("m_kk", [L, 1024], BF16, NB); t_kk = tk(NB)
            vv = mk("m_vv", [L, 2048], BF16, NB); t_vv = tk(NB)
            gg = mk("m_gg", [L, 16], F32, NB); t_gg = tk(NB)
            bb = mk("m_b", [L, 4], F32, 2); t_bb = tk(2)
            wprev = mk("m_wprev", [L, 4], F32, 2); t_wprev = tk(2)
            lmb = mk("m_lmb", [L, 4], F32, 2); t_lmb = tk(2)
            ws = mk("m_ws", [L, 4], F32, 2); t_ws = tk(2)
            decay = mk("m_decay", [128, 4], F32, 2); t_decay = tk(2)
            diagb = mk("m_diagb", [L, L], F32, 4); t_diagb = tk(4)
            Eh = mk("m_E", [L, L], F32, 4); t_Eh = tk(4)
            WT = mk("m_WT", [L, L], BF16, 4); t_WT = tk(4)
            n1 = mk("m_n1", [L, 512], F32, 4); t_n1 = tk(4)
            hh = mk("m_hh", [L, 512], F32, 4); t_hh = tk(4)
            den = mk("m_den", [L, 2], F32, 4); t_den = tk(4)
            kp = mk("m_kp", [L, 256], BF16, 4); t_kp = tk(4)
            ps, tps = self.ps, self.t_ps
            pA, tA = ps[0], tps[0]
            pB, tB = ps[1], tps[1]
            orders = [[S + c * L for c in range(ncc)] + [c * L for c in range(nxc)],
                      [S + c * L for c in reversed(range(ncc))] + [c * L for c in reversed(range(nxc))]]
            cidx = 0
            for step in range(ncc + nxc):
                for d in D2:
                    tk0 = orders[d][step]
                    lic, lfc = (0, 4) if d == 0 else (8, 12)
                    trid = tri[:, 0:64] if d == 0 else tri[:, 64:128]
                    cb = cidx % NB
                    cidx += 1
                    qT_, kT_, kk_, vv_, G_ = qT[cb], kT[cb], kk[cb], vv[cb], gg[cb]
                    tq, tkT, tkk, tvv, tgg = t_qT[cb], t_kT[cb], t_kk[cb], t_vv[cb], t_gg[cb]
                    kb.dma("sp", qT_[:], self.QT[0:1024, tk0:tk0 + L].rearrange("(c p) n -> p c n", p=128), reads=[self.t_QT], writes=[tq])
                    kb.dma("sp", kT_[:], self.KT[0:1024, tk0:tk0 + L].rearrange("(c p) n -> p c n", p=128), reads=[self.t_KT], writes=[tkT])
                    kb.dma("sp", kk_[:], self.Ktok[tk0:tk0 + L, :], reads=[self.t_Ktok], writes=[tkk])
                    kb.dma("sp", vv_[:], self.V[tk0:tk0 + L, :], reads=[self.t_V], writes=[tvv])
                    kb.dma("sp", G_[:], self.G[tk0:tk0 + L, :], reads=[self.t_G], writes=[tgg])
                    bb_, wprev_, lmb_, ws_, decay_ = bb[cb], wprev[cb], lmb[cb], ws[cb], decay[cb]
                    tbb, twp, tlmb, tws, tdec = t_bb[cb], t_wprev[cb], t_lmb[cb], t_ws[cb], t_decay[cb]
                    H4 = range(4)
                    kb.op("pe", lambda e: e.matmul(pA[0:L, 0:4], lhsT=trid, rhs=G_[:, lfc:lfc + 4], start=True, stop=True),
                          reads=[t_c, tgg], writes=[tA])
                    kb.op("pe", lambda e: e.matmul(pA[:, 16:20], lhsT=onesf[0:L, :], rhs=G_[:, lfc:lfc + 4], start=True, stop=True),
                          reads=[t_c, tgg], writes=[tA])
                    kb.op("dve", lambda e: e.tensor_copy(bb_[:, :], pA[0:L, 0:4]), reads=[tA], writes=[tbb])
                    kb.op("act", lambda e: e.activation(out=wprev_[:, :], in_=pA[0:L, 0:4], func=AF.Exp), reads=[tA], writes=[twp])
                    kb.op("act", lambda e: e.activation(out=decay_[:, :], in_=pA[:, 16:20], func=AF.Exp), reads=[tA], writes=[tdec])
                    kb.op("dve", lambda e: e.tensor_tensor(lmb_[:, :], G_[:, lic:lic + 4], bb_[:, :], ALU.subtract), reads=[tgg, tbb], writes=[tlmb])
                    kb.op("dve", lambda e: e.tensor_tensor(ws_[:, :], lmb_[:, :], pA[0:L, 16:20], ALU.add), reads=[tlmb, tA], writes=[tws])
                    kb.op("act", lambda e: e.activation(out=ws_[:, :], in_=ws_[:, :], func=AF.Exp), reads=[tws], writes=[tws])
                    for h in H4:
                        kb.op("dve", lambda e: e.tensor_scalar_mul(diagb[h][:, :], identf[0:L, 0:L], bb_[:, h:h + 1]), reads=[t_c, tbb], writes=[t_diagb[h]])
                    for h in H4:
                        kb.op("pe", lambda e: e.matmul(pA[0:L, 64 + 64 * h:128 + 64 * h], lhsT=onesf[0:L, 0:L], rhs=diagb[h][:, :], start=True, stop=True),
                              reads=[t_c, t_diagb[h]], writes=[tA])
                    for h in H4:
                        for i in range(2):
                            kb.op("pe", lambda e: e.matmul(pB[0:L, 64 * h:64 * h + 64], lhsT=kT_[:, 2 * h + i, :], rhs=qT_[:, 2 * h + i, :], start=(i == 0), stop=(i == 1)),
                                  reads=[tkT, tq], writes=[tB], signal=(i == 1))
                    for h in H4:
                        kb.op("dve", lambda e: e.tensor_scalar(Eh[h][:, :], pA[0:L, 64 + 64 * h:128 + 64 * h], lmb_[:, h:h + 1], 60.0, ALU.add, ALU.min),
                              reads=[tA, tlmb], writes=[t_Eh[h]])
                    for h in H4:
                        kb.op("act", lambda e: e.activation(out=Eh[h][:, :], in_=Eh[h][:, :], func=AF.Exp), reads=[t_Eh[h]], writes=[t_Eh[h]])
                    for h in H4:
                        kb.op("pool", lambda e: e.tensor_tensor(Eh[h][:, :], Eh[h][:, :], trid, ALU.mult), reads=[t_Eh[h], t_c], writes=[t_Eh[h]])
                    for h in H4:
                        kb.op("dve", lambda e: e.tensor_tensor(WT[h][:, :], Eh[h][:, :], pB[0:L, 64 * h:64 * h + 64], ALU.mult), reads=[t_Eh[h], tB], writes=[t_WT[h]])
                    for h in H4:
                        kb.op("pool", lambda e: e.tensor_scalar_mul(kp[h][:, :], kk_[:, h * 256:(h + 1) * 256], ws_[:, h:h + 1]),
                              reads=[tkk, tws], writes=[t_kp[h]])
                    for h in H4:
                        kb.op("pe", lambda e: e.matmul(pA[0:L, 320 + h:321 + h], lhsT=WT[h][:, :], rhs=self.ones[0:L, 0:1], start=True, stop=True),
                              reads=[t_WT[h], self.t_const], writes=[tA])
                        for i in range(2):
                            kb.op("pe", lambda e: e.matmul(pA[0:L, 328 + h:329 + h], lhsT=qT_[:, 2 * h + i, :], rhs=nb[d][h][:, i:i + 1], start=(i == 0), stop=(i == 1)),
                                  reads=[tq, t_nb[d][h]], writes=[tA], signal=(i == 1))
                        for i in range(2):
                            kb.op("pe", lambda e: e.matmul(pA[:, 336 + 2 * h + i:337 + 2 * h + i], lhsT=kp[h][:, i * 128:(i + 1) * 128], rhs=self.ones[0:L, 0:1],
                                                           start=True, stop=True),
                                  reads=[t_kp[h], self.t_const], writes=[tA])
                    for h in H4:
                        kb.op("act", lambda e: e.activation(out=den[h][:, 0:1], in_=pA[0:L, 320 + h:321 + h], func=AF.Copy), reads=[tA], writes=[t_den[h]])
                    for h in H4:
                        kb.op("dve", lambda e: e.scalar_tensor_tensor(den[h][:, 1:2], pA[0:L, 328 + h:329 + h], wprev_[:, h:h + 1], den[h][:, 0:1], ALU.mult, ALU.add),
                              reads=[tA, twp, t_den[h]], writes=[t_den[h]])
                    for h in H4:
                        for i in range(2):
                            kb.op("dve", lambda e: e.scalar_tensor_tensor(nn[d][h][:, i:i + 1], nn[d][h][:, i:i + 1], decay_[:, h:h + 1],
                                                                          pA[:, 336 + 2 * h + i:337 + 2 * h + i], ALU.mult, ALU.add),
                                  reads=[t_n[d][h], tdec, tA], writes=[t_n[d][h]])
                    for h in H4:
                        kb.op("act", lambda e: e.activation(out=den[h][:, 1:2], in_=den[h][:, 1:2], func=AF.Abs), reads=[t_den[h]], writes=[t_den[h]])
                    for h in H4:
                        kb.op("dve", lambda e: e.tensor_scalar_max(den[h][:, 1:2], den[h][:, 1:2], 1.0), reads=[t_den[h]], writes=[t_den[h]])
                    for h in H4:
                        kb.op("dve", lambda e: e.reciprocal(den[h][:, 1:2], den[h][:, 1:2]), reads=[t_den[h]], writes=[t_den[h]])
                    for h in H4:
                        vh = vv_[:, h * 512:(h + 1) * 512]
                        pN1, tN1 = ps[2 + h % 2], tps[2 + h % 2]
                        pN2, tN2 = ps[4 + h % 2], tps[4 + h % 2]
                        kb.op("pe", lambda e: e.matmul(pN1[0:L, :], lhsT=WT[h][:, :], rhs=vh, start=True, stop=True),
                              reads=[t_WT[h], tvv], writes=[tN1])
                        for i in range(2):
                            kb.op("pe", lambda e: e.matmul(pN2[0:L, :], lhsT=qT_[:, 2 * h + i, :], rhs=Cb[d][h][:, i, :], start=(i == 0), stop=(i == 1)),
                                  reads=[tq, t_Cb[d][h]], writes=[tN2], signal=(i == 1))
                        kb.op("act", lambda e: e.activation(out=n1[h][:, :], in_=pN1[0:L, :], func=AF.Copy), reads=[tN1], writes=[t_n1[h]])
                        kb.op("dve", lambda e: e.scalar_tensor_tensor(hh[h][:, :], pN2[0:L, :], wprev_[:, h:h + 1], n1[h][:, :], ALU.mult, ALU.add),
                              reads=[tN2, twp, t_n1[h]], writes=[t_hh[h]])
                        kb.op("pool", lambda e: e.tensor_scalar_mul(hh[h][:, :], hh[h][:, :], den[h][:, 1:2]), reads=[t_hh[h], t_den[h]], writes=[t_hh[h]])
                        kb.dma("pool", self.H[d][tk0:tk0 + L, h * 512:(h + 1) * 512], hh[h][:, :], reads=[t_hh[h]], writes=[self.t_H[d]])
                    for h in H4:
                        vh = vv_[:, h * 512:(h + 1) * 512]
                        for i in range(2):
                            pDC, tDC = ps[6 + i], tps[6 + i]
                            kb.op("pe", lambda e: e.matmul(pDC[:, :], lhsT=kp[h][:, i * 128:(i + 1) * 128], rhs=vh, start=True, stop=True),
                                  reads=[t_kp[h], tvv], writes=[tDC])
                            kb.op("dve", lambda e: e.scalar_tensor_tensor(C[d][h][:, i, :], C[d][h][:, i, :], decay_[:, h:h + 1], pDC[:, :], ALU.mult, ALU.add),
                                  reads=[t_C[d][h], tdec, tDC], writes=[t_C[d][h]])
                            kb.op("act", lambda e: e.activation(out=Cb[d][h][:, i, :], in_=C[d][h][:, i, :], func=AF.Copy), reads=[t_C[d][h]], writes=[t_Cb[d][h]])
                        kb.op("act", lambda e: e.activation(out=nb[d][h][:, :], in_=nn[d][h][:, :], func=AF.Copy), reads=[t_n[d][h]], writes=[t_nb[d][h]])
            kb.barrier()

    def phaseR_ml(self, li):
        kb, nc = self.kb, self.nc
        S, T = self.S, self.T
        w = self.W[li]
        with ExitStack() as pes:
            sb = lambda name, shape, dt: pes.enter_context(nc.sbuf_tensor(self.uname(name), list(shape), dt))
            hg = sb("r_hg", [128, D], F32); t_hg = Tok()
            kb.dma("sp", hg[:], w["headg"][:, :], writes=[t_hg])
            hf = [sb(f"r_hf{i}", [128, D], F32) for i in range(2)]; t_hf = [Tok(), Tok()]
            hb_ = [sb(f"r_hb{i}", [128, D], F32) for i in range(2)]; t_hb = [Tok(), Tok()]
            og = [sb(f"r_og{i}", [128, D], BF16) for i in range(2)]; t_og = [Tok(), Tok()]
            sgm = sb("r_sg", [128, D], F32); t_sg = Tok()
            sqb = sb("r_sq", [128, D], F32); t_sqb = Tok()
            ss = sb("r_ss", [128, 4], F32); t_ss = Tok()
            y = sb("r_y", [128, D], BF16); t_y = Tok()
            ot = [sb(f"r_ot{i}", [128, 512], BF16) for i in range(2)]; t_ot = [Tok(), Tok()]
            cnt = 0
            for bi in range(T // 128):
                r0 = bi * 128
                b = bi % 2
                kb.dma("sp", hf[b][:], self.H[0][r0:r0 + 128, :], reads=[self.t_H[0]], writes=[t_hf[b]])
                kb.dma("sp", hb_[b][:], self.H[1][r0:r0 + 128, :], reads=[self.t_H[1]], writes=[t_hb[b]])
                kb.dma("sp", og[b][:], self.Og[r0:r0 + 128, :], reads=[self.t_Og], writes=[t_og[b]])
                kb.op("dve", lambda e: e.tensor_tensor(hf[b][:], hf[b][:], hb_[b][:], ALU.add), reads=[t_hf[b], t_hb[b]], writes=[t_hf[b]])
                kb.op("act", lambda e: e.activation(out=sqb[:], in_=hf[b][:], func=AF.Square), reads=[t_hf[b]], writes=[t_sqb])
                kb.op("dve", lambda e: e.reduce_sum(ss[:, :], sqb[:].rearrange("p (h n) -> p h n", h=4), mybir.AxisListType.X), reads=[t_sqb], writes=[t_ss])
                kb.op("dve", lambda e: e.tensor_scalar(ss[:, :], ss[:, :], 1.0 / 512, EPS, ALU.mult, ALU.add), reads=[t_ss], writes=[t_ss])
                kb.op("act", lambda e: e.activation(out=ss[:, :], in_=ss[:, :], func=AF.Sqrt), reads=[t_ss], writes=[t_ss])
                kb.op("dve", lambda e: e.reciprocal(ss[:, :], ss[:, :]), reads=[t_ss], writes=[t_ss])
                kb.op("act", lambda e: e.activation(out=sgm[:], in_=og[b][:], func=AF.Sigmoid), reads=[t_og[b]], writes=[t_sg])
                kb.op("pool", lambda e: e.tensor_tensor(sgm[:], sgm[:], hg[:], ALU.mult), reads=[t_sg, t_hg], writes=[t_sg])
                for h in range(4):
                    kb.op("dve", lambda e: e.scalar_tensor_tensor(y[:, h * 512:(h + 1) * 512], hf[b][:, h * 512:(h + 1) * 512], ss[:, h:h + 1],
                                                                  sgm[:, h * 512:(h + 1) * 512], ALU.mult, ALU.mult),
                          reads=[t_hf[b], t_ss, t_sg], writes=[t_y])
                for g4 in range(4):
                    pi = cnt % 2
                    ob = cnt % 2
                    cnt += 1
                    for jj in range(4):
                        j = g4 * 4 + jj
                        kb.op("pe", lambda e: e.matmul(self.ps[pi][:, jj * 128:(jj + 1) * 128], lhsT=y[:, j * 128:(j + 1) * 128], rhs=self.ident[:, :],
                                                       start=True, stop=True),
                              reads=[t_y, self.t_const], writes=[self.t_ps[pi]], signal=(jj == 3))
                    kb.op("act", lambda e: e.activation(out=ot[ob][:, :], in_=self.ps[pi][:, :], func=AF.Copy), reads=[self.t_ps[pi]], writes=[t_ot[ob]])
                    kb.dma("pool", self.OT[g4 * 512:(g4 + 1) * 512, r0:r0 + 128].rearrange("(jj p) n -> p jj n", p=128),
                           ot[ob][:, :].rearrange("p (jj n) -> p jj n", n=128), reads=[t_ot[ob]], writes=[self.t_OT])
            kb.barrier()

    def build(self):
        self.phase0()
        xi = 0
        for idx, (li, kind, need_ctx) in enumerate(self.layers):
            final = idx == len(self.layers) - 1
            self.kb.flush_bg(self.t_Wl[li])
            if kind == 2:
                self.phaseA_ml(li, xi)
                self.phaseB_ml(li)
                self.phaseR_ml(li)
            else:
                self.phaseA(li, kind, xi)
                self.phaseB(li, kind, need_ctx)
            self.phaseC(li, need_ctx, xi, final)
            xi = 1 - xi
        self.kb.barrier(engines=("sp",))
        self.es.close()
        return self.nc


def na_base_block(ci, SB):
    return min(max(4 * ci - 2, 0), SB - 8)


def na_patterns(S):
    rows = S // GW
    kr = min(8, rows)
    SB = S // 128
    pats, pat_of_C, keymap = [], [], {}
    for ci in range(S // 512):
        bbw = na_base_block(ci, SB)
        r0s = tuple(min(max(8 * ci + a - kr // 2, 0), rows - kr) - 8 * ci for a in range(8))
        key = (bbw - 4 * ci, r0s)
        if key not in keymap:
            keymap[key] = len(pats)
            pats.append(ci)
        pat_of_C.append(keymap[key])
    return pat_of_C, pats


def build_na_table(rel_bias, S):
    rows = S // GW
    kr = min(8, rows)
    SB = S // 128
    pat_of_C, pats = na_patterns(S)
    npat = len(pats)
    rb = np.asarray(rel_bias, np.float32)
    tab = np.full((npat, 16, 128, 8, 512), -30000.0, np.float32)
    i = np.arange(512)
    qc = i % GW
    c0 = np.clip(qc - 8, 0, GW - 16)
    j = np.arange(128)
    for p, ci in enumerate(pats):
        bbw = na_base_block(ci, SB)
        qr = 8 * ci + i // GW
        r0 = np.clip(qr - kr // 2, 0, rows - kr)
        for jb in range(8):
            kr_ = 2 * (bbw + jb) + j // GW
            kc_ = j % GW
            ok = ((kr_[:, None] >= r0[None, :]) & (kr_[:, None] < r0[None, :] + kr)
                  & (kc_[:, None] >= c0[None, :]) & (kc_[:, None] < c0[None, :] + 16))
            drow = np.clip(kr_[:, None] - qr[None, :] + 7, 0, 14)
            dcol = np.clip(kc_[:, None] - qc[None, :] + 15, 0, 30)
            vals = rb[:, drow, dcol]
            tab[p, :, :, jb, :] = np.where(ok[None], vals, np.float32(-30000.0))
    return tab.reshape(npat * 16 * 128, 8 * 512)


def swa_mask_table():
    j = np.arange(128)
    prev = (j[:, None] >= j[None, :]).astype(np.float32)
    nxt = (j[:, None] <= j[None, :]).astype(np.float32)
    t = np.zeros((128, 6, 4, 128), np.float32)
    for r in range(6):
        for a in range(4):
            rel = r - 1 - a
            if rel == -1:
                t[:, r, a, :] = prev
            elif rel == 0:
                t[:, r, a, :] = 1.0
            elif rel == 1:
                t[:, r, a, :] = nxt
    return t.reshape(128, 6 * 512).astype(ml_dtypes.bfloat16)


def rope_tables(S):
    t = np.arange(S)
    row = (t // GW).astype(np.float32)
    colp = (t % GW).astype(np.float32)
    inv = (10000.0 ** (-np.arange(32, dtype=np.float32) / 32)).astype(np.float32)
    cosT = np.zeros((128, S), np.float32)
    sinT = np.zeros((128, S), np.float32)
    for a, pos in enumerate((row, colp)):
        ang = (pos[None, :] * inv[:, None]).astype(np.float32)
        for p in range(2):
            cosT[a * 64 + p * 32:a * 64 + (p + 1) * 32] = np.cos(ang)
            sinT[a * 64 + p * 32:a * 64 + (p + 1) * 32] = np.sin(ang)
    rot = np.zeros((128, 128), np.float32)
    for a in range(2):
        for f in range(32):
            d1, d2 = a * 64 + f, a * 64 + 32 + f
            rot[d2, d1] = -1.0
            rot[d1, d2] = 1.0
    return cosT, sinT, rot


def fm(v, ncol):
    return np.ascontiguousarray(np.asarray(v, np.float32).reshape(ncol, 128).T)


def pack_sv(li, kind, P):
    sv = np.zeros((128, NSV), np.float32)
    sv[:, SV_ADAB:SV_ADAB + 96] = fm(P["ada_b"][li], 96)
    sv[:, SV_N1:SV_N1 + 16] = fm(P["norm1_g"][li], 16)
    sv[:, SV_N2:SV_N2 + 16] = fm(P["norm2_g"][li], 16)
    for j in range(3):
        sv[:, SV_CW + j * FC:SV_CW + (j + 1) * FC] = fm(P["ffn_conv_w"][li][j], FC)
    sv[:, SV_CB:SV_CB + FC] = fm(P["ffn_conv_b"][li], FC)
    pre = {0: "na", 1: "swa", 3: "gqa"}.get(kind)
    if pre:
        sv[:, SV_QG] = np.asarray(P[pre + "_q_g"][0], np.float32)
        sv[:, SV_KG] = np.asarray(P[pre + "_k_g"][0], np.float32)
    if kind == 1:
        sv[:, SV_SINK:SV_SINK + 16] = np.asarray(P["swa_sinks"][0], np.float32)[None, :]
    if kind == 2:
        sv[:, SV_SINK:SV_SINK + 16] = np.asarray(P["ml_gate_b"][0], np.float32)[None, :]
    return sv


def core_inputs(b, S, layers, P, consts):
    m = dict(consts)
    m["xT"] = np.ascontiguousarray(np.asarray(P["x"][b, :S], np.float32).T)
    m["cxT"] = np.ascontiguousarray(np.asarray(P["ctx"][b], np.float32).T)
    cc = np.stack([fm(P["c"][b], KC), fm(P["c_ctx"], KC)], axis=-1)
    m["cc"] = np.ascontiguousarray(cc)
    return m


def shared_inputs(S, layers, P):
    cosT, sinT, rot = rope_tables(S)
    bf = ml_dtypes.bfloat16
    j = np.arange(128)
    m = {
        "ones": np.ones((128, 128), bf), "ident": np.eye(128, dtype=np.float32).astype(bf), "rot": rot.astype(bf),
        "cosT": cosT, "sinT": sinT,
        "mprev": (j[:, None] >= j[None, :]).astype(np.float32).astype(bf),
        "mnext": (j[:, None] <= j[None, :]).astype(np.float32).astype(bf),
        "swam": swa_mask_table(),
    }
    if any(k == 2 for (_, k, _) in layers):
        jj = np.arange(64)
        m["identf"] = np.eye(128, dtype=np.float32)
        m["onesf"] = np.ones((128, 128), np.float32)
        m["tri"] = np.concatenate([(jj[:, None] <= jj[None, :]), (jj[:, None] >= jj[None, :])], axis=1).astype(np.float32)
    mixw = {0: ("na_w_qkv", "na_w_o"), 1: ("swa_w_qkv", "swa_w_o"), 2: ("ml_w_in", "ml_w_o"), 3: ("gqa_w_qkv", "gqa_w_o")}
    for (li, kind, _) in layers:
        m[f"ada_w{li}"] = np.asarray(P["ada_w"][li], np.float32)
        m[f"ffn_w_in{li}"] = np.asarray(P["ffn_w_in"][li], np.float32)
        m[f"ffn_w_out{li}"] = np.asarray(P["ffn_w_out"][li], np.float32)
        m[f"mix_w_in{li}"] = np.asarray(P[mixw[kind][0]][0], np.float32)
        m[f"mix_w_o{li}"] = np.asarray(P[mixw[kind][1]][0], np.float32)
        m[f"sv{li}"] = pack_sv(li, kind, P)
        if kind == 0:
            m[f"natab{li}"] = build_na_table(P["na_rel_bias"][0], S)
        if kind == 2:
            m[f"headg{li}"] = np.ascontiguousarray(np.broadcast_to(np.asarray(P["ml_head_g"][0], np.float32)[None, :], (128, D)))
    return m


def run_model(P, S, layers, batches, trace=False):
    prog = Prog(S, layers)
    nc = prog.build()
    shared = shared_inputs(S, layers, P)
    in_maps = [core_inputs(b, S, layers, P, shared) for b in batches]
    res = run_bass_kernel_spmd(nc, in_maps, core_ids=list(range(len(batches))), trace=trace)
    outs = [np.ascontiguousarray(r["outT"].T) for r in res.results]
    return np.stack(outs, 0), res


def kernel(**inputs):
    S = inputs["x"].shape[1]
    layers = [(i, i % 4, i < 3) for i in range(4)]
    out, _ = run_model(inputs, S, layers, list(range(inputs["x"].shape[0])))
    return out.astype(np.float32)
```

```python
import math
from contextlib import ExitStack

import ml_dtypes
import numpy as np

import concourse.bass as bass
import concourse.mybir as mybir
from concourse.bass_utils import run_bass_kernel_spmd

F32, BF16 = mybir.dt.float32, mybir.dt.bfloat16
AF = mybir.ActivationFunctionType
ALU = mybir.AluOpType

D = 2048
KC = 16
DFF = 5632
FC = 44
CTX = 256
GW = 64
EPS = 1e-6
NSV = 96 + 16 + 16 + 132 + 44 + 1 + 1 + 1 + 16
SV_ADAB, SV_N1, SV_N2, SV_CW, SV_CB, SV_QG, SV_KG, SV_GB, SV_SINK = 0, 96, 112, 128, 260, 304, 305, 306, 307
GK = 1.5957691216057308
MIX_COLS = {0: 6144, 1: 2560, 2: 6160, 3: 3072}
NKV = {0: 16, 1: 2, 3: 4}


class Tok:
    __slots__ = ("w", "r")

    def __init__(self):
        self.w = {}
        self.r = {}


class KB:
    def __init__(self, nc, es, nring=6):
        self.nc = nc
        self.E = {"pe": nc.tensor, "act": nc.scalar, "dve": nc.vector, "pool": nc.gpsimd, "sp": nc.sync}
        self.csem = {e: es.enter_context(nc.semaphore("c_" + e)) for e in ("pe", "act", "dve", "pool")}
        self.ccnt = {e: 0 for e in self.csem}
        self.NR = nring
        self.dsem = {e: [es.enter_context(nc.semaphore(f"d_{e}{i}")) for i in range(nring)] for e in ("sp", "pool", "act")}
        self.dcnt = {e: [0] * nring for e in self.dsem}
        self.dnext = {e: 0 for e in self.dsem}
        self.waited = {e: {} for e in self.E}
        self.pending = {e: [] for e in self.csem}
        self.ninst = 0
        self.bg = []
        self.bgcnt = 0

    def _wait(self, e, ev):
        if ev is None:
            return
        sem, val, key = ev
        if key == "cpe" and e == "pe":
            return
        w = self.waited[e]
        if w.get(key, 0) >= val:
            return
        self.E[e].wait_ge(sem, val)
        self.ninst += 1
        w[key] = val

    def _deps(self, e, reads, writes):
        for t in reads:
            for ev in list(t.w.values()):
                self._wait(e, ev)
        for t in writes:
            for ev in list(t.w.values()):
                self._wait(e, ev)
            for ev in list(t.r.values()):
                self._wait(e, ev)

    @staticmethod
    def _commit(ev, reads, writes):
        for t in reads:
            t.r[ev[2]] = ev
        for t in writes:
            t.w[ev[2]] = ev
            t.r = {}

    def op(self, e, fn, reads=(), writes=(), signal=True):
        self._deps(e, reads, writes)
        ins = fn(self.E[e])
        self.ninst += 1
        if signal:
            self.ccnt[e] += 1
            ins.then_inc(self.csem[e], 1)
            ev = (self.csem[e], self.ccnt[e], "c" + e)
            self._commit(ev, reads, writes)
            for (r, w) in self.pending[e]:
                self._commit(ev, r, w)
            self.pending[e] = []
        else:
            self.pending[e].append((tuple(reads), tuple(writes)))
        return ins

    def dma(self, q, out, in_, reads=(), writes=()):
        i = self.dnext[q]
        self.dnext[q] = (i + 1) % self.NR
        sem = self.dsem[q][i]
        key = f"d{q}{i}"
        if self.dcnt[q][i]:
            self._wait(q, (sem, self.dcnt[q][i], key))
        self._deps(q, reads, writes)
        self.E[q].dma_start(out=out, in_=in_).then_inc(sem, 16)
        self.ninst += 1
        self.dcnt[q][i] += 16
        ev = (sem, self.dcnt[q][i], key)
        self._commit(ev, reads, writes)
        if q == "pool" and self.bg and not getattr(self, "_in_bg", False):
            self.bgcnt += 1
            if self.bgcnt % 2 == 0:
                self._in_bg = True
                o, i_, wr = self.bg.pop(0)
                self.dma("pool", o, i_, writes=wr)
                self._in_bg = False

    def flush_bg(self, tok):
        keep = []
        self._in_bg = True
        for (o, i_, wr) in self.bg:
            if tok in wr:
                self.dma("pool", o, i_, writes=wr)
            else:
                keep.append((o, i_, wr))
        self._in_bg = False
        self.bg = keep

    def barrier(self, engines=("pe", "act", "dve", "pool", "sp")):
        for e in engines:
            for q in self.dsem:
                for i in range(self.NR):
                    if self.dcnt[q][i]:
                        self._wait(e, (self.dsem[q][i], self.dcnt[q][i], f"d{q}{i}"))
            for c in self.csem:
                if self.ccnt[c] and c != e:
                    self._wait(e, (self.csem[c], self.ccnt[c], "c" + c))


class Prog:
    def __init__(self, S, layers, final_out=True):
        self.S, self.T = S, S + CTX
        self.layers = layers
        self.nc = nc = bass.Bass("TRN2", target_bir_lowering=False)
        self.es = ExitStack()
        self.kb = KB(nc, self.es)
        T = self.T
        di = lambda name, shape, dt=F32: nc.dram_tensor(name, list(shape), dt, kind="ExternalInput").ap()
        ds = lambda name, shape, dt: nc.dram_tensor(name, list(shape), dt).ap()
        self.in_xT = di("xT", [D, S])
        self.in_cxT = di("cxT", [D, CTX])
        self.in_cc = di("cc", [128, KC, 2])
        self.in_ones = di("ones", [128, 128], BF16)
        self.in_ident = di("ident", [128, 128], BF16)
        self.in_rot = di("rot", [128, 128], BF16)
        self.in_cos = di("cosT", [128, S])
        self.in_sin = di("sinT", [128, S])
        self.in_mprev = di("mprev", [128, 128], BF16)
        self.in_mnext = di("mnext", [128, 128], BF16)
        self.in_swam = di("swam", [128, 6 * 512], BF16)
        self.W = {}
        self.natab = None
        for (li, kind, _) in layers:
            w = {}
            w["ada"] = di(f"ada_w{li}", [D, 6 * D])
            w["win"] = di(f"ffn_w_in{li}", [D, 2 * DFF])
            w["wout"] = di(f"ffn_w_out{li}", [DFF, D])
            w["mix"] = di(f"mix_w_in{li}", [D, MIX_COLS[kind]])
            w["wo"] = di(f"mix_w_o{li}", [D, D])
            w["sv"] = di(f"sv{li}", [128, NSV])
            for k in ("ada", "win", "wout", "mix", "wo"):
                w[k + "_b"] = ds(f"{k}_b{li}", w[k].shape, BF16)
            if kind == 0:
                self.npat = len(na_patterns(S)[1])
                w["natab"] = di(f"natab{li}", [self.npat * 16 * 128, 8 * 512])
            if kind == 2:
                w["headg"] = di(f"headg{li}", [128, D])
            self.W[li] = w
        self.out = nc.dram_tensor("outT", [D, S], F32, kind="ExternalOutput").ap()
        self.XT = [ds("XT0", [D, T], F32), ds("XT1", [D, T], F32)]
        self.QT = ds("QT", [D, T], BF16)
        self.KT = ds("KT", [D, T], BF16)
        self.V = ds("V", [T, D], BF16)
        self.OT = ds("OT", [D, T], BF16)
        self.in_identf = di("identf", [128, 128])
        self.in_onesf = di("onesf", [128, 128])
        self.in_tri = di("tri", [64, 128])
        if any(k == 2 for (_, k, _) in layers):
            self.Ktok = ds("Ktok", [T, 1024], BF16)
            self.Og = ds("Og", [T, D], BF16)
            self.G = ds("G", [T, 16], F32)
            self.H = [ds("H0", [T, D], F32), ds("H1", [T, D], F32)]
            self.t_Ktok, self.t_Og, self.t_G, self.t_H = Tok(), Tok(), Tok(), [Tok(), Tok()]
        self.t_XT = [Tok(), Tok()]
        self.t_QT, self.t_KT, self.t_V, self.t_OT = Tok(), Tok(), Tok(), Tok()
        self.t_W = Tok()
        self.t_Wl = {li: Tok() for (li, _, _) in layers}
        es = self.es
        sb = lambda name, shape, dt: es.enter_context(nc.sbuf_tensor(self.uname(name), list(shape), dt))
        self.ps = [es.enter_context(nc.psum_tensor(f"ps{i}", [128, 512], F32)) for i in range(8)]
        self.t_ps = [Tok() for _ in range(8)]
        self.ones = sb("ones_sb", [128, 128], BF16)
        self.ident = sb("ident_sb", [128, 128], BF16)
        self.rot = sb("rot_sb", [128, 128], BF16)
        self.mprev = sb("mprev_sb", [128, 128], BF16)
        self.mnext = sb("mnext_sb", [128, 128], BF16)
        self.t_const = Tok()
        self.epsT = sb("eps_sb", [128, 1], F32)
        self.MOD, self.SCA1, self.SCA2, self.SV, self.t_mod = {}, {}, {}, {}, {}
        for (li, _, _) in layers:
            self.MOD[li] = sb(f"mod{li}", [128, 96, 2], F32)
            self.SCA1[li] = sb(f"sca1_{li}", [128, KC, 2], F32)
            self.SCA2[li] = sb(f"sca2_{li}", [128, KC, 2], F32)
            self.SV[li] = sb(f"svs{li}", [128, NSV], F32)
            self.t_mod[li] = Tok()

    def uname(self, name):
        self._uid = getattr(self, "_uid", 0) + 1
        return f"s{self._uid}_{name}"

    def wview(self, buf, kc, n):
        return buf[:, 0:kc * n].rearrange("p (k n) -> p k n", n=n)

    def norm_mod(self, pes, xt, t_xt, N, col, SCA, shift_base, li, ht, t_ht, sq, t_sq, tmp, t_tmp, psn, rs, t_rs):
        kb = self.kb
        ps, tps = self.ps[psn], self.t_ps[psn]
        for kc in range(KC):
            b = kc % 2
            kb.op("act", lambda e: e.activation(out=sq[b][:, :N], in_=xt[:, kc, :N], func=AF.Square),
                  reads=[t_xt], writes=[t_sq[b]])
            kb.op("pe", lambda e: e.matmul(ps[:, :N], lhsT=self.ones[:], rhs=sq[b][:, :N], start=(kc == 0), stop=(kc == KC - 1)),
                  reads=[t_sq[b], self.t_const], writes=[tps])
        kb.op("act", lambda e: e.activation(out=rs[:, :N], in_=ps[:, :N], func=AF.Sqrt, scale=1.0 / D, bias=self.epsT[:, 0:1]),
              reads=[tps, self.t_const], writes=[t_rs])
        kb.op("dve", lambda e: e.reciprocal(rs[:, :N], rs[:, :N]), reads=[t_rs], writes=[t_rs])
        for kc in range(KC):
            b = kc % 2
            kb.op("dve", lambda e: e.tensor_tensor(tmp[b][:, :N], xt[:, kc, :N], rs[:, :N], ALU.mult),
                  reads=[t_xt, t_rs], writes=[t_tmp[b]])
            kb.op("act", lambda e: e.activation(out=ht[:, kc, :N], in_=tmp[b][:, :N], func=AF.Identity,
                                                scale=SCA[:, kc, col:col + 1], bias=self.MOD[li][:, shift_base + kc, col:col + 1]),
                  reads=[t_tmp[b], self.t_mod[li]], writes=[t_ht])

    def phase0(self):
        kb, nc = self.kb, self.nc
        S, T = self.S, self.T
        for (dst, src) in ((self.ones, self.in_ones), (self.ident, self.in_ident), (self.rot, self.in_rot),
                           (self.mprev, self.in_mprev), (self.mnext, self.in_mnext)):
            kb.dma("sp", dst[:], src[:, :], writes=[self.t_const])
        kb.op("pool", lambda e: e.memset(self.epsT[:], EPS), writes=[self.t_const])
        for r in range(0, D, 256):
            kb.dma("sp", self.XT[0][r:r + 256, 0:S], self.in_xT[r:r + 256, :], writes=[self.t_XT[0]])
        kb.dma("sp", self.XT[0][:, S:T], self.in_cxT[:, :], writes=[self.t_XT[0]])
        for (li, kind, _) in self.layers:
            w = self.W[li]
            for r in range(0, D, 256):
                kb.dma("pool", w["ada_b"][r:r + 256, :], w["ada"][r:r + 256, :], writes=[self.t_W])
        for idx, (li, kind, _) in enumerate(self.layers):
            w = self.W[li]
            for k in ("mix", "wo", "win", "wout"):
                rows = w[k].shape[0]
                step = 256 if idx == 0 else 64
                for r in range(0, rows, step):
                    r1 = min(rows, r + step)
                    if idx == 0:
                        kb.dma("pool", w[k + "_b"][r:r1, :], w[k][r:r1, :], writes=[self.t_Wl[li]])
                    else:
                        kb.bg.append((w[k + "_b"][r:r1, :], w[k][r:r1, :], [self.t_Wl[li]]))
        with ExitStack() as pes:
            sb = lambda name, shape, dt: pes.enter_context(nc.sbuf_tensor(self.uname(name), list(shape), dt))
            cc = sb("cc", [128, KC, 2], F32)
            scb = sb("scb", [128, KC, 2], BF16)
            t_cc, t_scb = Tok(), Tok()
            wt = [sb(f"p0w{i}", [128, KC * 512], BF16) for i in range(2)]
            t_wt = [Tok(), Tok()]
            tmp = sb("p0tmp", [128, KC, 2], F32)
            t_tmp = Tok()
            kb.dma("sp", cc[:], self.in_cc[:, :, :], writes=[t_cc])
            kb.op("act", lambda e: e.activation(out=scb[:], in_=cc[:], func=AF.Silu), reads=[t_cc], writes=[t_scb])
            g = 0
            for (li, kind, _) in self.layers:
                w = self.W[li]
                kb.dma("sp", self.SV[li][:], w["sv"][:, :], writes=[self.t_mod[li]])
                for cg in range(24):
                    b = g % 2
                    g += 1
                    wv = self.wview(wt[b], KC, 512)
                    kb.dma("sp", wv, w["ada_b"][:, cg * 512:(cg + 1) * 512].rearrange("(kc p) n -> p kc n", p=128),
                           reads=[self.t_W], writes=[t_wt[b]])
                    for j in range(4):
                        c = cg * 4 + j
                        pi = c % 2
                        for kc in range(KC):
                            kb.op("pe", lambda e: e.matmul(self.ps[pi][:, 0:2], lhsT=wv[:, kc, j * 128:(j + 1) * 128], rhs=scb[:, kc, :],
                                                           start=(kc == 0), stop=(kc == KC - 1)),
                                  reads=[t_wt[b], t_scb], writes=[self.t_ps[pi]], signal=(kc == KC - 1))
                        kb.op("act", lambda e: e.activation(out=self.MOD[li][:, c, :], in_=self.ps[pi][:, 0:2], func=AF.Identity,
                                                            bias=self.SV[li][:, SV_ADAB + c:SV_ADAB + c + 1], scale=1.0),
                              reads=[self.t_ps[pi], self.t_mod[li]], writes=[self.t_mod[li]])
                for (SCA, sc0, g0) in ((self.SCA1[li], 16, SV_N1), (self.SCA2[li], 64, SV_N2)):
                    kb.op("dve", lambda e: e.tensor_scalar_add(tmp[:], self.MOD[li][:, sc0:sc0 + 16, :], 1.0),
                          reads=[self.t_mod[li]], writes=[t_tmp])
                    for col in range(2):
                        kb.op("dve", lambda e: e.tensor_tensor(SCA[:, :, col], tmp[:, :, col], self.SV[li][:, g0:g0 + 16], ALU.mult),
                              reads=[t_tmp, self.t_mod[li]], writes=[self.t_mod[li]])
            kb.barrier()

    def phaseA(self, li, kind, xi):
        kb, nc = self.kb, self.nc
        S, T = self.S, self.T
        w = self.W[li]
        nq, nkv = 16, NKV[kind]
        rope = kind in (1, 3)
        XT, t_XT = self.XT[xi], self.t_XT[xi]
        tiles = [(t0, min(512, S - t0), 0) for t0 in range(0, S, 512)] + [(S, CTX, 1)]
        with ExitStack() as pes:
            sb = lambda name, shape, dt: pes.enter_context(nc.sbuf_tensor(self.uname(name), list(shape), dt))
            xt = sb("a_xt", [128, KC, 512], F32); t_xt = Tok()
            ht = sb("a_ht", [128, KC, 512], BF16); t_ht = Tok()
            sq = [sb(f"a_sq{i}", [128, 512], BF16) for i in range(2)]; t_sq = [Tok(), Tok()]
            tmp = [sb(f"a_tmp{i}", [128, 512], F32) for i in range(2)]; t_tmp = [Tok(), Tok()]
            rs = sb("a_rs", [128, 512], F32); t_rs = Tok()
            wt = [sb(f"a_w{i}", [128, KC * 512], BF16) for i in range(3)]; t_wt = [Tok(), Tok(), Tok()]
            cs = [sb("a_cos", [128, 512], F32), sb("a_sin", [128, 512], F32)]; t_cs = Tok()
            sq2 = [sb(f"a_sq2{i}", [128, 512], BF16) for i in range(2)]; t_sq2 = [Tok(), Tok()]
            r2 = [sb(f"a_r2{i}", [128, 512], F32) for i in range(2)]; t_r2 = [Tok(), Tok()]
            qn = [sb(f"a_qn{i}", [128, 512], BF16) for i in range(2)]; t_qn = [Tok(), Tok()]
            t1 = [sb(f"a_t1{i}", [128, 512], F32) for i in range(2)]; t_t1 = [Tok(), Tok()]
            t2 = [sb(f"a_t2{i}", [128, 512], F32) for i in range(2)]; t_t2 = [Tok(), Tok()]
            qo = [sb(f"a_qo{i}", [128, 512], BF16) for i in range(2)]; t_qo = [Tok(), Tok()]
            vt = [sb(f"a_vt{i}", [128, 512], BF16) for i in range(2)]; t_vt = [Tok(), Tok()]
            gs = sb("a_gs", [128, 2], F32); t_gs = Tok()
            kb.op("dve", lambda e: e.tensor_scalar_mul(gs[:, 0:1], self.SV[li][:, SV_QG:SV_QG + 1], 128.0 ** -0.5),
                  reads=[self.t_mod[li]], writes=[t_gs])
            kb.op("dve", lambda e: e.tensor_copy(gs[:, 1:2], self.SV[li][:, SV_KG:SV_KG + 1]), reads=[self.t_mod[li]], writes=[t_gs])
            wg = 0
            cnt = 0
            for (t0, N, col) in tiles:
                kb.dma("sp", xt[:, :, :N], XT[:, t0:t0 + N].rearrange("(kc p) n -> p kc n", p=128), reads=[t_XT], writes=[t_xt])
                if rope and col == 0:
                    kb.dma("sp", cs[0][:, :N], self.in_cos[:, t0:t0 + N], writes=[t_cs])
                    kb.dma("sp", cs[1][:, :N], self.in_sin[:, t0:t0 + N], writes=[t_cs])
                self.norm_mod(pes, xt, t_xt, N, col, self.SCA1[li], 0, li, ht, t_ht, sq, t_sq, tmp, t_tmp, 7, rs, t_rs)
                nqk = nq + nkv
                pend1, pend2 = None, None

                def make_stage1(c, pi, sb_, N=N, t0=t0, col=col):
                    def stage1():
                        isq = c < nq
                        psq, tpsq = self.ps[pi], self.t_ps[pi]
                        kb.op("act", lambda e: e.activation(out=sq2[sb_][:, :N], in_=psq[:, :N], func=AF.Square), reads=[tpsq], writes=[t_sq2[sb_]])
                        pss, tpss = self.ps[3], self.t_ps[3]
                        kb.op("pe", lambda e: e.matmul(pss[:, :N], lhsT=self.ones[:], rhs=sq2[sb_][:, :N], start=True, stop=True),
                              reads=[t_sq2[sb_], self.t_const], writes=[tpss])
                        kb.op("act", lambda e: e.activation(out=r2[sb_][:, :N], in_=pss[:, :N], func=AF.Sqrt, scale=1.0 / 128, bias=self.epsT[:, 0:1]),
                              reads=[tpss, self.t_const], writes=[t_r2[sb_]])
                        kb.op("dve", lambda e: e.reciprocal(r2[sb_][:, :N], r2[sb_][:, :N]), reads=[t_r2[sb_]], writes=[t_r2[sb_]])
                        gcol = gs[:, 0:1] if isq else gs[:, 1:2]
                        dstT = self.QT if isq else self.KT
                        t_dst = self.t_QT if isq else self.t_KT
                        row0 = (c if isq else c - nq) * 128
                        if rope and col == 0:
                            kb.op("dve", lambda e: e.scalar_tensor_tensor(qn[sb_][:, :N], psq[:, :N], gcol, r2[sb_][:, :N], ALU.mult, ALU.mult),
                                  reads=[tpsq, t_r2[sb_], t_gs], writes=[t_qn[sb_]])

                            def stage2():
                                psr, tpsr = self.ps[4 + sb_], self.t_ps[4 + sb_]
                                kb.op("pe", lambda e: e.matmul(psr[:, :N], lhsT=self.rot[:], rhs=qn[sb_][:, :N], start=True, stop=True),
                                      reads=[t_qn[sb_], self.t_const], writes=[tpsr])
                                kb.op("pool", lambda e: e.tensor_tensor(t1[sb_][:, :N], qn[sb_][:, :N], cs[0][:, :N], ALU.mult),
                                      reads=[t_qn[sb_], t_cs], writes=[t_t1[sb_]])
                                kb.op("dve", lambda e: e.tensor_tensor(t2[sb_][:, :N], psr[:, :N], cs[1][:, :N], ALU.mult),
                                      reads=[tpsr, t_cs], writes=[t_t2[sb_]])
                                kb.op("dve", lambda e: e.tensor_tensor(qo[sb_][:, :N], t1[sb_][:, :N], t2[sb_][:, :N], ALU.add),
                                      reads=[t_t1[sb_], t_t2[sb_]], writes=[t_qo[sb_]])
                                kb.dma("pool", dstT[row0:row0 + 128, t0:t0 + N], qo[sb_][:, :N], reads=[t_qo[sb_]], writes=[t_dst])
                            return stage2
                        kb.op("dve", lambda e: e.scalar_tensor_tensor(qo[sb_][:, :N], psq[:, :N], gcol, r2[sb_][:, :N], ALU.mult, ALU.mult),
                              reads=[tpsq, t_r2[sb_], t_gs], writes=[t_qo[sb_]])
                        kb.dma("pool", dstT[row0:row0 + 128, t0:t0 + N], qo[sb_][:, :N], reads=[t_qo[sb_]], writes=[t_dst])
                        return None
                    return stage1

                for cg in range((nqk + 3) // 4):
                    b = wg % 3
                    wg += 1
                    nch = min(4, nqk - cg * 4)
                    wv = self.wview(wt[b], KC, 512)
                    kb.dma("sp", wv[:, :, :nch * 128], w["mix_b"][:, cg * 512:cg * 512 + nch * 128].rearrange("(kc p) n -> p kc n", p=128),
                           reads=[self.t_Wl[li]], writes=[t_wt[b]])
                    for j in range(nch):
                        c = cg * 4 + j
                        pi = cnt % 3
                        sb_ = cnt % 2
                        cnt += 1
                        psq, tpsq = self.ps[pi], self.t_ps[pi]
                        for kc in range(KC):
                            kb.op("pe", lambda e: e.matmul(psq[:, :N], lhsT=wv[:, kc, j * 128:(j + 1) * 128], rhs=ht[:, kc, :N],
                                                           start=(kc == 0), stop=(kc == KC - 1)),
                                  reads=[t_wt[b], t_ht], writes=[tpsq], signal=(kc == KC - 1))
                        s2 = pend1() if pend1 else None
                        if pend2:
                            pend2()
                        pend2 = s2
                        pend1 = make_stage1(c, pi, sb_)
                s2 = pend1() if pend1 else None
                if pend2:
                    pend2()
                if s2:
                    s2()
                vbase = nqk * 128
                vw = nkv * 128
                for vg in range((vw + 511) // 512):
                    b = wg % 3
                    wg += 1
                    wd = min(512, vw - vg * 512)
                    wv = self.wview(wt[b], KC, 512)
                    kb.dma("sp", wv[:, :, :wd], w["mix_b"][:, vbase + vg * 512:vbase + vg * 512 + wd].rearrange("(kc p) n -> p kc n", p=128),
                           reads=[self.t_Wl[li]], writes=[t_wt[b]])
                    for tb in range(N // 128):
                        pi = 6 + cnt % 2
                        vb = cnt % 2
                        cnt += 1
                        for kc in range(KC):
                            kb.op("pe", lambda e: e.matmul(self.ps[pi][:, :wd], lhsT=ht[:, kc, tb * 128:(tb + 1) * 128], rhs=wv[:, kc, :wd],
                                                           start=(kc == 0), stop=(kc == KC - 1)),
                                  reads=[t_wt[b], t_ht], writes=[self.t_ps[pi]], signal=(kc == KC - 1))
                        kb.op("act", lambda e: e.activation(out=vt[vb][:, :wd], in_=self.ps[pi][:, :wd], func=AF.Copy),
                              reads=[self.t_ps[pi]], writes=[t_vt[vb]])
                        kb.dma("pool", self.V[t0 + tb * 128:t0 + (tb + 1) * 128, vg * 512:vg * 512 + wd], vt[vb][:, :wd],
                               reads=[t_vt[vb]], writes=[self.t_V])
            kb.barrier()

    def phaseB(self, li, kind, need_ctx):
        kb, nc = self.kb, self.nc
        S, T = self.S, self.T
        w = self.W[li]
        nkv = NKV[kind]
        grp = 16 // nkv
        TB = T // 128
        SB = S // 128
        ctxb = [SB, SB + 1]
        sink = kind == 1
        with ExitStack() as pes:
            sb = lambda name, shape, dt: pes.enter_context(nc.sbuf_tensor(self.uname(name), list(shape), dt))
            KTh = sb("b_kt", [128, T], BF16); t_k = Tok()
            Vh = sb("b_v", [128, TB, 128], BF16); t_v = Tok()
            QTh = [sb(f"b_qt{i}", [128, T], BF16) for i in range(2)]; t_q = [Tok(), Tok()]
            pt = [sb(f"b_pt{i}", [128, 512], BF16) for i in range(3)]; t_pt = [Tok() for _ in range(3)]
            rec = sb("b_rec", [128, 512], F32); t_rec = Tok()
            ot = [sb(f"b_ot{i}", [128, 512], BF16) for i in range(2)]; t_ot = [Tok(), Tok()]
            esink = sb("b_esink", [128, 16], F32); t_es = Tok()
            accden = kind == 3
            if accden:
                onesf = sb("b_onesf", [128, 128], F32); t_onesf = Tok()
                kb.dma("sp", onesf[:], self.in_onesf[:, :], writes=[t_onesf])
                accD = [sb(f"b_accD{i}", [128, 512], F32) for i in range(2)]; t_accD = [Tok(), Tok()]
                accP = [sb(f"b_accP{i}", [128, 512], F32) for i in range(2)]; t_accP = [Tok(), Tok()]
            if kind == 0:
                npat = self.npat
                tab = sb("b_tab", [128, 8 * 512], F32); t_tab = Tok()
                etab = sb("b_etab", [128, npat, 8 * 512], BF16); t_etab = Tok()
                pat_of_C, pats = na_patterns(S)
            if kind == 1:
                swam = sb("b_swam", [128, 6, 512], BF16); t_swam = Tok()
                kb.dma("sp", swam[:], self.in_swam[:, :].rearrange("p (r n) -> p r n", n=512), writes=[t_swam])
            if sink:
                kb.op("act", lambda e: e.activation(out=esink[:], in_=self.SV[li][:, SV_SINK:SV_SINK + 16], func=AF.Exp),
                      reads=[self.t_mod[li]], writes=[t_es])
            chunks = []
            if kind == 3:
                for q0 in range(0, S, 512):
                    chunks.append((q0, min(512, S - q0), [(b, None) for b in range(TB)]))
            elif kind == 1:
                for ci in range(S // 512):
                    bl = [(kbk, ("swaw", kbk - 4 * ci + 1)) for kbk in range(4 * ci - 1, 4 * ci + 5) if 0 <= kbk < SB]
                    bl += [(b, None) for b in ctxb]
                    chunks.append((ci * 512, 512, bl))
            else:
                for ci in range(S // 512):
                    p = pat_of_C[ci]
                    bbw = na_base_block(ci, SB)
                    bl = [(bbw + j, ("naw", p, j)) for j in range(8)] + [(b, None) for b in ctxb]
                    chunks.append((ci * 512, 512, bl))
            if need_ctx:
                chunks.append((S, CTX, [(b, None) for b in ctxb]))
            cnt = 0
            qi = 0
            for kvh in range(nkv):
                kb.dma("sp", KTh[:], self.KT[kvh * 128:(kvh + 1) * 128, :], reads=[self.t_KT], writes=[t_k])
                kb.dma("sp", Vh[:], self.V[:, kvh * 128:(kvh + 1) * 128].rearrange("(tb p) d -> p tb d", p=128), reads=[self.t_V], writes=[t_v])
                for qh in range(kvh * grp, (kvh + 1) * grp):
                    qb = qi % 2
                    qi += 1
                    kb.dma("sp", QTh[qb][:], self.QT[qh * 128:(qh + 1) * 128, :], reads=[self.t_QT], writes=[t_q[qb]])
                    if kind == 0:
                        for p in range(npat):
                            r0 = (p * 16 + qh) * 128
                            kb.dma("sp", tab[:, :], w["natab"][r0:r0 + 128, :], writes=[t_tab])
                            kb.op("act", lambda e: e.activation(out=etab[:, p, :], in_=tab[:, :], func=AF.Exp), reads=[t_tab], writes=[t_etab])
                    for (q0, N, blocks) in chunks:
                        po, tpo = self.ps[0 + (cnt % 2)], self.t_ps[0 + (cnt % 2)]
                        pd, tpd = self.ps[2 + (cnt % 2)], self.t_ps[2 + (cnt % 2)]
                        ob = cnt % 2
                        cnt += 1
                        nb = len(blocks)

                        def emit_s(i):
                            kbk, mask = blocks[i]
                            psn = 4 + (i % 3)
                            pb = i % 3
                            kb.op("pe", lambda e: e.matmul(self.ps[psn][:, :N], lhsT=KTh[:, kbk * 128:(kbk + 1) * 128], rhs=QTh[qb][:, q0:q0 + N],
                                                           start=True, stop=True),
                                  reads=[t_k, t_q[qb]], writes=[self.t_ps[psn]])
                            kb.op("act", lambda e: e.activation(out=pt[pb][:, :N], in_=self.ps[psn][:, :N], func=AF.Exp),
                                  reads=[self.t_ps[psn]], writes=[t_pt[pb]])
                            if mask is not None:
                                if mask[0] == "swaw":
                                    m, tm = swam[:, mask[1], :N], t_swam
                                else:
                                    m, tm = etab[:, mask[1], mask[2] * 512:mask[2] * 512 + N], t_etab
                                kb.op("dve", lambda e: e.tensor_tensor(pt[pb][:, :N], pt[pb][:, :N], m, ALU.mult),
                                      reads=[t_pt[pb], tm], writes=[t_pt[pb]])

                        def emit_pv(i):
                            kbk, _ = blocks[i]
                            pb = i % 3
                            kb.op("pe", lambda e: e.matmul(po[:, :N], lhsT=Vh[:, kbk, :], rhs=pt[pb][:, :N], start=(i == 0), stop=(i == nb - 1)),
                                  reads=[t_v, t_pt[pb]], writes=[tpo], signal=(i == nb - 1))
                            if not accden:
                                kb.op("pe", lambda e: e.matmul(pd[:, :N], lhsT=self.ones[:], rhs=pt[pb][:, :N], start=(i == 0), stop=(i == nb - 1)),
                                      reads=[self.t_const, t_pt[pb]], writes=[tpd], signal=True)
                            else:
                                onp = (i % 3 == 2)
                                eng = "pool" if onp else "dve"
                                acc, tacc = (accP[ob], t_accP[ob]) if onp else (accD[ob], t_accD[ob])
                                first = (i == 2) if onp else (i == 0)
                                if first:
                                    kb.op(eng, lambda e: e.tensor_copy(acc[:, :N], pt[pb][:, :N]), reads=[t_pt[pb]], writes=[tacc])
                                else:
                                    kb.op(eng, lambda e: e.tensor_tensor(acc[:, :N], acc[:, :N], pt[pb][:, :N], ALU.add), reads=[t_pt[pb], tacc], writes=[tacc])

                        DEPTH = 2
                        for i in range(min(DEPTH, nb)):
                            emit_s(i)
                        for i in range(nb):
                            if i + DEPTH < nb:
                                emit_s(i + DEPTH)
                            emit_pv(i)
                        if accden:
                            if nb > 2:
                                kb.op("dve", lambda e: e.tensor_tensor(accD[ob][:, :N], accD[ob][:, :N], accP[ob][:, :N], ALU.add),
                                      reads=[t_accP[ob], t_accD[ob]], writes=[t_accD[ob]])
                            kb.op("pe", lambda e: e.matmul(pd[:, :N], lhsT=onesf[:, :], rhs=accD[ob][:, :N], start=True, stop=True),
                                  reads=[t_onesf, t_accD[ob]], writes=[tpd])
                        if sink:
                            kb.op("dve", lambda e: e.tensor_scalar_add(rec[:, :N], pd[:, :N], esink[:, qh:qh + 1]), reads=[tpd, t_es], writes=[t_rec])
                            kb.op("dve", lambda e: e.reciprocal(rec[:, :N], rec[:, :N]), reads=[t_rec], writes=[t_rec])
                        else:
                            kb.op("dve", lambda e: e.reciprocal(rec[:, :N], pd[:, :N]), reads=[tpd], writes=[t_rec])
                        kb.op("dve", lambda e: e.tensor_tensor(ot[ob][:, :N], po[:, :N], rec[:, :N], ALU.mult), reads=[tpo, t_rec], writes=[t_ot[ob]])
                        kb.dma("pool", self.OT[qh * 128:(qh + 1) * 128, q0:q0 + N], ot[ob][:, :N], reads=[t_ot[ob]], writes=[self.t_OT])
            kb.barrier()

    def phaseC(self, li, need_ctx, xi, final):
        kb, nc = self.kb, self.nc
        S, T = self.S, self.T
        w = self.W[li]
        XT, t_XT = self.XT[xi], self.t_XT[xi]
        XO, t_XO = self.XT[1 - xi], self.t_XT[1 - xi]
        wins = []
        for w0 in range(0, S, 510):
            n = min(510, S - w0)
            wins.append((w0, n, w0 > 0, w0 + n < S, 0))
        if need_ctx:
            wins.append((S, CTX, False, False, 1))
        with ExitStack() as pes:
            sb = lambda name, shape, dt: pes.enter_context(nc.sbuf_tensor(self.uname(name), list(shape), dt))
            xt = sb("c_xt", [128, KC, 512], F32); t_xt = Tok()
            ab = sb("c_ab", [128, KC, 512], BF16); t_ab = Tok()
            act = sb("c_act", [128, FC, 512], BF16); t_act = Tok()
            wt = [sb(f"c_w{i}", [128, FC * 256], BF16) for i in range(3)]; t_wt = [Tok() for _ in range(3)]
            sq = [sb(f"c_sq{i}", [128, 512], BF16) for i in range(2)]; t_sq = [Tok(), Tok()]
            tmp = [sb(f"c_tmp{i}", [128, 512], F32) for i in range(2)]; t_tmp = [Tok(), Tok()]
            rs = sb("c_rs", [128, 512], F32); t_rs = Tok()
            ga = [sb(f"c_ga{i}", [128, 512], F32) for i in range(2)]; t_ga = [Tok(), Tok()]
            gq = [sb(f"c_gq{i}", [128, 512], F32) for i in range(2)]; t_gq = [Tok(), Tok()]
            gz = [sb(f"c_gz{i}", [128, 512], F32) for i in range(2)]; t_gz = [Tok(), Tok()]
            gy = [sb(f"c_gy{i}", [128, 512], F32) for i in range(2)]; t_gy = [Tok(), Tok()]
            xo = [sb(f"c_xo{i}", [128, 512], F32) for i in range(2)]; t_xo = [Tok(), Tok()]
            SVl = self.SV[li]
            MOD = self.MOD[li]
            wg = 0
            cnt = 0
            for (w0, n, left, right, col) in wins:
                a0 = w0 - (1 if left else 0)
                N = n + (1 if left else 0) + (1 if right else 0)
                lo = 1 if left else 0
                kb.dma("sp", ab[:, :, :N], self.OT[:, a0:a0 + N].rearrange("(kc p) n -> p kc n", p=128), reads=[self.t_OT], writes=[t_ab])
                kb.dma("sp", xt[:, :, :N], XT[:, a0:a0 + N].rearrange("(kc p) n -> p kc n", p=128), reads=[t_XT], writes=[t_xt])
                for ng in range(4):
                    b = wg % 3
                    wg += 1
                    wv = self.wview(wt[b], KC, 512)
                    kb.dma("sp", wv, w["wo_b"][:, ng * 512:(ng + 1) * 512].rearrange("(kc p) n -> p kc n", p=128), reads=[self.t_Wl[li]], writes=[t_wt[b]])
                    for j in range(4):
                        c = ng * 4 + j
                        pi = cnt % 2
                        cnt += 1
                        for kc in range(KC):
                            kb.op("pe", lambda e: e.matmul(self.ps[pi][:, :N], lhsT=wv[:, kc, j * 128:(j + 1) * 128], rhs=ab[:, kc, :N],
                                                           start=(kc == 0), stop=(kc == KC - 1)),
                                  reads=[t_wt[b], t_ab], writes=[self.t_ps[pi]], signal=(kc == KC - 1))
                        kb.op("dve", lambda e: e.scalar_tensor_tensor(xt[:, c, :N], self.ps[pi][:, :N], MOD[:, 32 + c, col:col + 1], xt[:, c, :N],
                                                                      ALU.mult, ALU.add),
                              reads=[self.t_ps[pi], t_xt, self.t_mod[li]], writes=[t_xt])
                self.norm_mod(pes, xt, t_xt, N, col, self.SCA2[li], 48, li, ab, t_ab, sq, t_sq, tmp, t_tmp, 7, rs, t_rs)
                for cg in range(FC // 4):
                    bg = wg % 3
                    wg += 1
                    bu = wg % 3
                    wg += 1
                    wvg = self.wview(wt[bg], KC, 512)
                    wvu = self.wview(wt[bu], KC, 512)
                    kb.dma("sp", wvg, w["win_b"][:, cg * 512:(cg + 1) * 512].rearrange("(kc p) n -> p kc n", p=128), reads=[self.t_Wl[li]], writes=[t_wt[bg]])
                    kb.dma("sp", wvu, w["win_b"][:, DFF + cg * 512:DFF + (cg + 1) * 512].rearrange("(kc p) n -> p kc n", p=128),
                           reads=[self.t_Wl[li]], writes=[t_wt[bu]])
                    for j in range(4):
                        c = cg * 4 + j
                        eb = cnt % 2
                        cnt += 1
                        pg, tpg = self.ps[2 + eb], self.t_ps[2 + eb]
                        pu, tpu = self.ps[4 + eb], self.t_ps[4 + eb]
                        for kc in range(KC):
                            kb.op("pe", lambda e: e.matmul(pg[:, :N], lhsT=wvg[:, kc, j * 128:(j + 1) * 128], rhs=ab[:, kc, :N],
                                                           start=(kc == 0), stop=(kc == KC - 1)),
                                  reads=[t_wt[bg], t_ab], writes=[tpg], signal=(kc == KC - 1))
                        for kc in range(KC):
                            kb.op("pe", lambda e: e.matmul(pu[:, :N], lhsT=wvu[:, kc, j * 128:(j + 1) * 128], rhs=ab[:, kc, :N],
                                                           start=(kc == 0), stop=(kc == KC - 1)),
                                  reads=[t_wt[bu], t_ab], writes=[tpu], signal=(kc == KC - 1))
                        cw = lambda jj: SVl[:, SV_CW + jj * FC + c:SV_CW + jj * FC + c + 1]
                        a_, ta_ = ga[eb], t_ga[eb]
                        kb.op("act", lambda e: e.activation(out=a_[:, :n], in_=pg[:, lo:lo + n], func=AF.Identity, scale=cw(1),
                                                            bias=SVl[:, SV_CB + c:SV_CB + c + 1]),
                              reads=[tpg, self.t_mod[li]], writes=[ta_])
                        if left:
                            kb.op("dve", lambda e: e.scalar_tensor_tensor(a_[:, :n], pg[:, lo - 1:lo - 1 + n], cw(0), a_[:, :n], ALU.mult, ALU.add),
                                  reads=[tpg, ta_, self.t_mod[li]], writes=[ta_])
                        else:
                            kb.op("dve", lambda e: e.scalar_tensor_tensor(a_[:, 1:n], pg[:, lo:lo + n - 1], cw(0), a_[:, 1:n], ALU.mult, ALU.add),
                                  reads=[tpg, ta_, self.t_mod[li]], writes=[ta_])
                        if right:
                            kb.op("dve", lambda e: e.scalar_tensor_tensor(a_[:, :n], pg[:, lo + 1:lo + 1 + n], cw(2), a_[:, :n], ALU.mult, ALU.add),
                                  reads=[tpg, ta_, self.t_mod[li]], writes=[ta_])
                        else:
                            kb.op("dve", lambda e: e.scalar_tensor_tensor(a_[:, :n - 1], pg[:, lo + 1:lo + n], cw(2), a_[:, :n - 1], ALU.mult, ALU.add),
                                  reads=[tpg, ta_, self.t_mod[li]], writes=[ta_])
                        kb.op("act", lambda e: e.activation(out=gq[eb][:, :n], in_=a_[:, :n], func=AF.Square), reads=[ta_], writes=[t_gq[eb]])
                        kb.op("dve", lambda e: e.tensor_scalar(gq[eb][:, :n], gq[eb][:, :n], 0.044715 * GK, GK, ALU.mult, ALU.add),
                              reads=[t_gq[eb]], writes=[t_gq[eb]])
                        kb.op("pool", lambda e: e.tensor_tensor(gz[eb][:, :n], gq[eb][:, :n], a_[:, :n], ALU.mult), reads=[t_gq[eb], ta_], writes=[t_gz[eb]])
                        kb.op("act", lambda e: e.activation(out=gz[eb][:, :n], in_=gz[eb][:, :n], func=AF.Sigmoid), reads=[t_gz[eb]], writes=[t_gz[eb]])
                        kb.op("pool", lambda e: e.tensor_tensor(gy[eb][:, :n], gz[eb][:, :n], a_[:, :n], ALU.mult), reads=[t_gz[eb], ta_], writes=[t_gy[eb]])
                        kb.op("dve", lambda e: e.tensor_tensor(act[:, c, :n], gy[eb][:, :n], pu[:, lo:lo + n], ALU.mult),
                              reads=[t_gy[eb], tpu], writes=[t_act])
                for ng in range(8):
                    b = wg % 3
                    wg += 1
                    wv = self.wview(wt[b], FC, 256)
                    kb.dma("sp", wv, w["wout_b"][:, ng * 256:(ng + 1) * 256].rearrange("(kc p) n -> p kc n", p=128), reads=[self.t_Wl[li]], writes=[t_wt[b]])
                    for j in range(2):
                        c = ng * 2 + j
                        pi = cnt % 2
                        ob = cnt % 2
                        cnt += 1
                        for kc in range(FC):
                            kb.op("pe", lambda e: e.matmul(self.ps[pi][:, :n], lhsT=wv[:, kc, j * 128:(j + 1) * 128], rhs=act[:, kc, :n],
                                                           start=(kc == 0), stop=(kc == FC - 1)),
                                  reads=[t_wt[b], t_act], writes=[self.t_ps[pi]], signal=(kc == FC - 1))
                        kb.op("dve", lambda e: e.scalar_tensor_tensor(xo[ob][:, :n], self.ps[pi][:, :n], MOD[:, 80 + c, col:col + 1], xt[:, c, lo:lo + n],
                                                                      ALU.mult, ALU.add),
                              reads=[self.t_ps[pi], t_xt, self.t_mod[li]], writes=[t_xo[ob]])
                        if final:
                            if col == 0:
                                kb.dma("pool", self.out[c * 128:(c + 1) * 128, w0:w0 + n], xo[ob][:, :n], reads=[t_xo[ob]])
                        else:
                            kb.dma("pool", XO[c * 128:(c + 1) * 128, w0:w0 + n], xo[ob][:, :n], reads=[t_xo[ob]], writes=[t_XO])
            kb.barrier()

    def phaseA_ml(self, li, xi):
        kb, nc = self.kb, self.nc
        S, T = self.S, self.T
        w = self.W[li]
        XT, t_XT = self.XT[xi], self.t_XT[xi]
        tiles = [(t0, min(512, S - t0), 0) for t0 in range(0, S, 512)] + [(S, CTX, 1)]
        with ExitStack() as pes:
            sb = lambda name, shape, dt: pes.enter_context(nc.sbuf_tensor(self.uname(name), list(shape), dt))
            xt = sb("a_xt", [128, KC, 512], F32); t_xt = Tok()
            ht = sb("a_ht", [128, KC, 512], BF16); t_ht = Tok()
            sq = [sb(f"a_sq{i}", [128, 512], BF16) for i in range(2)]; t_sq = [Tok(), Tok()]
            tmp = [sb(f"a_tmp{i}", [128, 512], F32) for i in range(2)]; t_tmp = [Tok(), Tok()]
            rs = sb("a_rs", [128, 512], F32); t_rs = Tok()
            wt = [sb(f"a_w{i}", [128, KC * 512], BF16) for i in range(3)]; t_wt = [Tok(), Tok(), Tok()]
            qo = [sb(f"a_qo{i}", [128, 512], BF16) for i in range(2)]; t_qo = [Tok(), Tok()]
            vt = [sb(f"a_vt{i}", [128, 512], BF16) for i in range(2)]; t_vt = [Tok(), Tok()]
            gt = [sb(f"a_gt{i}", [128, 16], F32) for i in range(2)]; t_gt = [Tok(), Tok()]
            ge = [sb(f"a_ge{i}", [128, 8], F32) for i in range(2)]; t_ge = [Tok(), Tok()]
            SVl = self.SV[li]
            wg = 0
            cnt = 0
            for (t0, N, col) in tiles:
                kb.dma("sp", xt[:, :, :N], XT[:, t0:t0 + N].rearrange("(kc p) n -> p kc n", p=128), reads=[t_XT], writes=[t_xt])
                self.norm_mod(pes, xt, t_xt, N, col, self.SCA1[li], 0, li, ht, t_ht, sq, t_sq, tmp, t_tmp, 7, rs, t_rs)
                for cg in range(4):
                    b = wg % 3
                    wg += 1
                    wv = self.wview(wt[b], KC, 512)
                    kb.dma("sp", wv, w["mix_b"][:, cg * 512:(cg + 1) * 512].rearrange("(kc p) n -> p kc n", p=128), reads=[self.t_Wl[li]], writes=[t_wt[b]])
                    for j in range(4):
                        c = cg * 4 + j
                        pi = cnt % 2
                        ob = cnt % 2
                        cnt += 1
                        for kc in range(KC):
                            kb.op("pe", lambda e: e.matmul(self.ps[pi][:, :N], lhsT=wv[:, kc, j * 128:(j + 1) * 128], rhs=ht[:, kc, :N],
                                                           start=(kc == 0), stop=(kc == KC - 1)),
                                  reads=[t_wt[b], t_ht], writes=[self.t_ps[pi]], signal=(kc == KC - 1))
                        isq = c < 8
                        kb.op("act", lambda e: e.activation(out=qo[ob][:, :N], in_=self.ps[pi][:, :N], func=AF.Copy, scale=(1.0 if isq else 0.0625)),
                              reads=[self.t_ps[pi]], writes=[t_qo[ob]])
                        dstT, t_dst = (self.QT, self.t_QT) if isq else (self.KT, self.t_KT)
                        row0 = (c if isq else c - 8) * 128
                        kb.dma("pool", dstT[row0:row0 + 128, t0:t0 + N], qo[ob][:, :N], reads=[t_qo[ob]], writes=[t_dst])
                groups = [(1024 + g * 512, 512, "k", g * 512) for g in range(2)] + [(2048 + g * 512, 512, "v", g * 512) for g in range(4)] \
                    + [(4096 + g * 512, 512, "o", g * 512) for g in range(4)] + [(6144, 16, "g", 0)]
                for (c0, wd, what, d0) in groups:
                    b = wg % 3
                    wg += 1
                    wv = self.wview(wt[b], KC, 512)
                    kb.dma("sp", wv[:, :, :wd], w["mix_b"][:, c0:c0 + wd].rearrange("(kc p) n -> p kc n", p=128), reads=[self.t_Wl[li]], writes=[t_wt[b]])
                    for tb in range(N // 128):
                        pi = 2 + cnt % 2
                        vb = cnt % 2
                        cnt += 1
                        r0 = t0 + tb * 128
                        for kc in range(KC):
                            kb.op("pe", lambda e: e.matmul(self.ps[pi][:, :wd], lhsT=ht[:, kc, tb * 128:(tb + 1) * 128], rhs=wv[:, kc, :wd],
                                                           start=(kc == 0), stop=(kc == KC - 1)),
                                  reads=[t_wt[b], t_ht], writes=[self.t_ps[pi]], signal=(kc == KC - 1))
                        if what == "g":
                            g_, tg_ = gt[vb], t_gt[vb]
                            kb.op("dve", lambda e: e.tensor_tensor(g_[:, :], self.ps[pi][:, :16], SVl[:, SV_SINK:SV_SINK + 16], ALU.add),
                                  reads=[self.t_ps[pi], self.t_mod[li]], writes=[tg_])
                            e_, te_ = ge[vb], t_ge[vb]
                            for (src, dst) in ((4, 0), (12, 4)):
                                kb.op("act", lambda e: e.activation(out=e_[:, dst:dst + 4], in_=g_[:, src:src + 4], func=AF.Exp, scale=-1.0),
                                      reads=[tg_], writes=[te_])
                            kb.op("dve", lambda e: e.tensor_scalar_add(e_[:, :], e_[:, :], 1.0), reads=[te_], writes=[te_])
                            kb.op("act", lambda e: e.activation(out=e_[:, :], in_=e_[:, :], func=AF.Ln), reads=[te_], writes=[te_])
                            for (src, dst) in ((0, 4), (4, 12)):
                                kb.op("dve", lambda e: e.tensor_scalar_mul(g_[:, dst:dst + 4], e_[:, src:src + 4], -1.0), reads=[te_, tg_], writes=[tg_])
                            kb.dma("pool", self.G[r0:r0 + 128, :], g_[:, :], reads=[tg_], writes=[self.t_G])
                        else:
                            kb.op("act", lambda e: e.activation(out=vt[vb][:, :wd], in_=self.ps[pi][:, :wd], func=AF.Copy,
                                                                scale=(0.0625 if what == "k" else 1.0)),
                                  reads=[self.t_ps[pi]], writes=[t_vt[vb]])
                            dst, t_dst = {"k": (self.Ktok, self.t_Ktok), "v": (self.V, self.t_V), "o": (self.Og, self.t_Og)}[what]
                            kb.dma("pool", dst[r0:r0 + 128, d0:d0 + wd], vt[vb][:, :wd], reads=[t_vt[vb]], writes=[t_dst])
            kb.barrier()

    def phaseB_ml(self, li):
        kb, nc = self.kb, self.nc
        S, T = self.S, self.T
        L = 64
        nxc, ncc = S // L, CTX // L
        with ExitStack() as pes:
            sb = lambda name, shape, dt: pes.enter_context(nc.sbuf_tensor(self.uname(name), list(shape), dt))
            identf = sb("m_identf", [128, 128], F32)
            onesf = sb("m_onesf", [128, 128], F32)
            tri = sb("m_tri", [64, 128], F32)
            t_c = Tok()
            kb.dma("sp", identf[:], self.in_identf[:, :], writes=[t_c])
            kb.dma("sp", onesf[:], self.in_onesf[:, :], writes=[t_c])
            kb.dma("sp", tri[:], self.in_tri[:, :], writes=[t_c])
            C = [[sb(f"m_C{d}{h}", [128, 2, 512], F32) for h in range(4)] for d in range(2)]
            Cb = [[sb(f"m_Cb{d}{h}", [128, 2, 512], BF16) for h in range(4)] for d in range(2)]
            nn = [[sb(f"m_n{d}{h}", [128, 2], F32) for h in range(4)] for d in range(2)]
            nb = [[sb(f"m_nb{d}{h}", [128, 2], BF16) for h in range(4)] for d in range(2)]
            t_C = [[Tok() for h in range(4)] for d in range(2)]
            t_Cb = [[Tok() for h in range(4)] for d in range(2)]
            t_n = [[Tok() for h in range(4)] for d in range(2)]
            t_nb = [[Tok() for h in range(4)] for d in range(2)]
            for d in range(2):
                for h in range(4):
                    kb.op("pool", lambda e: e.memset(C[d][h][:], 0.0), writes=[t_C[d][h]])
                    kb.op("pool", lambda e: e.memset(Cb[d][h][:], 0.0), writes=[t_Cb[d][h]])
                    kb.op("pool", lambda e: e.memset(nn[d][h][:], 0.0), writes=[t_n[d][h]])
                    kb.op("pool", lambda e: e.memset(nb[d][h][:], 0.0), writes=[t_nb[d][h]])
            NB = 2
            D2 = range(2)
            mk = lambda name, shape, dt, n: [sb(f"{name}_{i}", shape, dt) for i in range(n)]
            tk = lambda n: [Tok() for _ in range(n)]
            qT = mk("m_qT", [128, 8, L], BF16, NB); t_qT = tk(NB)
            kT = mk("m_kT", [128, 8, L], BF16, NB); t_kT = tk(NB)
            kk = mk("m_kk", [L, 1024], BF16, NB); t_kk = tk(NB)
            vv = mk("m_vv", [L, 2048], BF16, NB); t_vv = tk(NB)
            gg = mk("m_gg", [L, 16], F32, NB); t_gg = tk(NB)
            bb = mk("m_b", [L, 4], F32, 2); t_bb = tk(2)
            wprev = mk("m_wprev", [L, 4], F32, 2); t_wprev = tk(2)
            lmb = mk("m_lmb", [L, 4], F32, 2); t_lmb = tk(2)
            ws = mk("m_ws", [L, 4], F32, 2); t_ws = tk(2)
            decay = mk("m_decay", [128, 4], F32, 2); t_decay = tk(2)
            diagb = mk("m_diagb", [L, L], F32, 4); t_diagb = tk(4)
            Eh = mk("m_E", [L, L], F32, 4); t_Eh = tk(4)
            WT = mk("m_WT", [L, L], BF16, 4); t_WT = tk(4)
            n1 = mk("m_n1", [L, 512], F32, 4); t_n1 = tk(4)
            hh = mk("m_hh", [L, 512], F32, 4); t_hh = tk(4)
            den = mk("m_den", [L, 2], F32, 4); t_den = tk(4)
            kp = mk("m_kp", [L, 256], BF16, 4); t_kp = tk(4)
            ps, tps = self.ps, self.t_ps
            pA, tA = ps[0], tps[0]
            pB, tB = ps[1], tps[1]
            orders = [[S + c * L for c in range(ncc)] + [c * L for c in range(nxc)],
                      [S + c * L for c in reversed(range(ncc))] + [c * L for c in reversed(range(nxc))]]
            cidx = 0
            for step in range(ncc + nxc):
                for d in D2:
                    tk0 = orders[d][step]
                    lic, lfc = (0, 4) if d == 0 else (8, 12)
                    trid = tri[:, 0:64] if d == 0 else tri[:, 64:128]
                    cb = cidx % NB
                    cidx += 1
                    qT_, kT_, kk_, vv_, G_ = qT[cb], kT[cb], kk[cb], vv[cb], gg[cb]
                    tq, tkT, tkk, tvv, tgg = t_qT[cb], t_kT[cb], t_kk[cb], t_vv[cb], t_gg[cb]
                    kb.dma("sp", qT_[:], self.QT[0:1024, tk0:tk0 + L].rearrange("(c p) n -> p c n", p=128), reads=[self.t_QT], writes=[tq])
                    kb.dma("sp", kT_[:], self.KT[0:1024, tk0:tk0 + L].rearrange("(c p) n -> p c n", p=128), reads=[self.t_KT], writes=[tkT])
                    kb.dma("sp", kk_[:], self.Ktok[tk0:tk0 + L, :], reads=[self.t_Ktok], writes=[tkk])
                    kb.dma("sp", vv_[:], self.V[tk0:tk0 + L, :], reads=[self.t_V], writes=[tvv])
                    kb.dma("sp", G_[:], self.G[tk0:tk0 + L, :], reads=[self.t_G], writes=[tgg])
                    bb_, wprev_, lmb_, ws_, decay_ = bb[cb], wprev[cb], lmb[cb], ws[cb], decay[cb]
                    tbb, twp, tlmb, tws, tdec = t_bb[cb], t_wprev[cb], t_lmb[cb], t_ws[cb], t_decay[cb]
                    H4 = range(4)
                    kb.op("pe", lambda e: e.matmul(pA[0:L, 0:4], lhsT=trid, rhs=G_[:, lfc:lfc + 4], start=True, stop=True),
                          reads=[t_c, tgg], writes=[tA])
                    kb.op("pe", lambda e: e.matmul(pA[:, 16:20], lhsT=onesf[0:L, :], rhs=G_[:, lfc:lfc + 4], start=True, stop=True),
                          reads=[t_c, tgg], writes=[tA])
                    kb.op("dve", lambda e: e.tensor_copy(bb_[:, :], pA[0:L, 0:4]), reads=[tA], writes=[tbb])
                    kb.op("act", lambda e: e.activation(out=wprev_[:, :], in_=pA[0:L, 0:4], func=AF.Exp), reads=[tA], writes=[twp])
                    kb.op("act", lambda e: e.activation(out=decay_[:, :], in_=pA[:, 16:20], func=AF.Exp), reads=[tA], writes=[tdec])
                    kb.op("dve", lambda e: e.tensor_tensor(lmb_[:, :], G_[:, lic:lic + 4], bb_[:, :], ALU.subtract), reads=[tgg, tbb], writes=[tlmb])
                    kb.op("dve", lambda e: e.tensor_tensor(ws_[:, :], lmb_[:, :], pA[0:L, 16:20], ALU.add), reads=[tlmb, tA], writes=[tws])
                    kb.op("act", lambda e: e.activation(out=ws_[:, :], in_=ws_[:, :], func=AF.Exp), reads=[tws], writes=[tws])
                    for h in H4:
                        kb.op("dve", lambda e: e.tensor_scalar_mul(diagb[h][:, :], identf[0:L, 0:L], bb_[:, h:h + 1]), reads=[t_c, tbb], writes=[t_diagb[h]])
                    for h in H4:
                        kb.op("pe", lambda e: e.matmul(pA[0:L, 64 + 64 * h:128 + 64 * h], lhsT=onesf[0:L, 0:L], rhs=diagb[h][:, :], start=True, stop=True),
                              reads=[t_c, t_diagb[h]], writes=[tA])
                    for h in H4:
                        for i in range(2):
                            kb.op("pe", lambda e: e.matmul(pB[0:L, 64 * h:64 * h + 64], lhsT=kT_[:, 2 * h + i, :], rhs=qT_[:, 2 * h + i, :], start=(i == 0), stop=(i == 1)),
                                  reads=[tkT, tq], writes=[tB], signal=(i == 1))
                    for h in H4:
                        kb.op("dve", lambda e: e.tensor_scalar(Eh[h][:, :], pA[0:L, 64 + 64 * h:128 + 64 * h], lmb_[:, h:h + 1], 60.0, ALU.add, ALU.min),
                              reads=[tA, tlmb], writes=[t_Eh[h]])
                    for h in H4:
                        kb.op("act", lambda e: e.activation(out=Eh[h][:, :], in_=Eh[h][:, :], func=AF.Exp), reads=[t_Eh[h]], writes=[t_Eh[h]])
                    for h in H4:
                        kb.op("dve", lambda e: e.tensor_tensor(Eh[h][:, :], Eh[h][:, :], trid, ALU.mult), reads=[t_Eh[h], t_c], writes=[t_Eh[h]])
                    for h in H4:
                        kb.op("dve", lambda e: e.tensor_tensor(WT[h][:, :], Eh[h][:, :], pB[0:L, 64 * h:64 * h + 64], ALU.mult), reads=[t_Eh[h], tB], writes=[t_WT[h]])
                    for h in H4:
                        kb.op("act", lambda e: e.activation(out=kp[h][:, :], in_=kk_[:, h * 256:(h + 1) * 256], func=AF.Copy, scale=ws_[:, h:h + 1]),
                              reads=[tkk, tws], writes=[t_kp[h]])
                    for h in H4:
                        kb.op("pe", lambda e: e.matmul(pA[0:L, 320 + h:321 + h], lhsT=WT[h][:, :], rhs=self.ones[0:L, 0:1], start=True, stop=True),
                              reads=[t_WT[h], self.t_const], writes=[tA])
                        for i in range(2):
                            kb.op("pe", lambda e: e.matmul(pA[0:L, 328 + h:329 + h], lhsT=qT_[:, 2 * h + i, :], rhs=nb[d][h][:, i:i + 1], start=(i == 0), stop=(i == 1)),
                                  reads=[tq, t_nb[d][h]], writes=[tA], signal=(i == 1))
                        for i in range(2):
                            kb.op("pe", lambda e: e.matmul(pA[:, 336 + 2 * h + i:337 + 2 * h + i], lhsT=kp[h][:, i * 128:(i + 1) * 128], rhs=self.ones[0:L, 0:1],
                                                           start=True, stop=True),
                                  reads=[t_kp[h], self.t_const], writes=[tA])
                    for h in H4:
                        kb.op("act", lambda e: e.activation(out=den[h][:, 0:1], in_=pA[0:L, 320 + h:321 + h], func=AF.Copy), reads=[tA], writes=[t_den[h]])
                    for h in H4:
                        kb.op("dve", lambda e: e.scalar_tensor_tensor(den[h][:, 1:2], pA[0:L, 328 + h:329 + h], wprev_[:, h:h + 1], den[h][:, 0:1], ALU.mult, ALU.add),
                              reads=[tA, twp, t_den[h]], writes=[t_den[h]])
                    for h in H4:
                        for i in range(2):
                            kb.op("dve", lambda e: e.scalar_tensor_tensor(nn[d][h][:, i:i + 1], nn[d][h][:, i:i + 1], decay_[:, h:h + 1],
                                                                          pA[:, 336 + 2 * h + i:337 + 2 * h + i], ALU.mult, ALU.add),
                                  reads=[t_n[d][h], tdec, tA], writes=[t_n[d][h]])
                    for h in H4:
                        kb.op("act", lambda e: e.activation(out=den[h][:, 1:2], in_=den[h][:, 1:2], func=AF.Abs), reads=[t_den[h]], writes=[t_den[h]])
                    for h in H4:
                        kb.op("dve", lambda e: e.tensor_scalar_max(den[h][:, 1:2], den[h][:, 1:2], 1.0), reads=[t_den[h]], writes=[t_den[h]])
                    for h in H4:
                        kb.op("dve", lambda e: e.reciprocal(den[h][:, 1:2], den[h][:, 1:2]), reads=[t_den[h]], writes=[t_den[h]])
                    for h in H4:
                        vh = vv_[:, h * 512:(h + 1) * 512]
                        pN1, tN1 = ps[2 + h % 2], tps[2 + h % 2]
                        pN2, tN2 = ps[4 + h % 2], tps[4 + h % 2]
                        kb.op("pe", lambda e: e.matmul(pN1[0:L, :], lhsT=WT[h][:, :], rhs=vh, start=True, stop=True),
                              reads=[t_WT[h], tvv], writes=[tN1])
                        for i in range(2):
                            kb.op("pe", lambda e: e.matmul(pN2[0:L, :], lhsT=qT_[:, 2 * h + i, :], rhs=Cb[d][h][:, i, :], start=(i == 0), stop=(i == 1)),
                                  reads=[tq, t_Cb[d][h]], writes=[tN2], signal=(i == 1))
                        kb.op("act", lambda e: e.activation(out=n1[h][:, :], in_=pN1[0:L, :], func=AF.Copy), reads=[tN1], writes=[t_n1[h]])
                        kb.op("dve", lambda e: e.scalar_tensor_tensor(hh[h][:, :], pN2[0:L, :], wprev_[:, h:h + 1], n1[h][:, :], ALU.mult, ALU.add),
                              reads=[tN2, twp, t_n1[h]], writes=[t_hh[h]])
                        kb.op("act", lambda e: e.activation(out=hh[h][:, :], in_=hh[h][:, :], func=AF.Copy, scale=den[h][:, 1:2]), reads=[t_hh[h], t_den[h]], writes=[t_hh[h]])
                        kb.dma("act", self.H[d][tk0:tk0 + L, h * 512:(h + 1) * 512], hh[h][:, :], reads=[t_hh[h]], writes=[self.t_H[d]])
                    for h in H4:
                        vh = vv_[:, h * 512:(h + 1) * 512]
                        for i in range(2):
                            pDC, tDC = ps[6 + i], tps[6 + i]
                            kb.op("pe", lambda e: e.matmul(pDC[:, :], lhsT=kp[h][:, i * 128:(i + 1) * 128], rhs=vh, start=True, stop=True),
                                  reads=[t_kp[h], tvv], writes=[tDC])
                            kb.op("dve", lambda e: e.scalar_tensor_tensor(C[d][h][:, i, :], C[d][h][:, i, :], decay_[:, h:h + 1], pDC[:, :], ALU.mult, ALU.add),
                                  reads=[t_C[d][h], tdec, tDC], writes=[t_C[d][h]])
                            kb.op("act", lambda e: e.activation(out=Cb[d][h][:, i, :], in_=C[d][h][:, i, :], func=AF.Copy), reads=[t_C[d][h]], writes=[t_Cb[d][h]])
                        kb.op("act", lambda e: e.activation(out=nb[d][h][:, :], in_=nn[d][h][:, :], func=AF.Copy), reads=[t_n[d][h]], writes=[t_nb[d][h]])
            kb.barrier()

    def phaseR_ml(self, li):
        kb, nc = self.kb, self.nc
        S, T = self.S, self.T
        w = self.W[li]
        with ExitStack() as pes:
            sb = lambda name, shape, dt: pes.enter_context(nc.sbuf_tensor(self.uname(name), list(shape), dt))
            hg = sb("r_hg", [128, D], F32); t_hg = Tok()
            kb.dma("sp", hg[:], w["headg"][:, :], writes=[t_hg])
            hf = [sb(f"r_hf{i}", [128, D], F32) for i in range(2)]; t_hf = [Tok(), Tok()]
            hb_ = [sb(f"r_hb{i}", [128, D], F32) for i in range(2)]; t_hb = [Tok(), Tok()]
            og = [sb(f"r_og{i}", [128, D], BF16) for i in range(2)]; t_og = [Tok(), Tok()]
            sgm = sb("r_sg", [128, D], F32); t_sg = Tok()
            sqb = sb("r_sq", [128, D], F32); t_sqb = Tok()
            ss = sb("r_ss", [128, 4], F32); t_ss = Tok()
            y = sb("r_y", [128, D], BF16); t_y = Tok()
            ot = [sb(f"r_ot{i}", [128, 512], BF16) for i in range(2)]; t_ot = [Tok(), Tok()]
            cnt = 0
            for bi in range(T // 128):
                r0 = bi * 128
                b = bi % 2
                kb.dma("sp", hf[b][:], self.H[0][r0:r0 + 128, :], reads=[self.t_H[0]], writes=[t_hf[b]])
                kb.dma("sp", hb_[b][:], self.H[1][r0:r0 + 128, :], reads=[self.t_H[1]], writes=[t_hb[b]])
                kb.dma("sp", og[b][:], self.Og[r0:r0 + 128, :], reads=[self.t_Og], writes=[t_og[b]])
                kb.op("dve", lambda e: e.tensor_tensor(hf[b][:], hf[b][:], hb_[b][:], ALU.add), reads=[t_hf[b], t_hb[b]], writes=[t_hf[b]])
                kb.op("act", lambda e: e.activation(out=sqb[:], in_=hf[b][:], func=AF.Square), reads=[t_hf[b]], writes=[t_sqb])
                kb.op("dve", lambda e: e.reduce_sum(ss[:, :], sqb[:].rearrange("p (h n) -> p h n", h=4), mybir.AxisListType.X), reads=[t_sqb], writes=[t_ss])
                kb.op("dve", lambda e: e.tensor_scalar(ss[:, :], ss[:, :], 1.0 / 512, EPS, ALU.mult, ALU.add), reads=[t_ss], writes=[t_ss])
                kb.op("act", lambda e: e.activation(out=ss[:, :], in_=ss[:, :], func=AF.Sqrt), reads=[t_ss], writes=[t_ss])
                kb.op("dve", lambda e: e.reciprocal(ss[:, :], ss[:, :]), reads=[t_ss], writes=[t_ss])
                kb.op("act", lambda e: e.activation(out=sgm[:], in_=og[b][:], func=AF.Sigmoid), reads=[t_og[b]], writes=[t_sg])
                kb.op("pool", lambda e: e.tensor_tensor(sgm[:], sgm[:], hg[:], ALU.mult), reads=[t_sg, t_hg], writes=[t_sg])
                for h in range(4):
                    kb.op("dve", lambda e: e.scalar_tensor_tensor(y[:, h * 512:(h + 1) * 512], hf[b][:, h * 512:(h + 1) * 512], ss[:, h:h + 1],
                                                                  sgm[:, h * 512:(h + 1) * 512], ALU.mult, ALU.mult),
                          reads=[t_hf[b], t_ss, t_sg], writes=[t_y])
                for g4 in range(4):
                    pi = cnt % 2
                    ob = cnt % 2
                    cnt += 1
                    for jj in range(4):
                        j = g4 * 4 + jj
                        kb.op("pe", lambda e: e.matmul(self.ps[pi][:, jj * 128:(jj + 1) * 128], lhsT=y[:, j * 128:(j + 1) * 128], rhs=self.ident[:, :],
                                                       start=True, stop=True),
                              reads=[t_y, self.t_const], writes=[self.t_ps[pi]], signal=(jj == 3))
                    kb.op("act", lambda e: e.activation(out=ot[ob][:, :], in_=self.ps[pi][:, :], func=AF.Copy), reads=[self.t_ps[pi]], writes=[t_ot[ob]])
                    kb.dma("pool", self.OT[g4 * 512:(g4 + 1) * 512, r0:r0 + 128].rearrange("(jj p) n -> p jj n", p=128),
                           ot[ob][:, :].rearrange("p (jj n) -> p jj n", n=128), reads=[t_ot[ob]], writes=[self.t_OT])
            kb.barrier()

    def build(self):
        self.phase0()
        xi = 0
        for idx, (li, kind, need_ctx) in enumerate(self.layers):
            final = idx == len(self.layers) - 1
            self.kb.flush_bg(self.t_Wl[li])
            if kind == 2:
                self.phaseA_ml(li, xi)
                self.phaseB_ml(li)
                self.phaseR_ml(li)
            else:
                self.phaseA(li, kind, xi)
                self.phaseB(li, kind, need_ctx)
            self.phaseC(li, need_ctx, xi, final)
            xi = 1 - xi
        self.kb.barrier(engines=("sp",))
        self.es.close()
        return self.nc


def na_base_block(ci, SB):
    return min(max(4 * ci - 2, 0), SB - 8)


def na_patterns(S):
    rows = S // GW
    kr = min(8, rows)
    SB = S // 128
    pats, pat_of_C, keymap = [], [], {}
    for ci in range(S // 512):
        bbw = na_base_block(ci, SB)
        r0s = tuple(min(max(8 * ci + a - kr // 2, 0), rows - kr) - 8 * ci for a in range(8))
        key = (bbw - 4 * ci, r0s)
        if key not in keymap:
            keymap[key] = len(pats)
            pats.append(ci)
        pat_of_C.append(keymap[key])
    return pat_of_C, pats


def build_na_table(rel_bias, S):
    rows = S // GW
    kr = min(8, rows)
    SB = S // 128
    pat_of_C, pats = na_patterns(S)
    npat = len(pats)
    rb = np.asarray(rel_bias, np.float32)
    tab = np.full((npat, 16, 128, 8, 512), -30000.0, np.float32)
    i = np.arange(512)
    qc = i % GW
    c0 = np.clip(qc - 8, 0, GW - 16)
    j = np.arange(128)
    for p, ci in enumerate(pats):
        bbw = na_base_block(ci, SB)
        qr = 8 * ci + i // GW
        r0 = np.clip(qr - kr // 2, 0, rows - kr)
        for jb in range(8):
            kr_ = 2 * (bbw + jb) + j // GW
            kc_ = j % GW
            ok = ((kr_[:, None] >= r0[None, :]) & (kr_[:, None] < r0[None, :] + kr)
                  & (kc_[:, None] >= c0[None, :]) & (kc_[:, None] < c0[None, :] + 16))
            drow = np.clip(kr_[:, None] - qr[None, :] + 7, 0, 14)
            dcol = np.clip(kc_[:, None] - qc[None, :] + 15, 0, 30)
            vals = rb[:, drow, dcol]
            tab[p, :, :, jb, :] = np.where(ok[None], vals, np.float32(-30000.0))
    return tab.reshape(npat * 16 * 128, 8 * 512)


def swa_mask_table():
    j = np.arange(128)
    prev = (j[:, None] >= j[None, :]).astype(np.float32)
    nxt = (j[:, None] <= j[None, :]).astype(np.float32)
    t = np.zeros((128, 6, 4, 128), np.float32)
    for r in range(6):
        for a in range(4):
            rel = r - 1 - a
            if rel == -1:
                t[:, r, a, :] = prev
            elif rel == 0:
                t[:, r, a, :] = 1.0
            elif rel == 1:
                t[:, r, a, :] = nxt
    return t.reshape(128, 6 * 512).astype(ml_dtypes.bfloat16)


def rope_tables(S):
    t = np.arange(S)
    row = (t // GW).astype(np.float32)
    colp = (t % GW).astype(np.float32)
    inv = (10000.0 ** (-np.arange(32, dtype=np.float32) / 32)).astype(np.float32)
    cosT = np.zeros((128, S), np.float32)
    sinT = np.zeros((128, S), np.float32)
    for a, pos in enumerate((row, colp)):
        ang = (pos[None, :] * inv[:, None]).astype(np.float32)
        for p in range(2):
            cosT[a * 64 + p * 32:a * 64 + (p + 1) * 32] = np.cos(ang)
            sinT[a * 64 + p * 32:a * 64 + (p + 1) * 32] = np.sin(ang)
    rot = np.zeros((128, 128), np.float32)
    for a in range(2):
        for f in range(32):
            d1, d2 = a * 64 + f, a * 64 + 32 + f
            rot[d2, d1] = -1.0
            rot[d1, d2] = 1.0
    return cosT, sinT, rot


def fm(v, ncol):
    return np.ascontiguousarray(np.asarray(v, np.float32).reshape(ncol, 128).T)


def pack_sv(li, kind, P):
    sv = np.zeros((128, NSV), np.float32)
    sv[:, SV_ADAB:SV_ADAB + 96] = fm(P["ada_b"][li], 96)
    sv[:, SV_N1:SV_N1 + 16] = fm(P["norm1_g"][li], 16)
    sv[:, SV_N2:SV_N2 + 16] = fm(P["norm2_g"][li], 16)
    for j in range(3):
        sv[:, SV_CW + j * FC:SV_CW + (j + 1) * FC] = fm(P["ffn_conv_w"][li][j], FC)
    sv[:, SV_CB:SV_CB + FC] = fm(P["ffn_conv_b"][li], FC)
    pre = {0: "na", 1: "swa", 3: "gqa"}.get(kind)
    if pre:
        sv[:, SV_QG] = np.asarray(P[pre + "_q_g"][0], np.float32)
        sv[:, SV_KG] = np.asarray(P[pre + "_k_g"][0], np.float32)
    if kind == 1:
        sv[:, SV_SINK:SV_SINK + 16] = np.asarray(P["swa_sinks"][0], np.float32)[None, :]
    if kind == 2:
        sv[:, SV_SINK:SV_SINK + 16] = np.asarray(P["ml_gate_b"][0], np.float32)[None, :]
    return sv


def core_inputs(b, S, layers, P, consts):
    m = dict(consts)
    m["xT"] = np.ascontiguousarray(np.asarray(P["x"][b, :S], np.float32).T)
    m["cxT"] = np.ascontiguousarray(np.asarray(P["ctx"][b], np.float32).T)
    cc = np.stack([fm(P["c"][b], KC), fm(P["c_ctx"], KC)], axis=-1)
    m["cc"] = np.ascontiguousarray(cc)
    return m


def shared_inputs(S, layers, P):
    cosT, sinT, rot = rope_tables(S)
    bf = ml_dtypes.bfloat16
    j = np.arange(128)
    m = {
        "ones": np.ones((128, 128), bf), "ident": np.eye(128, dtype=np.float32).astype(bf), "rot": rot.astype(bf),
        "cosT": cosT, "sinT": sinT,
        "mprev": (j[:, None] >= j[None, :]).astype(np.float32).astype(bf),
        "mnext": (j[:, None] <= j[None, :]).astype(np.float32).astype(bf),
        "swam": swa_mask_table(),
    }
    if True:
        jj = np.arange(64)
        m["identf"] = np.eye(128, dtype=np.float32)
        m["onesf"] = np.ones((128, 128), np.float32)
        m["tri"] = np.concatenate([(jj[:, None] <= jj[None, :]), (jj[:, None] >= jj[None, :])], axis=1).astype(np.float32)
    mixw = {0: ("na_w_qkv", "na_w_o"), 1: ("swa_w_qkv", "swa_w_o"), 2: ("ml_w_in", "ml_w_o"), 3: ("gqa_w_qkv", "gqa_w_o")}
    for (li, kind, _) in layers:
        m[f"ada_w{li}"] = np.asarray(P["ada_w"][li], np.float32)
        m[f"ffn_w_in{li}"] = np.asarray(P["ffn_w_in"][li], np.float32)
        m[f"ffn_w_out{li}"] = np.asarray(P["ffn_w_out"][li], np.float32)
        m[f"mix_w_in{li}"] = np.asarray(P[mixw[kind][0]][0], np.float32)
        m[f"mix_w_o{li}"] = np.asarray(P[mixw[kind][1]][0], np.float32)
        m[f"sv{li}"] = pack_sv(li, kind, P)
        if kind == 0:
            m[f"natab{li}"] = build_na_table(P["na_rel_bias"][0], S)
        if kind == 2:
            m[f"headg{li}"] = np.ascontiguousarray(np.broadcast_to(np.asarray(P["ml_head_g"][0], np.float32)[None, :], (128, D)))
    return m


def run_model(P, S, layers, batches, trace=False, spread=False):
    prog = Prog(S, layers)
    nc = prog.build()
    shared = shared_inputs(S, layers, P)
    in_maps = [core_inputs(b, S, layers, P, shared) for b in batches]
    slots = list(range(len(batches)))
    if spread and len(batches) == 4:
        big = ("ada_w", "ffn_w_in", "ffn_w_out", "mix_w_in", "mix_w_o", "sv", "natab", "headg")
        zmap = {}
        for k, v in in_maps[0].items():
            if k.startswith(big) or k in ("xT", "cxT", "cc"):
                zmap[k] = np.zeros(v.shape, v.dtype)
            else:
                zmap[k] = v
        slots = [0, 1, 4, 5]
        full = [zmap] * 8
        full = list(full)
        for sl, m in zip(slots, in_maps):
            full[sl] = m
        in_maps = full
    res = run_bass_kernel_spmd(nc, in_maps, core_ids=list(range(len(in_maps))), trace=trace)
    outs = [np.ascontiguousarray(res.results[sl]["outT"].T) for sl in slots]
    return np.stack(outs, 0), res


def kernel(**inputs):
    S = inputs["x"].shape[1]
    layers = [(i, i % 4, i < 3) for i in range(4)]
    out, _ = run_model(inputs, S, layers, list(range(inputs["x"].shape[0])), spread=False)
    return out.astype(np.float32)
```

```python
import math
from contextlib import ExitStack

import ml_dtypes
import numpy as np

import concourse.bass as bass
import concourse.mybir as mybir
from concourse.bass_utils import run_bass_kernel_spmd

F32, BF16 = mybir.dt.float32, mybir.dt.bfloat16
AF = mybir.ActivationFunctionType
ALU = mybir.AluOpType

D = 2048
KC = 16
DFF = 5632
FC = 44
CTX = 256
GW = 64
EPS = 1e-6
NSV = 96 + 16 + 16 + 132 + 44 + 1 + 1 + 1 + 16
SV_ADAB, SV_N1, SV_N2, SV_CW, SV_CB, SV_QG, SV_KG, SV_GB, SV_SINK = 0, 96, 112, 128, 260, 304, 305, 306, 307
GK = 1.5957691216057308
MIX_COLS = {0: 6144, 1: 2560, 2: 6160, 3: 3072}
NKV = {0: 16, 1: 2, 3: 4}


class Tok:
    __slots__ = ("w", "r", "ex")

    def __init__(self, ex=False):
        self.w = {}
        self.r = {}
        self.ex = ex


class KB:
    def __init__(self, nc, es, nring=6):
        self.nc = nc
        self.E = {"pe": nc.tensor, "act": nc.scalar, "dve": nc.vector, "pool": nc.gpsimd, "sp": nc.sync}
        self.csem = {e: es.enter_context(nc.semaphore("c_" + e)) for e in ("pe", "act", "dve", "pool")}
        self.ccnt = {e: 0 for e in self.csem}
        self.NR = nring
        self.dsem = {e: [es.enter_context(nc.semaphore(f"d_{e}{i}")) for i in range(nring)] for e in ("sp", "pool", "act")}
        self.dcnt = {e: [0] * nring for e in self.dsem}
        self.dnext = {e: 0 for e in self.dsem}
        self.waited = {e: {} for e in self.E}
        self.pending = {e: [] for e in self.csem}
        self.ninst = 0
        self.bg = []
        self.bgcnt = 0

    def _wait(self, e, ev):
        if ev is None:
            return
        sem, val, key = ev
        if key == "cpe" and e == "pe":
            return
        w = self.waited[e]
        if w.get(key, 0) >= val:
            return
        self.E[e].wait_ge(sem, val)
        self.ninst += 1
        w[key] = val

    def _deps(self, e, reads, writes):
        for t in reads:
            for ev in list(t.w.values()):
                self._wait(e, ev)
            if t.ex:
                for ev in list(t.r.values()):
                    if ev[2] != "c" + e:
                        self._wait(e, ev)
        for t in writes:
            for ev in list(t.w.values()):
                self._wait(e, ev)
            for ev in list(t.r.values()):
                self._wait(e, ev)

    @staticmethod
    def _commit(ev, reads, writes):
        for t in reads:
            t.r[ev[2]] = ev
        for t in writes:
            t.w[ev[2]] = ev
            t.r = {}

    def op(self, e, fn, reads=(), writes=(), signal=True):
        self._deps(e, reads, writes)
        ins = fn(self.E[e])
        self.ninst += 1
        if signal:
            self.ccnt[e] += 1
            ins.then_inc(self.csem[e], 1)
            ev = (self.csem[e], self.ccnt[e], "c" + e)
            self._commit(ev, reads, writes)
            for (r, w) in self.pending[e]:
                self._commit(ev, r, w)
            self.pending[e] = []
        else:
            self.pending[e].append((tuple(reads), tuple(writes)))
        return ins

    def dma(self, q, out, in_, reads=(), writes=()):
        i = self.dnext[q]
        self.dnext[q] = (i + 1) % self.NR
        sem = self.dsem[q][i]
        key = f"d{q}{i}"
        if self.dcnt[q][i]:
            self._wait(q, (sem, self.dcnt[q][i], key))
        self._deps(q, reads, writes)
        self.E[q].dma_start(out=out, in_=in_).then_inc(sem, 16)
        self.ninst += 1
        self.dcnt[q][i] += 16
        ev = (sem, self.dcnt[q][i], key)
        self._commit(ev, reads, writes)
        if q == "pool" and self.bg and not getattr(self, "_in_bg", False):
            self.bgcnt += 1
            if self.bgcnt % 2 == 0:
                self._in_bg = True
                o, i_, wr = self.bg.pop(0)
                self.dma("pool", o, i_, writes=wr)
                self._in_bg = False

    def flush_bg(self, tok):
        keep = []
        self._in_bg = True
        for (o, i_, wr) in self.bg:
            if tok in wr:
                self.dma("pool", o, i_, writes=wr)
            else:
                keep.append((o, i_, wr))
        self._in_bg = False
        self.bg = keep

    def barrier(self, engines=("pe", "act", "dve", "pool", "sp")):
        for e in engines:
            for q in self.dsem:
                for i in range(self.NR):
                    if self.dcnt[q][i]:
                        self._wait(e, (self.dsem[q][i], self.dcnt[q][i], f"d{q}{i}"))
            for c in self.csem:
                if self.ccnt[c] and c != e:
                    self._wait(e, (self.csem[c], self.ccnt[c], "c" + c))


class Prog:
    def __init__(self, S, layers, final_out=True):
        self.S, self.T = S, S + CTX
        self.layers = layers
        self.nc = nc = bass.Bass("TRN2", target_bir_lowering=False)
        self.es = ExitStack()
        self.kb = KB(nc, self.es)
        T = self.T
        di = lambda name, shape, dt=F32: nc.dram_tensor(name, list(shape), dt, kind="ExternalInput").ap()
        ds = lambda name, shape, dt: nc.dram_tensor(name, list(shape), dt).ap()
        self.in_xT = di("xT", [D, S])
        self.in_cxT = di("cxT", [D, CTX])
        self.in_cc = di("cc", [128, KC, 2])
        self.in_ones = di("ones", [128, 128], BF16)
        self.in_ident = di("ident", [128, 128], BF16)
        self.in_rot = di("rot", [128, 128], BF16)
        self.in_cos = di("cosT", [128, S])
        self.in_sin = di("sinT", [128, S])
        self.in_mprev = di("mprev", [128, 128], BF16)
        self.in_mnext = di("mnext", [128, 128], BF16)
        self.in_swam = di("swam", [128, 6 * 512], BF16)
        self.W = {}
        self.natab = None
        for (li, kind, _) in layers:
            w = {}
            w["ada"] = di(f"ada_w{li}", [D, 6 * D])
            w["win"] = di(f"ffn_w_in{li}", [D, 2 * DFF])
            w["wout"] = di(f"ffn_w_out{li}", [DFF, D])
            w["mix"] = di(f"mix_w_in{li}", [D, MIX_COLS[kind]])
            w["wo"] = di(f"mix_w_o{li}", [D, D])
            w["sv"] = di(f"sv{li}", [128, NSV])
            for k in ("ada", "win", "wout", "mix", "wo"):
                w[k + "_b"] = ds(f"{k}_b{li}", w[k].shape, BF16)
            if kind == 0:
                self.npat = len(na_patterns(S)[1])
                w["natab"] = di(f"natab{li}", [self.npat * 16 * 128, 8 * 512])
            if kind == 2:
                w["headg"] = di(f"headg{li}", [128, D])
            self.W[li] = w
        self.out = nc.dram_tensor("outT", [D, S], F32, kind="ExternalOutput").ap()
        self.XT = [ds("XT0", [D, T], F32), ds("XT1", [D, T], F32)]
        self.QT = ds("QT", [D, T], BF16)
        self.KT = ds("KT", [D, T], BF16)
        self.V = ds("V", [T, D], BF16)
        self.OT = ds("OT", [D, T], BF16)
        self.in_identf = di("identf", [128, 128])
        self.in_onesf = di("onesf", [128, 128])
        self.in_tri = di("tri", [64, 128])
        if any(k == 2 for (_, k, _) in layers):
            self.Ktok = ds("Ktok", [T, 1024], BF16)
            self.Og = ds("Og", [T, D], BF16)
            self.G = ds("G", [T, 16], F32)
            self.H = [ds("H0", [T, D], F32), ds("H1", [T, D], F32)]
            self.t_Ktok, self.t_Og, self.t_G, self.t_H = Tok(), Tok(), Tok(), [Tok(), Tok()]
        self.t_XT = [Tok(), Tok()]
        self.t_QT, self.t_KT, self.t_V, self.t_OT = Tok(), Tok(), Tok(), Tok()
        self.t_W = Tok()
        self.t_Wl = {li: Tok() for (li, _, _) in layers}
        es = self.es
        sb = lambda name, shape, dt: es.enter_context(nc.sbuf_tensor(self.uname(name), list(shape), dt))
        self.ps = [es.enter_context(nc.psum_tensor(f"ps{i}", [128, 512], F32)) for i in range(8)]
        self.t_ps = [Tok(ex=True) for _ in range(8)]
        self.ones = sb("ones_sb", [128, 128], BF16)
        self.ident = sb("ident_sb", [128, 128], BF16)
        self.rot = sb("rot_sb", [128, 128], BF16)
        self.mprev = sb("mprev_sb", [128, 128], BF16)
        self.mnext = sb("mnext_sb", [128, 128], BF16)
        self.t_const = Tok()
        self.epsT = sb("eps_sb", [128, 1], F32)
        self.MOD, self.SCA1, self.SCA2, self.SV, self.t_mod = {}, {}, {}, {}, {}
        for (li, _, _) in layers:
            self.MOD[li] = sb(f"mod{li}", [128, 96, 2], F32)
            self.SCA1[li] = sb(f"sca1_{li}", [128, KC, 2], F32)
            self.SCA2[li] = sb(f"sca2_{li}", [128, KC, 2], F32)
            self.SV[li] = sb(f"svs{li}", [128, NSV], F32)
            self.t_mod[li] = Tok()

    def uname(self, name):
        self._uid = getattr(self, "_uid", 0) + 1
        return f"s{self._uid}_{name}"

    def wview(self, buf, kc, n):
        return buf[:, 0:kc * n].rearrange("p (k n) -> p k n", n=n)

    def norm_mod(self, pes, xt, t_xt, N, col, SCA, shift_base, li, ht, t_ht, sq, t_sq, tmp, t_tmp, psn, rs, t_rs):
        kb = self.kb
        ps, tps = self.ps[psn], self.t_ps[psn]
        for kc in range(KC):
            b = kc % 2
            kb.op("act", lambda e: e.activation(out=sq[b][:, :N], in_=xt[:, kc, :N], func=AF.Square),
                  reads=[t_xt], writes=[t_sq[b]])
            kb.op("pe", lambda e: e.matmul(ps[:, :N], lhsT=self.ones[:], rhs=sq[b][:, :N], start=(kc == 0), stop=(kc == KC - 1)),
                  reads=[t_sq[b], self.t_const], writes=[tps])
        kb.op("act", lambda e: e.activation(out=rs[:, :N], in_=ps[:, :N], func=AF.Sqrt, scale=1.0 / D, bias=self.epsT[:, 0:1]),
              reads=[tps, self.t_const], writes=[t_rs])
        kb.op("dve", lambda e: e.reciprocal(rs[:, :N], rs[:, :N]), reads=[t_rs], writes=[t_rs])
        for kc in range(KC):
            b = kc % 2
            kb.op("dve", lambda e: e.tensor_tensor(tmp[b][:, :N], xt[:, kc, :N], rs[:, :N], ALU.mult),
                  reads=[t_xt, t_rs], writes=[t_tmp[b]])
            kb.op("act", lambda e: e.activation(out=ht[:, kc, :N], in_=tmp[b][:, :N], func=AF.Identity,
                                                scale=SCA[:, kc, col:col + 1], bias=self.MOD[li][:, shift_base + kc, col:col + 1]),
                  reads=[t_tmp[b], self.t_mod[li]], writes=[t_ht])

    def phase0(self):
        kb, nc = self.kb, self.nc
        S, T = self.S, self.T
        for (dst, src) in ((self.ones, self.in_ones), (self.ident, self.in_ident), (self.rot, self.in_rot),
                           (self.mprev, self.in_mprev), (self.mnext, self.in_mnext)):
            kb.dma("sp", dst[:], src[:, :], writes=[self.t_const])
        kb.op("pool", lambda e: e.memset(self.epsT[:], EPS), writes=[self.t_const])
        for r in range(0, D, 256):
            kb.dma("sp", self.XT[0][r:r + 256, 0:S], self.in_xT[r:r + 256, :], writes=[self.t_XT[0]])
        kb.dma("sp", self.XT[0][:, S:T], self.in_cxT[:, :], writes=[self.t_XT[0]])
        for (li, kind, _) in self.layers:
            w = self.W[li]
            for r in range(0, D, 256):
                kb.dma("pool", w["ada_b"][r:r + 256, :], w["ada"][r:r + 256, :], writes=[self.t_W])
        for idx, (li, kind, _) in enumerate(self.layers):
            w = self.W[li]
            for k in ("mix", "wo", "win", "wout"):
                rows = w[k].shape[0]
                step = 256 if idx == 0 else 64
                for r in range(0, rows, step):
                    r1 = min(rows, r + step)
                    if idx == 0:
                        kb.dma("pool", w[k + "_b"][r:r1, :], w[k][r:r1, :], writes=[self.t_Wl[li]])
                    else:
                        kb.bg.append((w[k + "_b"][r:r1, :], w[k][r:r1, :], [self.t_Wl[li]]))
        with ExitStack() as pes:
            sb = lambda name, shape, dt: pes.enter_context(nc.sbuf_tensor(self.uname(name), list(shape), dt))
            cc = sb("cc", [128, KC, 2], F32)
            scb = sb("scb", [128, KC, 2], BF16)
            t_cc, t_scb = Tok(), Tok()
            wt = [sb(f"p0w{i}", [128, KC * 512], BF16) for i in range(2)]
            t_wt = [Tok(), Tok()]
            tmp = sb("p0tmp", [128, KC, 2], F32)
            t_tmp = Tok()
            kb.dma("sp", cc[:], self.in_cc[:, :, :], writes=[t_cc])
            kb.op("act", lambda e: e.activation(out=scb[:], in_=cc[:], func=AF.Silu), reads=[t_cc], writes=[t_scb])
            g = 0
            for (li, kind, _) in self.layers:
                w = self.W[li]
                kb.dma("sp", self.SV[li][:], w["sv"][:, :], writes=[self.t_mod[li]])
                for cg in range(24):
                    b = g % 2
                    g += 1
                    wv = self.wview(wt[b], KC, 512)
                    kb.dma("sp", wv, w["ada_b"][:, cg * 512:(cg + 1) * 512].rearrange("(kc p) n -> p kc n", p=128),
                           reads=[self.t_W], writes=[t_wt[b]])
                    for j in range(4):
                        c = cg * 4 + j
                        pi = c % 2
                        for kc in range(KC):
                            kb.op("pe", lambda e: e.matmul(self.ps[pi][:, 0:2], lhsT=wv[:, kc, j * 128:(j + 1) * 128], rhs=scb[:, kc, :],
                                                           start=(kc == 0), stop=(kc == KC - 1)),
                                  reads=[t_wt[b], t_scb], writes=[self.t_ps[pi]], signal=(kc == KC - 1))
                        kb.op("act", lambda e: e.activation(out=self.MOD[li][:, c, :], in_=self.ps[pi][:, 0:2], func=AF.Identity,
                                                            bias=self.SV[li][:, SV_ADAB + c:SV_ADAB + c + 1], scale=1.0),
                              reads=[self.t_ps[pi], self.t_mod[li]], writes=[self.t_mod[li]])
                for (SCA, sc0, g0) in ((self.SCA1[li], 16, SV_N1), (self.SCA2[li], 64, SV_N2)):
                    kb.op("dve", lambda e: e.tensor_scalar_add(tmp[:], self.MOD[li][:, sc0:sc0 + 16, :], 1.0),
                          reads=[self.t_mod[li]], writes=[t_tmp])
                    for col in range(2):
                        kb.op("dve", lambda e: e.tensor_tensor(SCA[:, :, col], tmp[:, :, col], self.SV[li][:, g0:g0 + 16], ALU.mult),
                              reads=[t_tmp, self.t_mod[li]], writes=[self.t_mod[li]])
            kb.barrier()

    def phaseA(self, li, kind, xi):
        kb, nc = self.kb, self.nc
        S, T = self.S, self.T
        w = self.W[li]
        nq, nkv = 16, NKV[kind]
        rope = kind in (1, 3)
        XT, t_XT = self.XT[xi], self.t_XT[xi]
        tiles = [(t0, min(512, S - t0), 0) for t0 in range(0, S, 512)] + [(S, CTX, 1)]
        with ExitStack() as pes:
            sb = lambda name, shape, dt: pes.enter_context(nc.sbuf_tensor(self.uname(name), list(shape), dt))
            xt = sb("a_xt", [128, KC, 512], F32); t_xt = Tok()
            ht = sb("a_ht", [128, KC, 512], BF16); t_ht = Tok()
            sq = [sb(f"a_sq{i}", [128, 512], BF16) for i in range(2)]; t_sq = [Tok(), Tok()]
            tmp = [sb(f"a_tmp{i}", [128, 512], F32) for i in range(2)]; t_tmp = [Tok(), Tok()]
            rs = sb("a_rs", [128, 512], F32); t_rs = Tok()
            wt = [sb(f"a_w{i}", [128, KC * 512], BF16) for i in range(3)]; t_wt = [Tok(), Tok(), Tok()]
            cs = [sb("a_cos", [128, 512], F32), sb("a_sin", [128, 512], F32)]; t_cs = Tok()
            sq2 = [sb(f"a_sq2{i}", [128, 512], BF16) for i in range(2)]; t_sq2 = [Tok(), Tok()]
            r2 = [sb(f"a_r2{i}", [128, 512], F32) for i in range(2)]; t_r2 = [Tok(), Tok()]
            qn = [sb(f"a_qn{i}", [128, 512], BF16) for i in range(2)]; t_qn = [Tok(), Tok()]
            t1 = [sb(f"a_t1{i}", [128, 512], F32) for i in range(2)]; t_t1 = [Tok(), Tok()]
            t2 = [sb(f"a_t2{i}", [128, 512], F32) for i in range(2)]; t_t2 = [Tok(), Tok()]
            qo = [sb(f"a_qo{i}", [128, 512], BF16) for i in range(2)]; t_qo = [Tok(), Tok()]
            vt = [sb(f"a_vt{i}", [128, 512], BF16) for i in range(2)]; t_vt = [Tok(), Tok()]
            gs = sb("a_gs", [128, 2], F32); t_gs = Tok()
            kb.op("dve", lambda e: e.tensor_scalar_mul(gs[:, 0:1], self.SV[li][:, SV_QG:SV_QG + 1], 128.0 ** -0.5),
                  reads=[self.t_mod[li]], writes=[t_gs])
            kb.op("dve", lambda e: e.tensor_copy(gs[:, 1:2], self.SV[li][:, SV_KG:SV_KG + 1]), reads=[self.t_mod[li]], writes=[t_gs])
            wg = 0
            cnt = 0
            for (t0, N, col) in tiles:
                kb.dma("sp", xt[:, :, :N], XT[:, t0:t0 + N].rearrange("(kc p) n -> p kc n", p=128), reads=[t_XT], writes=[t_xt])
                if rope and col == 0:
                    kb.dma("sp", cs[0][:, :N], self.in_cos[:, t0:t0 + N], writes=[t_cs])
                    kb.dma("sp", cs[1][:, :N], self.in_sin[:, t0:t0 + N], writes=[t_cs])
                self.norm_mod(pes, xt, t_xt, N, col, self.SCA1[li], 0, li, ht, t_ht, sq, t_sq, tmp, t_tmp, 7, rs, t_rs)
                nqk = nq + nkv
                pend1, pend2 = None, None

                def make_stage1(c, pi, sb_, N=N, t0=t0, col=col):
                    def stage1():
                        isq = c < nq
                        psq, tpsq = self.ps[pi], self.t_ps[pi]
                        kb.op("act", lambda e: e.activation(out=sq2[sb_][:, :N], in_=psq[:, :N], func=AF.Square), reads=[tpsq], writes=[t_sq2[sb_]])
                        pss, tpss = self.ps[3], self.t_ps[3]
                        kb.op("pe", lambda e: e.matmul(pss[:, :N], lhsT=self.ones[:], rhs=sq2[sb_][:, :N], start=True, stop=True),
                              reads=[t_sq2[sb_], self.t_const], writes=[tpss])
                        kb.op("act", lambda e: e.activation(out=r2[sb_][:, :N], in_=pss[:, :N], func=AF.Sqrt, scale=1.0 / 128, bias=self.epsT[:, 0:1]),
                              reads=[tpss, self.t_const], writes=[t_r2[sb_]])
                        kb.op("dve", lambda e: e.reciprocal(r2[sb_][:, :N], r2[sb_][:, :N]), reads=[t_r2[sb_]], writes=[t_r2[sb_]])
                        gcol = gs[:, 0:1] if isq else gs[:, 1:2]
                        dstT = self.QT if isq else self.KT
                        t_dst = self.t_QT if isq else self.t_KT
                        row0 = (c if isq else c - nq) * 128
                        if rope and col == 0:
                            kb.op("dve", lambda e: e.scalar_tensor_tensor(qn[sb_][:, :N], psq[:, :N], gcol, r2[sb_][:, :N], ALU.mult, ALU.mult),
                                  reads=[tpsq, t_r2[sb_], t_gs], writes=[t_qn[sb_]])

                            def stage2():
                                psr, tpsr = self.ps[4 + sb_], self.t_ps[4 + sb_]
                                kb.op("pe", lambda e: e.matmul(psr[:, :N], lhsT=self.rot[:], rhs=qn[sb_][:, :N], start=True, stop=True),
                                      reads=[t_qn[sb_], self.t_const], writes=[tpsr])
                                kb.op("pool", lambda e: e.tensor_tensor(t1[sb_][:, :N], qn[sb_][:, :N], cs[0][:, :N], ALU.mult),
                                      reads=[t_qn[sb_], t_cs], writes=[t_t1[sb_]])
                                kb.op("dve", lambda e: e.tensor_tensor(t2[sb_][:, :N], psr[:, :N], cs[1][:, :N], ALU.mult),
                                      reads=[tpsr, t_cs], writes=[t_t2[sb_]])
                                kb.op("dve", lambda e: e.tensor_tensor(qo[sb_][:, :N], t1[sb_][:, :N], t2[sb_][:, :N], ALU.add),
                                      reads=[t_t1[sb_], t_t2[sb_]], writes=[t_qo[sb_]])
                                kb.dma("pool", dstT[row0:row0 + 128, t0:t0 + N], qo[sb_][:, :N], reads=[t_qo[sb_]], writes=[t_dst])
                            return stage2
                        kb.op("dve", lambda e: e.scalar_tensor_tensor(qo[sb_][:, :N], psq[:, :N], gcol, r2[sb_][:, :N], ALU.mult, ALU.mult),
                              reads=[tpsq, t_r2[sb_], t_gs], writes=[t_qo[sb_]])
                        kb.dma("pool", dstT[row0:row0 + 128, t0:t0 + N], qo[sb_][:, :N], reads=[t_qo[sb_]], writes=[t_dst])
                        return None
                    return stage1

                for cg in range((nqk + 3) // 4):
                    b = wg % 3
                    wg += 1
                    nch = min(4, nqk - cg * 4)
                    wv = self.wview(wt[b], KC, 512)
                    kb.dma("sp", wv[:, :, :nch * 128], w["mix_b"][:, cg * 512:cg * 512 + nch * 128].rearrange("(kc p) n -> p kc n", p=128),
                           reads=[self.t_Wl[li]], writes=[t_wt[b]])
                    for j in range(nch):
                        c = cg * 4 + j
                        pi = cnt % 3
                        sb_ = cnt % 2
                        cnt += 1
                        psq, tpsq = self.ps[pi], self.t_ps[pi]
                        for kc in range(KC):
                            kb.op("pe", lambda e: e.matmul(psq[:, :N], lhsT=wv[:, kc, j * 128:(j + 1) * 128], rhs=ht[:, kc, :N],
                                                           start=(kc == 0), stop=(kc == KC - 1)),
                                  reads=[t_wt[b], t_ht], writes=[tpsq], signal=(kc == KC - 1))
                        s2 = pend1() if pend1 else None
                        if pend2:
                            pend2()
                        pend2 = s2
                        pend1 = make_stage1(c, pi, sb_)
                s2 = pend1() if pend1 else None
                if pend2:
                    pend2()
                if s2:
                    s2()
                vbase = nqk * 128
                vw = nkv * 128
                for vg in range((vw + 511) // 512):
                    b = wg % 3
                    wg += 1
                    wd = min(512, vw - vg * 512)
                    wv = self.wview(wt[b], KC, 512)
                    kb.dma("sp", wv[:, :, :wd], w["mix_b"][:, vbase + vg * 512:vbase + vg * 512 + wd].rearrange("(kc p) n -> p kc n", p=128),
                           reads=[self.t_Wl[li]], writes=[t_wt[b]])
                    for tb in range(N // 128):
                        pi = 6 + cnt % 2
                        vb = cnt % 2
                        cnt += 1
                        for kc in range(KC):
                            kb.op("pe", lambda e: e.matmul(self.ps[pi][:, :wd], lhsT=ht[:, kc, tb * 128:(tb + 1) * 128], rhs=wv[:, kc, :wd],
                                                           start=(kc == 0), stop=(kc == KC - 1)),
                                  reads=[t_wt[b], t_ht], writes=[self.t_ps[pi]], signal=(kc == KC - 1))
                        kb.op("act", lambda e: e.activation(out=vt[vb][:, :wd], in_=self.ps[pi][:, :wd], func=AF.Copy),
                              reads=[self.t_ps[pi]], writes=[t_vt[vb]])
                        kb.dma("pool", self.V[t0 + tb * 128:t0 + (tb + 1) * 128, vg * 512:vg * 512 + wd], vt[vb][:, :wd],
                               reads=[t_vt[vb]], writes=[self.t_V])
            kb.barrier()

    def phaseB(self, li, kind, need_ctx):
        kb, nc = self.kb, self.nc
        S, T = self.S, self.T
        w = self.W[li]
        nkv = NKV[kind]
        grp = 16 // nkv
        TB = T // 128
        SB = S // 128
        ctxb = [SB, SB + 1]
        sink = kind == 1
        with ExitStack() as pes:
            sb = lambda name, shape, dt: pes.enter_context(nc.sbuf_tensor(self.uname(name), list(shape), dt))
            KTh = sb("b_kt", [128, T], BF16); t_k = Tok()
            Vh = sb("b_v", [128, TB, 128], BF16); t_v = Tok()
            QTh = [sb(f"b_qt{i}", [128, T], BF16) for i in range(2)]; t_q = [Tok(), Tok()]
            pt = [sb(f"b_pt{i}", [128, 512], BF16) for i in range(3)]; t_pt = [Tok() for _ in range(3)]
            rec = sb("b_rec", [128, 512], F32); t_rec = Tok()
            ot = [sb(f"b_ot{i}", [128, 512], BF16) for i in range(2)]; t_ot = [Tok(), Tok()]
            esink = sb("b_esink", [128, 16], F32); t_es = Tok()
            accden = kind == 3
            if accden:
                onesf = sb("b_onesf", [128, 128], F32); t_onesf = Tok()
                kb.dma("sp", onesf[:], self.in_onesf[:, :], writes=[t_onesf])
                accD = [sb(f"b_accD{i}", [128, 512], F32) for i in range(2)]; t_accD = [Tok(), Tok()]
                accP = [sb(f"b_accP{i}", [128, 512], F32) for i in range(2)]; t_accP = [Tok(), Tok()]
            if kind == 0:
                npat = self.npat
                tab = sb("b_tab", [128, 8 * 512], F32); t_tab = Tok()
                etab = sb("b_etab", [128, npat, 8 * 512], BF16); t_etab = Tok()
                pat_of_C, pats = na_patterns(S)
            if kind == 1:
                swam = sb("b_swam", [128, 6, 512], BF16); t_swam = Tok()
                kb.dma("sp", swam[:], self.in_swam[:, :].rearrange("p (r n) -> p r n", n=512), writes=[t_swam])
            if sink:
                kb.op("act", lambda e: e.activation(out=esink[:], in_=self.SV[li][:, SV_SINK:SV_SINK + 16], func=AF.Exp),
                      reads=[self.t_mod[li]], writes=[t_es])
            chunks = []
            if kind == 3:
                for q0 in range(0, S, 512):
                    chunks.append((q0, min(512, S - q0), [(b, None) for b in range(TB)]))
            elif kind == 1:
                for ci in range(S // 512):
                    bl = [(kbk, ("swaw", kbk - 4 * ci + 1)) for kbk in range(4 * ci - 1, 4 * ci + 5) if 0 <= kbk < SB]
                    bl += [(b, None) for b in ctxb]
                    chunks.append((ci * 512, 512, bl))
            else:
                for ci in range(S // 512):
                    p = pat_of_C[ci]
                    bbw = na_base_block(ci, SB)
                    bl = [(bbw + j, ("naw", p, j)) for j in range(8)] + [(b, None) for b in ctxb]
                    chunks.append((ci * 512, 512, bl))
            if need_ctx:
                chunks.append((S, CTX, [(b, None) for b in ctxb]))
            cnt = 0
            qi = 0
            for kvh in range(nkv):
                kb.dma("sp", KTh[:], self.KT[kvh * 128:(kvh + 1) * 128, :], reads=[self.t_KT], writes=[t_k])
                kb.dma("sp", Vh[:], self.V[:, kvh * 128:(kvh + 1) * 128].rearrange("(tb p) d -> p tb d", p=128), reads=[self.t_V], writes=[t_v])
                for qh in range(kvh * grp, (kvh + 1) * grp):
                    qb = qi % 2
                    qi += 1
                    kb.dma("sp", QTh[qb][:], self.QT[qh * 128:(qh + 1) * 128, :], reads=[self.t_QT], writes=[t_q[qb]])
                    if kind == 0:
                        for p in range(npat):
                            r0 = (p * 16 + qh) * 128
                            kb.dma("sp", tab[:, :], w["natab"][r0:r0 + 128, :], writes=[t_tab])
                            kb.op("act", lambda e: e.activation(out=etab[:, p, :], in_=tab[:, :], func=AF.Exp), reads=[t_tab], writes=[t_etab])
                    for (q0, N, blocks) in chunks:
                        po, tpo = self.ps[0 + (cnt % 2)], self.t_ps[0 + (cnt % 2)]
                        pd, tpd = self.ps[2 + (cnt % 2)], self.t_ps[2 + (cnt % 2)]
                        ob = cnt % 2
                        cnt += 1
                        nb = len(blocks)

                        def emit_s(i):
                            kbk, mask = blocks[i]
                            psn = 4 + (i % 3)
                            pb = i % 3
                            kb.op("pe", lambda e: e.matmul(self.ps[psn][:, :N], lhsT=KTh[:, kbk * 128:(kbk + 1) * 128], rhs=QTh[qb][:, q0:q0 + N],
                                                           start=True, stop=True),
                                  reads=[t_k, t_q[qb]], writes=[self.t_ps[psn]])
                            kb.op("act", lambda e: e.activation(out=pt[pb][:, :N], in_=self.ps[psn][:, :N], func=AF.Exp),
                                  reads=[self.t_ps[psn]], writes=[t_pt[pb]])
                            if mask is not None:
                                if mask[0] == "swaw":
                                    m, tm = swam[:, mask[1], :N], t_swam
                                else:
                                    m, tm = etab[:, mask[1], mask[2] * 512:mask[2] * 512 + N], t_etab
                                kb.op("dve", lambda e: e.tensor_tensor(pt[pb][:, :N], pt[pb][:, :N], m, ALU.mult),
                                      reads=[t_pt[pb], tm], writes=[t_pt[pb]])

                        def emit_pv(i):
                            kbk, _ = blocks[i]
                            pb = i % 3
                            kb.op("pe", lambda e: e.matmul(po[:, :N], lhsT=Vh[:, kbk, :], rhs=pt[pb][:, :N], start=(i == 0), stop=(i == nb - 1)),
                                  reads=[t_v, t_pt[pb]], writes=[tpo], signal=(i == nb - 1))
                            if not accden:
                                kb.op("pe", lambda e: e.matmul(pd[:, :N], lhsT=self.ones[:], rhs=pt[pb][:, :N], start=(i == 0), stop=(i == nb - 1)),
                                      reads=[self.t_const, t_pt[pb]], writes=[tpd], signal=True)
                            else:
                                onp = (i % 3 == 2)
                                eng = "pool" if onp else "dve"
                                acc, tacc = (accP[ob], t_accP[ob]) if onp else (accD[ob], t_accD[ob])
                                first = (i == 2) if onp else (i == 0)
                                if first:
                                    kb.op(eng, lambda e: e.tensor_copy(acc[:, :N], pt[pb][:, :N]), reads=[t_pt[pb]], writes=[tacc])
                                else:
                                    kb.op(eng, lambda e: e.tensor_tensor(acc[:, :N], acc[:, :N], pt[pb][:, :N], ALU.add), reads=[t_pt[pb], tacc], writes=[tacc])

                        DEPTH = 2
                        for i in range(min(DEPTH, nb)):
                            emit_s(i)
                        for i in range(nb):
                            if i + DEPTH < nb:
                                emit_s(i + DEPTH)
                            emit_pv(i)
                        if accden:
                            if nb > 2:
                                kb.op("dve", lambda e: e.tensor_tensor(accD[ob][:, :N], accD[ob][:, :N], accP[ob][:, :N], ALU.add),
                                      reads=[t_accP[ob], t_accD[ob]], writes=[t_accD[ob]])
                            kb.op("pe", lambda e: e.matmul(pd[:, :N], lhsT=onesf[:, :], rhs=accD[ob][:, :N], start=True, stop=True),
                                  reads=[t_onesf, t_accD[ob]], writes=[tpd])
                        if sink:
                            kb.op("dve", lambda e: e.tensor_scalar_add(rec[:, :N], pd[:, :N], esink[:, qh:qh + 1]), reads=[tpd, t_es], writes=[t_rec])
                            kb.op("dve", lambda e: e.reciprocal(rec[:, :N], rec[:, :N]), reads=[t_rec], writes=[t_rec])
                        else:
                            kb.op("dve", lambda e: e.reciprocal(rec[:, :N], pd[:, :N]), reads=[tpd], writes=[t_rec])
                        kb.op("dve", lambda e: e.tensor_tensor(ot[ob][:, :N], po[:, :N], rec[:, :N], ALU.mult), reads=[tpo, t_rec], writes=[t_ot[ob]])
                        kb.dma("pool", self.OT[qh * 128:(qh + 1) * 128, q0:q0 + N], ot[ob][:, :N], reads=[t_ot[ob]], writes=[self.t_OT])
            kb.barrier()

    def phaseC(self, li, need_ctx, xi, final):
        kb, nc = self.kb, self.nc
        S, T = self.S, self.T
        w = self.W[li]
        XT, t_XT = self.XT[xi], self.t_XT[xi]
        XO, t_XO = self.XT[1 - xi], self.t_XT[1 - xi]
        wins = []
        for w0 in range(0, S, 510):
            n = min(510, S - w0)
            wins.append((w0, n, w0 > 0, w0 + n < S, 0))
        if need_ctx:
            wins.append((S, CTX, False, False, 1))
        with ExitStack() as pes:
            sb = lambda name, shape, dt: pes.enter_context(nc.sbuf_tensor(self.uname(name), list(shape), dt))
            xt = sb("c_xt", [128, KC, 512], F32); t_xt = Tok()
            ab = sb("c_ab", [128, KC, 512], BF16); t_ab = Tok()
            act = sb("c_act", [128, FC, 512], BF16); t_act = Tok()
            wt = [sb(f"c_w{i}", [128, FC * 256], BF16) for i in range(3)]; t_wt = [Tok() for _ in range(3)]
            sq = [sb(f"c_sq{i}", [128, 512], BF16) for i in range(2)]; t_sq = [Tok(), Tok()]
            tmp = [sb(f"c_tmp{i}", [128, 512], F32) for i in range(2)]; t_tmp = [Tok(), Tok()]
            rs = sb("c_rs", [128, 512], F32); t_rs = Tok()
            ga = [sb(f"c_ga{i}", [128, 512], F32) for i in range(2)]; t_ga = [Tok(), Tok()]
            gq = [sb(f"c_gq{i}", [128, 512], F32) for i in range(2)]; t_gq = [Tok(), Tok()]
            gz = [sb(f"c_gz{i}", [128, 512], F32) for i in range(2)]; t_gz = [Tok(), Tok()]
            gy = [sb(f"c_gy{i}", [128, 512], F32) for i in range(2)]; t_gy = [Tok(), Tok()]
            xo = [sb(f"c_xo{i}", [128, 512], F32) for i in range(2)]; t_xo = [Tok(), Tok()]
            SVl = self.SV[li]
            MOD = self.MOD[li]
            wg = 0
            cnt = 0
            for (w0, n, left, right, col) in wins:
                a0 = w0 - (1 if left else 0)
                N = n + (1 if left else 0) + (1 if right else 0)
                lo = 1 if left else 0
                kb.dma("sp", ab[:, :, :N], self.OT[:, a0:a0 + N].rearrange("(kc p) n -> p kc n", p=128), reads=[self.t_OT], writes=[t_ab])
                kb.dma("sp", xt[:, :, :N], XT[:, a0:a0 + N].rearrange("(kc p) n -> p kc n", p=128), reads=[t_XT], writes=[t_xt])
                for ng in range(4):
                    b = wg % 3
                    wg += 1
                    wv = self.wview(wt[b], KC, 512)
                    kb.dma("sp", wv, w["wo_b"][:, ng * 512:(ng + 1) * 512].rearrange("(kc p) n -> p kc n", p=128), reads=[self.t_Wl[li]], writes=[t_wt[b]])
                    for j in range(4):
                        c = ng * 4 + j
                        pi = cnt % 2
                        cnt += 1
                        for kc in range(KC):
                            kb.op("pe", lambda e: e.matmul(self.ps[pi][:, :N], lhsT=wv[:, kc, j * 128:(j + 1) * 128], rhs=ab[:, kc, :N],
                                                           start=(kc == 0), stop=(kc == KC - 1)),
                                  reads=[t_wt[b], t_ab], writes=[self.t_ps[pi]], signal=(kc == KC - 1))
                        kb.op("dve", lambda e: e.scalar_tensor_tensor(xt[:, c, :N], self.ps[pi][:, :N], MOD[:, 32 + c, col:col + 1], xt[:, c, :N],
                                                                      ALU.mult, ALU.add),
                              reads=[self.t_ps[pi], t_xt, self.t_mod[li]], writes=[t_xt])
                self.norm_mod(pes, xt, t_xt, N, col, self.SCA2[li], 48, li, ab, t_ab, sq, t_sq, tmp, t_tmp, 7, rs, t_rs)
                for cg in range(FC // 4):
                    bg = wg % 3
                    wg += 1
                    bu = wg % 3
                    wg += 1
                    wvg = self.wview(wt[bg], KC, 512)
                    wvu = self.wview(wt[bu], KC, 512)
                    kb.dma("sp", wvg, w["win_b"][:, cg * 512:(cg + 1) * 512].rearrange("(kc p) n -> p kc n", p=128), reads=[self.t_Wl[li]], writes=[t_wt[bg]])
                    kb.dma("sp", wvu, w["win_b"][:, DFF + cg * 512:DFF + (cg + 1) * 512].rearrange("(kc p) n -> p kc n", p=128),
                           reads=[self.t_Wl[li]], writes=[t_wt[bu]])
                    for j in range(4):
                        c = cg * 4 + j
                        eb = cnt % 2
                        cnt += 1
                        pg, tpg = self.ps[2 + eb], self.t_ps[2 + eb]
                        pu, tpu = self.ps[4 + eb], self.t_ps[4 + eb]
                        for kc in range(KC):
                            kb.op("pe", lambda e: e.matmul(pg[:, :N], lhsT=wvg[:, kc, j * 128:(j + 1) * 128], rhs=ab[:, kc, :N],
                                                           start=(kc == 0), stop=(kc == KC - 1)),
                                  reads=[t_wt[bg], t_ab], writes=[tpg], signal=(kc == KC - 1))
                        for kc in range(KC):
                            kb.op("pe", lambda e: e.matmul(pu[:, :N], lhsT=wvu[:, kc, j * 128:(j + 1) * 128], rhs=ab[:, kc, :N],
                                                           start=(kc == 0), stop=(kc == KC - 1)),
                                  reads=[t_wt[bu], t_ab], writes=[tpu], signal=(kc == KC - 1))
                        cw = lambda jj: SVl[:, SV_CW + jj * FC + c:SV_CW + jj * FC + c + 1]
                        a_, ta_ = ga[eb], t_ga[eb]
                        kb.op("act", lambda e: e.activation(out=a_[:, :n], in_=pg[:, lo:lo + n], func=AF.Identity, scale=cw(1),
                                                            bias=SVl[:, SV_CB + c:SV_CB + c + 1]),
                              reads=[tpg, self.t_mod[li]], writes=[ta_])
                        if left:
                            kb.op("dve", lambda e: e.scalar_tensor_tensor(a_[:, :n], pg[:, lo - 1:lo - 1 + n], cw(0), a_[:, :n], ALU.mult, ALU.add),
                                  reads=[tpg, ta_, self.t_mod[li]], writes=[ta_])
                        else:
                            kb.op("dve", lambda e: e.scalar_tensor_tensor(a_[:, 1:n], pg[:, lo:lo + n - 1], cw(0), a_[:, 1:n], ALU.mult, ALU.add),
                                  reads=[tpg, ta_, self.t_mod[li]], writes=[ta_])
                        if right:
                            kb.op("dve", lambda e: e.scalar_tensor_tensor(a_[:, :n], pg[:, lo + 1:lo + 1 + n], cw(2), a_[:, :n], ALU.mult, ALU.add),
                                  reads=[tpg, ta_, self.t_mod[li]], writes=[ta_])
                        else:
                            kb.op("dve", lambda e: e.scalar_tensor_tensor(a_[:, :n - 1], pg[:, lo + 1:lo + n], cw(2), a_[:, :n - 1], ALU.mult, ALU.add),
                                  reads=[tpg, ta_, self.t_mod[li]], writes=[ta_])
                        kb.op("act", lambda e: e.activation(out=gq[eb][:, :n], in_=a_[:, :n], func=AF.Square), reads=[ta_], writes=[t_gq[eb]])
                        kb.op("dve", lambda e: e.tensor_scalar(gq[eb][:, :n], gq[eb][:, :n], 0.044715 * GK, GK, ALU.mult, ALU.add),
                              reads=[t_gq[eb]], writes=[t_gq[eb]])
                        kb.op("pool", lambda e: e.tensor_tensor(gz[eb][:, :n], gq[eb][:, :n], a_[:, :n], ALU.mult), reads=[t_gq[eb], ta_], writes=[t_gz[eb]])
                        kb.op("act", lambda e: e.activation(out=gz[eb][:, :n], in_=gz[eb][:, :n], func=AF.Sigmoid), reads=[t_gz[eb]], writes=[t_gz[eb]])
                        kb.op("pool", lambda e: e.tensor_tensor(gy[eb][:, :n], gz[eb][:, :n], a_[:, :n], ALU.mult), reads=[t_gz[eb], ta_], writes=[t_gy[eb]])
                        kb.op("dve", lambda e: e.tensor_tensor(act[:, c, :n], gy[eb][:, :n], pu[:, lo:lo + n], ALU.mult),
                              reads=[t_gy[eb], tpu], writes=[t_act])
                for ng in range(8):
                    b = wg % 3
                    wg += 1
                    wv = self.wview(wt[b], FC, 256)
                    kb.dma("sp", wv, w["wout_b"][:, ng * 256:(ng + 1) * 256].rearrange("(kc p) n -> p kc n", p=128), reads=[self.t_Wl[li]], writes=[t_wt[b]])
                    for j in range(2):
                        c = ng * 2 + j
                        pi = cnt % 2
                        ob = cnt % 2
                        cnt += 1
                        for kc in range(FC):
                            kb.op("pe", lambda e: e.matmul(self.ps[pi][:, :n], lhsT=wv[:, kc, j * 128:(j + 1) * 128], rhs=act[:, kc, :n],
                                                           start=(kc == 0), stop=(kc == FC - 1)),
                                  reads=[t_wt[b], t_act], writes=[self.t_ps[pi]], signal=(kc == FC - 1))
                        kb.op("dve", lambda e: e.scalar_tensor_tensor(xo[ob][:, :n], self.ps[pi][:, :n], MOD[:, 80 + c, col:col + 1], xt[:, c, lo:lo + n],
                                                                      ALU.mult, ALU.add),
                              reads=[self.t_ps[pi], t_xt, self.t_mod[li]], writes=[t_xo[ob]])
                        if final:
                            if col == 0:
                                kb.dma("pool", self.out[c * 128:(c + 1) * 128, w0:w0 + n], xo[ob][:, :n], reads=[t_xo[ob]])
                        else:
                            kb.dma("pool", XO[c * 128:(c + 1) * 128, w0:w0 + n], xo[ob][:, :n], reads=[t_xo[ob]], writes=[t_XO])
            kb.barrier()

    def phaseA_ml(self, li, xi):
        kb, nc = self.kb, self.nc
        S, T = self.S, self.T
        w = self.W[li]
        XT, t_XT = self.XT[xi], self.t_XT[xi]
        tiles = [(t0, min(512, S - t0), 0) for t0 in range(0, S, 512)] + [(S, CTX, 1)]
        with ExitStack() as pes:
            sb = lambda name, shape, dt: pes.enter_context(nc.sbuf_tensor(self.uname(name), list(shape), dt))
            xt = sb("a_xt", [128, KC, 512], F32); t_xt = Tok()
            ht = sb("a_ht", [128, KC, 512], BF16); t_ht = Tok()
            sq = [sb(f"a_sq{i}", [128, 512], BF16) for i in range(2)]; t_sq = [Tok(), Tok()]
            tmp = [sb(f"a_tmp{i}", [128, 512], F32) for i in range(2)]; t_tmp = [Tok(), Tok()]
            rs = sb("a_rs", [128, 512], F32); t_rs = Tok()
            wt = [sb(f"a_w{i}", [128, KC * 512], BF16) for i in range(3)]; t_wt = [Tok(), Tok(), Tok()]
            qo = [sb(f"a_qo{i}", [128, 512], BF16) for i in range(2)]; t_qo = [Tok(), Tok()]
            vt = [sb(f"a_vt{i}", [128, 512], BF16) for i in range(2)]; t_vt = [Tok(), Tok()]
            gt = [sb(f"a_gt{i}", [128, 16], F32) for i in range(2)]; t_gt = [Tok(), Tok()]
            ge = [sb(f"a_ge{i}", [128, 8], F32) for i in range(2)]; t_ge = [Tok(), Tok()]
            SVl = self.SV[li]
            wg = 0
            cnt = 0
            for (t0, N, col) in tiles:
                kb.dma("sp", xt[:, :, :N], XT[:, t0:t0 + N].rearrange("(kc p) n -> p kc n", p=128), reads=[t_XT], writes=[t_xt])
                self.norm_mod(pes, xt, t_xt, N, col, self.SCA1[li], 0, li, ht, t_ht, sq, t_sq, tmp, t_tmp, 7, rs, t_rs)
                for cg in range(4):
                    b = wg % 3
                    wg += 1
                    wv = self.wview(wt[b], KC, 512)
                    kb.dma("sp", wv, w["mix_b"][:, cg * 512:(cg + 1) * 512].rearrange("(kc p) n -> p kc n", p=128), reads=[self.t_Wl[li]], writes=[t_wt[b]])
                    for j in range(4):
                        c = cg * 4 + j
                        pi = cnt % 2
                        ob = cnt % 2
                        cnt += 1
                        for kc in range(KC):
                            kb.op("pe", lambda e: e.matmul(self.ps[pi][:, :N], lhsT=wv[:, kc, j * 128:(j + 1) * 128], rhs=ht[:, kc, :N],
                                                           start=(kc == 0), stop=(kc == KC - 1)),
                                  reads=[t_wt[b], t_ht], writes=[self.t_ps[pi]], signal=(kc == KC - 1))
                        isq = c < 8
                        kb.op("act", lambda e: e.activation(out=qo[ob][:, :N], in_=self.ps[pi][:, :N], func=AF.Copy, scale=(1.0 if isq else 0.0625)),
                              reads=[self.t_ps[pi]], writes=[t_qo[ob]])
                        dstT, t_dst = (self.QT, self.t_QT) if isq else (self.KT, self.t_KT)
                        row0 = (c if isq else c - 8) * 128
                        kb.dma("pool", dstT[row0:row0 + 128, t0:t0 + N], qo[ob][:, :N], reads=[t_qo[ob]], writes=[t_dst])
                groups = [(1024 + g * 512, 512, "k", g * 512) for g in range(2)] + [(2048 + g * 512, 512, "v", g * 512) for g in range(4)] \
                    + [(4096 + g * 512, 512, "o", g * 512) for g in range(4)] + [(6144, 16, "g", 0)]
                for (c0, wd, what, d0) in groups:
                    b = wg % 3
                    wg += 1
                    wv = self.wview(wt[b], KC, 512)
                    kb.dma("sp", wv[:, :, :wd], w["mix_b"][:, c0:c0 + wd].rearrange("(kc p) n -> p kc n", p=128), reads=[self.t_Wl[li]], writes=[t_wt[b]])
                    for tb in range(N // 128):
                        pi = 2 + cnt % 2
                        vb = cnt % 2
                        cnt += 1
                        r0 = t0 + tb * 128
                        for kc in range(KC):
                            kb.op("pe", lambda e: e.matmul(self.ps[pi][:, :wd], lhsT=ht[:, kc, tb * 128:(tb + 1) * 128], rhs=wv[:, kc, :wd],
                                                           start=(kc == 0), stop=(kc == KC - 1)),
                                  reads=[t_wt[b], t_ht], writes=[self.t_ps[pi]], signal=(kc == KC - 1))
                        if what == "g":
                            g_, tg_ = gt[vb], t_gt[vb]
                            kb.op("dve", lambda e: e.tensor_tensor(g_[:, :], self.ps[pi][:, :16], SVl[:, SV_SINK:SV_SINK + 16], ALU.add),
                                  reads=[self.t_ps[pi], self.t_mod[li]], writes=[tg_])
                            e_, te_ = ge[vb], t_ge[vb]
                            for (src, dst) in ((4, 0), (12, 4)):
                                kb.op("act", lambda e: e.activation(out=e_[:, dst:dst + 4], in_=g_[:, src:src + 4], func=AF.Exp, scale=-1.0),
                                      reads=[tg_], writes=[te_])
                            kb.op("dve", lambda e: e.tensor_scalar_add(e_[:, :], e_[:, :], 1.0), reads=[te_], writes=[te_])
                            kb.op("act", lambda e: e.activation(out=e_[:, :], in_=e_[:, :], func=AF.Ln), reads=[te_], writes=[te_])
                            for (src, dst) in ((0, 4), (4, 12)):
                                kb.op("dve", lambda e: e.tensor_scalar_mul(g_[:, dst:dst + 4], e_[:, src:src + 4], -1.0), reads=[te_, tg_], writes=[tg_])
                            kb.dma("pool", self.G[r0:r0 + 128, :], g_[:, :], reads=[tg_], writes=[self.t_G])
                        else:
                            kb.op("act", lambda e: e.activation(out=vt[vb][:, :wd], in_=self.ps[pi][:, :wd], func=AF.Copy,
                                                                scale=(0.0625 if what == "k" else 1.0)),
                                  reads=[self.t_ps[pi]], writes=[t_vt[vb]])
                            dst, t_dst = {"k": (self.Ktok, self.t_Ktok), "v": (self.V, self.t_V), "o": (self.Og, self.t_Og)}[what]
                            kb.dma("pool", dst[r0:r0 + 128, d0:d0 + wd], vt[vb][:, :wd], reads=[t_vt[vb]], writes=[t_dst])
            kb.barrier()

    def phaseB_ml(self, li):
        kb, nc = self.kb, self.nc
        S, T = self.S, self.T
        L = 64
        nxc, ncc = S // L, CTX // L
        with ExitStack() as pes:
            sb = lambda name, shape, dt: pes.enter_context(nc.sbuf_tensor(self.uname(name), list(shape), dt))
            identf = sb("m_identf", [128, 128], F32)
            onesf = sb("m_onesf", [128, 128], F32)
            tri = sb("m_tri", [64, 128], F32)
            t_c = Tok()
            kb.dma("sp", identf[:], self.in_identf[:, :], writes=[t_c])
            kb.dma("sp", onesf[:], self.in_onesf[:, :], writes=[t_c])
            kb.dma("sp", tri[:], self.in_tri[:, :], writes=[t_c])
            C = [[sb(f"m_C{d}{h}", [128, 2, 512], F32) for h in range(4)] for d in range(2)]
            Cb = [[sb(f"m_Cb{d}{h}", [128, 2, 512], BF16) for h in range(4)] for d in range(2)]
            nn = [[sb(f"m_n{d}{h}", [128, 2], F32) for h in range(4)] for d in range(2)]
            nb = [[sb(f"m_nb{d}{h}", [128, 2], BF16) for h in range(4)] for d in range(2)]
            t_C = [[Tok() for h in range(4)] for d in range(2)]
            t_Cb = [[Tok() for h in range(4)] for d in range(2)]
            t_n = [[Tok() for h in range(4)] for d in range(2)]
            t_nb = [[Tok() for h in range(4)] for d in range(2)]
            for d in range(2):
                for h in range(4):
                    kb.op("pool", lambda e: e.memset(C[d][h][:], 0.0), writes=[t_C[d][h]])
                    kb.op("pool", lambda e: e.memset(Cb[d][h][:], 0.0), writes=[t_Cb[d][h]])
                    kb.op("pool", lambda e: e.memset(nn[d][h][:], 0.0), writes=[t_n[d][h]])
                    kb.op("pool", lambda e: e.memset(nb[d][h][:], 0.0), writes=[t_nb[d][h]])
            NB = 2
            D2 = range(2)
            mk = lambda name, shape, dt, n: [sb(f"{name}_{i}", shape, dt) for i in range(n)]
            tk = lambda n: [Tok() for _ in range(n)]
            qT = mk("m_qT", [128, 8, L], BF16, NB); t_qT = tk(NB)
            kT = mk("m_kT", [128, 8, L], BF16, NB); t_kT = tk(NB)
            kk = mk("m_kk", [L, 1024], BF16, NB); t_kk = tk(NB)
            vv = mk("m_vv", [L, 2048], BF16, NB); t_vv = tk(NB)
            gg = mk("m_gg", [L, 16], F32, NB); t_gg = tk(NB)
            bb = mk("m_b", [L, 4], F32, 2); t_bb = tk(2)
            wprev = mk("m_wprev", [L, 4], F32, 2); t_wprev = tk(2)
            lmb = mk("m_lmb", [L, 4], F32, 2); t_lmb = tk(2)
            ws = mk("m_ws", [L, 4], F32, 2); t_ws = tk(2)
            decay = mk("m_decay", [128, 4], F32, 2); t_decay = tk(2)
            diagb = mk("m_diagb", [L, L], F32, 4); t_diagb = tk(4)
            Eh = mk("m_E", [L, L], F32, 4); t_Eh = tk(4)
            WT = mk("m_WT", [L, L], BF16, 4); t_WT = tk(4)
            n1 = mk("m_n1", [L, 512], F32, 4); t_n1 = tk(4)
            hh = mk("m_hh", [L, 512], F32, 4); t_hh = tk(4)
            den = mk("m_den", [L, 2], F32, 4); t_den = tk(4)
            kp = mk("m_kp", [L, 256], BF16, 4); t_kp = tk(4)
            ps, tps = self.ps, self.t_ps
            pA, tA = ps[0], tps[0]
            pB, tB = ps[1], tps[1]
            orders = [[S + c * L for c in range(ncc)] + [c * L for c in range(nxc)],
                      [S + c * L for c in reversed(range(ncc))] + [c * L for c in reversed(range(nxc))]]
            cidx = 0
            for step in range(ncc + nxc):
                for d in D2:
                    tk0 = orders[d][step]
                    lic, lfc = (0, 4) if d == 0 else (8, 12)
                    trid = tri[:, 0:64] if d == 0 else tri[:, 64:128]
                    cb = cidx % NB
                    cidx += 1
                    qT_, kT_, kk_, vv_, G_ = qT[cb], kT[cb], kk[cb], vv[cb], gg[cb]
                    tq, tkT, tkk, tvv, tgg = t_qT[cb], t_kT[cb], t_kk[cb], t_vv[cb], t_gg[cb]
                    kb.dma("sp", qT_[:], self.QT[0:1024, tk0:tk0 + L].rearrange("(c p) n -> p c n", p=128), reads=[self.t_QT], writes=[tq])
                    kb.dma("sp", kT_[:], self.KT[0:1024, tk0:tk0 + L].rearrange("(c p) n -> p c n", p=128), reads=[self.t_KT], writes=[tkT])
                    kb.dma("sp", kk_[:], self.Ktok[tk0:tk0 + L, :], reads=[self.t_Ktok], writes=[tkk])
                    kb.dma("sp", vv_[:], self.V[tk0:tk0 + L, :], reads=[self.t_V], writes=[tvv])
                    kb.dma("sp", G_[:], self.G[tk0:tk0 + L, :], reads=[self.t_G], writes=[tgg])
                    bb_, wprev_, lmb_, ws_, decay_ = bb[cb], wprev[cb], lmb[cb], ws[cb], decay[cb]
                    tbb, twp, tlmb, tws, tdec = t_bb[cb], t_wprev[cb], t_lmb[cb], t_ws[cb], t_decay[cb]
                    H4 = range(4)
                    kb.op("pe", lambda e: e.matmul(pA[0:L, 0:4], lhsT=trid, rhs=G_[:, lfc:lfc + 4], start=True, stop=True),
                          reads=[t_c, tgg], writes=[tA])
                    kb.op("pe", lambda e: e.matmul(pA[:, 16:20], lhsT=onesf[0:L, :], rhs=G_[:, lfc:lfc + 4], start=True, stop=True),
                          reads=[t_c, tgg], writes=[tA])
                    kb.op("dve", lambda e: e.tensor_copy(bb_[:, :], pA[0:L, 0:4]), reads=[tA], writes=[tbb])
                    kb.op("act", lambda e: e.activation(out=wprev_[:, :], in_=pA[0:L, 0:4], func=AF.Exp), reads=[tA], writes=[twp])
                    kb.op("act", lambda e: e.activation(out=decay_[:, :], in_=pA[:, 16:20], func=AF.Exp), reads=[tA], writes=[tdec])
                    kb.op("dve", lambda e: e.tensor_tensor(lmb_[:, :], G_[:, lic:lic + 4], bb_[:, :], ALU.subtract), reads=[tgg, tbb], writes=[tlmb])
                    kb.op("dve", lambda e: e.tensor_tensor(ws_[:, :], lmb_[:, :], pA[0:L, 16:20], ALU.add), reads=[tlmb, tA], writes=[tws])
                    kb.op("act", lambda e: e.activation(out=ws_[:, :], in_=ws_[:, :], func=AF.Exp), reads=[tws], writes=[tws])
                    for h in H4:
                        kb.op("dve", lambda e: e.tensor_scalar_mul(diagb[h][:, :], identf[0:L, 0:L], bb_[:, h:h + 1]), reads=[t_c, tbb], writes=[t_diagb[h]])
                    for h in H4:
                        kb.op("pe", lambda e: e.matmul(pA[0:L, 64 + 64 * h:128 + 64 * h], lhsT=onesf[0:L, 0:L], rhs=diagb[h][:, :], start=True, stop=True),
                              reads=[t_c, t_diagb[h]], writes=[tA])
                    for h in H4:
                        for i in range(2):
                            kb.op("pe", lambda e: e.matmul(pB[0:L, 64 * h:64 * h + 64], lhsT=kT_[:, 2 * h + i, :], rhs=qT_[:, 2 * h + i, :], start=(i == 0), stop=(i == 1)),
                                  reads=[tkT, tq], writes=[tB], signal=(i == 1))
                    for h in H4:
                        kb.op("dve", lambda e: e.tensor_scalar(Eh[h][:, :], pA[0:L, 64 + 64 * h:128 + 64 * h], lmb_[:, h:h + 1], 60.0, ALU.add, ALU.min),
                              reads=[tA, tlmb], writes=[t_Eh[h]])
                    for h in H4:
                        kb.op("act", lambda e: e.activation(out=Eh[h][:, :], in_=Eh[h][:, :], func=AF.Exp), reads=[t_Eh[h]], writes=[t_Eh[h]])
                    for h in H4:
                        kb.op("dve", lambda e: e.tensor_tensor(Eh[h][:, :], Eh[h][:, :], trid, ALU.mult), reads=[t_Eh[h], t_c], writes=[t_Eh[h]])
                    for h in H4:
                        kb.op("dve", lambda e: e.tensor_tensor(WT[h][:, :], Eh[h][:, :], pB[0:L, 64 * h:64 * h + 64], ALU.mult), reads=[t_Eh[h], tB], writes=[t_WT[h]])
                    for h in H4:
                        kb.op("act", lambda e: e.activation(out=kp[h][:, :], in_=kk_[:, h * 256:(h + 1) * 256], func=AF.Copy, scale=ws_[:, h:h + 1]),
                              reads=[tkk, tws], writes=[t_kp[h]])
                    for h in H4:
                        kb.op("pe", lambda e: e.matmul(pA[0:L, 320 + h:321 + h], lhsT=WT[h][:, :], rhs=self.ones[0:L, 0:1], start=True, stop=True),
                              reads=[t_WT[h], self.t_const], writes=[tA])
                        for i in range(2):
                            kb.op("pe", lambda e: e.matmul(pA[0:L, 328 + h:329 + h], lhsT=qT_[:, 2 * h + i, :], rhs=nb[d][h][:, i:i + 1], start=(i == 0), stop=(i == 1)),
                                  reads=[tq, t_nb[d][h]], writes=[tA], signal=(i == 1))
                        for i in range(2):
                            kb.op("pe", lambda e: e.matmul(pA[:, 336 + 2 * h + i:337 + 2 * h + i], lhsT=kp[h][:, i * 128:(i + 1) * 128], rhs=self.ones[0:L, 0:1],
                                                           start=True, stop=True),
                                  reads=[t_kp[h], self.t_const], writes=[tA])
                    for h in H4:
                        kb.op("act", lambda e: e.activation(out=den[h][:, 0:1], in_=pA[0:L, 320 + h:321 + h], func=AF.Copy), reads=[tA], writes=[t_den[h]])
                    for h in H4:
                        kb.op("dve", lambda e: e.scalar_tensor_tensor(den[h][:, 1:2], pA[0:L, 328 + h:329 + h], wprev_[:, h:h + 1], den[h][:, 0:1], ALU.mult, ALU.add),
                              reads=[tA, twp, t_den[h]], writes=[t_den[h]])
                    for h in H4:
                        for i in range(2):
                            kb.op("dve", lambda e: e.scalar_tensor_tensor(nn[d][h][:, i:i + 1], nn[d][h][:, i:i + 1], decay_[:, h:h + 1],
                                                                          pA[:, 336 + 2 * h + i:337 + 2 * h + i], ALU.mult, ALU.add),
                                  reads=[t_n[d][h], tdec, tA], writes=[t_n[d][h]])
                    for h in H4:
                        kb.op("act", lambda e: e.activation(out=den[h][:, 1:2], in_=den[h][:, 1:2], func=AF.Abs), reads=[t_den[h]], writes=[t_den[h]])
                    for h in H4:
                        kb.op("dve", lambda e: e.tensor_scalar_max(den[h][:, 1:2], den[h][:, 1:2], 1.0), reads=[t_den[h]], writes=[t_den[h]])
                    for h in H4:
                        kb.op("dve", lambda e: e.reciprocal(den[h][:, 1:2], den[h][:, 1:2]), reads=[t_den[h]], writes=[t_den[h]])
                    for h in H4:
                        vh = vv_[:, h * 512:(h + 1) * 512]
                        pN1, tN1 = ps[2 + h % 2], tps[2 + h % 2]
                        pN2, tN2 = ps[4 + h % 2], tps[4 + h % 2]
                        kb.op("pe", lambda e: e.matmul(pN1[0:L, :], lhsT=WT[h][:, :], rhs=vh, start=True, stop=True),
                              reads=[t_WT[h], tvv], writes=[tN1])
                        for i in range(2):
                            kb.op("pe", lambda e: e.matmul(pN2[0:L, :], lhsT=qT_[:, 2 * h + i, :], rhs=Cb[d][h][:, i, :], start=(i == 0), stop=(i == 1)),
                                  reads=[tq, t_Cb[d][h]], writes=[tN2], signal=(i == 1))
                        kb.op("act", lambda e: e.activation(out=n1[h][:, :], in_=pN1[0:L, :], func=AF.Copy), reads=[tN1], writes=[t_n1[h]])
                        kb.op("dve", lambda e: e.scalar_tensor_tensor(hh[h][:, :], pN2[0:L, :], wprev_[:, h:h + 1], n1[h][:, :], ALU.mult, ALU.add),
                              reads=[tN2, twp, t_n1[h]], writes=[t_hh[h]])
                        kb.op("act", lambda e: e.activation(out=hh[h][:, :], in_=hh[h][:, :], func=AF.Copy, scale=den[h][:, 1:2]), reads=[t_hh[h], t_den[h]], writes=[t_hh[h]])
                        kb.dma("act", self.H[d][tk0:tk0 + L, h * 512:(h + 1) * 512], hh[h][:, :], reads=[t_hh[h]], writes=[self.t_H[d]])
                    for h in H4:
                        vh = vv_[:, h * 512:(h + 1) * 512]
                        for i in range(2):
                            pDC, tDC = ps[6 + i], tps[6 + i]
                            kb.op("pe", lambda e: e.matmul(pDC[:, :], lhsT=kp[h][:, i * 128:(i + 1) * 128], rhs=vh, start=True, stop=True),
                                  reads=[t_kp[h], tvv], writes=[tDC])
                            kb.op("dve", lambda e: e.scalar_tensor_tensor(C[d][h][:, i, :], C[d][h][:, i, :], decay_[:, h:h + 1], pDC[:, :], ALU.mult, ALU.add),
                                  reads=[t_C[d][h], tdec, tDC], writes=[t_C[d][h]])
                            kb.op("act", lambda e: e.activation(out=Cb[d][h][:, i, :], in_=C[d][h][:, i, :], func=AF.Copy), reads=[t_C[d][h]], writes=[t_Cb[d][h]])
                        kb.op("act", lambda e: e.activation(out=nb[d][h][:, :], in_=nn[d][h][:, :], func=AF.Copy), reads=[t_n[d][h]], writes=[t_nb[d][h]])
            kb.barrier()

    def phaseR_ml(self, li):
        kb, nc = self.kb, self.nc
        S, T = self.S, self.T
        w = self.W[li]
        with ExitStack() as pes:
            sb = lambda name, shape, dt: pes.enter_context(nc.sbuf_tensor(self.uname(name), list(shape), dt))
            hg = sb("r_hg", [128, D], F32); t_hg = Tok()
            kb.dma("sp", hg[:], w["headg"][:, :], writes=[t_hg])
            hf = [sb(f"r_hf{i}", [128, D], F32) for i in range(2)]; t_hf = [Tok(), Tok()]
            hb_ = [sb(f"r_hb{i}", [128, D], F32) for i in range(2)]; t_hb = [Tok(), Tok()]
            og = [sb(f"r_og{i}", [128, D], BF16) for i in range(2)]; t_og = [Tok(), Tok()]
            sgm = sb("r_sg", [128, D], F32); t_sg = Tok()
            sqb = sb("r_sq", [128, D], F32); t_sqb = Tok()
            ss = sb("r_ss", [128, 4], F32); t_ss = Tok()
            y = sb("r_y", [128, D], BF16); t_y = Tok()
            ot = [sb(f"r_ot{i}", [128, 512], BF16) for i in range(2)]; t_ot = [Tok(), Tok()]
            cnt = 0
            for bi in range(T // 128):
                r0 = bi * 128
                b = bi % 2
                kb.dma("sp", hf[b][:], self.H[0][r0:r0 + 128, :], reads=[self.t_H[0]], writes=[t_hf[b]])
                kb.dma("sp", hb_[b][:], self.H[1][r0:r0 + 128, :], reads=[self.t_H[1]], writes=[t_hb[b]])
                kb.dma("sp", og[b][:], self.Og[r0:r0 + 128, :], reads=[self.t_Og], writes=[t_og[b]])
                kb.op("dve", lambda e: e.tensor_tensor(hf[b][:], hf[b][:], hb_[b][:], ALU.add), reads=[t_hf[b], t_hb[b]], writes=[t_hf[b]])
                kb.op("act", lambda e: e.activation(out=sqb[:], in_=hf[b][:], func=AF.Square), reads=[t_hf[b]], writes=[t_sqb])
                kb.op("dve", lambda e: e.reduce_sum(ss[:, :], sqb[:].rearrange("p (h n) -> p h n", h=4), mybir.AxisListType.X), reads=[t_sqb], writes=[t_ss])
                kb.op("dve", lambda e: e.tensor_scalar(ss[:, :], ss[:, :], 1.0 / 512, EPS, ALU.mult, ALU.add), reads=[t_ss], writes=[t_ss])
                kb.op("act", lambda e: e.activation(out=ss[:, :], in_=ss[:, :], func=AF.Sqrt), reads=[t_ss], writes=[t_ss])
                kb.op("dve", lambda e: e.reciprocal(ss[:, :], ss[:, :]), reads=[t_ss], writes=[t_ss])
                kb.op("act", lambda e: e.activation(out=sgm[:], in_=og[b][:], func=AF.Sigmoid), reads=[t_og[b]], writes=[t_sg])
                kb.op("pool", lambda e: e.tensor_tensor(sgm[:], sgm[:], hg[:], ALU.mult), reads=[t_sg, t_hg], writes=[t_sg])
                for h in range(4):
                    kb.op("dve", lambda e: e.scalar_tensor_tensor(y[:, h * 512:(h + 1) * 512], hf[b][:, h * 512:(h + 1) * 512], ss[:, h:h + 1],
                                                                  sgm[:, h * 512:(h + 1) * 512], ALU.mult, ALU.mult),
                          reads=[t_hf[b], t_ss, t_sg], writes=[t_y])
                for g4 in range(4):
                    pi = cnt % 2
                    ob = cnt % 2
                    cnt += 1
                    for jj in range(4):
                        j = g4 * 4 + jj
                        kb.op("pe", lambda e: e.matmul(self.ps[pi][:, jj * 128:(jj + 1) * 128], lhsT=y[:, j * 128:(j + 1) * 128], rhs=self.ident[:, :],
                                                       start=True, stop=True),
                              reads=[t_y, self.t_const], writes=[self.t_ps[pi]], signal=(jj == 3))
                    kb.op("act", lambda e: e.activation(out=ot[ob][:, :], in_=self.ps[pi][:, :], func=AF.Copy), reads=[self.t_ps[pi]], writes=[t_ot[ob]])
                    kb.dma("pool", self.OT[g4 * 512:(g4 + 1) * 512, r0:r0 + 128].rearrange("(jj p) n -> p jj n", p=128),
                           ot[ob][:, :].rearrange("p (jj n) -> p jj n", n=128), reads=[t_ot[ob]], writes=[self.t_OT])
            kb.barrier()

    def build(self):
        self.phase0()
        xi = 0
        for idx, (li, kind, need_ctx) in enumerate(self.layers):
            final = idx == len(self.layers) - 1
            self.kb.flush_bg(self.t_Wl[li])
            if kind == 2:
                self.phaseA_ml(li, xi)
                self.phaseB_ml(li)
                self.phaseR_ml(li)
            else:
                self.phaseA(li, kind, xi)
                self.phaseB(li, kind, need_ctx)
            self.phaseC(li, need_ctx, xi, final)
            xi = 1 - xi
        self.kb.barrier(engines=("sp",))
        self.es.close()
        return self.nc


def na_base_block(ci, SB):
    return min(max(4 * ci - 2, 0), SB - 8)


def na_patterns(S):
    rows = S // GW
    kr = min(8, rows)
    SB = S // 128
    pats, pat_of_C, keymap = [], [], {}
    for ci in range(S // 512):
        bbw = na_base_block(ci, SB)
        r0s = tuple(min(max(8 * ci + a - kr // 2, 0), rows - kr) - 8 * ci for a in range(8))
        key = (bbw - 4 * ci, r0s)
        if key not in keymap:
            keymap[key] = len(pats)
            pats.append(ci)
        pat_of_C.append(keymap[key])
    return pat_of_C, pats


def build_na_table(rel_bias, S):
    rows = S // GW
    kr = min(8, rows)
    SB = S // 128
    pat_of_C, pats = na_patterns(S)
    npat = len(pats)
    rb = np.asarray(rel_bias, np.float32)
    tab = np.full((npat, 16, 128, 8, 512), -30000.0, np.float32)
    i = np.arange(512)
    qc = i % GW
    c0 = np.clip(qc - 8, 0, GW - 16)
    j = np.arange(128)
    for p, ci in enumerate(pats):
        bbw = na_base_block(ci, SB)
        qr = 8 * ci + i // GW
        r0 = np.clip(qr - kr // 2, 0, rows - kr)
        for jb in range(8):
            kr_ = 2 * (bbw + jb) + j // GW
            kc_ = j % GW
            ok = ((kr_[:, None] >= r0[None, :]) & (kr_[:, None] < r0[None, :] + kr)
                  & (kc_[:, None] >= c0[None, :]) & (kc_[:, None] < c0[None, :] + 16))
            drow = np.clip(kr_[:, None] - qr[None, :] + 7, 0, 14)
            dcol = np.clip(kc_[:, None] - qc[None, :] + 15, 0, 30)
            vals = rb[:, drow, dcol]
            tab[p, :, :, jb, :] = np.where(ok[None], vals, np.float32(-30000.0))
    return tab.reshape(npat * 16 * 128, 8 * 512)


def swa_mask_table():
    j = np.arange(128)
    prev = (j[:, None] >= j[None, :]).astype(np.float32)
    nxt = (j[:, None] <= j[None, :]).astype(np.float32)
    t = np.zeros((128, 6, 4, 128), np.float32)
    for r in range(6):
        for a in range(4):
            rel = r - 1 - a
            if rel == -1:
                t[:, r, a, :] = prev
            elif rel == 0:
                t[:, r, a, :] = 1.0
            elif rel == 1:
                t[:, r, a, :] = nxt
    return t.reshape(128, 6 * 512).astype(ml_dtypes.bfloat16)


def rope_tables(S):
    t = np.arange(S)
    row = (t // GW).astype(np.float32)
    colp = (t % GW).astype(np.float32)
    inv = (10000.0 ** (-np.arange(32, dtype=np.float32) / 32)).astype(np.float32)
    cosT = np.zeros((128, S), np.float32)
    sinT = np.zeros((128, S), np.float32)
    for a, pos in enumerate((row, colp)):
        ang = (pos[None, :] * inv[:, None]).astype(np.float32)
        for p in range(2):
            cosT[a * 64 + p * 32:a * 64 + (p + 1) * 32] = np.cos(ang)
            sinT[a * 64 + p * 32:a * 64 + (p + 1) * 32] = np.sin(ang)
    rot = np.zeros((128, 128), np.float32)
    for a in range(2):
        for f in range(32):
            d1, d2 = a * 64 + f, a * 64 + 32 + f
            rot[d2, d1] = -1.0
            rot[d1, d2] = 1.0
    return cosT, sinT, rot


def fm(v, ncol):
    return np.ascontiguousarray(np.asarray(v, np.float32).reshape(ncol, 128).T)


def pack_sv(li, kind, P):
    sv = np.zeros((128, NSV), np.float32)
    sv[:, SV_ADAB:SV_ADAB + 96] = fm(P["ada_b"][li], 96)
    sv[:, SV_N1:SV_N1 + 16] = fm(P["norm1_g"][li], 16)
    sv[:, SV_N2:SV_N2 + 16] = fm(P["norm2_g"][li], 16)
    for j in range(3):
        sv[:, SV_CW + j * FC:SV_CW + (j + 1) * FC] = fm(P["ffn_conv_w"][li][j], FC)
    sv[:, SV_CB:SV_CB + FC] = fm(P["ffn_conv_b"][li], FC)
    pre = {0: "na", 1: "swa", 3: "gqa"}.get(kind)
    if pre:
        sv[:, SV_QG] = np.asarray(P[pre + "_q_g"][0], np.float32)
        sv[:, SV_KG] = np.asarray(P[pre + "_k_g"][0], np.float32)
    if kind == 1:
        sv[:, SV_SINK:SV_SINK + 16] = np.asarray(P["swa_sinks"][0], np.float32)[None, :]
    if kind == 2:
        sv[:, SV_SINK:SV_SINK + 16] = np.asarray(P["ml_gate_b"][0], np.float32)[None, :]
    return sv


def core_inputs(b, S, layers, P, consts):
    m = dict(consts)
    m["xT"] = np.ascontiguousarray(np.asarray(P["x"][b, :S], np.float32).T)
    m["cxT"] = np.ascontiguousarray(np.asarray(P["ctx"][b], np.float32).T)
    cc = np.stack([fm(P["c"][b], KC), fm(P["c_ctx"], KC)], axis=-1)
    m["cc"] = np.ascontiguousarray(cc)
    return m


def shared_inputs(S, layers, P):
    cosT, sinT, rot = rope_tables(S)
    bf = ml_dtypes.bfloat16
    j = np.arange(128)
    m = {
        "ones": np.ones((128, 128), bf), "ident": np.eye(128, dtype=np.float32).astype(bf), "rot": rot.astype(bf),
        "cosT": cosT, "sinT": sinT,
        "mprev": (j[:, None] >= j[None, :]).astype(np.float32).astype(bf),
        "mnext": (j[:, None] <= j[None, :]).astype(np.float32).astype(bf),
        "swam": swa_mask_table(),
    }
    if True:
        jj = np.arange(64)
        m["identf"] = np.eye(128, dtype=np.float32)
        m["onesf"] = np.ones((128, 128), np.float32)
        m["tri"] = np.concatenate([(jj[:, None] <= jj[None, :]), (jj[:, None] >= jj[None, :])], axis=1).astype(np.float32)
    mixw = {0: ("na_w_qkv", "na_w_o"), 1: ("swa_w_qkv", "swa_w_o"), 2: ("ml_w_in", "ml_w_o"), 3: ("gqa_w_qkv", "gqa_w_o")}
    for (li, kind, _) in layers:
        m[f"ada_w{li}"] = np.asarray(P["ada_w"][li], np.float32)
        m[f"ffn_w_in{li}"] = np.asarray(P["ffn_w_in"][li], np.float32)
        m[f"ffn_w_out{li}"] = np.asarray(P["ffn_w_out"][li], np.float32)
        m[f"mix_w_in{li}"] = np.asarray(P[mixw[kind][0]][0], np.float32)
        m[f"mix_w_o{li}"] = np.asarray(P[mixw[kind][1]][0], np.float32)
        m[f"sv{li}"] = pack_sv(li, kind, P)
        if kind == 0:
            m[f"natab{li}"] = build_na_table(P["na_rel_bias"][0], S)
        if kind == 2:
            m[f"headg{li}"] = np.ascontiguousarray(np.broadcast_to(np.asarray(P["ml_head_g"][0], np.float32)[None, :], (128, D)))
    return m


def run_model(P, S, layers, batches, trace=False, spread=False):
    prog = Prog(S, layers)
    nc = prog.build()
    shared = shared_inputs(S, layers, P)
    in_maps = [core_inputs(b, S, layers, P, shared) for b in batches]
    slots = list(range(len(batches)))
    if spread and len(batches) == 4:
        big = ("ada_w", "ffn_w_in", "ffn_w_out", "mix_w_in", "mix_w_o", "sv", "natab", "headg")
        zmap = {}
        for k, v in in_maps[0].items():
            if k.startswith(big) or k in ("xT", "cxT", "cc"):
                zmap[k] = np.zeros(v.shape, v.dtype)
            else:
                zmap[k] = v
        slots = [0, 1, 4, 5]
        full = [zmap] * 8
        full = list(full)
        for sl, m in zip(slots, in_maps):
            full[sl] = m
        in_maps = full
    res = run_bass_kernel_spmd(nc, in_maps, core_ids=list(range(len(in_maps))), trace=trace)
    outs = [np.ascontiguousarray(res.results[sl]["outT"].T) for sl in slots]
    return np.stack(outs, 0), res


def kernel(**inputs):
    S = inputs["x"].shape[1]
    layers = [(i, i % 4, i < 3) for i in range(4)]
    out, _ = run_model(inputs, S, layers, list(range(inputs["x"].shape[0])), spread=False)
    return out.astype(np.float32)
```

```python
import math
from contextlib import ExitStack

import ml_dtypes
import numpy as np

import concourse.bass as bass
import concourse.mybir as mybir
from concourse.bass_utils import run_bass_kernel_spmd

F32, BF16 = mybir.dt.float32, mybir.dt.bfloat16
AF = mybir.ActivationFunctionType
ALU = mybir.AluOpType

D = 2048
KC = 16
DFF = 5632
FC = 44
CTX = 256
GW = 64
EPS = 1e-6
NSV = 96 + 16 + 16 + 132 + 44 + 1 + 1 + 1 + 16
SV_ADAB, SV_N1, SV_N2, SV_CW, SV_CB, SV_QG, SV_KG, SV_GB, SV_SINK = 0, 96, 112, 128, 260, 304, 305, 306, 307
GK = 1.5957691216057308
MIX_COLS = {0: 6144, 1: 2560, 2: 6160, 3: 3072}
NKV = {0: 16, 1: 2, 3: 4}


class Tok:
    __slots__ = ("w", "r", "ex")

    def __init__(self, ex=False):
        self.w = {}
        self.r = {}
        self.ex = ex


class KB:
    def __init__(self, nc, es, nring=6):
        self.nc = nc
        self.E = {"pe": nc.tensor, "act": nc.scalar, "dve": nc.vector, "pool": nc.gpsimd, "sp": nc.sync}
        self.csem = {e: es.enter_context(nc.semaphore("c_" + e)) for e in ("pe", "act", "dve", "pool")}
        self.ccnt = {e: 0 for e in self.csem}
        self.NR = nring
        self.dsem = {e: [es.enter_context(nc.semaphore(f"d_{e}{i}")) for i in range(nring)] for e in ("sp", "pool", "act")}
        self.dcnt = {e: [0] * nring for e in self.dsem}
        self.dnext = {e: 0 for e in self.dsem}
        self.waited = {e: {} for e in self.E}
        self.pending = {e: [] for e in self.csem}
        self.ninst = 0
        self.bg = []
        self.bgcnt = 0

    def _wait(self, e, ev):
        if ev is None:
            return
        sem, val, key = ev
        if key == "cpe" and e == "pe":
            return
        w = self.waited[e]
        if w.get(key, 0) >= val:
            return
        self.E[e].wait_ge(sem, val)
        self.ninst += 1
        w[key] = val

    def _deps(self, e, reads, writes):
        for t in reads:
            for ev in list(t.w.values()):
                self._wait(e, ev)
            if t.ex:
                for ev in list(t.r.values()):
                    if ev[2] != "c" + e:
                        self._wait(e, ev)
        for t in writes:
            for ev in list(t.w.values()):
                self._wait(e, ev)
            for ev in list(t.r.values()):
                self._wait(e, ev)

    @staticmethod
    def _commit(ev, reads, writes):
        for t in reads:
            t.r[ev[2]] = ev
        for t in writes:
            t.w[ev[2]] = ev
            t.r = {}

    def op(self, e, fn, reads=(), writes=(), signal=True):
        self._deps(e, reads, writes)
        ins = fn(self.E[e])
        self.ninst += 1
        if signal:
            self.ccnt[e] += 1
            ins.then_inc(self.csem[e], 1)
            ev = (self.csem[e], self.ccnt[e], "c" + e)
            self._commit(ev, reads, writes)
            for (r, w) in self.pending[e]:
                self._commit(ev, r, w)
            self.pending[e] = []
        else:
            self.pending[e].append((tuple(reads), tuple(writes)))
        return ins

    def dma(self, q, out, in_, reads=(), writes=()):
        i = self.dnext[q]
        self.dnext[q] = (i + 1) % self.NR
        sem = self.dsem[q][i]
        key = f"d{q}{i}"
        if self.dcnt[q][i]:
            self._wait(q, (sem, self.dcnt[q][i], key))
        self._deps(q, reads, writes)
        self.E[q].dma_start(out=out, in_=in_).then_inc(sem, 16)
        self.ninst += 1
        self.dcnt[q][i] += 16
        ev = (sem, self.dcnt[q][i], key)
        self._commit(ev, reads, writes)
        if q == "pool" and self.bg and not getattr(self, "_in_bg", False):
            self.bgcnt += 1
            if self.bgcnt % 2 == 0:
                self._in_bg = True
                o, i_, wr = self.bg.pop(0)
                self.dma("pool", o, i_, writes=wr)
                self._in_bg = False

    def flush_bg(self, tok):
        keep = []
        self._in_bg = True
        for (o, i_, wr) in self.bg:
            if tok in wr:
                self.dma("pool", o, i_, writes=wr)
            else:
                keep.append((o, i_, wr))
        self._in_bg = False
        self.bg = keep

    def barrier(self, engines=("pe", "act", "dve", "pool", "sp")):
        for e in engines:
            for q in self.dsem:
                for i in range(self.NR):
                    if self.dcnt[q][i]:
                        self._wait(e, (self.dsem[q][i], self.dcnt[q][i], f"d{q}{i}"))
            for c in self.csem:
                if self.ccnt[c] and c != e:
                    self._wait(e, (self.csem[c], self.ccnt[c], "c" + c))


class Prog:
    def __init__(self, S, layers, final_out=True):
        self.S, self.T = S, S + CTX
        self.layers = layers
        self.nc = nc = bass.Bass("TRN2", target_bir_lowering=False)
        self.es = ExitStack()
        self.kb = KB(nc, self.es)
        T = self.T
        di = lambda name, shape, dt=F32: nc.dram_tensor(name, list(shape), dt, kind="ExternalInput").ap()
        ds = lambda name, shape, dt: nc.dram_tensor(name, list(shape), dt).ap()
        self.in_xT = di("xT", [D, S])
        self.in_cxT = di("cxT", [D, CTX])
        self.in_cc = di("cc", [128, KC, 2])
        self.in_ones = di("ones", [128, 128], BF16)
        self.in_ident = di("ident", [128, 128], BF16)
        self.in_rot = di("rot", [128, 128], BF16)
        self.in_cos = di("cosT", [128, S])
        self.in_sin = di("sinT", [128, S])
        self.in_mprev = di("mprev", [128, 128], BF16)
        self.in_mnext = di("mnext", [128, 128], BF16)
        self.in_swam = di("swam", [128, 6 * 512], BF16)
        self.W = {}
        self.natab = None
        for (li, kind, _) in layers:
            w = {}
            w["ada"] = di(f"ada_w{li}", [D, 6 * D])
            w["win"] = di(f"ffn_w_in{li}", [D, 2 * DFF])
            w["wout"] = di(f"ffn_w_out{li}", [DFF, D])
            w["mix"] = di(f"mix_w_in{li}", [D, MIX_COLS[kind]])
            w["wo"] = di(f"mix_w_o{li}", [D, D])
            w["sv"] = di(f"sv{li}", [128, NSV])
            for k in ("ada", "win", "wout", "mix", "wo"):
                w[k + "_b"] = ds(f"{k}_b{li}", w[k].shape, BF16)
            if kind == 0:
                self.npat = len(na_patterns(S)[1])
                w["natab"] = di(f"natab{li}", [self.npat * 16 * 128, 8 * 512])
            if kind == 2:
                w["headg"] = di(f"headg{li}", [128, D])
            self.W[li] = w
        self.out = nc.dram_tensor("outT", [D, S], F32, kind="ExternalOutput").ap()
        self.XT = [ds("XT0", [D, T], F32), ds("XT1", [D, T], F32)]
        self.QT = ds("QT", [D, T], BF16)
        self.KT = ds("KT", [D, T], BF16)
        self.V = ds("V", [T, D], BF16)
        self.OT = ds("OT", [D, T], BF16)
        self.in_identf = di("identf", [128, 128])
        self.in_onesf = di("onesf", [128, 128])
        self.in_tri = di("tri", [64, 128])
        if any(k == 2 for (_, k, _) in layers):
            self.Ktok = ds("Ktok", [T, 1024], BF16)
            self.Og = ds("Og", [T, D], BF16)
            self.G = ds("G", [T, 16], F32)
            self.H = [ds("H0", [T, D], F32), ds("H1", [T, D], F32)]
            self.t_Ktok, self.t_Og, self.t_G, self.t_H = Tok(), Tok(), Tok(), [Tok(), Tok()]
        self.t_XT = [Tok(), Tok()]
        self.t_QT, self.t_KT, self.t_V, self.t_OT = Tok(), Tok(), Tok(), Tok()
        self.t_W = Tok()
        self.t_Wl = {li: Tok() for (li, _, _) in layers}
        es = self.es
        sb = lambda name, shape, dt: es.enter_context(nc.sbuf_tensor(self.uname(name), list(shape), dt))
        self.ps = [es.enter_context(nc.psum_tensor(f"ps{i}", [128, 512], F32)) for i in range(8)]
        self.t_ps = [Tok(ex=True) for _ in range(8)]
        self.ones = sb("ones_sb", [128, 128], BF16)
        self.ident = sb("ident_sb", [128, 128], BF16)
        self.rot = sb("rot_sb", [128, 128], BF16)
        self.mprev = sb("mprev_sb", [128, 128], BF16)
        self.mnext = sb("mnext_sb", [128, 128], BF16)
        self.t_const = Tok()
        self.epsT = sb("eps_sb", [128, 1], F32)
        self.MOD, self.SCA1, self.SCA2, self.SV, self.t_mod = {}, {}, {}, {}, {}
        for (li, _, _) in layers:
            self.MOD[li] = sb(f"mod{li}", [128, 96, 2], F32)
            self.SCA1[li] = sb(f"sca1_{li}", [128, KC, 2], F32)
            self.SCA2[li] = sb(f"sca2_{li}", [128, KC, 2], F32)
            self.SV[li] = sb(f"svs{li}", [128, NSV], F32)
            self.t_mod[li] = Tok()

    def uname(self, name):
        self._uid = getattr(self, "_uid", 0) + 1
        return f"s{self._uid}_{name}"

    def wview(self, buf, kc, n):
        return buf[:, 0:kc * n].rearrange("p (k n) -> p k n", n=n)

    def norm_mod(self, pes, xt, t_xt, N, col, SCA, shift_base, li, ht, t_ht, sq, t_sq, tmp, t_tmp, psn, rs, t_rs):
        kb = self.kb
        ps, tps = self.ps[psn], self.t_ps[psn]
        for kc in range(KC):
            b = kc % 2
            kb.op("act", lambda e: e.activation(out=sq[b][:, :N], in_=xt[:, kc, :N], func=AF.Square),
                  reads=[t_xt], writes=[t_sq[b]])
            kb.op("pe", lambda e: e.matmul(ps[:, :N], lhsT=self.ones[:], rhs=sq[b][:, :N], start=(kc == 0), stop=(kc == KC - 1)),
                  reads=[t_sq[b], self.t_const], writes=[tps])
        kb.op("act", lambda e: e.activation(out=rs[:, :N], in_=ps[:, :N], func=AF.Sqrt, scale=1.0 / D, bias=self.epsT[:, 0:1]),
              reads=[tps, self.t_const], writes=[t_rs])
        kb.op("dve", lambda e: e.reciprocal(rs[:, :N], rs[:, :N]), reads=[t_rs], writes=[t_rs])
        for kc in range(KC):
            b = kc % 2
            kb.op("dve", lambda e: e.tensor_tensor(tmp[b][:, :N], xt[:, kc, :N], rs[:, :N], ALU.mult),
                  reads=[t_xt, t_rs], writes=[t_tmp[b]])
            kb.op("act", lambda e: e.activation(out=ht[:, kc, :N], in_=tmp[b][:, :N], func=AF.Identity,
                                                scale=SCA[:, kc, col:col + 1], bias=self.MOD[li][:, shift_base + kc, col:col + 1]),
                  reads=[t_tmp[b], self.t_mod[li]], writes=[t_ht])

    def phase0(self):
        kb, nc = self.kb, self.nc
        S, T = self.S, self.T
        for (dst, src) in ((self.ones, self.in_ones), (self.ident, self.in_ident), (self.rot, self.in_rot),
                           (self.mprev, self.in_mprev), (self.mnext, self.in_mnext)):
            kb.dma("sp", dst[:], src[:, :], writes=[self.t_const])
        kb.op("pool", lambda e: e.memset(self.epsT[:], EPS), writes=[self.t_const])
        for r in range(0, D, 256):
            kb.dma("sp", self.XT[0][r:r + 256, 0:S], self.in_xT[r:r + 256, :], writes=[self.t_XT[0]])
        kb.dma("sp", self.XT[0][:, S:T], self.in_cxT[:, :], writes=[self.t_XT[0]])
        for (li, kind, _) in self.layers:
            w = self.W[li]
            for r in range(0, D, 256):
                kb.dma("pool", w["ada_b"][r:r + 256, :], w["ada"][r:r + 256, :], writes=[self.t_W])
        for idx, (li, kind, _) in enumerate(self.layers):
            w = self.W[li]
            for k in ("mix", "wo", "win", "wout"):
                rows = w[k].shape[0]
                step = 256 if idx == 0 else 64
                for r in range(0, rows, step):
                    r1 = min(rows, r + step)
                    if idx == 0:
                        kb.dma("pool", w[k + "_b"][r:r1, :], w[k][r:r1, :], writes=[self.t_Wl[li]])
                    else:
                        kb.bg.append((w[k + "_b"][r:r1, :], w[k][r:r1, :], [self.t_Wl[li]]))
        with ExitStack() as pes:
            sb = lambda name, shape, dt: pes.enter_context(nc.sbuf_tensor(self.uname(name), list(shape), dt))
            cc = sb("cc", [128, KC, 2], F32)
            scb = sb("scb", [128, KC, 2], BF16)
            t_cc, t_scb = Tok(), Tok()
            wt = [sb(f"p0w{i}", [128, KC * 512], BF16) for i in range(2)]
            t_wt = [Tok(), Tok()]
            tmp = sb("p0tmp", [128, KC, 2], F32)
            t_tmp = Tok()
            kb.dma("sp", cc[:], self.in_cc[:, :, :], writes=[t_cc])
            kb.op("act", lambda e: e.activation(out=scb[:], in_=cc[:], func=AF.Silu), reads=[t_cc], writes=[t_scb])
            g = 0
            for (li, kind, _) in self.layers:
                w = self.W[li]
                kb.dma("sp", self.SV[li][:], w["sv"][:, :], writes=[self.t_mod[li]])
                for cg in range(24):
                    b = g % 2
                    g += 1
                    wv = self.wview(wt[b], KC, 512)
                    kb.dma("sp", wv, w["ada_b"][:, cg * 512:(cg + 1) * 512].rearrange("(kc p) n -> p kc n", p=128),
                           reads=[self.t_W], writes=[t_wt[b]])
                    for j in range(4):
                        c = cg * 4 + j
                        pi = c % 2
                        for kc in range(KC):
                            kb.op("pe", lambda e: e.matmul(self.ps[pi][:, 0:2], lhsT=wv[:, kc, j * 128:(j + 1) * 128], rhs=scb[:, kc, :],
                                                           start=(kc == 0), stop=(kc == KC - 1)),
                                  reads=[t_wt[b], t_scb], writes=[self.t_ps[pi]], signal=(kc == KC - 1))
                        kb.op("act", lambda e: e.activation(out=self.MOD[li][:, c, :], in_=self.ps[pi][:, 0:2], func=AF.Identity,
                                                            bias=self.SV[li][:, SV_ADAB + c:SV_ADAB + c + 1], scale=1.0),
                              reads=[self.t_ps[pi], self.t_mod[li]], writes=[self.t_mod[li]])
                for (SCA, sc0, g0) in ((self.SCA1[li], 16, SV_N1), (self.SCA2[li], 64, SV_N2)):
                    kb.op("dve", lambda e: e.tensor_scalar_add(tmp[:], self.MOD[li][:, sc0:sc0 + 16, :], 1.0),
                          reads=[self.t_mod[li]], writes=[t_tmp])
                    for col in range(2):
                        kb.op("dve", lambda e: e.tensor_tensor(SCA[:, :, col], tmp[:, :, col], self.SV[li][:, g0:g0 + 16], ALU.mult),
                              reads=[t_tmp, self.t_mod[li]], writes=[self.t_mod[li]])
            kb.barrier()

    def phaseA(self, li, kind, xi):
        kb, nc = self.kb, self.nc
        S, T = self.S, self.T
        w = self.W[li]
        nq, nkv = 16, NKV[kind]
        rope = kind in (1, 3)
        XT, t_XT = self.XT[xi], self.t_XT[xi]
        tiles = [(t0, min(512, S - t0), 0) for t0 in range(0, S, 512)] + [(S, CTX, 1)]
        with ExitStack() as pes:
            sb = lambda name, shape, dt: pes.enter_context(nc.sbuf_tensor(self.uname(name), list(shape), dt))
            xt = sb("a_xt", [128, KC, 512], F32); t_xt = Tok()
            ht = sb("a_ht", [128, KC, 512], BF16); t_ht = Tok()
            sq = [sb(f"a_sq{i}", [128, 512], BF16) for i in range(2)]; t_sq = [Tok(), Tok()]
            tmp = [sb(f"a_tmp{i}", [128, 512], F32) for i in range(2)]; t_tmp = [Tok(), Tok()]
            rs = sb("a_rs", [128, 512], F32); t_rs = Tok()
            wt = [sb(f"a_w{i}", [128, KC * 512], BF16) for i in range(3)]; t_wt = [Tok(), Tok(), Tok()]
            cs = [sb("a_cos", [128, 512], F32), sb("a_sin", [128, 512], F32)]; t_cs = Tok()
            sq2 = [sb(f"a_sq2{i}", [128, 512], BF16) for i in range(2)]; t_sq2 = [Tok(), Tok()]
            r2 = [sb(f"a_r2{i}", [128, 512], F32) for i in range(2)]; t_r2 = [Tok(), Tok()]
            qn = [sb(f"a_qn{i}", [128, 512], BF16) for i in range(2)]; t_qn = [Tok(), Tok()]
            t1 = [sb(f"a_t1{i}", [128, 512], F32) for i in range(2)]; t_t1 = [Tok(), Tok()]
            t2 = [sb(f"a_t2{i}", [128, 512], F32) for i in range(2)]; t_t2 = [Tok(), Tok()]
            qo = [sb(f"a_qo{i}", [128, 512], BF16) for i in range(2)]; t_qo = [Tok(), Tok()]
            vt = [sb(f"a_vt{i}", [128, 512], BF16) for i in range(2)]; t_vt = [Tok(), Tok()]
            gs = sb("a_gs", [128, 2], F32); t_gs = Tok()
            kb.op("dve", lambda e: e.tensor_scalar_mul(gs[:, 0:1], self.SV[li][:, SV_QG:SV_QG + 1], 128.0 ** -0.5),
                  reads=[self.t_mod[li]], writes=[t_gs])
            kb.op("dve", lambda e: e.tensor_copy(gs[:, 1:2], self.SV[li][:, SV_KG:SV_KG + 1]), reads=[self.t_mod[li]], writes=[t_gs])
            wg = 0
            cnt = 0
            for (t0, N, col) in tiles:
                kb.dma("sp", xt[:, :, :N], XT[:, t0:t0 + N].rearrange("(kc p) n -> p kc n", p=128), reads=[t_XT], writes=[t_xt])
                if rope and col == 0:
                    kb.dma("sp", cs[0][:, :N], self.in_cos[:, t0:t0 + N], writes=[t_cs])
                    kb.dma("sp", cs[1][:, :N], self.in_sin[:, t0:t0 + N], writes=[t_cs])
                self.norm_mod(pes, xt, t_xt, N, col, self.SCA1[li], 0, li, ht, t_ht, sq, t_sq, tmp, t_tmp, 7, rs, t_rs)
                nqk = nq + nkv
                pend1, pend2 = None, None

                def make_stage1(c, pi, sb_, N=N, t0=t0, col=col):
                    def stage1():
                        isq = c < nq
                        psq, tpsq = self.ps[pi], self.t_ps[pi]
                        kb.op("act", lambda e: e.activation(out=sq2[sb_][:, :N], in_=psq[:, :N], func=AF.Square), reads=[tpsq], writes=[t_sq2[sb_]])
                        pss, tpss = self.ps[3], self.t_ps[3]
                        kb.op("pe", lambda e: e.matmul(pss[:, :N], lhsT=self.ones[:], rhs=sq2[sb_][:, :N], start=True, stop=True),
                              reads=[t_sq2[sb_], self.t_const], writes=[tpss])
                        kb.op("act", lambda e: e.activation(out=r2[sb_][:, :N], in_=pss[:, :N], func=AF.Sqrt, scale=1.0 / 128, bias=self.epsT[:, 0:1]),
                              reads=[tpss, self.t_const], writes=[t_r2[sb_]])
                        kb.op("dve", lambda e: e.reciprocal(r2[sb_][:, :N], r2[sb_][:, :N]), reads=[t_r2[sb_]], writes=[t_r2[sb_]])
                        gcol = gs[:, 0:1] if isq else gs[:, 1:2]
                        dstT = self.QT if isq else self.KT
                        t_dst = self.t_QT if isq else self.t_KT
                        row0 = (c if isq else c - nq) * 128
                        if rope and col == 0:
                            kb.op("dve", lambda e: e.scalar_tensor_tensor(qn[sb_][:, :N], psq[:, :N], gcol, r2[sb_][:, :N], ALU.mult, ALU.mult),
                                  reads=[tpsq, t_r2[sb_], t_gs], writes=[t_qn[sb_]])

                            def stage2():
                                psr, tpsr = self.ps[4 + sb_], self.t_ps[4 + sb_]
                                kb.op("pe", lambda e: e.matmul(psr[:, :N], lhsT=self.rot[:], rhs=qn[sb_][:, :N], start=True, stop=True),
                                      reads=[t_qn[sb_], self.t_const], writes=[tpsr])
                                kb.op("pool", lambda e: e.tensor_tensor(t1[sb_][:, :N], qn[sb_][:, :N], cs[0][:, :N], ALU.mult),
                                      reads=[t_qn[sb_], t_cs], writes=[t_t1[sb_]])
                                kb.op("dve", lambda e: e.tensor_tensor(t2[sb_][:, :N], psr[:, :N], cs[1][:, :N], ALU.mult),
                                      reads=[tpsr, t_cs], writes=[t_t2[sb_]])
                                kb.op("dve", lambda e: e.tensor_tensor(qo[sb_][:, :N], t1[sb_][:, :N], t2[sb_][:, :N], ALU.add),
                                      reads=[t_t1[sb_], t_t2[sb_]], writes=[t_qo[sb_]])
                                kb.dma("pool", dstT[row0:row0 + 128, t0:t0 + N], qo[sb_][:, :N], reads=[t_qo[sb_]], writes=[t_dst])
                            return stage2
                        kb.op("dve", lambda e: e.scalar_tensor_tensor(qo[sb_][:, :N], psq[:, :N], gcol, r2[sb_][:, :N], ALU.mult, ALU.mult),
                              reads=[tpsq, t_r2[sb_], t_gs], writes=[t_qo[sb_]])
                        kb.dma("pool", dstT[row0:row0 + 128, t0:t0 + N], qo[sb_][:, :N], reads=[t_qo[sb_]], writes=[t_dst])
                        return None
                    return stage1

                for cg in range((nqk + 3) // 4):
                    b = wg % 3
                    wg += 1
                    nch = min(4, nqk - cg * 4)
                    wv = self.wview(wt[b], KC, 512)
                    kb.dma("sp", wv[:, :, :nch * 128], w["mix_b"][:, cg * 512:cg * 512 + nch * 128].rearrange("(kc p) n -> p kc n", p=128),
                           reads=[self.t_Wl[li]], writes=[t_wt[b]])
                    for j in range(nch):
                        c = cg * 4 + j
                        pi = cnt % 3
                        sb_ = cnt % 2
                        cnt += 1
                        psq, tpsq = self.ps[pi], self.t_ps[pi]
                        for kc in range(KC):
                            kb.op("pe", lambda e: e.matmul(psq[:, :N], lhsT=wv[:, kc, j * 128:(j + 1) * 128], rhs=ht[:, kc, :N],
                                                           start=(kc == 0), stop=(kc == KC - 1)),
                                  reads=[t_wt[b], t_ht], writes=[tpsq], signal=(kc == KC - 1))
                        s2 = pend1() if pend1 else None
                        if pend2:
                            pend2()
                        pend2 = s2
                        pend1 = make_stage1(c, pi, sb_)
                s2 = pend1() if pend1 else None
                if pend2:
                    pend2()
                if s2:
                    s2()
                vbase = nqk * 128
                vw = nkv * 128
                for vg in range((vw + 511) // 512):
                    b = wg % 3
                    wg += 1
                    wd = min(512, vw - vg * 512)
                    wv = self.wview(wt[b], KC, 512)
                    kb.dma("sp", wv[:, :, :wd], w["mix_b"][:, vbase + vg * 512:vbase + vg * 512 + wd].rearrange("(kc p) n -> p kc n", p=128),
                           reads=[self.t_Wl[li]], writes=[t_wt[b]])
                    for tb in range(N // 128):
                        pi = 6 + cnt % 2
                        vb = cnt % 2
                        cnt += 1
                        for kc in range(KC):
                            kb.op("pe", lambda e: e.matmul(self.ps[pi][:, :wd], lhsT=ht[:, kc, tb * 128:(tb + 1) * 128], rhs=wv[:, kc, :wd],
                                                           start=(kc == 0), stop=(kc == KC - 1)),
                                  reads=[t_wt[b], t_ht], writes=[self.t_ps[pi]], signal=(kc == KC - 1))
                        kb.op("act", lambda e: e.activation(out=vt[vb][:, :wd], in_=self.ps[pi][:, :wd], func=AF.Copy),
                              reads=[self.t_ps[pi]], writes=[t_vt[vb]])
                        kb.dma("pool", self.V[t0 + tb * 128:t0 + (tb + 1) * 128, vg * 512:vg * 512 + wd], vt[vb][:, :wd],
                               reads=[t_vt[vb]], writes=[self.t_V])
            kb.barrier()

    def phaseB(self, li, kind, need_ctx):
        kb, nc = self.kb, self.nc
        S, T = self.S, self.T
        w = self.W[li]
        nkv = NKV[kind]
        grp = 16 // nkv
        TB = T // 128
        SB = S // 128
        ctxb = [SB, SB + 1]
        sink = kind == 1
        with ExitStack() as pes:
            sb = lambda name, shape, dt: pes.enter_context(nc.sbuf_tensor(self.uname(name), list(shape), dt))
            KTh = sb("b_kt", [128, T], BF16); t_k = Tok()
            Vh = sb("b_v", [128, TB, 128], BF16); t_v = Tok()
            QTh = [sb(f"b_qt{i}", [128, T], BF16) for i in range(2)]; t_q = [Tok(), Tok()]
            pt = [sb(f"b_pt{i}", [128, 512], BF16) for i in range(3)]; t_pt = [Tok() for _ in range(3)]
            rec = sb("b_rec", [128, 512], F32); t_rec = Tok()
            ot = [sb(f"b_ot{i}", [128, 512], BF16) for i in range(2)]; t_ot = [Tok(), Tok()]
            esink = sb("b_esink", [128, 16], F32); t_es = Tok()
            accden = False
            if accden:
                onesf = sb("b_onesf", [128, 128], F32); t_onesf = Tok()
                kb.dma("sp", onesf[:], self.in_onesf[:, :], writes=[t_onesf])
                accD = [sb(f"b_accD{i}", [128, 512], F32) for i in range(2)]; t_accD = [Tok(), Tok()]
                accP = [sb(f"b_accP{i}", [128, 512], F32) for i in range(2)]; t_accP = [Tok(), Tok()]
            if kind == 0:
                npat = self.npat
                tab = sb("b_tab", [128, 8 * 512], F32); t_tab = Tok()
                etab = sb("b_etab", [128, npat, 8 * 512], BF16); t_etab = Tok()
                pat_of_C, pats = na_patterns(S)
            if kind == 1:
                swam = sb("b_swam", [128, 6, 512], BF16); t_swam = Tok()
                kb.dma("sp", swam[:], self.in_swam[:, :].rearrange("p (r n) -> p r n", n=512), writes=[t_swam])
            if sink:
                kb.op("act", lambda e: e.activation(out=esink[:], in_=self.SV[li][:, SV_SINK:SV_SINK + 16], func=AF.Exp),
                      reads=[self.t_mod[li]], writes=[t_es])
            chunks = []
            if kind == 3:
                for q0 in range(0, S, 512):
                    chunks.append((q0, min(512, S - q0), [(b, None) for b in range(TB)]))
            elif kind == 1:
                for ci in range(S // 512):
                    bl = [(kbk, ("swaw", kbk - 4 * ci + 1)) for kbk in range(4 * ci - 1, 4 * ci + 5) if 0 <= kbk < SB]
                    bl += [(b, None) for b in ctxb]
                    chunks.append((ci * 512, 512, bl))
            else:
                for ci in range(S // 512):
                    p = pat_of_C[ci]
                    bbw = na_base_block(ci, SB)
                    bl = [(bbw + j, ("naw", p, j)) for j in range(8)] + [(b, None) for b in ctxb]
                    chunks.append((ci * 512, 512, bl))
            if need_ctx:
                chunks.append((S, CTX, [(b, None) for b in ctxb]))
            cnt = 0
            qi = 0
            for kvh in range(nkv):
                kb.dma("sp", KTh[:], self.KT[kvh * 128:(kvh + 1) * 128, :], reads=[self.t_KT], writes=[t_k])
                kb.dma("sp", Vh[:], self.V[:, kvh * 128:(kvh + 1) * 128].rearrange("(tb p) d -> p tb d", p=128), reads=[self.t_V], writes=[t_v])
                for qh in range(kvh * grp, (kvh + 1) * grp):
                    qb = qi % 2
                    qi += 1
                    kb.dma("sp", QTh[qb][:], self.QT[qh * 128:(qh + 1) * 128, :], reads=[self.t_QT], writes=[t_q[qb]])
                    if kind == 0:
                        for p in range(npat):
                            r0 = (p * 16 + qh) * 128
                            kb.dma("sp", tab[:, :], w["natab"][r0:r0 + 128, :], writes=[t_tab])
                            kb.op("act", lambda e: e.activation(out=etab[:, p, :], in_=tab[:, :], func=AF.Exp), reads=[t_tab], writes=[t_etab])
                    for (q0, N, blocks) in chunks:
                        po, tpo = self.ps[0 + (cnt % 2)], self.t_ps[0 + (cnt % 2)]
                        pd, tpd = self.ps[2 + (cnt % 2)], self.t_ps[2 + (cnt % 2)]
                        ob = cnt % 2
                        cnt += 1
                        nb = len(blocks)

                        def emit_s(i):
                            kbk, mask = blocks[i]
                            psn = 4 + (i % 3)
                            pb = i % 3
                            kb.op("pe", lambda e: e.matmul(self.ps[psn][:, :N], lhsT=KTh[:, kbk * 128:(kbk + 1) * 128], rhs=QTh[qb][:, q0:q0 + N],
                                                           start=True, stop=True),
                                  reads=[t_k, t_q[qb]], writes=[self.t_ps[psn]])
                            kb.op("act", lambda e: e.activation(out=pt[pb][:, :N], in_=self.ps[psn][:, :N], func=AF.Exp),
                                  reads=[self.t_ps[psn]], writes=[t_pt[pb]])
                            if mask is not None:
                                if mask[0] == "swaw":
                                    m, tm = swam[:, mask[1], :N], t_swam
                                else:
                                    m, tm = etab[:, mask[1], mask[2] * 512:mask[2] * 512 + N], t_etab
                                kb.op("dve", lambda e: e.tensor_tensor(pt[pb][:, :N], pt[pb][:, :N], m, ALU.mult),
                                      reads=[t_pt[pb], tm], writes=[t_pt[pb]])

                        def emit_pv(i):
                            kbk, _ = blocks[i]
                            pb = i % 3
                            kb.op("pe", lambda e: e.matmul(po[:, :N], lhsT=Vh[:, kbk, :], rhs=pt[pb][:, :N], start=(i == 0), stop=(i == nb - 1)),
                                  reads=[t_v, t_pt[pb]], writes=[tpo], signal=(i == nb - 1))
                            if not accden:
                                kb.op("pe", lambda e: e.matmul(pd[:, :N], lhsT=self.ones[:], rhs=pt[pb][:, :N], start=(i == 0), stop=(i == nb - 1)),
                                      reads=[self.t_const, t_pt[pb]], writes=[tpd], signal=True)
                            else:
                                onp = (i % 3 == 2)
                                eng = "pool" if onp else "dve"
                                acc, tacc = (accP[ob], t_accP[ob]) if onp else (accD[ob], t_accD[ob])
                                first = (i == 2) if onp else (i == 0)
                                if first:
                                    kb.op(eng, lambda e: e.tensor_copy(acc[:, :N], pt[pb][:, :N]), reads=[t_pt[pb]], writes=[tacc])
                                else:
                                    kb.op(eng, lambda e: e.tensor_tensor(acc[:, :N], acc[:, :N], pt[pb][:, :N], ALU.add), reads=[t_pt[pb], tacc], writes=[tacc])

                        DEPTH = 2
                        for i in range(min(DEPTH, nb)):
                            emit_s(i)
                        for i in range(nb):
                            if i + DEPTH < nb:
                                emit_s(i + DEPTH)
                            emit_pv(i)
                        if accden:
                            if nb > 2:
                                kb.op("dve", lambda e: e.tensor_tensor(accD[ob][:, :N], accD[ob][:, :N], accP[ob][:, :N], ALU.add),
                                      reads=[t_accP[ob], t_accD[ob]], writes=[t_accD[ob]])
                            kb.op("pe", lambda e: e.matmul(pd[:, :N], lhsT=onesf[:, :], rhs=accD[ob][:, :N], start=True, stop=True),
                                  reads=[t_onesf, t_accD[ob]], writes=[tpd])
                        if sink:
                            kb.op("dve", lambda e: e.tensor_scalar_add(rec[:, :N], pd[:, :N], esink[:, qh:qh + 1]), reads=[tpd, t_es], writes=[t_rec])
                            kb.op("dve", lambda e: e.reciprocal(rec[:, :N], rec[:, :N]), reads=[t_rec], writes=[t_rec])
                        else:
                            kb.op("dve", lambda e: e.reciprocal(rec[:, :N], pd[:, :N]), reads=[tpd], writes=[t_rec])
                        kb.op("dve", lambda e: e.tensor_tensor(ot[ob][:, :N], po[:, :N], rec[:, :N], ALU.mult), reads=[tpo, t_rec], writes=[t_ot[ob]])
                        kb.dma("pool", self.OT[qh * 128:(qh + 1) * 128, q0:q0 + N], ot[ob][:, :N], reads=[t_ot[ob]], writes=[self.t_OT])
            kb.barrier()

    def phaseC(self, li, need_ctx, xi, final):
        kb, nc = self.kb, self.nc
        S, T = self.S, self.T
        w = self.W[li]
        XT, t_XT = self.XT[xi], self.t_XT[xi]
        XO, t_XO = self.XT[1 - xi], self.t_XT[1 - xi]
        wins = []
        for w0 in range(0, S, 510):
            n = min(510, S - w0)
            wins.append((w0, n, w0 > 0, w0 + n < S, 0))
        if need_ctx:
            wins.append((S, CTX, False, False, 1))
        with ExitStack() as pes:
            sb = lambda name, shape, dt: pes.enter_context(nc.sbuf_tensor(self.uname(name), list(shape), dt))
            xt = sb("c_xt", [128, KC, 512], F32); t_xt = Tok()
            ab = sb("c_ab", [128, KC, 512], BF16); t_ab = Tok()
            act = sb("c_act", [128, FC, 512], BF16); t_act = Tok()
            wt = [sb(f"c_w{i}", [128, FC * 256], BF16) for i in range(3)]; t_wt = [Tok() for _ in range(3)]
            sq = [sb(f"c_sq{i}", [128, 512], BF16) for i in range(2)]; t_sq = [Tok(), Tok()]
            tmp = [sb(f"c_tmp{i}", [128, 512], F32) for i in range(2)]; t_tmp = [Tok(), Tok()]
            rs = sb("c_rs", [128, 512], F32); t_rs = Tok()
            ga = [sb(f"c_ga{i}", [128, 512], F32) for i in range(2)]; t_ga = [Tok(), Tok()]
            gq = [sb(f"c_gq{i}", [128, 512], F32) for i in range(2)]; t_gq = [Tok(), Tok()]
            gz = [sb(f"c_gz{i}", [128, 512], F32) for i in range(2)]; t_gz = [Tok(), Tok()]
            gy = [sb(f"c_gy{i}", [128, 512], F32) for i in range(2)]; t_gy = [Tok(), Tok()]
            xo = [sb(f"c_xo{i}", [128, 512], F32) for i in range(2)]; t_xo = [Tok(), Tok()]
            SVl = self.SV[li]
            MOD = self.MOD[li]
            wg = 0
            cnt = 0
            for (w0, n, left, right, col) in wins:
                a0 = w0 - (1 if left else 0)
                N = n + (1 if left else 0) + (1 if right else 0)
                lo = 1 if left else 0
                kb.dma("sp", ab[:, :, :N], self.OT[:, a0:a0 + N].rearrange("(kc p) n -> p kc n", p=128), reads=[self.t_OT], writes=[t_ab])
                kb.dma("sp", xt[:, :, :N], XT[:, a0:a0 + N].rearrange("(kc p) n -> p kc n", p=128), reads=[t_XT], writes=[t_xt])
                for ng in range(4):
                    b = wg % 3
                    wg += 1
                    wv = self.wview(wt[b], KC, 512)
                    kb.dma("sp", wv, w["wo_b"][:, ng * 512:(ng + 1) * 512].rearrange("(kc p) n -> p kc n", p=128), reads=[self.t_Wl[li]], writes=[t_wt[b]])
                    for j in range(4):
                        c = ng * 4 + j
                        pi = cnt % 2
                        cnt += 1
                        for kc in range(KC):
                            kb.op("pe", lambda e: e.matmul(self.ps[pi][:, :N], lhsT=wv[:, kc, j * 128:(j + 1) * 128], rhs=ab[:, kc, :N],
                                                           start=(kc == 0), stop=(kc == KC - 1)),
                                  reads=[t_wt[b], t_ab], writes=[self.t_ps[pi]], signal=(kc == KC - 1))
                        kb.op("dve", lambda e: e.scalar_tensor_tensor(xt[:, c, :N], self.ps[pi][:, :N], MOD[:, 32 + c, col:col + 1], xt[:, c, :N],
                                                                      ALU.mult, ALU.add),
                              reads=[self.t_ps[pi], t_xt, self.t_mod[li]], writes=[t_xt])
                self.norm_mod(pes, xt, t_xt, N, col, self.SCA2[li], 48, li, ab, t_ab, sq, t_sq, tmp, t_tmp, 7, rs, t_rs)
                for cg in range(FC // 4):
                    bg = wg % 3
                    wg += 1
                    bu = wg % 3
                    wg += 1
                    wvg = self.wview(wt[bg], KC, 512)
                    wvu = self.wview(wt[bu], KC, 512)
                    kb.dma("sp", wvg, w["win_b"][:, cg * 512:(cg + 1) * 512].rearrange("(kc p) n -> p kc n", p=128), reads=[self.t_Wl[li]], writes=[t_wt[bg]])
                    kb.dma("sp", wvu, w["win_b"][:, DFF + cg * 512:DFF + (cg + 1) * 512].rearrange("(kc p) n -> p kc n", p=128),
                           reads=[self.t_Wl[li]], writes=[t_wt[bu]])
                    for j in range(4):
                        c = cg * 4 + j
                        eb = cnt % 2
                        cnt += 1
                        pg, tpg = self.ps[2 + eb], self.t_ps[2 + eb]
                        pu, tpu = self.ps[4 + eb], self.t_ps[4 + eb]
                        for kc in range(KC):
                            kb.op("pe", lambda e: e.matmul(pg[:, :N], lhsT=wvg[:, kc, j * 128:(j + 1) * 128], rhs=ab[:, kc, :N],
                                                           start=(kc == 0), stop=(kc == KC - 1)),
                                  reads=[t_wt[bg], t_ab], writes=[tpg], signal=(kc == KC - 1))
                        for kc in range(KC):
                            kb.op("pe", lambda e: e.matmul(pu[:, :N], lhsT=wvu[:, kc, j * 128:(j + 1) * 128], rhs=ab[:, kc, :N],
                                                           start=(kc == 0), stop=(kc == KC - 1)),
                                  reads=[t_wt[bu], t_ab], writes=[tpu], signal=(kc == KC - 1))
                        cw = lambda jj: SVl[:, SV_CW + jj * FC + c:SV_CW + jj * FC + c + 1]
                        a_, ta_ = ga[eb], t_ga[eb]
                        kb.op("act", lambda e: e.activation(out=a_[:, :n], in_=pg[:, lo:lo + n], func=AF.Identity, scale=cw(1),
                                                            bias=SVl[:, SV_CB + c:SV_CB + c + 1]),
                              reads=[tpg, self.t_mod[li]], writes=[ta_])
                        if left:
                            kb.op("dve", lambda e: e.scalar_tensor_tensor(a_[:, :n], pg[:, lo - 1:lo - 1 + n], cw(0), a_[:, :n], ALU.mult, ALU.add),
                                  reads=[tpg, ta_, self.t_mod[li]], writes=[ta_])
                        else:
                            kb.op("dve", lambda e: e.scalar_tensor_tensor(a_[:, 1:n], pg[:, lo:lo + n - 1], cw(0), a_[:, 1:n], ALU.mult, ALU.add),
                                  reads=[tpg, ta_, self.t_mod[li]], writes=[ta_])
                        if right:
                            kb.op("dve", lambda e: e.scalar_tensor_tensor(a_[:, :n], pg[:, lo + 1:lo + 1 + n], cw(2), a_[:, :n], ALU.mult, ALU.add),
                                  reads=[tpg, ta_, self.t_mod[li]], writes=[ta_])
                        else:
                            kb.op("dve", lambda e: e.scalar_tensor_tensor(a_[:, :n - 1], pg[:, lo + 1:lo + n], cw(2), a_[:, :n - 1], ALU.mult, ALU.add),
                                  reads=[tpg, ta_, self.t_mod[li]], writes=[ta_])
                        kb.op("act", lambda e: e.activation(out=gq[eb][:, :n], in_=a_[:, :n], func=AF.Square), reads=[ta_], writes=[t_gq[eb]])
                        kb.op("dve", lambda e: e.tensor_scalar(gq[eb][:, :n], gq[eb][:, :n], 0.044715 * GK, GK, ALU.mult, ALU.add),
                              reads=[t_gq[eb]], writes=[t_gq[eb]])
                        kb.op("pool", lambda e: e.tensor_tensor(gz[eb][:, :n], gq[eb][:, :n], a_[:, :n], ALU.mult), reads=[t_gq[eb], ta_], writes=[t_gz[eb]])
                        kb.op("act", lambda e: e.activation(out=gz[eb][:, :n], in_=gz[eb][:, :n], func=AF.Sigmoid), reads=[t_gz[eb]], writes=[t_gz[eb]])
                        kb.op("pool", lambda e: e.tensor_tensor(gy[eb][:, :n], gz[eb][:, :n], a_[:, :n], ALU.mult), reads=[t_gz[eb], ta_], writes=[t_gy[eb]])
                        kb.op("dve", lambda e: e.tensor_tensor(act[:, c, :n], gy[eb][:, :n], pu[:, lo:lo + n], ALU.mult),
                              reads=[t_gy[eb], tpu], writes=[t_act])
                for ng in range(8):
                    b = wg % 3
                    wg += 1
                    wv = self.wview(wt[b], FC, 256)
                    kb.dma("sp", wv, w["wout_b"][:, ng * 256:(ng + 1) * 256].rearrange("(kc p) n -> p kc n", p=128), reads=[self.t_Wl[li]], writes=[t_wt[b]])
                    for j in range(2):
                        c = ng * 2 + j
                        pi = cnt % 2
                        ob = cnt % 2
                        cnt += 1
                        for kc in range(FC):
                            kb.op("pe", lambda e: e.matmul(self.ps[pi][:, :n], lhsT=wv[:, kc, j * 128:(j + 1) * 128], rhs=act[:, kc, :n],
                                                           start=(kc == 0), stop=(kc == FC - 1)),
                                  reads=[t_wt[b], t_act], writes=[self.t_ps[pi]], signal=(kc == FC - 1))
                        kb.op("dve", lambda e: e.scalar_tensor_tensor(xo[ob][:, :n], self.ps[pi][:, :n], MOD[:, 80 + c, col:col + 1], xt[:, c, lo:lo + n],
                                                                      ALU.mult, ALU.add),
                              reads=[self.t_ps[pi], t_xt, self.t_mod[li]], writes=[t_xo[ob]])
                        if final:
                            if col == 0:
                                kb.dma("pool", self.out[c * 128:(c + 1) * 128, w0:w0 + n], xo[ob][:, :n], reads=[t_xo[ob]])
                        else:
                            kb.dma("pool", XO[c * 128:(c + 1) * 128, w0:w0 + n], xo[ob][:, :n], reads=[t_xo[ob]], writes=[t_XO])
            kb.barrier()

    def phaseA_ml(self, li, xi):
        kb, nc = self.kb, self.nc
        S, T = self.S, self.T
        w = self.W[li]
        XT, t_XT = self.XT[xi], self.t_XT[xi]
        tiles = [(t0, min(512, S - t0), 0) for t0 in range(0, S, 512)] + [(S, CTX, 1)]
        with ExitStack() as pes:
            sb = lambda name, shape, dt: pes.enter_context(nc.sbuf_tensor(self.uname(name), list(shape), dt))
            xt = sb("a_xt", [128, KC, 512], F32); t_xt = Tok()
            ht = sb("a_ht", [128, KC, 512], BF16); t_ht = Tok()
            sq = [sb(f"a_sq{i}", [128, 512], BF16) for i in range(2)]; t_sq = [Tok(), Tok()]
            tmp = [sb(f"a_tmp{i}", [128, 512], F32) for i in range(2)]; t_tmp = [Tok(), Tok()]
            rs = sb("a_rs", [128, 512], F32); t_rs = Tok()
            wt = [sb(f"a_w{i}", [128, KC * 512], BF16) for i in range(3)]; t_wt = [Tok(), Tok(), Tok()]
            qo = [sb(f"a_qo{i}", [128, 512], BF16) for i in range(2)]; t_qo = [Tok(), Tok()]
            vt = [sb(f"a_vt{i}", [128, 512], BF16) for i in range(2)]; t_vt = [Tok(), Tok()]
            gt = [sb(f"a_gt{i}", [128, 16], F32) for i in range(2)]; t_gt = [Tok(), Tok()]
            ge = [sb(f"a_ge{i}", [128, 8], F32) for i in range(2)]; t_ge = [Tok(), Tok()]
            SVl = self.SV[li]
            wg = 0
            cnt = 0
            for (t0, N, col) in tiles:
                kb.dma("sp", xt[:, :, :N], XT[:, t0:t0 + N].rearrange("(kc p) n -> p kc n", p=128), reads=[t_XT], writes=[t_xt])
                self.norm_mod(pes, xt, t_xt, N, col, self.SCA1[li], 0, li, ht, t_ht, sq, t_sq, tmp, t_tmp, 7, rs, t_rs)
                for cg in range(4):
                    b = wg % 3
                    wg += 1
                    wv = self.wview(wt[b], KC, 512)
                    kb.dma("sp", wv, w["mix_b"][:, cg * 512:(cg + 1) * 512].rearrange("(kc p) n -> p kc n", p=128), reads=[self.t_Wl[li]], writes=[t_wt[b]])
                    for j in range(4):
                        c = cg * 4 + j
                        pi = cnt % 2
                        ob = cnt % 2
                        cnt += 1
                        for kc in range(KC):
                            kb.op("pe", lambda e: e.matmul(self.ps[pi][:, :N], lhsT=wv[:, kc, j * 128:(j + 1) * 128], rhs=ht[:, kc, :N],
                                                           start=(kc == 0), stop=(kc == KC - 1)),
                                  reads=[t_wt[b], t_ht], writes=[self.t_ps[pi]], signal=(kc == KC - 1))
                        isq = c < 8
                        kb.op("act", lambda e: e.activation(out=qo[ob][:, :N], in_=self.ps[pi][:, :N], func=AF.Copy, scale=(1.0 if isq else 0.0625)),
                              reads=[self.t_ps[pi]], writes=[t_qo[ob]])
                        dstT, t_dst = (self.QT, self.t_QT) if isq else (self.KT, self.t_KT)
                        row0 = (c if isq else c - 8) * 128
                        kb.dma("pool", dstT[row0:row0 + 128, t0:t0 + N], qo[ob][:, :N], reads=[t_qo[ob]], writes=[t_dst])
                groups = [(1024 + g * 512, 512, "k", g * 512) for g in range(2)] + [(2048 + g * 512, 512, "v", g * 512) for g in range(4)] \
                    + [(4096 + g * 512, 512, "o", g * 512) for g in range(4)] + [(6144, 16, "g", 0)]
                for (c0, wd, what, d0) in groups:
                    b = wg % 3
                    wg += 1
                    wv = self.wview(wt[b], KC, 512)
                    kb.dma("sp", wv[:, :, :wd], w["mix_b"][:, c0:c0 + wd].rearrange("(kc p) n -> p kc n", p=128), reads=[self.t_Wl[li]], writes=[t_wt[b]])
                    for tb in range(N // 128):
                        pi = 2 + cnt % 2
                        vb = cnt % 2
                        cnt += 1
                        r0 = t0 + tb * 128
                        for kc in range(KC):
                            kb.op("pe", lambda e: e.matmul(self.ps[pi][:, :wd], lhsT=ht[:, kc, tb * 128:(tb + 1) * 128], rhs=wv[:, kc, :wd],
                                                           start=(kc == 0), stop=(kc == KC - 1)),
                                  reads=[t_wt[b], t_ht], writes=[self.t_ps[pi]], signal=(kc == KC - 1))
                        if what == "g":
                            g_, tg_ = gt[vb], t_gt[vb]
                            kb.op("dve", lambda e: e.tensor_tensor(g_[:, :], self.ps[pi][:, :16], SVl[:, SV_SINK:SV_SINK + 16], ALU.add),
                                  reads=[self.t_ps[pi], self.t_mod[li]], writes=[tg_])
                            e_, te_ = ge[vb], t_ge[vb]
                            for (src, dst) in ((4, 0), (12, 4)):
                                kb.op("act", lambda e: e.activation(out=e_[:, dst:dst + 4], in_=g_[:, src:src + 4], func=AF.Exp, scale=-1.0),
                                      reads=[tg_], writes=[te_])
                            kb.op("dve", lambda e: e.tensor_scalar_add(e_[:, :], e_[:, :], 1.0), reads=[te_], writes=[te_])
                            kb.op("act", lambda e: e.activation(out=e_[:, :], in_=e_[:, :], func=AF.Ln), reads=[te_], writes=[te_])
                            for (src, dst) in ((0, 4), (4, 12)):
                                kb.op("dve", lambda e: e.tensor_scalar_mul(g_[:, dst:dst + 4], e_[:, src:src + 4], -1.0), reads=[te_, tg_], writes=[tg_])
                            kb.dma("pool", self.G[r0:r0 + 128, :], g_[:, :], reads=[tg_], writes=[self.t_G])
                        else:
                            kb.op("act", lambda e: e.activation(out=vt[vb][:, :wd], in_=self.ps[pi][:, :wd], func=AF.Copy,
                                                                scale=(0.0625 if what == "k" else 1.0)),
                                  reads=[self.t_ps[pi]], writes=[t_vt[vb]])
                            dst, t_dst = {"k": (self.Ktok, self.t_Ktok), "v": (self.V, self.t_V), "o": (self.Og, self.t_Og)}[what]
                            kb.dma("pool", dst[r0:r0 + 128, d0:d0 + wd], vt[vb][:, :wd], reads=[t_vt[vb]], writes=[t_dst])
            kb.barrier()

    def phaseB_ml(self, li):
        kb, nc = self.kb, self.nc
        S, T = self.S, self.T
        L = 64
        nxc, ncc = S // L, CTX // L
        with ExitStack() as pes:
            sb = lambda name, shape, dt: pes.enter_context(nc.sbuf_tensor(self.uname(name), list(shape), dt))
            identf = sb("m_identf", [128, 128], F32)
            onesf = sb("m_onesf", [128, 128], F32)
            tri = sb("m_tri", [64, 128], F32)
            t_c = Tok()
            kb.dma("sp", identf[:], self.in_identf[:, :], writes=[t_c])
            kb.dma("sp", onesf[:], self.in_onesf[:, :], writes=[t_c])
            kb.dma("sp", tri[:], self.in_tri[:, :], writes=[t_c])
            C = [[sb(f"m_C{d}{h}", [128, 2, 512], F32) for h in range(4)] for d in range(2)]
            Cb = [[sb(f"m_Cb{d}{h}", [128, 2, 512], BF16) for h in range(4)] for d in range(2)]
            nn = [[sb(f"m_n{d}{h}", [128, 2], F32) for h in range(4)] for d in range(2)]
            nb = [[sb(f"m_nb{d}{h}", [128, 2], BF16) for h in range(4)] for d in range(2)]
            t_C = [[Tok() for h in range(4)] for d in range(2)]
            t_Cb = [[Tok() for h in range(4)] for d in range(2)]
            t_n = [[Tok() for h in range(4)] for d in range(2)]
            t_nb = [[Tok() for h in range(4)] for d in range(2)]
            for d in range(2):
                for h in range(4):
                    kb.op("pool", lambda e: e.memset(C[d][h][:], 0.0), writes=[t_C[d][h]])
                    kb.op("pool", lambda e: e.memset(Cb[d][h][:], 0.0), writes=[t_Cb[d][h]])
                    kb.op("pool", lambda e: e.memset(nn[d][h][:], 0.0), writes=[t_n[d][h]])
                    kb.op("pool", lambda e: e.memset(nb[d][h][:], 0.0), writes=[t_nb[d][h]])
            NB = 2
            D2 = range(2)
            mk = lambda name, shape, dt, n: [sb(f"{name}_{i}", shape, dt) for i in range(n)]
            tk = lambda n: [Tok() for _ in range(n)]
            qT = mk("m_qT", [128, 8, L], BF16, NB); t_qT = tk(NB)
            kT = mk("m_kT", [128, 8, L], BF16, NB); t_kT = tk(NB)
            kk = mk("m_kk", [L, 1024], BF16, NB); t_kk = tk(NB)
            vv = mk("m_vv", [L, 2048], BF16, NB); t_vv = tk(NB)
            gg = mk("m_gg", [L, 16], F32, NB); t_gg = tk(NB)
            bb = mk("m_b", [L, 4], F32, 2); t_bb = tk(2)
            wprev = mk("m_wprev", [L, 4], F32, 2); t_wprev = tk(2)
            lmb = mk("m_lmb", [L, 4], F32, 2); t_lmb = tk(2)
            ws = mk("m_ws", [L, 4], F32, 2); t_ws = tk(2)
            decay = mk("m_decay", [128, 4], F32, 2); t_decay = tk(2)
            diagb = mk("m_diagb", [L, L], F32, 4); t_diagb = tk(4)
            Eh = mk("m_E", [L, L], F32, 4); t_Eh = tk(4)
            WT = mk("m_WT", [L, L], BF16, 4); t_WT = tk(4)
            n1 = mk("m_n1", [L, 512], F32, 4); t_n1 = tk(4)
            hh = mk("m_hh", [L, 512], F32, 4); t_hh = tk(4)
            den = mk("m_den", [L, 2], F32, 4); t_den = tk(4)
            kp = mk("m_kp", [L, 256], BF16, 4); t_kp = tk(4)
            ps, tps = self.ps, self.t_ps
            pA, tA = ps[0], tps[0]
            pB, tB = ps[1], tps[1]
            orders = [[S + c * L for c in range(ncc)] + [c * L for c in range(nxc)],
                      [S + c * L for c in reversed(range(ncc))] + [c * L for c in reversed(range(nxc))]]
            cidx = 0
            for step in range(ncc + nxc):
                for d in D2:
                    tk0 = orders[d][step]
                    lic, lfc = (0, 4) if d == 0 else (8, 12)
                    trid = tri[:, 0:64] if d == 0 else tri[:, 64:128]
                    cb = cidx % NB
                    cidx += 1
                    qT_, kT_, kk_, vv_, G_ = qT[cb], kT[cb], kk[cb], vv[cb], gg[cb]
                    tq, tkT, tkk, tvv, tgg = t_qT[cb], t_kT[cb], t_kk[cb], t_vv[cb], t_gg[cb]
                    kb.dma("sp", qT_[:], self.QT[0:1024, tk0:tk0 + L].rearrange("(c p) n -> p c n", p=128), reads=[self.t_QT], writes=[tq])
                    kb.dma("sp", kT_[:], self.KT[0:1024, tk0:tk0 + L].rearrange("(c p) n -> p c n", p=128), reads=[self.t_KT], writes=[tkT])
                    kb.dma("sp", kk_[:], self.Ktok[tk0:tk0 + L, :], reads=[self.t_Ktok], writes=[tkk])
                    kb.dma("sp", vv_[:], self.V[tk0:tk0 + L, :], reads=[self.t_V], writes=[tvv])
                    kb.dma("sp", G_[:], self.G[tk0:tk0 + L, :], reads=[self.t_G], writes=[tgg])
                    bb_, wprev_, lmb_, ws_, decay_ = bb[cb], wprev[cb], lmb[cb], ws[cb], decay[cb]
                    tbb, twp, tlmb, tws, tdec = t_bb[cb], t_wprev[cb], t_lmb[cb], t_ws[cb], t_decay[cb]
                    H4 = range(4)
                    kb.op("pe", lambda e: e.matmul(pA[0:L, 0:4], lhsT=trid, rhs=G_[:, lfc:lfc + 4], start=True, stop=True),
                          reads=[t_c, tgg], writes=[tA])
                    kb.op("pe", lambda e: e.matmul(pA[:, 16:20], lhsT=onesf[0:L, :], rhs=G_[:, lfc:lfc + 4], start=True, stop=True),
                          reads=[t_c, tgg], writes=[tA])
                    kb.op("dve", lambda e: e.tensor_copy(bb_[:, :], pA[0:L, 0:4]), reads=[tA], writes=[tbb])
                    kb.op("act", lambda e: e.activation(out=wprev_[:, :], in_=pA[0:L, 0:4], func=AF.Exp), reads=[tA], writes=[twp])
                    kb.op("act", lambda e: e.activation(out=decay_[:, :], in_=pA[:, 16:20], func=AF.Exp), reads=[tA], writes=[tdec])
                    kb.op("dve", lambda e: e.tensor_tensor(lmb_[:, :], G_[:, lic:lic + 4], bb_[:, :], ALU.subtract), reads=[tgg, tbb], writes=[tlmb])
                    kb.op("dve", lambda e: e.tensor_tensor(ws_[:, :], lmb_[:, :], pA[0:L, 16:20], ALU.add), reads=[tlmb, tA], writes=[tws])
                    kb.op("act", lambda e: e.activation(out=ws_[:, :], in_=ws_[:, :], func=AF.Exp), reads=[tws], writes=[tws])
                    for h in H4:
                        kb.op("dve", lambda e: e.tensor_scalar_mul(diagb[h][:, :], identf[0:L, 0:L], bb_[:, h:h + 1]), reads=[t_c, tbb], writes=[t_diagb[h]])
                    for h in H4:
                        kb.op("pe", lambda e: e.matmul(pA[0:L, 64 + 64 * h:128 + 64 * h], lhsT=onesf[0:L, 0:L], rhs=diagb[h][:, :], start=True, stop=True),
                              reads=[t_c, t_diagb[h]], writes=[tA])
                    for h in H4:
                        for i in range(2):
                            kb.op("pe", lambda e: e.matmul(pB[0:L, 64 * h:64 * h + 64], lhsT=kT_[:, 2 * h + i, :], rhs=qT_[:, 2 * h + i, :], start=(i == 0), stop=(i == 1)),
                                  reads=[tkT, tq], writes=[tB], signal=(i == 1))
                    for h in H4:
                        kb.op("dve", lambda e: e.tensor_scalar(Eh[h][:, :], pA[0:L, 64 + 64 * h:128 + 64 * h], lmb_[:, h:h + 1], 60.0, ALU.add, ALU.min),
                              reads=[tA, tlmb], writes=[t_Eh[h]])
                    for h in H4:
                        kb.op("act", lambda e: e.activation(out=Eh[h][:, :], in_=Eh[h][:, :], func=AF.Exp), reads=[t_Eh[h]], writes=[t_Eh[h]])
                    for h in H4:
                        kb.op("dve", lambda e: e.tensor_tensor(Eh[h][:, :], Eh[h][:, :], trid, ALU.mult), reads=[t_Eh[h], t_c], writes=[t_Eh[h]])
                    for h in H4:
                        kb.op("dve", lambda e: e.tensor_tensor(WT[h][:, :], Eh[h][:, :], pB[0:L, 64 * h:64 * h + 64], ALU.mult), reads=[t_Eh[h], tB], writes=[t_WT[h]])
                    for h in H4:
                        kb.op("act", lambda e: e.activation(out=kp[h][:, :], in_=kk_[:, h * 256:(h + 1) * 256], func=AF.Copy, scale=ws_[:, h:h + 1]),
                              reads=[tkk, tws], writes=[t_kp[h]])
                    for h in H4:
                        kb.op("pe", lambda e: e.matmul(pA[0:L, 320 + h:321 + h], lhsT=WT[h][:, :], rhs=self.ones[0:L, 0:1], start=True, stop=True),
                              reads=[t_WT[h], self.t_const], writes=[tA])
                        for i in range(2):
                            kb.op("pe", lambda e: e.matmul(pA[0:L, 328 + h:329 + h], lhsT=qT_[:, 2 * h + i, :], rhs=nb[d][h][:, i:i + 1], start=(i == 0), stop=(i == 1)),
                                  reads=[tq, t_nb[d][h]], writes=[tA], signal=(i == 1))
                        for i in range(2):
                            kb.op("pe", lambda e: e.matmul(pA[:, 336 + 2 * h + i:337 + 2 * h + i], lhsT=kp[h][:, i * 128:(i + 1) * 128], rhs=self.ones[0:L, 0:1],
                                                           start=True, stop=True),
                                  reads=[t_kp[h], self.t_const], writes=[tA])
                    for h in H4:
                        kb.op("act", lambda e: e.activation(out=den[h][:, 0:1], in_=pA[0:L, 320 + h:321 + h], func=AF.Copy), reads=[tA], writes=[t_den[h]])
                    for h in H4:
                        kb.op("dve", lambda e: e.scalar_tensor_tensor(den[h][:, 1:2], pA[0:L, 328 + h:329 + h], wprev_[:, h:h + 1], den[h][:, 0:1], ALU.mult, ALU.add),
                              reads=[tA, twp, t_den[h]], writes=[t_den[h]])
                    for h in H4:
                        for i in range(2):
                            kb.op("dve", lambda e: e.scalar_tensor_tensor(nn[d][h][:, i:i + 1], nn[d][h][:, i:i + 1], decay_[:, h:h + 1],
                                                                          pA[:, 336 + 2 * h + i:337 + 2 * h + i], ALU.mult, ALU.add),
                                  reads=[t_n[d][h], tdec, tA], writes=[t_n[d][h]])
                    for h in H4:
                        kb.op("act", lambda e: e.activation(out=den[h][:, 1:2], in_=den[h][:, 1:2], func=AF.Abs), reads=[t_den[h]], writes=[t_den[h]])
                    for h in H4:
                        kb.op("dve", lambda e: e.tensor_scalar_max(den[h][:, 1:2], den[h][:, 1:2], 1.0), reads=[t_den[h]], writes=[t_den[h]])
                    for h in H4:
                        kb.op("dve", lambda e: e.reciprocal(den[h][:, 1:2], den[h][:, 1:2]), reads=[t_den[h]], writes=[t_den[h]])
                    for h in H4:
                        vh = vv_[:, h * 512:(h + 1) * 512]
                        pN1, tN1 = ps[2 + h % 2], tps[2 + h % 2]
                        pN2, tN2 = ps[4 + h % 2], tps[4 + h % 2]
                        kb.op("pe", lambda e: e.matmul(pN1[0:L, :], lhsT=WT[h][:, :], rhs=vh, start=True, stop=True),
                              reads=[t_WT[h], tvv], writes=[tN1])
                        for i in range(2):
                            kb.op("pe", lambda e: e.matmul(pN2[0:L, :], lhsT=qT_[:, 2 * h + i, :], rhs=Cb[d][h][:, i, :], start=(i == 0), stop=(i == 1)),
                                  reads=[tq, t_Cb[d][h]], writes=[tN2], signal=(i == 1))
                        kb.op("act", lambda e: e.activation(out=n1[h][:, :], in_=pN1[0:L, :], func=AF.Copy), reads=[tN1], writes=[t_n1[h]])
                        kb.op("dve", lambda e: e.scalar_tensor_tensor(hh[h][:, :], pN2[0:L, :], wprev_[:, h:h + 1], n1[h][:, :], ALU.mult, ALU.add),
                              reads=[tN2, twp, t_n1[h]], writes=[t_hh[h]])
                        kb.op("act", lambda e: e.activation(out=hh[h][:, :], in_=hh[h][:, :], func=AF.Copy, scale=den[h][:, 1:2]), reads=[t_hh[h], t_den[h]], writes=[t_hh[h]])
                        kb.dma("act", self.H[d][tk0:tk0 + L, h * 512:(h + 1) * 512], hh[h][:, :], reads=[t_hh[h]], writes=[self.t_H[d]])
                    for h in H4:
                        vh = vv_[:, h * 512:(h + 1) * 512]
                        for i in range(2):
                            pDC, tDC = ps[6 + i], tps[6 + i]
                            kb.op("pe", lambda e: e.matmul(pDC[:, :], lhsT=kp[h][:, i * 128:(i + 1) * 128], rhs=vh, start=True, stop=True),
                                  reads=[t_kp[h], tvv], writes=[tDC])
                            kb.op("dve", lambda e: e.scalar_tensor_tensor(C[d][h][:, i, :], C[d][h][:, i, :], decay_[:, h:h + 1], pDC[:, :], ALU.mult, ALU.add),
                                  reads=[t_C[d][h], tdec, tDC], writes=[t_C[d][h]])
                            kb.op("act", lambda e: e.activation(out=Cb[d][h][:, i, :], in_=C[d][h][:, i, :], func=AF.Copy), reads=[t_C[d][h]], writes=[t_Cb[d][h]])
                        kb.op("act", lambda e: e.activation(out=nb[d][h][:, :], in_=nn[d][h][:, :], func=AF.Copy), reads=[t_n[d][h]], writes=[t_nb[d][h]])
            kb.barrier()

    def phaseR_ml(self, li):
        kb, nc = self.kb, self.nc
        S, T = self.S, self.T
        w = self.W[li]
        with ExitStack() as pes:
            sb = lambda name, shape, dt: pes.enter_context(nc.sbuf_tensor(self.uname(name), list(shape), dt))
            hg = sb("r_hg", [128, D], F32); t_hg = Tok()
            kb.dma("sp", hg[:], w["headg"][:, :], writes=[t_hg])
            hf = [sb(f"r_hf{i}", [128, D], F32) for i in range(2)]; t_hf = [Tok(), Tok()]
            hb_ = [sb(f"r_hb{i}", [128, D], F32) for i in range(2)]; t_hb = [Tok(), Tok()]
            og = [sb(f"r_og{i}", [128, D], BF16) for i in range(2)]; t_og = [Tok(), Tok()]
            sgm = sb("r_sg", [128, D], F32); t_sg = Tok()
            sqb = sb("r_sq", [128, D], F32); t_sqb = Tok()
            ss = sb("r_ss", [128, 4], F32); t_ss = Tok()
            y = sb("r_y", [128, D], BF16); t_y = Tok()
            ot = [sb(f"r_ot{i}", [128, 512], BF16) for i in range(2)]; t_ot = [Tok(), Tok()]
            cnt = 0
            for bi in range(T // 128):
                r0 = bi * 128
                b = bi % 2
                kb.dma("sp", hf[b][:], self.H[0][r0:r0 + 128, :], reads=[self.t_H[0]], writes=[t_hf[b]])
                kb.dma("sp", hb_[b][:], self.H[1][r0:r0 + 128, :], reads=[self.t_H[1]], writes=[t_hb[b]])
                kb.dma("sp", og[b][:], self.Og[r0:r0 + 128, :], reads=[self.t_Og], writes=[t_og[b]])
                kb.op("dve", lambda e: e.tensor_tensor(hf[b][:], hf[b][:], hb_[b][:], ALU.add), reads=[t_hf[b], t_hb[b]], writes=[t_hf[b]])
                kb.op("act", lambda e: e.activation(out=sqb[:], in_=hf[b][:], func=AF.Square), reads=[t_hf[b]], writes=[t_sqb])
                kb.op("dve", lambda e: e.reduce_sum(ss[:, :], sqb[:].rearrange("p (h n) -> p h n", h=4), mybir.AxisListType.X), reads=[t_sqb], writes=[t_ss])
                kb.op("dve", lambda e: e.tensor_scalar(ss[:, :], ss[:, :], 1.0 / 512, EPS, ALU.mult, ALU.add), reads=[t_ss], writes=[t_ss])
                kb.op("act", lambda e: e.activation(out=ss[:, :], in_=ss[:, :], func=AF.Sqrt), reads=[t_ss], writes=[t_ss])
                kb.op("dve", lambda e: e.reciprocal(ss[:, :], ss[:, :]), reads=[t_ss], writes=[t_ss])
                kb.op("act", lambda e: e.activation(out=sgm[:], in_=og[b][:], func=AF.Sigmoid), reads=[t_og[b]], writes=[t_sg])
                kb.op("pool", lambda e: e.tensor_tensor(sgm[:], sgm[:], hg[:], ALU.mult), reads=[t_sg, t_hg], writes=[t_sg])
                for h in range(4):
                    kb.op("dve", lambda e: e.scalar_tensor_tensor(y[:, h * 512:(h + 1) * 512], hf[b][:, h * 512:(h + 1) * 512], ss[:, h:h + 1],
                                                                  sgm[:, h * 512:(h + 1) * 512], ALU.mult, ALU.mult),
                          reads=[t_hf[b], t_ss, t_sg], writes=[t_y])
                for g4 in range(4):
                    pi = cnt % 2
                    ob = cnt % 2
                    cnt += 1
                    for jj in range(4):
                        j = g4 * 4 + jj
                        kb.op("pe", lambda e: e.matmul(self.ps[pi][:, jj * 128:(jj + 1) * 128], lhsT=y[:, j * 128:(j + 1) * 128], rhs=self.ident[:, :],
                                                       start=True, stop=True),
                              reads=[t_y, self.t_const], writes=[self.t_ps[pi]], signal=(jj == 3))
                    kb.op("act", lambda e: e.activation(out=ot[ob][:, :], in_=self.ps[pi][:, :], func=AF.Copy), reads=[self.t_ps[pi]], writes=[t_ot[ob]])
                    kb.dma("pool", self.OT[g4 * 512:(g4 + 1) * 512, r0:r0 + 128].rearrange("(jj p) n -> p jj n", p=128),
                           ot[ob][:, :].rearrange("p (jj n) -> p jj n", n=128), reads=[t_ot[ob]], writes=[self.t_OT])
            kb.barrier()

    def build(self):
        self.phase0()
        xi = 0
        for idx, (li, kind, need_ctx) in enumerate(self.layers):
            final = idx == len(self.layers) - 1
            self.kb.flush_bg(self.t_Wl[li])
            if kind == 2:
                self.phaseA_ml(li, xi)
                self.phaseB_ml(li)
                self.phaseR_ml(li)
            else:
                self.phaseA(li, kind, xi)
                self.phaseB(li, kind, need_ctx)
            self.phaseC(li, need_ctx, xi, final)
            xi = 1 - xi
        self.kb.barrier(engines=("sp",))
        self.es.close()
        return self.nc


def na_base_block(ci, SB):
    return min(max(4 * ci - 2, 0), SB - 8)


def na_patterns(S):
    rows = S // GW
    kr = min(8, rows)
    SB = S // 128
    pats, pat_of_C, keymap = [], [], {}
    for ci in range(S // 512):
        bbw = na_base_block(ci, SB)
        r0s = tuple(min(max(8 * ci + a - kr // 2, 0), rows - kr) - 8 * ci for a in range(8))
        key = (bbw - 4 * ci, r0s)
        if key not in keymap:
            keymap[key] = len(pats)
            pats.append(ci)
        pat_of_C.append(keymap[key])
    return pat_of_C, pats


def build_na_table(rel_bias, S):
    rows = S // GW
    kr = min(8, rows)
    SB = S // 128
    pat_of_C, pats = na_patterns(S)
    npat = len(pats)
    rb = np.asarray(rel_bias, np.float32)
    tab = np.full((npat, 16, 128, 8, 512), -30000.0, np.float32)
    i = np.arange(512)
    qc = i % GW
    c0 = np.clip(qc - 8, 0, GW - 16)
    j = np.arange(128)
    for p, ci in enumerate(pats):
        bbw = na_base_block(ci, SB)
        qr = 8 * ci + i // GW
        r0 = np.clip(qr - kr // 2, 0, rows - kr)
        for jb in range(8):
            kr_ = 2 * (bbw + jb) + j // GW
            kc_ = j % GW
            ok = ((kr_[:, None] >= r0[None, :]) & (kr_[:, None] < r0[None, :] + kr)
                  & (kc_[:, None] >= c0[None, :]) & (kc_[:, None] < c0[None, :] + 16))
            drow = np.clip(kr_[:, None] - qr[None, :] + 7, 0, 14)
            dcol = np.clip(kc_[:, None] - qc[None, :] + 15, 0, 30)
            vals = rb[:, drow, dcol]
            tab[p, :, :, jb, :] = np.where(ok[None], vals, np.float32(-30000.0))
    return tab.reshape(npat * 16 * 128, 8 * 512)


def swa_mask_table():
    j = np.arange(128)
    prev = (j[:, None] >= j[None, :]).astype(np.float32)
    nxt = (j[:, None] <= j[None, :]).astype(np.float32)
    t = np.zeros((128, 6, 4, 128), np.float32)
    for r in range(6):
        for a in range(4):
            rel = r - 1 - a
            if rel == -1:
                t[:, r, a, :] = prev
            elif rel == 0:
                t[:, r, a, :] = 1.0
            elif rel == 1:
                t[:, r, a, :] = nxt
    return t.reshape(128, 6 * 512).astype(ml_dtypes.bfloat16)


def rope_tables(S):
    t = np.arange(S)
    row = (t // GW).astype(np.float32)
    colp = (t % GW).astype(np.float32)
    inv = (10000.0 ** (-np.arange(32, dtype=np.float32) / 32)).astype(np.float32)
    cosT = np.zeros((128, S), np.float32)
    sinT = np.zeros((128, S), np.float32)
    for a, pos in enumerate((row, colp)):
        ang = (pos[None, :] * inv[:, None]).astype(np.float32)
        for p in range(2):
            cosT[a * 64 + p * 32:a * 64 + (p + 1) * 32] = np.cos(ang)
            sinT[a * 64 + p * 32:a * 64 + (p + 1) * 32] = np.sin(ang)
    rot = np.zeros((128, 128), np.float32)
    for a in range(2):
        for f in range(32):
            d1, d2 = a * 64 + f, a * 64 + 32 + f
            rot[d2, d1] = -1.0
            rot[d1, d2] = 1.0
    return cosT, sinT, rot


def fm(v, ncol):
    return np.ascontiguousarray(np.asarray(v, np.float32).reshape(ncol, 128).T)


def pack_sv(li, kind, P):
    sv = np.zeros((128, NSV), np.float32)
    sv[:, SV_ADAB:SV_ADAB + 96] = fm(P["ada_b"][li], 96)
    sv[:, SV_N1:SV_N1 + 16] = fm(P["norm1_g"][li], 16)
    sv[:, SV_N2:SV_N2 + 16] = fm(P["norm2_g"][li], 16)
    for j in range(3):
        sv[:, SV_CW + j * FC:SV_CW + (j + 1) * FC] = fm(P["ffn_conv_w"][li][j], FC)
    sv[:, SV_CB:SV_CB + FC] = fm(P["ffn_conv_b"][li], FC)
    pre = {0: "na", 1: "swa", 3: "gqa"}.get(kind)
    if pre:
        sv[:, SV_QG] = np.asarray(P[pre + "_q_g"][0], np.float32)
        sv[:, SV_KG] = np.asarray(P[pre + "_k_g"][0], np.float32)
    if kind == 1:
        sv[:, SV_SINK:SV_SINK + 16] = np.asarray(P["swa_sinks"][0], np.float32)[None, :]
    if kind == 2:
        sv[:, SV_SINK:SV_SINK + 16] = np.asarray(P["ml_gate_b"][0], np.float32)[None, :]
    return sv


def core_inputs(b, S, layers, P, consts):
    m = dict(consts)
    m["xT"] = np.ascontiguousarray(np.asarray(P["x"][b, :S], np.float32).T)
    m["cxT"] = np.ascontiguousarray(np.asarray(P["ctx"][b], np.float32).T)
    cc = np.stack([fm(P["c"][b], KC), fm(P["c_ctx"], KC)], axis=-1)
    m["cc"] = np.ascontiguousarray(cc)
    return m


def shared_inputs(S, layers, P):
    cosT, sinT, rot = rope_tables(S)
    bf = ml_dtypes.bfloat16
    j = np.arange(128)
    m = {
        "ones": np.ones((128, 128), bf), "ident": np.eye(128, dtype=np.float32).astype(bf), "rot": rot.astype(bf),
        "cosT": cosT, "sinT": sinT,
        "mprev": (j[:, None] >= j[None, :]).astype(np.float32).astype(bf),
        "mnext": (j[:, None] <= j[None, :]).astype(np.float32).astype(bf),
        "swam": swa_mask_table(),
    }
    if True:
        jj = np.arange(64)
        m["identf"] = np.eye(128, dtype=np.float32)
        m["onesf"] = np.ones((128, 128), np.float32)
        m["tri"] = np.concatenate([(jj[:, None] <= jj[None, :]), (jj[:, None] >= jj[None, :])], axis=1).astype(np.float32)
    mixw = {0: ("na_w_qkv", "na_w_o"), 1: ("swa_w_qkv", "swa_w_o"), 2: ("ml_w_in", "ml_w_o"), 3: ("gqa_w_qkv", "gqa_w_o")}
    for (li, kind, _) in layers:
        m[f"ada_w{li}"] = np.asarray(P["ada_w"][li], np.float32)
        m[f"ffn_w_in{li}"] = np.asarray(P["ffn_w_in"][li], np.float32)
        m[f"ffn_w_out{li}"] = np.asarray(P["ffn_w_out"][li], np.float32)
        m[f"mix_w_in{li}"] = np.asarray(P[mixw[kind][0]][0], np.float32)
        m[f"mix_w_o{li}"] = np.asarray(P[mixw[kind][1]][0], np.float32)
        m[f"sv{li}"] = pack_sv(li, kind, P)
        if kind == 0:
            m[f"natab{li}"] = build_na_table(P["na_rel_bias"][0], S)
        if kind == 2:
            m[f"headg{li}"] = np.ascontiguousarray(np.broadcast_to(np.asarray(P["ml_head_g"][0], np.float32)[None, :], (128, D)))
    return m


def run_model(P, S, layers, batches, trace=False, spread=False):
    prog = Prog(S, layers)
    nc = prog.build()
    shared = shared_inputs(S, layers, P)
    in_maps = [core_inputs(b, S, layers, P, shared) for b in batches]
    slots = list(range(len(batches)))
    if spread and len(batches) == 4:
        big = ("ada_w", "ffn_w_in", "ffn_w_out", "mix_w_in", "mix_w_o", "sv", "natab", "headg")
        zmap = {}
        for k, v in in_maps[0].items():
            if k.startswith(big) or k in ("xT", "cxT", "cc"):
                zmap[k] = np.zeros(v.shape, v.dtype)
            else:
                zmap[k] = v
        slots = [0, 1, 4, 5]
        full = [zmap] * 8
        full = list(full)
        for sl, m in zip(slots, in_maps):
            full[sl] = m
        in_maps = full
    res = run_bass_kernel_spmd(nc, in_maps, core_ids=list(range(len(in_maps))), trace=trace)
    outs = [np.ascontiguousarray(res.results[sl]["outT"].T) for sl in slots]
    return np.stack(outs, 0), res


def kernel(**inputs):
    S = inputs["x"].shape[1]
    layers = [(i, i % 4, i < 3) for i in range(4)]
    out, _ = run_model(inputs, S, layers, list(range(inputs["x"].shape[0])), spread=False)
    return out.astype(np.float32)
```

```python
import math
from contextlib import ExitStack

import ml_dtypes
import numpy as np

import concourse.bass as bass
import concourse.mybir as mybir
from concourse.bass_utils import run_bass_kernel_spmd

F32, BF16 = mybir.dt.float32, mybir.dt.bfloat16
AF = mybir.ActivationFunctionType
ALU = mybir.AluOpType

D = 2048
KC = 16
DFF = 5632
FC = 44
CTX = 256
GW = 64
EPS = 1e-6
NSV = 96 + 16 + 16 + 132 + 44 + 1 + 1 + 1 + 16
SV_ADAB, SV_N1, SV_N2, SV_CW, SV_CB, SV_QG, SV_KG, SV_GB, SV_SINK = 0, 96, 112, 128, 260, 304, 305, 306, 307
GK = 1.5957691216057308
MIX_COLS = {0: 6144, 1: 2560, 2: 6160, 3: 3072}
NKV = {0: 16, 1: 2, 3: 4}


class Tok:
    __slots__ = ("w", "r", "ex")

    def __init__(self, ex=False):
        self.w = {}
        self.r = {}
        self.ex = ex


class KB:
    def __init__(self, nc, es, nring=6):
        self.nc = nc
        self.E = {"pe": nc.tensor, "act": nc.scalar, "dve": nc.vector, "pool": nc.gpsimd, "sp": nc.sync}
        self.csem = {e: es.enter_context(nc.semaphore("c_" + e)) for e in ("pe", "act", "dve", "pool")}
        self.ccnt = {e: 0 for e in self.csem}
        self.NR = nring
        self.dsem = {e: [es.enter_context(nc.semaphore(f"d_{e}{i}")) for i in range(nring)] for e in ("sp", "pool", "act")}
        self.dcnt = {e: [0] * nring for e in self.dsem}
        self.dnext = {e: 0 for e in self.dsem}
        self.waited = {e: {} for e in self.E}
        self.pending = {e: [] for e in self.csem}
        self.ninst = 0
        self.bg = []
        self.bgcnt = 0

    def _wait(self, e, ev):
        if ev is None:
            return
        sem, val, key = ev
        if key == "cpe" and e == "pe":
            return
        w = self.waited[e]
        if w.get(key, 0) >= val:
            return
        self.E[e].wait_ge(sem, val)
        self.ninst += 1
        w[key] = val

    def _deps(self, e, reads, writes):
        for t in reads:
            for ev in list(t.w.values()):
                self._wait(e, ev)
            if t.ex:
                for ev in list(t.r.values()):
                    if ev[2] != "c" + e:
                        self._wait(e, ev)
        for t in writes:
            for ev in list(t.w.values()):
                self._wait(e, ev)
            for ev in list(t.r.values()):
                self._wait(e, ev)

    @staticmethod
    def _commit(ev, reads, writes):
        for t in reads:
            t.r[ev[2]] = ev
        for t in writes:
            t.w[ev[2]] = ev
            t.r = {}

    def op(self, e, fn, reads=(), writes=(), signal=True):
        self._deps(e, reads, writes)
        ins = fn(self.E[e])
        self.ninst += 1
        if signal:
            self.ccnt[e] += 1
            ins.then_inc(self.csem[e], 1)
            ev = (self.csem[e], self.ccnt[e], "c" + e)
            self._commit(ev, reads, writes)
            for (r, w) in self.pending[e]:
                self._commit(ev, r, w)
            self.pending[e] = []
        else:
            self.pending[e].append((tuple(reads), tuple(writes)))
        return ins

    def dma(self, q, out, in_, reads=(), writes=()):
        i = self.dnext[q]
        self.dnext[q] = (i + 1) % self.NR
        sem = self.dsem[q][i]
        key = f"d{q}{i}"
        if self.dcnt[q][i]:
            self._wait(q, (sem, self.dcnt[q][i], key))
        self._deps(q, reads, writes)
        self.E[q].dma_start(out=out, in_=in_).then_inc(sem, 16)
        self.ninst += 1
        self.dcnt[q][i] += 16
        ev = (sem, self.dcnt[q][i], key)
        self._commit(ev, reads, writes)
        if q == "pool" and self.bg and not getattr(self, "_in_bg", False):
            self.bgcnt += 1
            if self.bgcnt % 2 == 0:
                self._in_bg = True
                o, i_, wr = self.bg.pop(0)
                self.dma("pool", o, i_, writes=wr)
                self._in_bg = False

    def flush_bg(self, tok):
        keep = []
        self._in_bg = True
        for (o, i_, wr) in self.bg:
            if tok in wr:
                self.dma("pool", o, i_, writes=wr)
            else:
                keep.append((o, i_, wr))
        self._in_bg = False
        self.bg = keep

    def barrier(self, engines=("pe", "act", "dve", "pool", "sp")):
        for e in engines:
            for q in self.dsem:
                for i in range(self.NR):
                    if self.dcnt[q][i]:
                        self._wait(e, (self.dsem[q][i], self.dcnt[q][i], f"d{q}{i}"))
            for c in self.csem:
                if self.ccnt[c] and c != e:
                    self._wait(e, (self.csem[c], self.ccnt[c], "c" + c))


class Prog:
    def __init__(self, S, layers, final_out=True):
        self.S, self.T = S, S + CTX
        self.layers = layers
        self.nc = nc = bass.Bass("TRN2", target_bir_lowering=False)
        self.es = ExitStack()
        self.kb = KB(nc, self.es)
        T = self.T
        di = lambda name, shape, dt=F32: nc.dram_tensor(name, list(shape), dt, kind="ExternalInput").ap()
        ds = lambda name, shape, dt: nc.dram_tensor(name, list(shape), dt).ap()
        self.in_xT = di("xT", [D, S])
        self.in_cxT = di("cxT", [D, CTX])
        self.in_cc = di("cc", [128, KC, 2])
        self.in_ones = di("ones", [128, 128], BF16)
        self.in_ident = di("ident", [128, 128], BF16)
        self.in_rot = di("rot", [128, 128], BF16)
        self.in_cos = di("cosT", [128, S])
        self.in_sin = di("sinT", [128, S])
        self.in_mprev = di("mprev", [128, 128], BF16)
        self.in_mnext = di("mnext", [128, 128], BF16)
        self.in_swam = di("swam", [128, 6 * 512], BF16)
        self.W = {}
        self.natab = None
        for (li, kind, _) in layers:
            w = {}
            w["ada"] = di(f"ada_w{li}", [D, 6 * D])
            w["win"] = di(f"ffn_w_in{li}", [D, 2 * DFF])
            w["wout"] = di(f"ffn_w_out{li}", [DFF, D])
            w["mix"] = di(f"mix_w_in{li}", [D, MIX_COLS[kind]])
            w["wo"] = di(f"mix_w_o{li}", [D, D])
            w["sv"] = di(f"sv{li}", [128, NSV])
            for k in ("ada", "win", "wout", "mix", "wo"):
                w[k + "_b"] = ds(f"{k}_b{li}", w[k].shape, BF16)
            if kind == 0:
                self.npat = len(na_patterns(S)[1])
                w["natab"] = di(f"natab{li}", [self.npat * 16 * 128, 8 * 512])
            if kind == 2:
                w["headg"] = di(f"headg{li}", [128, D])
            self.W[li] = w
        self.out = nc.dram_tensor("outT", [D, S], F32, kind="ExternalOutput").ap()
        self.XT = [ds("XT0", [D, T], F32), ds("XT1", [D, T], F32)]
        self.QT = ds("QT", [D, T], BF16)
        self.KT = ds("KT", [D, T], BF16)
        self.V = ds("V", [T, D], BF16)
        self.OT = ds("OT", [D, T], BF16)
        self.in_identf = di("identf", [128, 128])
        self.in_onesf = di("onesf", [128, 128])
        self.in_tri = di("tri", [64, 128])
        if any(k == 2 for (_, k, _) in layers):
            self.Ktok = ds("Ktok", [T, 1024], BF16)
            self.Og = ds("Og", [T, D], BF16)
            self.G = ds("G", [T, 16], F32)
            self.H = [ds("H0", [T, D], F32), ds("H1", [T, D], F32)]
            self.t_Ktok, self.t_Og, self.t_G, self.t_H = Tok(), Tok(), Tok(), [Tok(), Tok()]
        self.t_XT = [Tok(), Tok()]
        self.t_QT, self.t_KT, self.t_V, self.t_OT = Tok(), Tok(), Tok(), Tok()
        self.t_W = Tok()
        self.t_Wl = {li: Tok() for (li, _, _) in layers}
        es = self.es
        sb = lambda name, shape, dt: es.enter_context(nc.sbuf_tensor(self.uname(name), list(shape), dt))
        self.ps = [es.enter_context(nc.psum_tensor(f"ps{i}", [128, 512], F32)) for i in range(8)]
        self.t_ps = [Tok(ex=True) for _ in range(8)]
        self.ones = sb("ones_sb", [128, 128], BF16)
        self.ident = sb("ident_sb", [128, 128], BF16)
        self.rot = sb("rot_sb", [128, 128], BF16)
        self.mprev = sb("mprev_sb", [128, 128], BF16)
        self.mnext = sb("mnext_sb", [128, 128], BF16)
        self.t_const = Tok()
        self.epsT = sb("eps_sb", [128, 1], F32)
        self.MOD, self.SCA1, self.SCA2, self.SV, self.t_mod = {}, {}, {}, {}, {}
        for (li, _, _) in layers:
            self.MOD[li] = sb(f"mod{li}", [128, 96, 2], F32)
            self.SCA1[li] = sb(f"sca1_{li}", [128, KC, 2], F32)
            self.SCA2[li] = sb(f"sca2_{li}", [128, KC, 2], F32)
            self.SV[li] = sb(f"svs{li}", [128, NSV], F32)
            self.t_mod[li] = Tok()

    def uname(self, name):
        self._uid = getattr(self, "_uid", 0) + 1
        return f"s{self._uid}_{name}"

    def wview(self, buf, kc, n):
        return buf[:, 0:kc * n].rearrange("p (k n) -> p k n", n=n)

    def norm_mod(self, pes, xt, t_xt, N, col, SCA, shift_base, li, ht, t_ht, sq, t_sq, tmp, t_tmp, psn, rs, t_rs):
        kb = self.kb
        ps, tps = self.ps[psn], self.t_ps[psn]
        for kc in range(KC):
            b = kc % 2
            kb.op("act", lambda e: e.activation(out=sq[b][:, :N], in_=xt[:, kc, :N], func=AF.Square),
                  reads=[t_xt], writes=[t_sq[b]])
            kb.op("pe", lambda e: e.matmul(ps[:, :N], lhsT=self.ones[:], rhs=sq[b][:, :N], start=(kc == 0), stop=(kc == KC - 1)),
                  reads=[t_sq[b], self.t_const], writes=[tps])
        kb.op("act", lambda e: e.activation(out=rs[:, :N], in_=ps[:, :N], func=AF.Sqrt, scale=1.0 / D, bias=self.epsT[:, 0:1]),
              reads=[tps, self.t_const], writes=[t_rs])
        kb.op("dve", lambda e: e.reciprocal(rs[:, :N], rs[:, :N]), reads=[t_rs], writes=[t_rs])
        for kc in range(KC):
            b = kc % 2
            kb.op("dve", lambda e: e.tensor_tensor(tmp[b][:, :N], xt[:, kc, :N], rs[:, :N], ALU.mult),
                  reads=[t_xt, t_rs], writes=[t_tmp[b]])
            kb.op("act", lambda e: e.activation(out=ht[:, kc, :N], in_=tmp[b][:, :N], func=AF.Identity,
                                                scale=SCA[:, kc, col:col + 1], bias=self.MOD[li][:, shift_base + kc, col:col + 1]),
                  reads=[t_tmp[b], self.t_mod[li]], writes=[t_ht])

    def phase0(self):
        kb, nc = self.kb, self.nc
        S, T = self.S, self.T
        for (dst, src) in ((self.ones, self.in_ones), (self.ident, self.in_ident), (self.rot, self.in_rot),
                           (self.mprev, self.in_mprev), (self.mnext, self.in_mnext)):
            kb.dma("sp", dst[:], src[:, :], writes=[self.t_const])
        kb.op("pool", lambda e: e.memset(self.epsT[:], EPS), writes=[self.t_const])
        for r in range(0, D, 256):
            kb.dma("sp", self.XT[0][r:r + 256, 0:S], self.in_xT[r:r + 256, :], writes=[self.t_XT[0]])
        kb.dma("sp", self.XT[0][:, S:T], self.in_cxT[:, :], writes=[self.t_XT[0]])
        for (li, kind, _) in self.layers:
            w = self.W[li]
            for r in range(0, D, 256):
                kb.dma("pool", w["ada_b"][r:r + 256, :], w["ada"][r:r + 256, :], writes=[self.t_W])
        for idx, (li, kind, _) in enumerate(self.layers):
            w = self.W[li]
            for k in ("mix", "wo", "win", "wout"):
                rows = w[k].shape[0]
                step = 256 if idx == 0 else 64
                for r in range(0, rows, step):
                    r1 = min(rows, r + step)
                    if idx == 0:
                        kb.dma("pool", w[k + "_b"][r:r1, :], w[k][r:r1, :], writes=[self.t_Wl[li]])
                    else:
                        kb.bg.append((w[k + "_b"][r:r1, :], w[k][r:r1, :], [self.t_Wl[li]]))
        with ExitStack() as pes:
            sb = lambda name, shape, dt: pes.enter_context(nc.sbuf_tensor(self.uname(name), list(shape), dt))
            cc = sb("cc", [128, KC, 2], F32)
            scb = sb("scb", [128, KC, 2], BF16)
            t_cc, t_scb = Tok(), Tok()
            wt = [sb(f"p0w{i}", [128, KC * 512], BF16) for i in range(2)]
            t_wt = [Tok(), Tok()]
            tmp = sb("p0tmp", [128, KC, 2], F32)
            t_tmp = Tok()
            kb.dma("sp", cc[:], self.in_cc[:, :, :], writes=[t_cc])
            kb.op("act", lambda e: e.activation(out=scb[:], in_=cc[:], func=AF.Silu), reads=[t_cc], writes=[t_scb])
            g = 0
            for (li, kind, _) in self.layers:
                w = self.W[li]
                kb.dma("sp", self.SV[li][:], w["sv"][:, :], writes=[self.t_mod[li]])
                for cg in range(24):
                    b = g % 2
                    g += 1
                    wv = self.wview(wt[b], KC, 512)
                    kb.dma("sp", wv, w["ada_b"][:, cg * 512:(cg + 1) * 512].rearrange("(kc p) n -> p kc n", p=128),
                           reads=[self.t_W], writes=[t_wt[b]])
                    for j in range(4):
                        c = cg * 4 + j
                        pi = c % 2
                        for kc in range(KC):
                            kb.op("pe", lambda e: e.matmul(self.ps[pi][:, 0:2], lhsT=wv[:, kc, j * 128:(j + 1) * 128], rhs=scb[:, kc, :],
                                                           start=(kc == 0), stop=(kc == KC - 1)),
                                  reads=[t_wt[b], t_scb], writes=[self.t_ps[pi]], signal=(kc == KC - 1))
                        kb.op("act", lambda e: e.activation(out=self.MOD[li][:, c, :], in_=self.ps[pi][:, 0:2], func=AF.Identity,
                                                            bias=self.SV[li][:, SV_ADAB + c:SV_ADAB + c + 1], scale=1.0),
                              reads=[self.t_ps[pi], self.t_mod[li]], writes=[self.t_mod[li]])
                for (SCA, sc0, g0) in ((self.SCA1[li], 16, SV_N1), (self.SCA2[li], 64, SV_N2)):
                    kb.op("dve", lambda e: e.tensor_scalar_add(tmp[:], self.MOD[li][:, sc0:sc0 + 16, :], 1.0),
                          reads=[self.t_mod[li]], writes=[t_tmp])
                    for col in range(2):
                        kb.op("dve", lambda e: e.tensor_tensor(SCA[:, :, col], tmp[:, :, col], self.SV[li][:, g0:g0 + 16], ALU.mult),
                              reads=[t_tmp, self.t_mod[li]], writes=[self.t_mod[li]])
            kb.barrier()

    def phaseA(self, li, kind, xi):
        kb, nc = self.kb, self.nc
        S, T = self.S, self.T
        w = self.W[li]
        nq, nkv = 16, NKV[kind]
        rope = kind in (1, 3)
        XT, t_XT = self.XT[xi], self.t_XT[xi]
        tiles = [(t0, min(512, S - t0), 0) for t0 in range(0, S, 512)] + [(S, CTX, 1)]
        with ExitStack() as pes:
            sb = lambda name, shape, dt: pes.enter_context(nc.sbuf_tensor(self.uname(name), list(shape), dt))
            xt = sb("a_xt", [128, KC, 512], F32); t_xt = Tok()
            ht = sb("a_ht", [128, KC, 512], BF16); t_ht = Tok()
            sq = [sb(f"a_sq{i}", [128, 512], BF16) for i in range(2)]; t_sq = [Tok(), Tok()]
            tmp = [sb(f"a_tmp{i}", [128, 512], F32) for i in range(2)]; t_tmp = [Tok(), Tok()]
            rs = sb("a_rs", [128, 512], F32); t_rs = Tok()
            wt = [sb(f"a_w{i}", [128, KC * 512], BF16) for i in range(3)]; t_wt = [Tok(), Tok(), Tok()]
            cs = [sb("a_cos", [128, 512], F32), sb("a_sin", [128, 512], F32)]; t_cs = Tok()
            sq2 = [sb(f"a_sq2{i}", [128, 512], BF16) for i in range(2)]; t_sq2 = [Tok(), Tok()]
            r2 = [sb(f"a_r2{i}", [128, 512], F32) for i in range(2)]; t_r2 = [Tok(), Tok()]
            qn = [sb(f"a_qn{i}", [128, 512], BF16) for i in range(2)]; t_qn = [Tok(), Tok()]
            t1 = [sb(f"a_t1{i}", [128, 512], F32) for i in range(2)]; t_t1 = [Tok(), Tok()]
            t2 = [sb(f"a_t2{i}", [128, 512], F32) for i in range(2)]; t_t2 = [Tok(), Tok()]
            qo = [sb(f"a_qo{i}", [128, 512], BF16) for i in range(2)]; t_qo = [Tok(), Tok()]
            vt = [sb(f"a_vt{i}", [128, 512], BF16) for i in range(2)]; t_vt = [Tok(), Tok()]
            gs = sb("a_gs", [128, 2], F32); t_gs = Tok()
            kb.op("dve", lambda e: e.tensor_scalar_mul(gs[:, 0:1], self.SV[li][:, SV_QG:SV_QG + 1], 128.0 ** -0.5),
                  reads=[self.t_mod[li]], writes=[t_gs])
            kb.op("dve", lambda e: e.tensor_copy(gs[:, 1:2], self.SV[li][:, SV_KG:SV_KG + 1]), reads=[self.t_mod[li]], writes=[t_gs])
            wg = 0
            cnt = 0
            for (t0, N, col) in tiles:
                kb.dma("sp", xt[:, :, :N], XT[:, t0:t0 + N].rearrange("(kc p) n -> p kc n", p=128), reads=[t_XT], writes=[t_xt])
                if rope and col == 0:
                    kb.dma("sp", cs[0][:, :N], self.in_cos[:, t0:t0 + N], writes=[t_cs])
                    kb.dma("sp", cs[1][:, :N], self.in_sin[:, t0:t0 + N], writes=[t_cs])
                self.norm_mod(pes, xt, t_xt, N, col, self.SCA1[li], 0, li, ht, t_ht, sq, t_sq, tmp, t_tmp, 7, rs, t_rs)
                nqk = nq + nkv
                pend1, pend2 = None, None

                def make_stage1(c, pi, sb_, N=N, t0=t0, col=col):
                    def stage1():
                        isq = c < nq
                        psq, tpsq = self.ps[pi], self.t_ps[pi]
                        kb.op("act", lambda e: e.activation(out=sq2[sb_][:, :N], in_=psq[:, :N], func=AF.Square), reads=[tpsq], writes=[t_sq2[sb_]])
                        pss, tpss = self.ps[3], self.t_ps[3]
                        kb.op("pe", lambda e: e.matmul(pss[:, :N], lhsT=self.ones[:], rhs=sq2[sb_][:, :N], start=True, stop=True),
                              reads=[t_sq2[sb_], self.t_const], writes=[tpss])
                        kb.op("act", lambda e: e.activation(out=r2[sb_][:, :N], in_=pss[:, :N], func=AF.Sqrt, scale=1.0 / 128, bias=self.epsT[:, 0:1]),
                              reads=[tpss, self.t_const], writes=[t_r2[sb_]])
                        kb.op("dve", lambda e: e.reciprocal(r2[sb_][:, :N], r2[sb_][:, :N]), reads=[t_r2[sb_]], writes=[t_r2[sb_]])
                        gcol = gs[:, 0:1] if isq else gs[:, 1:2]
                        dstT = self.QT if isq else self.KT
                        t_dst = self.t_QT if isq else self.t_KT
                        row0 = (c if isq else c - nq) * 128
                        if rope and col == 0:
                            kb.op("dve", lambda e: e.scalar_tensor_tensor(qn[sb_][:, :N], psq[:, :N], gcol, r2[sb_][:, :N], ALU.mult, ALU.mult),
                                  reads=[tpsq, t_r2[sb_], t_gs], writes=[t_qn[sb_]])

                            def stage2():
                                psr, tpsr = self.ps[4 + sb_], self.t_ps[4 + sb_]
                                kb.op("pe", lambda e: e.matmul(psr[:, :N], lhsT=self.rot[:], rhs=qn[sb_][:, :N], start=True, stop=True),
                                      reads=[t_qn[sb_], self.t_const], writes=[tpsr])
                                kb.op("pool", lambda e: e.tensor_tensor(t1[sb_][:, :N], qn[sb_][:, :N], cs[0][:, :N], ALU.mult),
                                      reads=[t_qn[sb_], t_cs], writes=[t_t1[sb_]])
                                kb.op("dve", lambda e: e.tensor_tensor(t2[sb_][:, :N], psr[:, :N], cs[1][:, :N], ALU.mult),
                                      reads=[tpsr, t_cs], writes=[t_t2[sb_]])
                                kb.op("dve", lambda e: e.tensor_tensor(qo[sb_][:, :N], t1[sb_][:, :N], t2[sb_][:, :N], ALU.add),
                                      reads=[t_t1[sb_], t_t2[sb_]], writes=[t_qo[sb_]])
                                kb.dma("pool", dstT[row0:row0 + 128, t0:t0 + N], qo[sb_][:, :N], reads=[t_qo[sb_]], writes=[t_dst])
                            return stage2
                        kb.op("dve", lambda e: e.scalar_tensor_tensor(qo[sb_][:, :N], psq[:, :N], gcol, r2[sb_][:, :N], ALU.mult, ALU.mult),
                              reads=[tpsq, t_r2[sb_], t_gs], writes=[t_qo[sb_]])
                        kb.dma("pool", dstT[row0:row0 + 128, t0:t0 + N], qo[sb_][:, :N], reads=[t_qo[sb_]], writes=[t_dst])
                        return None
                    return stage1

                for cg in range((nqk + 3) // 4):
                    b = wg % 3
                    wg += 1
                    nch = min(4, nqk - cg * 4)
                    wv = self.wview(wt[b], KC, 512)
                    kb.dma("sp", wv[:, :, :nch * 128], w["mix_b"][:, cg * 512:cg * 512 + nch * 128].rearrange("(kc p) n -> p kc n", p=128),
                           reads=[self.t_Wl[li]], writes=[t_wt[b]])
                    for j in range(nch):
                        c = cg * 4 + j
                        pi = cnt % 3
                        sb_ = cnt % 2
                        cnt += 1
                        psq, tpsq = self.ps[pi], self.t_ps[pi]
                        for kc in range(KC):
                            kb.op("pe", lambda e: e.matmul(psq[:, :N], lhsT=wv[:, kc, j * 128:(j + 1) * 128], rhs=ht[:, kc, :N],
                                                           start=(kc == 0), stop=(kc == KC - 1)),
                                  reads=[t_wt[b], t_ht], writes=[tpsq], signal=(kc == KC - 1))
                        s2 = pend1() if pend1 else None
                        if pend2:
                            pend2()
                        pend2 = s2
                        pend1 = make_stage1(c, pi, sb_)
                s2 = pend1() if pend1 else None
                if pend2:
                    pend2()
                if s2:
                    s2()
                vbase = nqk * 128
                vw = nkv * 128
                for vg in range((vw + 511) // 512):
                    b = wg % 3
                    wg += 1
                    wd = min(512, vw - vg * 512)
                    wv = self.wview(wt[b], KC, 512)
                    kb.dma("sp", wv[:, :, :wd], w["mix_b"][:, vbase + vg * 512:vbase + vg * 512 + wd].rearrange("(kc p) n -> p kc n", p=128),
                           reads=[self.t_Wl[li]], writes=[t_wt[b]])
                    for tb in range(N // 128):
                        pi = 6 + cnt % 2
                        vb = cnt % 2
                        cnt += 1
                        for kc in range(KC):
                            kb.op("pe", lambda e: e.matmul(self.ps[pi][:, :wd], lhsT=ht[:, kc, tb * 128:(tb + 1) * 128], rhs=wv[:, kc, :wd],
                                                           start=(kc == 0), stop=(kc == KC - 1)),
                                  reads=[t_wt[b], t_ht], writes=[self.t_ps[pi]], signal=(kc == KC - 1))
                        kb.op("act", lambda e: e.activation(out=vt[vb][:, :wd], in_=self.ps[pi][:, :wd], func=AF.Copy),
                              reads=[self.t_ps[pi]], writes=[t_vt[vb]])
                        kb.dma("pool", self.V[t0 + tb * 128:t0 + (tb + 1) * 128, vg * 512:vg * 512 + wd], vt[vb][:, :wd],
                               reads=[t_vt[vb]], writes=[self.t_V])
            kb.barrier()

    def phaseB(self, li, kind, need_ctx):
        kb, nc = self.kb, self.nc
        S, T = self.S, self.T
        w = self.W[li]
        nkv = NKV[kind]
        grp = 16 // nkv
        TB = T // 128
        SB = S // 128
        ctxb = [SB, SB + 1]
        sink = kind == 1
        with ExitStack() as pes:
            sb = lambda name, shape, dt: pes.enter_context(nc.sbuf_tensor(self.uname(name), list(shape), dt))
            KTh = sb("b_kt", [128, T], BF16); t_k = Tok()
            Vh = sb("b_v", [128, TB, 128], BF16); t_v = Tok()
            QTh = [sb(f"b_qt{i}", [128, T], BF16) for i in range(2)]; t_q = [Tok(), Tok()]
            pt = [sb(f"b_pt{i}", [128, 512], BF16) for i in range(3)]; t_pt = [Tok() for _ in range(3)]
            rec = sb("b_rec", [128, 512], F32); t_rec = Tok()
            ot = [sb(f"b_ot{i}", [128, 512], BF16) for i in range(2)]; t_ot = [Tok(), Tok()]
            esink = sb("b_esink", [128, 16], F32); t_es = Tok()
            accden = kind == 3
            if accden:
                onesf = sb("b_onesf", [128, 128], F32); t_onesf = Tok()
                kb.dma("sp", onesf[:], self.in_onesf[:, :], writes=[t_onesf])
                accD = [sb(f"b_accD{i}", [128, 512], F32) for i in range(2)]; t_accD = [Tok(), Tok()]
                accP = [sb(f"b_accP{i}", [128, 512], F32) for i in range(2)]; t_accP = [Tok(), Tok()]
            if kind == 0:
                npat = self.npat
                tab = sb("b_tab", [128, 8 * 512], F32); t_tab = Tok()
                etab = sb("b_etab", [128, npat, 8 * 512], BF16); t_etab = Tok()
                pat_of_C, pats = na_patterns(S)
            if kind == 1:
                swam = sb("b_swam", [128, 6, 512], BF16); t_swam = Tok()
                kb.dma("sp", swam[:], self.in_swam[:, :].rearrange("p (r n) -> p r n", n=512), writes=[t_swam])
            if sink:
                kb.op("act", lambda e: e.activation(out=esink[:], in_=self.SV[li][:, SV_SINK:SV_SINK + 16], func=AF.Exp),
                      reads=[self.t_mod[li]], writes=[t_es])
            chunks = []
            if kind == 3:
                for q0 in range(0, S, 512):
                    chunks.append((q0, min(512, S - q0), [(b, None) for b in range(TB)]))
            elif kind == 1:
                for ci in range(S // 512):
                    bl = [(kbk, ("swaw", kbk - 4 * ci + 1)) for kbk in range(4 * ci - 1, 4 * ci + 5) if 0 <= kbk < SB]
                    bl += [(b, None) for b in ctxb]
                    chunks.append((ci * 512, 512, bl))
            else:
                for ci in range(S // 512):
                    p = pat_of_C[ci]
                    bbw = na_base_block(ci, SB)
                    bl = [(bbw + j, ("naw", p, j)) for j in range(8)] + [(b, None) for b in ctxb]
                    chunks.append((ci * 512, 512, bl))
            if need_ctx:
                chunks.append((S, CTX, [(b, None) for b in ctxb]))
            cnt = 0
            qi = 0
            for kvh in range(nkv):
                kb.dma("sp", KTh[:], self.KT[kvh * 128:(kvh + 1) * 128, :], reads=[self.t_KT], writes=[t_k])
                kb.dma("sp", Vh[:], self.V[:, kvh * 128:(kvh + 1) * 128].rearrange("(tb p) d -> p tb d", p=128), reads=[self.t_V], writes=[t_v])
                for qh in range(kvh * grp, (kvh + 1) * grp):
                    qb = qi % 2
                    qi += 1
                    kb.dma("sp", QTh[qb][:], self.QT[qh * 128:(qh + 1) * 128, :], reads=[self.t_QT], writes=[t_q[qb]])
                    if kind == 0:
                        for p in range(npat):
                            r0 = (p * 16 + qh) * 128
                            kb.dma("sp", tab[:, :], w["natab"][r0:r0 + 128, :], writes=[t_tab])
                            kb.op("act", lambda e: e.activation(out=etab[:, p, :], in_=tab[:, :], func=AF.Exp), reads=[t_tab], writes=[t_etab])
                    for (q0, N, blocks) in chunks:
                        po, tpo = self.ps[0 + (cnt % 2)], self.t_ps[0 + (cnt % 2)]
                        pd, tpd = self.ps[2 + (cnt % 2)], self.t_ps[2 + (cnt % 2)]
                        ob = cnt % 2
                        cnt += 1
                        nb = len(blocks)

                        def emit_s(i):
                            kbk, mask = blocks[i]
                            psn = 4 + (i % 3)
                            pb = i % 3
                            kb.op("pe", lambda e: e.matmul(self.ps[psn][:, :N], lhsT=KTh[:, kbk * 128:(kbk + 1) * 128], rhs=QTh[qb][:, q0:q0 + N],
                                                           start=True, stop=True),
                                  reads=[t_k, t_q[qb]], writes=[self.t_ps[psn]])
                            kb.op("act", lambda e: e.activation(out=pt[pb][:, :N], in_=self.ps[psn][:, :N], func=AF.Exp),
                                  reads=[self.t_ps[psn]], writes=[t_pt[pb]])
                            if mask is not None:
                                if mask[0] == "swaw":
                                    m, tm = swam[:, mask[1], :N], t_swam
                                else:
                                    m, tm = etab[:, mask[1], mask[2] * 512:mask[2] * 512 + N], t_etab
                                kb.op("dve", lambda e: e.tensor_tensor(pt[pb][:, :N], pt[pb][:, :N], m, ALU.mult),
                                      reads=[t_pt[pb], tm], writes=[t_pt[pb]])

                        def emit_pv(i):
                            kbk, _ = blocks[i]
                            pb = i % 3
                            kb.op("pe", lambda e: e.matmul(po[:, :N], lhsT=Vh[:, kbk, :], rhs=pt[pb][:, :N], start=(i == 0), stop=(i == nb - 1)),
                                  reads=[t_v, t_pt[pb]], writes=[tpo], signal=(i == nb - 1))
                            if not accden:
                                kb.op("pe", lambda e: e.matmul(pd[:, :N], lhsT=self.ones[:], rhs=pt[pb][:, :N], start=(i == 0), stop=(i == nb - 1)),
                                      reads=[self.t_const, t_pt[pb]], writes=[tpd], signal=True)
                            else:
                                r = i % 4
                                if r == 3:
                                    kb.op("pe", lambda e: e.matmul(pd[:, :N], lhsT=self.ones[:], rhs=pt[pb][:, :N], start=(not st["pe"]), stop=False),
                                          reads=[self.t_const, t_pt[pb]], writes=[tpd], signal=True)
                                    st["pe"] = True
                                else:
                                    onp = (r == 1)
                                    eng = "pool" if onp else "dve"
                                    acc, tacc = (accP[ob], t_accP[ob]) if onp else (accD[ob], t_accD[ob])
                                    if not st[eng]:
                                        kb.op(eng, lambda e: e.tensor_copy(acc[:, :N], pt[pb][:, :N]), reads=[t_pt[pb]], writes=[tacc])
                                        st[eng] = True
                                    else:
                                        kb.op(eng, lambda e: e.tensor_tensor(acc[:, :N], acc[:, :N], pt[pb][:, :N], ALU.add), reads=[t_pt[pb], tacc], writes=[tacc])

                        st = {"pe": False, "dve": False, "pool": False}
                        DEPTH = 2
                        for i in range(min(DEPTH, nb)):
                            emit_s(i)
                        for i in range(nb):
                            if i + DEPTH < nb:
                                emit_s(i + DEPTH)
                            emit_pv(i)
                        if accden:
                            if st["pool"]:
                                kb.op("dve", lambda e: e.tensor_tensor(accD[ob][:, :N], accD[ob][:, :N], accP[ob][:, :N], ALU.add),
                                      reads=[t_accP[ob], t_accD[ob]], writes=[t_accD[ob]])
                            kb.op("pe", lambda e: e.matmul(pd[:, :N], lhsT=onesf[:, :], rhs=accD[ob][:, :N], start=(not st["pe"]), stop=True),
                                  reads=[t_onesf, t_accD[ob]], writes=[tpd])
                        if sink:
                            kb.op("dve", lambda e: e.tensor_scalar_add(rec[:, :N], pd[:, :N], esink[:, qh:qh + 1]), reads=[tpd, t_es], writes=[t_rec])
                            kb.op("dve", lambda e: e.reciprocal(rec[:, :N], rec[:, :N]), reads=[t_rec], writes=[t_rec])
                        else:
                            kb.op("dve", lambda e: e.reciprocal(rec[:, :N], pd[:, :N]), reads=[tpd], writes=[t_rec])
                        kb.op("dve", lambda e: e.tensor_tensor(ot[ob][:, :N], po[:, :N], rec[:, :N], ALU.mult), reads=[tpo, t_rec], writes=[t_ot[ob]])
                        kb.dma("pool", self.OT[qh * 128:(qh + 1) * 128, q0:q0 + N], ot[ob][:, :N], reads=[t_ot[ob]], writes=[self.t_OT])
            kb.barrier()

    def phaseC(self, li, need_ctx, xi, final):
        kb, nc = self.kb, self.nc
        S, T = self.S, self.T
        w = self.W[li]
        XT, t_XT = self.XT[xi], self.t_XT[xi]
        XO, t_XO = self.XT[1 - xi], self.t_XT[1 - xi]
        wins = []
        for w0 in range(0, S, 510):
            n = min(510, S - w0)
            wins.append((w0, n, w0 > 0, w0 + n < S, 0))
        if need_ctx:
            wins.append((S, CTX, False, False, 1))
        with ExitStack() as pes:
            sb = lambda name, shape, dt: pes.enter_context(nc.sbuf_tensor(self.uname(name), list(shape), dt))
            xt = sb("c_xt", [128, KC, 512], F32); t_xt = Tok()
            ab = sb("c_ab", [128, KC, 512], BF16); t_ab = Tok()
            act = sb("c_act", [128, FC, 512], BF16); t_act = Tok()
            wt = [sb(f"c_w{i}", [128, FC * 256], BF16) for i in range(3)]; t_wt = [Tok() for _ in range(3)]
            sq = [sb(f"c_sq{i}", [128, 512], BF16) for i in range(2)]; t_sq = [Tok(), Tok()]
            tmp = [sb(f"c_tmp{i}", [128, 512], F32) for i in range(2)]; t_tmp = [Tok(), Tok()]
            rs = sb("c_rs", [128, 512], F32); t_rs = Tok()
            ga = [sb(f"c_ga{i}", [128, 512], F32) for i in range(2)]; t_ga = [Tok(), Tok()]
            gq = [sb(f"c_gq{i}", [128, 512], F32) for i in range(2)]; t_gq = [Tok(), Tok()]
            gz = [sb(f"c_gz{i}", [128, 512], F32) for i in range(2)]; t_gz = [Tok(), Tok()]
            gy = [sb(f"c_gy{i}", [128, 512], F32) for i in range(2)]; t_gy = [Tok(), Tok()]
            xo = [sb(f"c_xo{i}", [128, 512], F32) for i in range(2)]; t_xo = [Tok(), Tok()]
            SVl = self.SV[li]
            MOD = self.MOD[li]
            wg = 0
            cnt = 0
            for (w0, n, left, right, col) in wins:
                a0 = w0 - (1 if left else 0)
                N = n + (1 if left else 0) + (1 if right else 0)
                lo = 1 if left else 0
                kb.dma("sp", ab[:, :, :N], self.OT[:, a0:a0 + N].rearrange("(kc p) n -> p kc n", p=128), reads=[self.t_OT], writes=[t_ab])
                kb.dma("sp", xt[:, :, :N], XT[:, a0:a0 + N].rearrange("(kc p) n -> p kc n", p=128), reads=[t_XT], writes=[t_xt])
                for ng in range(4):
                    b = wg % 3
                    wg += 1
                    wv = self.wview(wt[b], KC, 512)
                    kb.dma("sp", wv, w["wo_b"][:, ng * 512:(ng + 1) * 512].rearrange("(kc p) n -> p kc n", p=128), reads=[self.t_Wl[li]], writes=[t_wt[b]])
                    for j in range(4):
                        c = ng * 4 + j
                        pi = cnt % 2
                        cnt += 1
                        for kc in range(KC):
                            kb.op("pe", lambda e: e.matmul(self.ps[pi][:, :N], lhsT=wv[:, kc, j * 128:(j + 1) * 128], rhs=ab[:, kc, :N],
                                                           start=(kc == 0), stop=(kc == KC - 1)),
                                  reads=[t_wt[b], t_ab], writes=[self.t_ps[pi]], signal=(kc == KC - 1))
                        kb.op("dve", lambda e: e.scalar_tensor_tensor(xt[:, c, :N], self.ps[pi][:, :N], MOD[:, 32 + c, col:col + 1], xt[:, c, :N],
                                                                      ALU.mult, ALU.add),
                              reads=[self.t_ps[pi], t_xt, self.t_mod[li]], writes=[t_xt])
                self.norm_mod(pes, xt, t_xt, N, col, self.SCA2[li], 48, li, ab, t_ab, sq, t_sq, tmp, t_tmp, 7, rs, t_rs)
                for cg in range(FC // 4):
                    bg = wg % 3
                    wg += 1
                    bu = wg % 3
                    wg += 1
                    wvg = self.wview(wt[bg], KC, 512)
                    wvu = self.wview(wt[bu], KC, 512)
                    kb.dma("sp", wvg, w["win_b"][:, cg * 512:(cg + 1) * 512].rearrange("(kc p) n -> p kc n", p=128), reads=[self.t_Wl[li]], writes=[t_wt[bg]])
                    kb.dma("sp", wvu, w["win_b"][:, DFF + cg * 512:DFF + (cg + 1) * 512].rearrange("(kc p) n -> p kc n", p=128),
                           reads=[self.t_Wl[li]], writes=[t_wt[bu]])
                    for j in range(4):
                        c = cg * 4 + j
                        eb = cnt % 2
                        cnt += 1
                        pg, tpg = self.ps[2 + eb], self.t_ps[2 + eb]
                        pu, tpu = self.ps[4 + eb], self.t_ps[4 + eb]
                        for kc in range(KC):
                            kb.op("pe", lambda e: e.matmul(pg[:, :N], lhsT=wvg[:, kc, j * 128:(j + 1) * 128], rhs=ab[:, kc, :N],
                                                           start=(kc == 0), stop=(kc == KC - 1)),
                                  reads=[t_wt[bg], t_ab], writes=[tpg], signal=(kc == KC - 1))
                        for kc in range(KC):
                            kb.op("pe", lambda e: e.matmul(pu[:, :N], lhsT=wvu[:, kc, j * 128:(j + 1) * 128], rhs=ab[:, kc, :N],
                                                           start=(kc == 0), stop=(kc == KC - 1)),
                                  reads=[t_wt[bu], t_ab], writes=[tpu], signal=(kc == KC - 1))
                        cw = lambda jj: SVl[:, SV_CW + jj * FC + c:SV_CW + jj * FC + c + 1]
                        a_, ta_ = ga[eb], t_ga[eb]
                        kb.op("act", lambda e: e.activation(out=a_[:, :n], in_=pg[:, lo:lo + n], func=AF.Identity, scale=cw(1),
                                                            bias=SVl[:, SV_CB + c:SV_CB + c + 1]),
                              reads=[tpg, self.t_mod[li]], writes=[ta_])
                        if left:
                            kb.op("dve", lambda e: e.scalar_tensor_tensor(a_[:, :n], pg[:, lo - 1:lo - 1 + n], cw(0), a_[:, :n], ALU.mult, ALU.add),
                                  reads=[tpg, ta_, self.t_mod[li]], writes=[ta_])
                        else:
                            kb.op("dve", lambda e: e.scalar_tensor_tensor(a_[:, 1:n], pg[:, lo:lo + n - 1], cw(0), a_[:, 1:n], ALU.mult, ALU.add),
                                  reads=[tpg, ta_, self.t_mod[li]], writes=[ta_])
                        if right:
                            kb.op("dve", lambda e: e.scalar_tensor_tensor(a_[:, :n], pg[:, lo + 1:lo + 1 + n], cw(2), a_[:, :n], ALU.mult, ALU.add),
                                  reads=[tpg, ta_, self.t_mod[li]], writes=[ta_])
                        else:
                            kb.op("dve", lambda e: e.scalar_tensor_tensor(a_[:, :n - 1], pg[:, lo + 1:lo + n], cw(2), a_[:, :n - 1], ALU.mult, ALU.add),
                                  reads=[tpg, ta_, self.t_mod[li]], writes=[ta_])
                        kb.op("act", lambda e: e.activation(out=gq[eb][:, :n], in_=a_[:, :n], func=AF.Square), reads=[ta_], writes=[t_gq[eb]])
                        kb.op("dve", lambda e: e.tensor_scalar(gq[eb][:, :n], gq[eb][:, :n], 0.044715 * GK, GK, ALU.mult, ALU.add),
                              reads=[t_gq[eb]], writes=[t_gq[eb]])
                        kb.op("pool", lambda e: e.tensor_tensor(gz[eb][:, :n], gq[eb][:, :n], a_[:, :n], ALU.mult), reads=[t_gq[eb], ta_], writes=[t_gz[eb]])
                        kb.op("act", lambda e: e.activation(out=gz[eb][:, :n], in_=gz[eb][:, :n], func=AF.Sigmoid), reads=[t_gz[eb]], writes=[t_gz[eb]])
                        kb.op("pool", lambda e: e.tensor_tensor(gy[eb][:, :n], gz[eb][:, :n], a_[:, :n], ALU.mult), reads=[t_gz[eb], ta_], writes=[t_gy[eb]])
                        kb.op("dve", lambda e: e.tensor_tensor(act[:, c, :n], gy[eb][:, :n], pu[:, lo:lo + n], ALU.mult),
                              reads=[t_gy[eb], tpu], writes=[t_act])
                for ng in range(8):
                    b = wg % 3
                    wg += 1
                    wv = self.wview(wt[b], FC, 256)
                    kb.dma("sp", wv, w["wout_b"][:, ng * 256:(ng + 1) * 256].rearrange("(kc p) n -> p kc n", p=128), reads=[self.t_Wl[li]], writes=[t_wt[b]])
                    for j in range(2):
                        c = ng * 2 + j
                        pi = cnt % 2
                        ob = cnt % 2
                        cnt += 1
                        for kc in range(FC):
                            kb.op("pe", lambda e: e.matmul(self.ps[pi][:, :n], lhsT=wv[:, kc, j * 128:(j + 1) * 128], rhs=act[:, kc, :n],
                                                           start=(kc == 0), stop=(kc == FC - 1)),
                                  reads=[t_wt[b], t_act], writes=[self.t_ps[pi]], signal=(kc == FC - 1))
                        kb.op("dve", lambda e: e.scalar_tensor_tensor(xo[ob][:, :n], self.ps[pi][:, :n], MOD[:, 80 + c, col:col + 1], xt[:, c, lo:lo + n],
                                                                      ALU.mult, ALU.add),
                              reads=[self.t_ps[pi], t_xt, self.t_mod[li]], writes=[t_xo[ob]])
                        if final:
                            if col == 0:
                                kb.dma("pool", self.out[c * 128:(c + 1) * 128, w0:w0 + n], xo[ob][:, :n], reads=[t_xo[ob]])
                        else:
                            kb.dma("pool", XO[c * 128:(c + 1) * 128, w0:w0 + n], xo[ob][:, :n], reads=[t_xo[ob]], writes=[t_XO])
            kb.barrier()

    def phaseA_ml(self, li, xi):
        kb, nc = self.kb, self.nc
        S, T = self.S, self.T
        w = self.W[li]
        XT, t_XT = self.XT[xi], self.t_XT[xi]
        tiles = [(t0, min(512, S - t0), 0) for t0 in range(0, S, 512)] + [(S, CTX, 1)]
        with ExitStack() as pes:
            sb = lambda name, shape, dt: pes.enter_context(nc.sbuf_tensor(self.uname(name), list(shape), dt))
            xt = sb("a_xt", [128, KC, 512], F32); t_xt = Tok()
            ht = sb("a_ht", [128, KC, 512], BF16); t_ht = Tok()
            sq = [sb(f"a_sq{i}", [128, 512], BF16) for i in range(2)]; t_sq = [Tok(), Tok()]
            tmp = [sb(f"a_tmp{i}", [128, 512], F32) for i in range(2)]; t_tmp = [Tok(), Tok()]
            rs = sb("a_rs", [128, 512], F32); t_rs = Tok()
            wt = [sb(f"a_w{i}", [128, KC * 512], BF16) for i in range(3)]; t_wt = [Tok(), Tok(), Tok()]
            qo = [sb(f"a_qo{i}", [128, 512], BF16) for i in range(2)]; t_qo = [Tok(), Tok()]
            vt = [sb(f"a_vt{i}", [128, 512], BF16) for i in range(2)]; t_vt = [Tok(), Tok()]
            gt = [sb(f"a_gt{i}", [128, 16], F32) for i in range(2)]; t_gt = [Tok(), Tok()]
            ge = [sb(f"a_ge{i}", [128, 8], F32) for i in range(2)]; t_ge = [Tok(), Tok()]
            SVl = self.SV[li]
            wg = 0
            cnt = 0
            for (t0, N, col) in tiles:
                kb.dma("sp", xt[:, :, :N], XT[:, t0:t0 + N].rearrange("(kc p) n -> p kc n", p=128), reads=[t_XT], writes=[t_xt])
                self.norm_mod(pes, xt, t_xt, N, col, self.SCA1[li], 0, li, ht, t_ht, sq, t_sq, tmp, t_tmp, 7, rs, t_rs)
                for cg in range(4):
                    b = wg % 3
                    wg += 1
                    wv = self.wview(wt[b], KC, 512)
                    kb.dma("sp", wv, w["mix_b"][:, cg * 512:(cg + 1) * 512].rearrange("(kc p) n -> p kc n", p=128), reads=[self.t_Wl[li]], writes=[t_wt[b]])
                    for j in range(4):
                        c = cg * 4 + j
                        pi = cnt % 2
                        ob = cnt % 2
                        cnt += 1
                        for kc in range(KC):
                            kb.op("pe", lambda e: e.matmul(self.ps[pi][:, :N], lhsT=wv[:, kc, j * 128:(j + 1) * 128], rhs=ht[:, kc, :N],
                                                           start=(kc == 0), stop=(kc == KC - 1)),
                                  reads=[t_wt[b], t_ht], writes=[self.t_ps[pi]], signal=(kc == KC - 1))
                        isq = c < 8
                        kb.op("act", lambda e: e.activation(out=qo[ob][:, :N], in_=self.ps[pi][:, :N], func=AF.Copy, scale=(1.0 if isq else 0.0625)),
                              reads=[self.t_ps[pi]], writes=[t_qo[ob]])
                        dstT, t_dst = (self.QT, self.t_QT) if isq else (self.KT, self.t_KT)
                        row0 = (c if isq else c - 8) * 128
                        kb.dma("pool", dstT[row0:row0 + 128, t0:t0 + N], qo[ob][:, :N], reads=[t_qo[ob]], writes=[t_dst])
                groups = [(1024 + g * 512, 512, "k", g * 512) for g in range(2)] + [(2048 + g * 512, 512, "v", g * 512) for g in range(4)] \
                    + [(4096 + g * 512, 512, "o", g * 512) for g in range(4)] + [(6144, 16, "g", 0)]
                for (c0, wd, what, d0) in groups:
                    b = wg % 3
                    wg += 1
                    wv = self.wview(wt[b], KC, 512)
                    kb.dma("sp", wv[:, :, :wd], w["mix_b"][:, c0:c0 + wd].rearrange("(kc p) n -> p kc n", p=128), reads=[self.t_Wl[li]], writes=[t_wt[b]])
                    for tb in range(N // 128):
                        pi = 2 + cnt % 2
                        vb = cnt % 2
                        cnt += 1
                        r0 = t0 + tb * 128
                        for kc in range(KC):
                            kb.op("pe", lambda e: e.matmul(self.ps[pi][:, :wd], lhsT=ht[:, kc, tb * 128:(tb + 1) * 128], rhs=wv[:, kc, :wd],
                                                           start=(kc == 0), stop=(kc == KC - 1)),
                                  reads=[t_wt[b], t_ht], writes=[self.t_ps[pi]], signal=(kc == KC - 1))
                        if what == "g":
                            g_, tg_ = gt[vb], t_gt[vb]
                            kb.op("dve", lambda e: e.tensor_tensor(g_[:, :], self.ps[pi][:, :16], SVl[:, SV_SINK:SV_SINK + 16], ALU.add),
                                  reads=[self.t_ps[pi], self.t_mod[li]], writes=[tg_])
                            e_, te_ = ge[vb], t_ge[vb]
                            for (src, dst) in ((4, 0), (12, 4)):
                                kb.op("act", lambda e: e.activation(out=e_[:, dst:dst + 4], in_=g_[:, src:src + 4], func=AF.Exp, scale=-1.0),
                                      reads=[tg_], writes=[te_])
                            kb.op("dve", lambda e: e.tensor_scalar_add(e_[:, :], e_[:, :], 1.0), reads=[te_], writes=[te_])
                            kb.op("act", lambda e: e.activation(out=e_[:, :], in_=e_[:, :], func=AF.Ln), reads=[te_], writes=[te_])
                            for (src, dst) in ((0, 4), (4, 12)):
                                kb.op("dve", lambda e: e.tensor_scalar_mul(g_[:, dst:dst + 4], e_[:, src:src + 4], -1.0), reads=[te_, tg_], writes=[tg_])
                            kb.dma("pool", self.G[r0:r0 + 128, :], g_[:, :], reads=[tg_], writes=[self.t_G])
                        else:
                            kb.op("act", lambda e: e.activation(out=vt[vb][:, :wd], in_=self.ps[pi][:, :wd], func=AF.Copy,
                                                                scale=(0.0625 if what == "k" else 1.0)),
                                  reads=[self.t_ps[pi]], writes=[t_vt[vb]])
                            dst, t_dst = {"k": (self.Ktok, self.t_Ktok), "v": (self.V, self.t_V), "o": (self.Og, self.t_Og)}[what]
                            kb.dma("pool", dst[r0:r0 + 128, d0:d0 + wd], vt[vb][:, :wd], reads=[t_vt[vb]], writes=[t_dst])
            kb.barrier()

    def phaseB_ml(self, li):
        kb, nc = self.kb, self.nc
        S, T = self.S, self.T
        L = 64
        nxc, ncc = S // L, CTX // L
        with ExitStack() as pes:
            sb = lambda name, shape, dt: pes.enter_context(nc.sbuf_tensor(self.uname(name), list(shape), dt))
            identf = sb("m_identf", [128, 128], F32)
            onesf = sb("m_onesf", [128, 128], F32)
            tri = sb("m_tri", [64, 128], F32)
            t_c = Tok()
            kb.dma("sp", identf[:], self.in_identf[:, :], writes=[t_c])
            kb.dma("sp", onesf[:], self.in_onesf[:, :], writes=[t_c])
            kb.dma("sp", tri[:], self.in_tri[:, :], writes=[t_c])
            C = [[sb(f"m_C{d}{h}", [128, 2, 512], F32) for h in range(4)] for d in range(2)]
            Cb = [[sb(f"m_Cb{d}{h}", [128, 2, 512], BF16) for h in range(4)] for d in range(2)]
            nn = [[sb(f"m_n{d}{h}", [128, 2], F32) for h in range(4)] for d in range(2)]
            nb = [[sb(f"m_nb{d}{h}", [128, 2], BF16) for h in range(4)] for d in range(2)]
            t_C = [[Tok() for h in range(4)] for d in range(2)]
            t_Cb = [[Tok() for h in range(4)] for d in range(2)]
            t_n = [[Tok() for h in range(4)] for d in range(2)]
            t_nb = [[Tok() for h in range(4)] for d in range(2)]
            for d in range(2):
                for h in range(4):
                    kb.op("pool", lambda e: e.memset(C[d][h][:], 0.0), writes=[t_C[d][h]])
                    kb.op("pool", lambda e: e.memset(Cb[d][h][:], 0.0), writes=[t_Cb[d][h]])
                    kb.op("pool", lambda e: e.memset(nn[d][h][:], 0.0), writes=[t_n[d][h]])
                    kb.op("pool", lambda e: e.memset(nb[d][h][:], 0.0), writes=[t_nb[d][h]])
            NB = 2
            D2 = range(2)
            mk = lambda name, shape, dt, n: [sb(f"{name}_{i}", shape, dt) for i in range(n)]
            tk = lambda n: [Tok() for _ in range(n)]
            qT = mk("m_qT", [128, 8, L], BF16, NB); t_qT = tk(NB)
            kT = mk("m_kT", [128, 8, L], BF16, NB); t_kT = tk(NB)
            kk = mk("m_kk", [L, 1024], BF16, NB); t_kk = tk(NB)
            vv = mk("m_vv", [L, 2048], BF16, NB); t_vv = tk(NB)
            gg = mk("m_gg", [L, 16], F32, NB); t_gg = tk(NB)
            bb = mk("m_b", [L, 4], F32, 2); t_bb = tk(2)
            wprev = mk("m_wprev", [L, 4], F32, 2); t_wprev = tk(2)
            lmb = mk("m_lmb", [L, 4], F32, 2); t_lmb = tk(2)
            ws = mk("m_ws", [L, 4], F32, 2); t_ws = tk(2)
            decay = mk("m_decay", [128, 4], F32, 2); t_decay = tk(2)
            diagb = mk("m_diagb", [L, L], F32, 4); t_diagb = tk(4)
            Eh = mk("m_E", [L, L], F32, 4); t_Eh = tk(4)
            WT = mk("m_WT", [L, L], BF16, 4); t_WT = tk(4)
            n1 = mk("m_n1", [L, 512], F32, 4); t_n1 = tk(4)
            hh = mk("m_hh", [L, 512], F32, 4); t_hh = tk(4)
            den = mk("m_den", [L, 2], F32, 4); t_den = tk(4)
            kp = mk("m_kp", [L, 256], BF16, 4); t_kp = tk(4)
            ps, tps = self.ps, self.t_ps
            pA, tA = ps[0], tps[0]
            pB, tB = ps[1], tps[1]
            orders = [[S + c * L for c in range(ncc)] + [c * L for c in range(nxc)],
                      [S + c * L for c in reversed(range(ncc))] + [c * L for c in reversed(range(nxc))]]
            cidx = 0
            for step in range(ncc + nxc):
                for d in D2:
                    tk0 = orders[d][step]
                    lic, lfc = (0, 4) if d == 0 else (8, 12)
                    trid = tri[:, 0:64] if d == 0 else tri[:, 64:128]
                    cb = cidx % NB
                    cidx += 1
                    qT_, kT_, kk_, vv_, G_ = qT[cb], kT[cb], kk[cb], vv[cb], gg[cb]
                    tq, tkT, tkk, tvv, tgg = t_qT[cb], t_kT[cb], t_kk[cb], t_vv[cb], t_gg[cb]
                    kb.dma("sp", qT_[:], self.QT[0:1024, tk0:tk0 + L].rearrange("(c p) n -> p c n", p=128), reads=[self.t_QT], writes=[tq])
                    kb.dma("sp", kT_[:], self.KT[0:1024, tk0:tk0 + L].rearrange("(c p) n -> p c n", p=128), reads=[self.t_KT], writes=[tkT])
                    kb.dma("sp", kk_[:], self.Ktok[tk0:tk0 + L, :], reads=[self.t_Ktok], writes=[tkk])
                    kb.dma("sp", vv_[:], self.V[tk0:tk0 + L, :], reads=[self.t_V], writes=[tvv])
                    kb.dma("sp", G_[:], self.G[tk0:tk0 + L, :], reads=[self.t_G], writes=[tgg])
                    bb_, wprev_, lmb_, ws_, decay_ = bb[cb], wprev[cb], lmb[cb], ws[cb], decay[cb]
                    tbb, twp, tlmb, tws, tdec = t_bb[cb], t_wprev[cb], t_lmb[cb], t_ws[cb], t_decay[cb]
                    H4 = range(4)
                    kb.op("pe", lambda e: e.matmul(pA[0:L, 0:4], lhsT=trid, rhs=G_[:, lfc:lfc + 4], start=True, stop=True),
                          reads=[t_c, tgg], writes=[tA])
                    kb.op("pe", lambda e: e.matmul(pA[:, 16:20], lhsT=onesf[0:L, :], rhs=G_[:, lfc:lfc + 4], start=True, stop=True),
                          reads=[t_c, tgg], writes=[tA])
                    kb.op("dve", lambda e: e.tensor_copy(bb_[:, :], pA[0:L, 0:4]), reads=[tA], writes=[tbb])
                    kb.op("act", lambda e: e.activation(out=wprev_[:, :], in_=pA[0:L, 0:4], func=AF.Exp), reads=[tA], writes=[twp])
                    kb.op("act", lambda e: e.activation(out=decay_[:, :], in_=pA[:, 16:20], func=AF.Exp), reads=[tA], writes=[tdec])
                    kb.op("dve", lambda e: e.tensor_tensor(lmb_[:, :], G_[:, lic:lic + 4], bb_[:, :], ALU.subtract), reads=[tgg, tbb], writes=[tlmb])
                    kb.op("dve", lambda e: e.tensor_tensor(ws_[:, :], lmb_[:, :], pA[0:L, 16:20], ALU.add), reads=[tlmb, tA], writes=[tws])
                    kb.op("act", lambda e: e.activation(out=ws_[:, :], in_=ws_[:, :], func=AF.Exp), reads=[tws], writes=[tws])
                    for h in H4:
                        kb.op("dve", lambda e: e.tensor_scalar_mul(diagb[h][:, :], identf[0:L, 0:L], bb_[:, h:h + 1]), reads=[t_c, tbb], writes=[t_diagb[h]])
                    for h in H4:
                        kb.op("pe", lambda e: e.matmul(pA[0:L, 64 + 64 * h:128 + 64 * h], lhsT=onesf[0:L, 0:L], rhs=diagb[h][:, :], start=True, stop=True),
                              reads=[t_c, t_diagb[h]], writes=[tA])
                    for h in H4:
                        for i in range(2):
                            kb.op("pe", lambda e: e.matmul(pB[0:L, 64 * h:64 * h + 64], lhsT=kT_[:, 2 * h + i, :], rhs=qT_[:, 2 * h + i, :], start=(i == 0), stop=(i == 1)),
                                  reads=[tkT, tq], writes=[tB], signal=(i == 1))
                    for h in H4:
                        kb.op("dve", lambda e: e.tensor_scalar(Eh[h][:, :], pA[0:L, 64 + 64 * h:128 + 64 * h], lmb_[:, h:h + 1], 60.0, ALU.add, ALU.min),
                              reads=[tA, tlmb], writes=[t_Eh[h]])
                    for h in H4:
                        kb.op("act", lambda e: e.activation(out=Eh[h][:, :], in_=Eh[h][:, :], func=AF.Exp), reads=[t_Eh[h]], writes=[t_Eh[h]])
                    for h in H4:
                        kb.op("dve", lambda e: e.tensor_tensor(Eh[h][:, :], Eh[h][:, :], trid, ALU.mult), reads=[t_Eh[h], t_c], writes=[t_Eh[h]])
                    for h in H4:
                        kb.op("dve", lambda e: e.tensor_tensor(WT[h][:, :], Eh[h][:, :], pB[0:L, 64 * h:64 * h + 64], ALU.mult), reads=[t_Eh[h], tB], writes=[t_WT[h]])
                    for h in H4:
                        kb.op("act", lambda e: e.activation(out=kp[h][:, :], in_=kk_[:, h * 256:(h + 1) * 256], func=AF.Copy, scale=ws_[:, h:h + 1]),
                              reads=[tkk, tws], writes=[t_kp[h]])
                    for h in H4:
                        kb.op("pe", lambda e: e.matmul(pA[0:L, 320 + h:321 + h], lhsT=WT[h][:, :], rhs=self.ones[0:L, 0:1], start=True, stop=True),
                              reads=[t_WT[h], self.t_const], writes=[tA])
                        for i in range(2):
                            kb.op("pe", lambda e: e.matmul(pA[0:L, 328 + h:329 + h], lhsT=qT_[:, 2 * h + i, :], rhs=nb[d][h][:, i:i + 1], start=(i == 0), stop=(i == 1)),
                                  reads=[tq, t_nb[d][h]], writes=[tA], signal=(i == 1))
                        for i in range(2):
                            kb.op("pe", lambda e: e.matmul(pA[:, 336 + 2 * h + i:337 + 2 * h + i], lhsT=kp[h][:, i * 128:(i + 1) * 128], rhs=self.ones[0:L, 0:1],
                                                           start=True, stop=True),
                                  reads=[t_kp[h], self.t_const], writes=[tA])
                    for h in H4:
                        kb.op("act", lambda e: e.activation(out=den[h][:, 0:1], in_=pA[0:L, 320 + h:321 + h], func=AF.Copy), reads=[tA], writes=[t_den[h]])
                    for h in H4:
                        kb.op("dve", lambda e: e.scalar_tensor_tensor(den[h][:, 1:2], pA[0:L, 328 + h:329 + h], wprev_[:, h:h + 1], den[h][:, 0:1], ALU.mult, ALU.add),
                              reads=[tA, twp, t_den[h]], writes=[t_den[h]])
                    for h in H4:
                        for i in range(2):
                            kb.op("dve", lambda e: e.scalar_tensor_tensor(nn[d][h][:, i:i + 1], nn[d][h][:, i:i + 1], decay_[:, h:h + 1],
                                                                          pA[:, 336 + 2 * h + i:337 + 2 * h + i], ALU.mult, ALU.add),
                                  reads=[t_n[d][h], tdec, tA], writes=[t_n[d][h]])
                    for h in H4:
                        kb.op("act", lambda e: e.activation(out=den[h][:, 1:2], in_=den[h][:, 1:2], func=AF.Abs), reads=[t_den[h]], writes=[t_den[h]])
                    for h in H4:
                        kb.op("dve", lambda e: e.tensor_scalar_max(den[h][:, 1:2], den[h][:, 1:2], 1.0), reads=[t_den[h]], writes=[t_den[h]])
                    for h in H4:
                        kb.op("dve", lambda e: e.reciprocal(den[h][:, 1:2], den[h][:, 1:2]), reads=[t_den[h]], writes=[t_den[h]])
                    for h in H4:
                        vh = vv_[:, h * 512:(h + 1) * 512]
                        pN1, tN1 = ps[2 + h % 2], tps[2 + h % 2]
                        pN2, tN2 = ps[4 + h % 2], tps[4 + h % 2]
                        kb.op("pe", lambda e: e.matmul(pN1[0:L, :], lhsT=WT[h][:, :], rhs=vh, start=True, stop=True),
                              reads=[t_WT[h], tvv], writes=[tN1])
                        for i in range(2):
                            kb.op("pe", lambda e: e.matmul(pN2[0:L, :], lhsT=qT_[:, 2 * h + i, :], rhs=Cb[d][h][:, i, :], start=(i == 0), stop=(i == 1)),
                                  reads=[tq, t_Cb[d][h]], writes=[tN2], signal=(i == 1))
                        kb.op("act", lambda e: e.activation(out=n1[h][:, :], in_=pN1[0:L, :], func=AF.Copy), reads=[tN1], writes=[t_n1[h]])
                        kb.op("dve", lambda e: e.scalar_tensor_tensor(hh[h][:, :], pN2[0:L, :], wprev_[:, h:h + 1], n1[h][:, :], ALU.mult, ALU.add),
                              reads=[tN2, twp, t_n1[h]], writes=[t_hh[h]])
                        kb.op("act", lambda e: e.activation(out=hh[h][:, :], in_=hh[h][:, :], func=AF.Copy, scale=den[h][:, 1:2]), reads=[t_hh[h], t_den[h]], writes=[t_hh[h]])
                        kb.dma("act", self.H[d][tk0:tk0 + L, h * 512:(h + 1) * 512], hh[h][:, :], reads=[t_hh[h]], writes=[self.t_H[d]])
                    for h in H4:
                        vh = vv_[:, h * 512:(h + 1) * 512]
                        for i in range(2):
                            pDC, tDC = ps[6 + i], tps[6 + i]
                            kb.op("pe", lambda e: e.matmul(pDC[:, :], lhsT=kp[h][:, i * 128:(i + 1) * 128], rhs=vh, start=True, stop=True),
                                  reads=[t_kp[h], tvv], writes=[tDC])
                            kb.op("dve", lambda e: e.scalar_tensor_tensor(C[d][h][:, i, :], C[d][h][:, i, :], decay_[:, h:h + 1], pDC[:, :], ALU.mult, ALU.add),
                                  reads=[t_C[d][h], tdec, tDC], writes=[t_C[d][h]])
                            kb.op("act", lambda e: e.activation(out=Cb[d][h][:, i, :], in_=C[d][h][:, i, :], func=AF.Copy), reads=[t_C[d][h]], writes=[t_Cb[d][h]])
                        kb.op("act", lambda e: e.activation(out=nb[d][h][:, :], in_=nn[d][h][:, :], func=AF.Copy), reads=[t_n[d][h]], writes=[t_nb[d][h]])
            kb.barrier()

    def phaseR_ml(self, li):
        kb, nc = self.kb, self.nc
        S, T = self.S, self.T
        w = self.W[li]
        with ExitStack() as pes:
            sb = lambda name, shape, dt: pes.enter_context(nc.sbuf_tensor(self.uname(name), list(shape), dt))
            hg = sb("r_hg", [128, D], F32); t_hg = Tok()
            kb.dma("sp", hg[:], w["headg"][:, :], writes=[t_hg])
            hf = [sb(f"r_hf{i}", [128, D], F32) for i in range(2)]; t_hf = [Tok(), Tok()]
            hb_ = [sb(f"r_hb{i}", [128, D], F32) for i in range(2)]; t_hb = [Tok(), Tok()]
            og = [sb(f"r_og{i}", [128, D], BF16) for i in range(2)]; t_og = [Tok(), Tok()]
            sgm = sb("r_sg", [128, D], F32); t_sg = Tok()
            sqb = sb("r_sq", [128, D], F32); t_sqb = Tok()
            ss = sb("r_ss", [128, 4], F32); t_ss = Tok()
            y = sb("r_y", [128, D], BF16); t_y = Tok()
            ot = [sb(f"r_ot{i}", [128, 512], BF16) for i in range(2)]; t_ot = [Tok(), Tok()]
            cnt = 0
            for bi in range(T // 128):
                r0 = bi * 128
                b = bi % 2
                kb.dma("sp", hf[b][:], self.H[0][r0:r0 + 128, :], reads=[self.t_H[0]], writes=[t_hf[b]])
                kb.dma("sp", hb_[b][:], self.H[1][r0:r0 + 128, :], reads=[self.t_H[1]], writes=[t_hb[b]])
                kb.dma("sp", og[b][:], self.Og[r0:r0 + 128, :], reads=[self.t_Og], writes=[t_og[b]])
                kb.op("dve", lambda e: e.tensor_tensor(hf[b][:], hf[b][:], hb_[b][:], ALU.add), reads=[t_hf[b], t_hb[b]], writes=[t_hf[b]])
                kb.op("act", lambda e: e.activation(out=sqb[:], in_=hf[b][:], func=AF.Square), reads=[t_hf[b]], writes=[t_sqb])
                kb.op("dve", lambda e: e.reduce_sum(ss[:, :], sqb[:].rearrange("p (h n) -> p h n", h=4), mybir.AxisListType.X), reads=[t_sqb], writes=[t_ss])
                kb.op("dve", lambda e: e.tensor_scalar(ss[:, :], ss[:, :], 1.0 / 512, EPS, ALU.mult, ALU.add), reads=[t_ss], writes=[t_ss])
                kb.op("act", lambda e: e.activation(out=ss[:, :], in_=ss[:, :], func=AF.Sqrt), reads=[t_ss], writes=[t_ss])
                kb.op("dve", lambda e: e.reciprocal(ss[:, :], ss[:, :]), reads=[t_ss], writes=[t_ss])
                kb.op("act", lambda e: e.activation(out=sgm[:], in_=og[b][:], func=AF.Sigmoid), reads=[t_og[b]], writes=[t_sg])
                kb.op("pool", lambda e: e.tensor_tensor(sgm[:], sgm[:], hg[:], ALU.mult), reads=[t_sg, t_hg], writes=[t_sg])
                for h in range(4):
                    kb.op("dve", lambda e: e.scalar_tensor_tensor(y[:, h * 512:(h + 1) * 512], hf[b][:, h * 512:(h + 1) * 512], ss[:, h:h + 1],
                                                                  sgm[:, h * 512:(h + 1) * 512], ALU.mult, ALU.mult),
                          reads=[t_hf[b], t_ss, t_sg], writes=[t_y])
                for g4 in range(4):
                    pi = cnt % 2
                    ob = cnt % 2
                    cnt += 1
                    for jj in range(4):
                        j = g4 * 4 + jj
                        kb.op("pe", lambda e: e.matmul(self.ps[pi][:, jj * 128:(jj + 1) * 128], lhsT=y[:, j * 128:(j + 1) * 128], rhs=self.ident[:, :],
                                                       start=True, stop=True),
                              reads=[t_y, self.t_const], writes=[self.t_ps[pi]], signal=(jj == 3))
                    kb.op("act", lambda e: e.activation(out=ot[ob][:, :], in_=self.ps[pi][:, :], func=AF.Copy), reads=[self.t_ps[pi]], writes=[t_ot[ob]])
                    kb.dma("pool", self.OT[g4 * 512:(g4 + 1) * 512, r0:r0 + 128].rearrange("(jj p) n -> p jj n", p=128),
                           ot[ob][:, :].rearrange("p (jj n) -> p jj n", n=128), reads=[t_ot[ob]], writes=[self.t_OT])
            kb.barrier()

    def build(self):
        self.phase0()
        xi = 0
        for idx, (li, kind, need_ctx) in enumerate(self.layers):
            final = idx == len(self.layers) - 1
            self.kb.flush_bg(self.t_Wl[li])
            if kind == 2:
                self.phaseA_ml(li, xi)
                self.phaseB_ml(li)
                self.phaseR_ml(li)
            else:
                self.phaseA(li, kind, xi)
                self.phaseB(li, kind, need_ctx)
            self.phaseC(li, need_ctx, xi, final)
            xi = 1 - xi
        self.kb.barrier(engines=("sp",))
        self.es.close()
        return self.nc


def na_base_block(ci, SB):
    return min(max(4 * ci - 2, 0), SB - 8)


def na_patterns(S):
    rows = S // GW
    kr = min(8, rows)
    SB = S // 128
    pats, pat_of_C, keymap = [], [], {}
    for ci in range(S // 512):
        bbw = na_base_block(ci, SB)
        r0s = tuple(min(max(8 * ci + a - kr // 2, 0), rows - kr) - 8 * ci for a in range(8))
        key = (bbw - 4 * ci, r0s)
        if key not in keymap:
            keymap[key] = len(pats)
            pats.append(ci)
        pat_of_C.append(keymap[key])
    return pat_of_C, pats


def build_na_table(rel_bias, S):
    rows = S // GW
    kr = min(8, rows)
    SB = S // 128
    pat_of_C, pats = na_patterns(S)
    npat = len(pats)
    rb = np.asarray(rel_bias, np.float32)
    tab = np.full((npat, 16, 128, 8, 512), -30000.0, np.float32)
    i = np.arange(512)
    qc = i % GW
    c0 = np.clip(qc - 8, 0, GW - 16)
    j = np.arange(128)
    for p, ci in enumerate(pats):
        bbw = na_base_block(ci, SB)
        qr = 8 * ci + i // GW
        r0 = np.clip(qr - kr // 2, 0, rows - kr)
        for jb in range(8):
            kr_ = 2 * (bbw + jb) + j // GW
            kc_ = j % GW
            ok = ((kr_[:, None] >= r0[None, :]) & (kr_[:, None] < r0[None, :] + kr)
                  & (kc_[:, None] >= c0[None, :]) & (kc_[:, None] < c0[None, :] + 16))
            drow = np.clip(kr_[:, None] - qr[None, :] + 7, 0, 14)
            dcol = np.clip(kc_[:, None] - qc[None, :] + 15, 0, 30)
            vals = rb[:, drow, dcol]
            tab[p, :, :, jb, :] = np.where(ok[None], vals, np.float32(-30000.0))
    return tab.reshape(npat * 16 * 128, 8 * 512)


def swa_mask_table():
    j = np.arange(128)
    prev = (j[:, None] >= j[None, :]).astype(np.float32)
    nxt = (j[:, None] <= j[None, :]).astype(np.float32)
    t = np.zeros((128, 6, 4, 128), np.float32)
    for r in range(6):
        for a in range(4):
            rel = r - 1 - a
            if rel == -1:
                t[:, r, a, :] = prev
            elif rel == 0:
                t[:, r, a, :] = 1.0
            elif rel == 1:
                t[:, r, a, :] = nxt
    return t.reshape(128, 6 * 512).astype(ml_dtypes.bfloat16)


def rope_tables(S):
    t = np.arange(S)
    row = (t // GW).astype(np.float32)
    colp = (t % GW).astype(np.float32)
    inv = (10000.0 ** (-np.arange(32, dtype=np.float32) / 32)).astype(np.float32)
    cosT = np.zeros((128, S), np.float32)
    sinT = np.zeros((128, S), np.float32)
    for a, pos in enumerate((row, colp)):
        ang = (pos[None, :] * inv[:, None]).astype(np.float32)
        for p in range(2):
            cosT[a * 64 + p * 32:a * 64 + (p + 1) * 32] = np.cos(ang)
            sinT[a * 64 + p * 32:a * 64 + (p + 1) * 32] = np.sin(ang)
    rot = np.zeros((128, 128), np.float32)
    for a in range(2):
        for f in range(32):
            d1, d2 = a * 64 + f, a * 64 + 32 + f
            rot[d2, d1] = -1.0
            rot[d1, d2] = 1.0
    return cosT, sinT, rot


def fm(v, ncol):
    return np.ascontiguousarray(np.asarray(v, np.float32).reshape(ncol, 128).T)


def pack_sv(li, kind, P):
    sv = np.zeros((128, NSV), np.float32)
    sv[:, SV_ADAB:SV_ADAB + 96] = fm(P["ada_b"][li], 96)
    sv[:, SV_N1:SV_N1 + 16] = fm(P["norm1_g"][li], 16)
    sv[:, SV_N2:SV_N2 + 16] = fm(P["norm2_g"][li], 16)
    for j in range(3):
        sv[:, SV_CW + j * FC:SV_CW + (j + 1) * FC] = fm(P["ffn_conv_w"][li][j], FC)
    sv[:, SV_CB:SV_CB + FC] = fm(P["ffn_conv_b"][li], FC)
    pre = {0: "na", 1: "swa", 3: "gqa"}.get(kind)
    if pre:
        sv[:, SV_QG] = np.asarray(P[pre + "_q_g"][0], np.float32)
        sv[:, SV_KG] = np.asarray(P[pre + "_k_g"][0], np.float32)
    if kind == 1:
        sv[:, SV_SINK:SV_SINK + 16] = np.asarray(P["swa_sinks"][0], np.float32)[None, :]
    if kind == 2:
        sv[:, SV_SINK:SV_SINK + 16] = np.asarray(P["ml_gate_b"][0], np.float32)[None, :]
    return sv


def core_inputs(b, S, layers, P, consts):
    m = dict(consts)
    m["xT"] = np.ascontiguousarray(np.asarray(P["x"][b, :S], np.float32).T)
    m["cxT"] = np.ascontiguousarray(np.asarray(P["ctx"][b], np.float32).T)
    cc = np.stack([fm(P["c"][b], KC), fm(P["c_ctx"], KC)], axis=-1)
    m["cc"] = np.ascontiguousarray(cc)
    return m


def shared_inputs(S, layers, P):
    cosT, sinT, rot = rope_tables(S)
    bf = ml_dtypes.bfloat16
    j = np.arange(128)
    m = {
        "ones": np.ones((128, 128), bf), "ident": np.eye(128, dtype=np.float32).astype(bf), "rot": rot.astype(bf),
        "cosT": cosT, "sinT": sinT,
        "mprev": (j[:, None] >= j[None, :]).astype(np.float32).astype(bf),
        "mnext": (j[:, None] <= j[None, :]).astype(np.float32).astype(bf),
        "swam": swa_mask_table(),
    }
    if True:
        jj = np.arange(64)
        m["identf"] = np.eye(128, dtype=np.float32)
        m["onesf"] = np.ones((128, 128), np.float32)
        m["tri"] = np.concatenate([(jj[:, None] <= jj[None, :]), (jj[:, None] >= jj[None, :])], axis=1).astype(np.float32)
    mixw = {0: ("na_w_qkv", "na_w_o"), 1: ("swa_w_qkv", "swa_w_o"), 2: ("ml_w_in", "ml_w_o"), 3: ("gqa_w_qkv", "gqa_w_o")}
    for (li, kind, _) in layers:
        m[f"ada_w{li}"] = np.asarray(P["ada_w"][li], np.float32)
        m[f"ffn_w_in{li}"] = np.asarray(P["ffn_w_in"][li], np.float32)
        m[f"ffn_w_out{li}"] = np.asarray(P["ffn_w_out"][li], np.float32)
        m[f"mix_w_in{li}"] = np.asarray(P[mixw[kind][0]][0], np.float32)
        m[f"mix_w_o{li}"] = np.asarray(P[mixw[kind][1]][0], np.float32)
        m[f"sv{li}"] = pack_sv(li, kind, P)
        if kind == 0:
            m[f"natab{li}"] = build_na_table(P["na_rel_bias"][0], S)
        if kind == 2:
            m[f"headg{li}"] = np.ascontiguousarray(np.broadcast_to(np.asarray(P["ml_head_g"][0], np.float32)[None, :], (128, D)))
    return m


def run_model(P, S, layers, batches, trace=False, spread=False):
    prog = Prog(S, layers)
    nc = prog.build()
    shared = shared_inputs(S, layers, P)
    in_maps = [core_inputs(b, S, layers, P, shared) for b in batches]
    slots = list(range(len(batches)))
    if spread and len(batches) == 4:
        big = ("ada_w", "ffn_w_in", "ffn_w_out", "mix_w_in", "mix_w_o", "sv", "natab", "headg")
        zmap = {}
        for k, v in in_maps[0].items():
            if k.startswith(big) or k in ("xT", "cxT", "cc"):
                zmap[k] = np.zeros(v.shape, v.dtype)
            else:
                zmap[k] = v
        slots = [0, 1, 4, 5]
        full = [zmap] * 8
        full = list(full)
        for sl, m in zip(slots, in_maps):
            full[sl] = m
        in_maps = full
    res = run_bass_kernel_spmd(nc, in_maps, core_ids=list(range(len(in_maps))), trace=trace)
    outs = [np.ascontiguousarray(res.results[sl]["outT"].T) for sl in slots]
    return np.stack(outs, 0), res


def kernel(**inputs):
    S = inputs["x"].shape[1]
    layers = [(i, i % 4, i < 3) for i in range(4)]
    out, _ = run_model(inputs, S, layers, list(range(inputs["x"].shape[0])), spread=False)
    return out.astype(np.float32)
```
